# Optimizing a Trainium2 kernel written in Bass

```python
import jax, jax.numpy as jnp
from jax import lax
import numpy as np

D_MODEL = 1024
BATCH = 2
SEQ = 16384
DEPTH = 2
DEC_BATCH = 8
DEC_SEQ = 32
PAST_LEN = 1024

CHUNK = 64
Q_BLOCK = 128
N_EVEN = (DEPTH + 1) // 2
N_ODD = DEPTH // 2
FOX_HEADS = 8
FOX_DH = 64
FOX_W = FOX_HEADS * FOX_DH
FOX_FORGET_BIAS = 4.0
GDN_HEADS = 4
GDN_DK = 128
GDN_DV = 128
GDN_QK = GDN_HEADS * GDN_DK
GDN_VW = GDN_HEADS * GDN_DV
GDN_CONV_DIM = 2 * GDN_QK + GDN_VW
CONV_W = 4
S5_WIDTH = D_MODEL
S5_GROUP = 16
S5_GROUPS = S5_WIDTH // S5_GROUP
S5_STATE = 64
D_FF = ((8 * D_MODEL // 3 + 127) // 128) * 128
ALPHA = (2.0 * DEPTH) ** 0.25
BETA_INIT = (8.0 * DEPTH) ** -0.25
LN_EPS = 1e-5
NORM_EPS = 1e-6
EVEN_SPLITS = (FOX_W, FOX_W, FOX_W, FOX_HEADS, GDN_CONV_DIM, GDN_HEADS, GDN_HEADS, GDN_VW)
EVEN_IN = sum(EVEN_SPLITS)
MIX_W = FOX_W + GDN_VW
F32 = jnp.float32

kernel_name = 'hybrid_fox_gdn_s5_stream_step'


def layer_norm(x, g, b):
    xf = x.astype(F32)
    mu = jnp.mean(xf, -1, keepdims=True)
    var = jnp.mean(jnp.square(xf - mu), -1, keepdims=True)
    return ((xf - mu) * lax.rsqrt(var + LN_EPS) * g.astype(F32) + b.astype(F32)).astype(x.dtype)


def swiglu(x, w_in, w_out):
    a, b = jnp.split(x @ w_in, 2, axis=-1)
    return (jax.nn.silu(a) * b) @ w_out


def macaron_half(x, w_in, w_out, g, b):
    return layer_norm(ALPHA * x + 0.5 * swiglu(x, w_in, w_out), g, b)


def split_cols(y, sizes):
    return jnp.split(y, np.cumsum(sizes)[:-1].tolist(), axis=-1)


def l2norm(x):
    return x * lax.rsqrt(jnp.sum(jnp.square(x), -1, keepdims=True) + NORM_EPS)


def fox_prompt(q, k, v, logf):
    B, T, H, dh = q.shape
    nblk = T // Q_BLOCK
    c = jnp.cumsum(logf, axis=1).transpose(0, 2, 1)
    qb = q.reshape(B, nblk, Q_BLOCK, H, dh).transpose(1, 0, 2, 3, 4)
    cb = c.reshape(B, H, nblk, Q_BLOCK).transpose(2, 0, 1, 3)
    kpos = jnp.arange(T)

    def block(args):
        i, q_i, c_i = args
        s = jnp.einsum('bqhd,bkhd->bhqk', q_i, k, preferred_element_type=F32) * (dh ** -0.5)
        s = s + c_i[..., :, None] - c[..., None, :]
        mask = kpos[None, :] <= (i * Q_BLOCK + jnp.arange(Q_BLOCK))[:, None]
        p = jax.nn.softmax(jnp.where(mask, s, -jnp.inf), axis=-1)
        return jnp.einsum('bhqk,bkhd->bqhd', p.astype(v.dtype), v)

    o = lax.map(block, (jnp.arange(nblk), qb, cb))
    return o.transpose(1, 0, 2, 3, 4).reshape(B, T, H, dh)


def fox_sample(q, k, v, logf, past_k, past_v, past_logf):
    T = q.shape[1]
    P = past_k.shape[1]
    kk = jnp.concatenate([past_k.astype(k.dtype), k], 1)
    vv = jnp.concatenate([past_v.astype(v.dtype), v], 1)
    c = jnp.cumsum(jnp.concatenate([past_logf.astype(F32), logf], 1), axis=1).transpose(0, 2, 1)
    s = jnp.einsum('bqhd,bkhd->bhqk', q, kk, preferred_element_type=F32) * (FOX_DH ** -0.5)
    s = s + c[..., P:, None] - c[..., None, :]
    mask = jnp.arange(P + T)[None, :] <= (P + jnp.arange(T))[:, None]
    p = jax.nn.softmax(jnp.where(mask, s, -jnp.inf), axis=-1)
    return jnp.einsum('bhqk,bkhd->bqhd', p.astype(vv.dtype), vv)


def causal_conv(x, buf, w):
    xp = jnp.concatenate([buf.astype(x.dtype), x], 1)
    y = lax.conv_general_dilated(xp, w[:, None, :].astype(x.dtype), window_strides=(1,), padding='VALID',
                                 dimension_numbers=('NWC', 'WIO', 'NWC'), feature_group_count=x.shape[-1])
    return y, xp[:, -(CONV_W - 1):]


def gdn_chunked(q, k, v, g, beta, s0, L):
    B, T, H, _ = q.shape
    dv = v.shape[-1]
    n = T // L

    def chunks(x):
        return jnp.moveaxis(x.reshape((B, n, L, H) + x.shape[3:]), 3, 2)

    qc, kc, vc, gc, bc = chunks(q), chunks(k), chunks(v), chunks(g), chunks(beta)
    gcum = jnp.cumsum(gc, -1)
    diff = gcum[..., :, None] - gcum[..., None, :]
    tril = jnp.tril(jnp.ones((L, L), bool))
    strict = jnp.tril(jnp.ones((L, L), bool), -1)
    decay = jnp.where(tril, jnp.exp(jnp.where(tril, diff, 0.0)), 0.0)
    kb = kc * bc[..., None]
    nmat = jnp.where(strict, jnp.einsum('bnhid,bnhjd->bnhij', kb, kc) * decay, 0.0)
    eye = jnp.eye(L, dtype=F32)
    tmat = lax.linalg.triangular_solve(nmat, jnp.broadcast_to(eye, nmat.shape), left_side=True,
                                       lower=True, unit_diagonal=True)
    u = tmat @ (vc * bc[..., None])
    w = tmat @ (kb * jnp.exp(gcum)[..., None])
    attn = jnp.einsum('bnhid,bnhjd->bnhij', qc, kc) * decay
    qg = qc * jnp.exp(gcum)[..., None]
    kdec = kc * jnp.exp(gcum[..., -1:] - gcum)[..., None]
    glast = jnp.exp(gcum[..., -1])

    def step(s, inp):
        u_i, w_i, a_i, qg_i, kd_i, gl_i = inp
        v_new = u_i - w_i @ s
        o_i = qg_i @ s + a_i @ v_new
        s = s * gl_i[..., None, None] + jnp.swapaxes(kd_i, -1, -2) @ v_new
        return s, o_i

    xs = tuple(jnp.moveaxis(t, 1, 0) for t in (u, w, attn, qg, kdec, glast))
    s_fin, o = lax.scan(step, s0, xs)
    return o.transpose(1, 0, 3, 2, 4).reshape(B, T, H, dv), s_fin


def even_mixer(h, fox_past, gdn_s0, gdn_buf, w_in, b_f, conv_w, a_log, dt_bias, norm_g, w_out):
    Bt, T, _ = h.shape
    fq, fk, fv, ff, gqkv, ga, gb, gz = split_cols(h @ w_in, EVEN_SPLITS)
    fq = fq.reshape(Bt, T, FOX_HEADS, FOX_DH)
    fk = fk.reshape(Bt, T, FOX_HEADS, FOX_DH)
    fv = fv.reshape(Bt, T, FOX_HEADS, FOX_DH)
    logf = jax.nn.log_sigmoid((ff + b_f).astype(F32))
    if fox_past is None:
        o_fox = fox_prompt(fq, fk, fv, logf)
    else:
        o_fox = fox_sample(fq, fk, fv, logf, *fox_past)
    conv_out, new_buf = causal_conv(gqkv, gdn_buf, conv_w)
    gq, gk, gv = split_cols(jax.nn.silu(conv_out.astype(F32)), (GDN_QK, GDN_QK, GDN_VW))
    gq = l2norm(gq.reshape(Bt, T, GDN_HEADS, GDN_DK)) * (GDN_DK ** -0.5)
    gk = l2norm(gk.reshape(Bt, T, GDN_HEADS, GDN_DK))
    gv = gv.reshape(Bt, T, GDN_HEADS, GDN_DV)
    g = -jnp.exp(a_log.astype(F32)) * jax.nn.softplus(ga.astype(F32) + dt_bias.astype(F32))
    beta = jax.nn.sigmoid(gb.astype(F32))
    o_gdn, s_new = gdn_chunked(gq, gk, gv, g, beta, gdn_s0.astype(F32), min(T, CHUNK))
    z = gz.reshape(Bt, T, GDN_HEADS, GDN_DV).astype(F32)
    o_gdn = o_gdn * lax.rsqrt(jnp.mean(jnp.square(o_gdn), -1, keepdims=True) + NORM_EPS) * norm_g.astype(F32) * jax.nn.silu(z)
    mixed = jnp.concatenate([o_fox.reshape(Bt, T, FOX_W).astype(h.dtype),
                             o_gdn.reshape(Bt, T, GDN_VW).astype(h.dtype)], -1)
    return mixed @ w_out, (fk, fv, logf, s_new, new_buf)


def s5_scan(u, h0_re, h0_im, lam_re, lam_im, log_step, b_re, b_im, c_re, c_im, d):
    Bt, T, _ = u.shape
    L = min(T, CHUNK)
    n = T // L
    lam = lax.complex(lam_re.astype(F32), lam_im.astype(F32))
    lam_bar = jnp.exp(lam * jnp.exp(log_step.astype(F32))[:, None])
    b_bar = ((lam_bar - 1.0) / lam)[..., None] * lax.complex(b_re.astype(F32), b_im.astype(F32))
    c_mat = lax.complex(c_re.astype(F32), c_im.astype(F32))
    uc = u.reshape(Bt, n, L, S5_GROUPS, S5_GROUP).transpose(1, 0, 2, 3, 4).astype(jnp.complex64)

    def combine(e1, e2):
        a1, b1 = e1
        a2, b2 = e2
        return a1 * a2, a2 * b1 + b2

    def step(hc, u_i):
        bu = jnp.einsum('gpc,blgc->blgp', b_bar, u_i)
        bu = bu.at[:, 0].add(lam_bar * hc)
        _, states = lax.associative_scan(combine, (jnp.broadcast_to(lam_bar, bu.shape), bu), axis=1)
        y = jnp.real(jnp.einsum('gcp,blgp->blgc', c_mat, states))
        return states[:, -1], y

    h_fin, ys = lax.scan(step, lax.complex(h0_re.astype(F32), h0_im.astype(F32)), uc)
    y = ys.transpose(1, 0, 2, 3, 4).reshape(Bt, T, S5_WIDTH) + d.astype(F32) * u
    return y, jnp.real(h_fin), jnp.imag(h_fin)


def odd_mixer(h, h0_re, h0_im, w_in, lam_re, lam_im, log_step, b_re, b_im, c_re, c_im, d, glu_w, glu_b, w_out):
    u = (h @ w_in).astype(F32)
    y, h_re, h_im = s5_scan(u, h0_re, h0_im, lam_re, lam_im, log_step, b_re, b_im, c_re, c_im, d)
    zz = jax.nn.gelu(y)
    gated = zz * jax.nn.sigmoid(zz @ glu_w.astype(F32) + glu_b.astype(F32))
    return gated.astype(h.dtype) @ w_out, (h_re, h_im)


def stack_field(states, idx):
    return jnp.stack([s[idx] for s in states])


def setup_inputs(seed: int = 0) -> dict:
    key = jax.random.key(seed)
    ks = iter(jax.random.split(key, 40))

    def nrm(shape, scale):
        return scale * jax.random.normal(next(ks), shape, F32)

    def unif(shape, lo, hi):
        return jax.random.uniform(next(ks), shape, F32, lo, hi)

    x_prompt = nrm((BATCH, SEQ, D_MODEL), 1.0)
    x_sample = nrm((DEC_BATCH, DEC_SEQ, D_MODEL), 1.0)
    cache_fox_k = nrm((N_EVEN, DEC_BATCH, PAST_LEN, FOX_HEADS, FOX_DH), 1.0)
    cache_fox_v = nrm((N_EVEN, DEC_BATCH, PAST_LEN, FOX_HEADS, FOX_DH), 1.0)
    cache_fox_logf = jax.nn.log_sigmoid(FOX_FORGET_BIAS + nrm((N_EVEN, DEC_BATCH, PAST_LEN, FOX_HEADS), 1.0))
    state_gdn = nrm((N_EVEN, DEC_BATCH, GDN_HEADS, GDN_DK, GDN_DV), 0.1)
    state_gdn_conv = nrm((N_EVEN, DEC_BATCH, CONV_W - 1, GDN_CONV_DIM), 1.0)
    state_s5_re = nrm((N_ODD, DEC_BATCH, S5_GROUPS, S5_STATE), 0.1)
    state_s5_im = nrm((N_ODD, DEC_BATCH, S5_GROUPS, S5_STATE), 0.1)
    ffn_w_in = nrm((DEPTH, 2, D_MODEL, 2 * D_FF), D_MODEL ** -0.5)
    ffn_w_out = nrm((DEPTH, 2, D_FF, D_MODEL), BETA_INIT * D_FF ** -0.5)
    ln_g = 1.0 + nrm((DEPTH, 3, D_MODEL), 0.02)
    ln_b = nrm((DEPTH, 3, D_MODEL), 0.02)
    even_w_in = nrm((N_EVEN, D_MODEL, EVEN_IN), D_MODEL ** -0.5)
    fox_b_f = FOX_FORGET_BIAS + nrm((N_EVEN, FOX_HEADS), 0.1)
    gdn_conv_w = nrm((N_EVEN, CONV_W, GDN_CONV_DIM), CONV_W ** -0.5)
    gdn_a_log = jnp.log(unif((N_EVEN, GDN_HEADS), 1.0, 16.0))
    dt = jnp.exp(unif((N_EVEN, GDN_HEADS), float(np.log(1e-3)), float(np.log(1e-1))))
    gdn_dt_bias = dt + jnp.log(-jnp.expm1(-dt))
    gdn_norm_g = 1.0 + nrm((N_EVEN, GDN_DV), 0.02)
    even_w_out = nrm((N_EVEN, MIX_W, D_MODEL), BETA_INIT * MIX_W ** -0.5)
    odd_w_in = nrm((N_ODD, D_MODEL, S5_WIDTH), D_MODEL ** -0.5)
    s5_lam_re = -0.5 + nrm((N_ODD, S5_GROUPS, S5_STATE), 0.01)
    s5_lam_im = float(np.pi) * jnp.arange(S5_STATE, dtype=F32) + nrm((N_ODD, S5_GROUPS, S5_STATE), 0.01)
    s5_log_step = unif((N_ODD, S5_GROUPS), float(np.log(1e-3)), float(np.log(1e-1)))
    s5_b_re = nrm((N_ODD, S5_GROUPS, S5_STATE, S5_GROUP), (2.0 * S5_GROUP) ** -0.5)
    s5_b_im = nrm((N_ODD, S5_GROUPS, S5_STATE, S5_GROUP), (2.0 * S5_GROUP) ** -0.5)
    s5_c_re = nrm((N_ODD, S5_GROUPS, S5_GROUP, S5_STATE), (2.0 * S5_STATE) ** -0.5)
    s5_c_im = nrm((N_ODD, S5_GROUPS, S5_GROUP, S5_STATE), (2.0 * S5_STATE) ** -0.5)
    s5_d = nrm((N_ODD, S5_WIDTH), 1.0)
    s5_glu_w = nrm((N_ODD, S5_WIDTH, S5_WIDTH), S5_WIDTH ** -0.5)
    s5_glu_b = nrm((N_ODD, S5_WIDTH), 0.02)
    odd_w_out = nrm((N_ODD, S5_WIDTH, D_MODEL), BETA_INIT * S5_WIDTH ** -0.5)
    return {'x_prompt': x_prompt, 'x_sample': x_sample, 'cache_fox_k': cache_fox_k, 'cache_fox_v': cache_fox_v,
            'cache_fox_logf': cache_fox_logf, 'state_gdn': state_gdn, 'state_gdn_conv': state_gdn_conv,
            'state_s5_re': state_s5_re, 'state_s5_im': state_s5_im, 'ffn_w_in': ffn_w_in, 'ffn_w_out': ffn_w_out,
            'ln_g': ln_g, 'ln_b': ln_b, 'even_w_in': even_w_in, 'fox_b_f': fox_b_f, 'gdn_conv_w': gdn_conv_w,
            'gdn_a_log': gdn_a_log, 'gdn_dt_bias': gdn_dt_bias, 'gdn_norm_g': gdn_norm_g, 'even_w_out': even_w_out,
            'odd_w_in': odd_w_in, 's5_lam_re': s5_lam_re, 's5_lam_im': s5_lam_im, 's5_log_step': s5_log_step,
            's5_b_re': s5_b_re, 's5_b_im': s5_b_im, 's5_c_re': s5_c_re, 's5_c_im': s5_c_im, 's5_d': s5_d,
            's5_glu_w': s5_glu_w, 's5_glu_b': s5_glu_b, 'odd_w_out': odd_w_out}


def reference(x_prompt, x_sample, cache_fox_k, cache_fox_v, cache_fox_logf, state_gdn, state_gdn_conv,
              state_s5_re, state_s5_im, ffn_w_in, ffn_w_out, ln_g, ln_b, even_w_in, fox_b_f, gdn_conv_w,
              gdn_a_log, gdn_dt_bias, gdn_norm_g, even_w_out, odd_w_in, s5_lam_re, s5_lam_im, s5_log_step,
              s5_b_re, s5_b_im, s5_c_re, s5_c_im, s5_d, s5_glu_w, s5_glu_b, odd_w_out):
    xp, xs = x_prompt, x_sample
    bp = xp.shape[0]
    even_p, even_s, odd_p, odd_s = [], [], [], []
    for layer in range(DEPTH):
        xp = macaron_half(xp, ffn_w_in[layer, 0], ffn_w_out[layer, 0], ln_g[layer, 0], ln_b[layer, 0])
        xs = macaron_half(xs, ffn_w_in[layer, 0], ffn_w_out[layer, 0], ln_g[layer, 0], ln_b[layer, 0])
        j = layer // 2
        if layer % 2 == 0:
            ew = (even_w_in[j], fox_b_f[j], gdn_conv_w[j], gdn_a_log[j], gdn_dt_bias[j], gdn_norm_g[j], even_w_out[j])
            s0 = jnp.zeros((bp, GDN_HEADS, GDN_DK, GDN_DV), F32)
            buf0 = jnp.zeros((bp, CONV_W - 1, GDN_CONV_DIM), xp.dtype)
            mp, st_p = even_mixer(xp, None, s0, buf0, *ew)
            ms, st_s = even_mixer(xs, (cache_fox_k[j], cache_fox_v[j], cache_fox_logf[j]),
                                  state_gdn[j], state_gdn_conv[j], *ew)
            even_p.append(st_p)
            even_s.append(st_s)
        else:
            ow = (odd_w_in[j], s5_lam_re[j], s5_lam_im[j], s5_log_step[j], s5_b_re[j], s5_b_im[j],
                  s5_c_re[j], s5_c_im[j], s5_d[j], s5_glu_w[j], s5_glu_b[j], odd_w_out[j])
            h0 = jnp.zeros((bp, S5_GROUPS, S5_STATE), F32)
            mp, st_p = odd_mixer(xp, h0, h0, *ow)
            ms, st_s = odd_mixer(xs, state_s5_re[j], state_s5_im[j], *ow)
            odd_p.append(st_p)
            odd_s.append(st_s)
        xp = layer_norm(ALPHA * xp + mp, ln_g[layer, 1], ln_b[layer, 1])
        xs = layer_norm(ALPHA * xs + ms, ln_g[layer, 1], ln_b[layer, 1])
        xp = macaron_half(xp, ffn_w_in[layer, 1], ffn_w_out[layer, 1], ln_g[layer, 2], ln_b[layer, 2])
        xs = macaron_half(xs, ffn_w_in[layer, 1], ffn_w_out[layer, 1], ln_g[layer, 2], ln_b[layer, 2])
    y_prompt, y_sample = xp, xs
    fox_k_prompt = stack_field(even_p, 0)
    fox_v_prompt = stack_field(even_p, 1)
    fox_logf_prompt = stack_field(even_p, 2)
    gdn_state_prompt = stack_field(even_p, 3)
    gdn_conv_prompt = stack_field(even_p, 4)
    s5_re_prompt = stack_field(odd_p, 0)
    s5_im_prompt = stack_field(odd_p, 1)
    fox_k_sample = stack_field(even_s, 0)
    fox_v_sample = stack_field(even_s, 1)
    fox_logf_sample = stack_field(even_s, 2)
    gdn_state_sample = stack_field(even_s, 3)
    gdn_conv_sample = stack_field(even_s, 4)
    s5_re_sample = stack_field(odd_s, 0)
    s5_im_sample = stack_field(odd_s, 1)
    return (y_prompt, y_sample, fox_k_prompt, fox_v_prompt, fox_logf_prompt, gdn_state_prompt, gdn_conv_prompt,
            s5_re_prompt, s5_im_prompt, fox_k_sample, fox_v_sample, fox_logf_sample, gdn_state_sample,
            gdn_conv_sample, s5_re_sample, s5_im_sample)
```

```python
import numpy as np
import ml_dtypes
import concourse.bass as bass
import concourse.mybir as mybir
from concourse.bass_utils import run_bass_kernel_spmd

F32 = mybir.dt.float32
BF16 = mybir.dt.bfloat16
AF = mybir.ActivationFunctionType
ALU = mybir.AluOpType
AX = mybir.AxisListType

D = 1024
SEQ = 16384
NCORE = 8
TPC = 4096
NS = 32
DFF = 2816
ALPHA = 4.0 ** 0.25
LN_EPS = 1e-5
NORM_EPS = 1e-6
DEBUG = {}


class Buf:
    __slots__ = ("name", "w", "r", "excl")

    def __init__(self, name, excl=False):
        self.name = name
        self.w = {}
        self.r = {}
        self.excl = excl


class Eng:
    def __init__(self, name, sem, step):
        self.name = name
        self.sem = sem
        self.step = step
        self.count = 0
        self.waited = {}
        self.prog = []


class Prog:
    def __init__(self, nc, stack):
        self.nc = nc
        self.stack = stack
        self.eng = {}
        for n in ("pe", "act", "dve", "pool", "sp"):
            self.eng[n] = Eng(n, stack.enter_context(nc.semaphore("s_" + n)), 1)
        self.nslot = 6
        self.slots = {}
        self.slot_rr = {}
        for q in ("sp", "pool", "act"):
            self.slots[q] = [Eng(f"dma_{q}{i}", stack.enter_context(nc.semaphore(f"d_{q}{i}")), 16)
                             for i in range(self.nslot)]
            self.slot_rr[q] = 0
        self.ntile = 0
        self.cc = Eng("cc", stack.enter_context(nc.semaphore("s_cc")), 1)

    def sb(self, shape, dt, name=None):
        self.ntile += 1
        name = f"{name or 't'}_s{self.ntile}"
        t = self.stack.enter_context(self.nc.sbuf_tensor(name, list(shape), dt))
        return t

    def ps(self, shape, dt, name=None):
        self.ntile += 1
        name = f"{name or 'p'}_p{self.ntile}"
        t = self.stack.enter_context(self.nc.psum_tensor(name, list(shape), dt))
        return t

    def _wait(self, e, deps):
        for src, val in deps.items():
            if src is e and e.name == "pe":
                continue
            if e.waited.get(src, 0) < val:
                e.prog.append(("w", src.sem, val))
                e.waited[src] = val

    @staticmethod
    def _deps(reads, writes):
        deps = {}
        for b in reads:
            for s, v in b.w.items():
                if deps.get(s, 0) < v:
                    deps[s] = v
            if b.excl:
                for s, v in b.r.items():
                    if deps.get(s, 0) < v:
                        deps[s] = v
        for b in writes:
            for s, v in b.w.items():
                if deps.get(s, 0) < v:
                    deps[s] = v
            for s, v in b.r.items():
                if deps.get(s, 0) < v:
                    deps[s] = v
        return deps

    def op(self, en, fn, reads=(), writes=()):
        e = self.eng[en]
        self._wait(e, self._deps(reads, writes))
        e.count += 1
        e.prog.append(("o", fn, e.sem, 1))
        for b in reads:
            if b.r.get(e, 0) < e.count:
                b.r[e] = e.count
        for b in writes:
            b.w = {e: e.count}
            b.r = {}

    def dma(self, q, fn, reads=(), writes=()):
        e = self.eng[q]
        sl = self.slots[q][self.slot_rr[q] % self.nslot]
        self.slot_rr[q] += 1
        deps = self._deps(reads, writes)
        if sl.count:
            deps[sl] = max(deps.get(sl, 0), sl.count)
        self._wait(e, deps)
        sl.count += 16
        e.prog.append(("o", fn, sl.sem, 16))
        for b in reads:
            b.r[sl] = sl.count
        for b in writes:
            b.w = {sl: sl.count}
            b.r = {}

    def wait_all(self, en, bufs):
        e = self.eng[en]
        deps = {}
        for b in bufs:
            for s, v in list(b.w.items()) + list(b.r.items()):
                if deps.get(s, 0) < v:
                    deps[s] = v
        self._wait(e, deps)

    def finish(self):
        e = self.eng["sp"]
        deps = {}
        for x in self.eng.values():
            if x.count:
                deps[x] = x.count
        for q in self.slots:
            for sl in self.slots[q]:
                if sl.count:
                    deps[sl] = sl.count
        self._wait(e, deps)

    def barrier(self):
        deps = {}
        for x in list(self.eng.values()) + [self.cc]:
            if x.count:
                deps[x] = x.count
        for q in self.slots:
            for sl in self.slots[q]:
                if sl.count:
                    deps[sl] = sl.count
        for e in self.eng.values():
            self._wait(e, dict(deps))

    def coll(self, fn, reads=(), writes=()):
        e = self.eng["pool"]
        self._wait(e, self._deps(reads, writes))
        self.cc.count += 1
        e.prog.append(("o", fn, self.cc.sem, 1))
        for b in reads:
            b.r[self.cc] = self.cc.count
        for b in writes:
            b.w = {self.cc: self.cc.count}
            b.r = {}

    def emit(self):
        nc = self.nc
        progs = {k: Eng(k, None, 1) for k in self.eng}
        for k in self.eng:
            progs[k].prog = self.eng[k].prog
            self.eng[k].prog = []

        def run(h, prog):
            for it in prog:
                if it[0] == "w":
                    h.wait_ge(it[1], it[2])
                else:
                    ins = it[1](h)
                    ins.then_inc(it[2], it[3])

        with nc.Block() as block:
            @block.sync
            def _(h):
                run(h, progs["sp"].prog)

            @block.tensor
            def _(h):
                run(h, progs["pe"].prog)

            @block.scalar
            def _(h):
                run(h, progs["act"].prog)

            @block.vector
            def _(h):
                run(h, progs["dve"].prog)

            @block.gpsimd
            def _(h):
                run(h, progs["pool"].prog)


class Rot:
    def __init__(self, P, n, shape, dt, name, psum=False):
        self.items = []
        for i in range(n):
            t = P.ps(shape, dt, f"{name}{i}") if psum else P.sb(shape, dt, f"{name}{i}")
            self.items.append((t, Buf(f"{name}{i}", excl=psum)))
        self.i = 0

    def next(self):
        it = self.items[self.i % len(self.items)]
        self.i += 1
        return it


class LNBase:
    def _ln_alloc(self, P, consts, NT=512):
        self.P = P
        self.c = consts
        self.sq = Rot(P, 2, [128, NT], F32, "sq_")
        self.ps1 = P.ps([128, NT], F32, "ps1")
        self.ps1b = Buf("ps1", excl=True)
        self.ps2 = P.ps([128, NT], F32, "ps2")
        self.ps2b = Buf("ps2", excl=True)
        self.mean = P.sb([128, NT], F32, "mean")
        self.meanb = Buf("mean")
        self.rstd = P.sb([128, NT], F32, "rstd")
        self.rstdb = Buf("rstd")
        self.msq = P.sb([128, NT], F32, "msq")
        self.msqb = Buf("msq")


class TPhase(LNBase):
    def __init__(self, P, consts):
        NT = 512
        self.NT = NT
        self._ln_alloc(P, consts, NT)
        self.x32 = Rot(P, 2, [128, 8, NT], F32, "x32_")
        self.xb = Rot(P, 2, [128, 8, NT], BF16, "xb_")
        self.h = P.sb([128, 22, NT], BF16, "h")
        self.hbuf = [Buf(f"h{j}") for j in range(22)]
        self.win = Rot(P, 4, [128, 8, 512], BF16, "win_")
        self.wstage = Rot(P, 2, [128, 8, 512], F32, "wst_")
        self.wout = P.sb([128, 22, 1024], BF16, "wout")
        self.woutb = Buf("wout")
        self.tmp = Rot(P, 4, [128, NT], F32, "silu_")
        self.pa = Rot(P, 2, [128, NT], F32, "pa_", psum=True)
        self.pb = Rot(P, 2, [128, NT], F32, "pb_", psum=True)
        self.po = Rot(P, 2, [128, NT], F32, "po_", psum=True)

    def load_wout(self, w_out_ap):
        P = self.P
        for i in range(11):
            ws, wsb = self.wstage.next()
            P.dma("act", lambda h, i=i, ws=ws: h.dma_start(
                out=ws[:, 0:4, :], in_=w_out_ap[:, i * 2048:(i + 1) * 2048].rearrange("p (j c) -> p j c", c=512)),
                writes=[wsb])
            P.op("pool", lambda h, i=i, ws=ws: h.tensor_copy(
                out=self.wout[:, 2 * i:2 * i + 2, :].rearrange("p j (a c) -> p (j a) c", c=512), in_=ws[:, 0:4, :]),
                reads=[wsb], writes=[self.woutb])

    def layer_norm(self, r32, r32b, xob, xobb, nt, g_ap, b_ap, eps):
        P = self.P
        c = self.c
        ones = c["ones_f32"]
        for m in range(8):
            sq, sqb = self.sq.next()
            P.op("act", lambda h, sq=sq, m=m: h.activation(out=sq[:, :nt], in_=r32[:, m, :nt], func=AF.Square),
                 reads=[r32b], writes=[sqb])
            P.op("pe", lambda h, m=m: h.matmul(self.ps1[:, :nt], lhsT=ones[:, :], rhs=r32[:, m, :nt],
                                               start=(m == 0), stop=(m == 7)),
                 reads=[r32b, c["constb"]], writes=[self.ps1b])
            P.op("pe", lambda h, m=m, sq=sq: h.matmul(self.ps2[:, :nt], lhsT=ones[:, :], rhs=sq[:, :nt],
                                                      start=(m == 0), stop=(m == 7)),
                 reads=[sqb, c["constb"]], writes=[self.ps2b])
        P.op("dve", lambda h: h.tensor_scalar(out=self.mean[:, :nt], in0=self.ps1[:, :nt], scalar1=1.0 / D,
                                              scalar2=None, op0=ALU.mult),
             reads=[self.ps1b], writes=[self.meanb])
        P.op("dve", lambda h: h.tensor_tensor(out=self.msq[:, :nt], in0=self.mean[:, :nt], in1=self.mean[:, :nt],
                                              op=ALU.mult),
             reads=[self.meanb], writes=[self.msqb])
        P.op("dve", lambda h: h.scalar_tensor_tensor(out=self.rstd[:, :nt], in0=self.ps2[:, :nt], scalar=1.0 / D,
                                                     in1=self.msq[:, :nt], op0=ALU.mult, op1=ALU.subtract),
             reads=[self.ps2b, self.msqb], writes=[self.rstdb])
        P.op("dve", lambda h: h.tensor_scalar(out=self.rstd[:, :nt], in0=self.rstd[:, :nt], scalar1=eps,
                                              scalar2=None, op0=ALU.add),
             reads=[self.rstdb], writes=[self.rstdb])
        P.op("act", lambda h: h.activation(out=self.rstd[:, :nt], in_=self.rstd[:, :nt], func=AF.Sqrt),
             reads=[self.rstdb], writes=[self.rstdb])
        P.op("dve", lambda h: h.reciprocal(out=self.rstd[:, :nt], in_=self.rstd[:, :nt]),
             reads=[self.rstdb], writes=[self.rstdb])
        xo32, xo32b = r32, r32b
        for m in range(8):
            P.op("pool", lambda h, m=m: h.tensor_tensor(out=r32[:, m, :nt], in0=r32[:, m, :nt], in1=self.mean[:, :nt],
                                                        op=ALU.subtract),
                 reads=[r32b, self.meanb], writes=[r32b])
            P.op("dve", lambda h, m=m: h.tensor_tensor(out=r32[:, m, :nt], in0=r32[:, m, :nt], in1=self.rstd[:, :nt],
                                                       op=ALU.mult),
                 reads=[r32b, self.rstdb], writes=[r32b])
            P.op("act", lambda h, m=m: h.activation(out=xo32[:, m, :nt], in_=r32[:, m, :nt], func=AF.Identity,
                                                    scale=g_ap[:, m:m + 1], bias=b_ap[:, m:m + 1]),
                 reads=[r32b, c["constb"]], writes=[xo32b])
            P.op("pool", lambda h, m=m: h.tensor_copy(out=xob[:, m, :nt], in_=xo32[:, m, :nt]),
                 reads=[xo32b], writes=[xobb])
        return xo32, xo32b, xob, xobb

    def ffn_ln(self, x32, x32b, xb, xbb, nt, w_in_ap, g_ap, b_ap):
        P = self.P
        jq = []
        for jp in range(11):
            w, wb = self.win.next()
            ws, wsb = self.wstage.next()
            P.dma("act", lambda h, ws=ws, jp=jp: h.dma_start(
                out=ws[:, :, :], in_=w_in_ap[jp].rearrange("p (k c) -> p k c", c=512)), writes=[wsb])
            P.op("pool", lambda h, w=w, ws=ws: h.tensor_copy(out=w[:, :, :], in_=ws[:, :, :]), reads=[wsb],
                 writes=[wb])
            for jj in range(2):
                j = jp * 2 + jj
                pa, pab = self.pa.next()
                pb, pbb = self.pb.next()
                for kt in range(8):
                    P.op("pe", lambda h, pa=pa, w=w, kt=kt, jj=jj: h.matmul(
                        pa[:, :nt], lhsT=w[:, kt, jj * 128:(jj + 1) * 128], rhs=xb[:, kt, :nt],
                        start=(kt == 0), stop=(kt == 7)), reads=[wb, xbb], writes=[pab])
                for kt in range(8):
                    P.op("pe", lambda h, pb=pb, w=w, kt=kt, jj=jj: h.matmul(
                        pb[:, :nt], lhsT=w[:, kt, 256 + jj * 128:256 + (jj + 1) * 128], rhs=xb[:, kt, :nt],
                        start=(kt == 0), stop=(kt == 7)), reads=[wb, xbb], writes=[pbb])
                t, tb = self.tmp.next()
                P.op("act", lambda h, t=t, pa=pa: h.activation(out=t[:, :nt], in_=pa[:, :nt], func=AF.Silu),
                     reads=[pab], writes=[tb])
                P.op("dve", lambda h, t=t, pb=pb, j=j: h.tensor_tensor(out=self.h[:, j, :nt], in0=t[:, :nt],
                                                                     in1=pb[:, :nt], op=ALU.mult),
                     reads=[tb, pbb], writes=[self.hbuf[j]])
        for m in range(8):
            po, pob = self.po.next()
            for j in range(22):
                P.op("pe", lambda h, po=po, j=j, m=m: h.matmul(
                    po[:, :nt], lhsT=self.wout[:, j, m * 128:(m + 1) * 128], rhs=self.h[:, j, :nt],
                    start=(j == 0), stop=(j == 21)), reads=[self.woutb, self.hbuf[j]], writes=[pob])
            P.op("dve", lambda h, po=po, m=m: h.scalar_tensor_tensor(
                out=x32[:, m, :nt], in0=po[:, :nt], scalar=0.5 / ALPHA, in1=x32[:, m, :nt],
                op0=ALU.mult, op1=ALU.add), reads=[pob, x32b], writes=[x32b])
        return self.layer_norm(x32, x32b, xb, xbb, nt, g_ap, b_ap, LN_EPS / (ALPHA * ALPHA))


LNBase.layer_norm = TPhase.layer_norm

def _tp_ffn_in(self, xb, xbbs, nt, w_in_ap, pending):
    P = self.P
    for jp in range(11):
        w, wb = w_in_ap()
        for jj in range(2):
            j = jp * 2 + jj
            pa, pab = self.pa.next()
            pb, pbb = self.pb.next()
            for kt in range(8):
                P.op("pe", lambda h, pa=pa, w=w, kt=kt, jj=jj: h.matmul(
                    pa[:, :nt], lhsT=w[:, kt, jj * 128:(jj + 1) * 128], rhs=xb[:, kt, :nt],
                    start=(kt == 0), stop=(kt == 7)), reads=[wb, xbbs[kt]], writes=[pab])
            for kt in range(8):
                P.op("pe", lambda h, pb=pb, w=w, kt=kt, jj=jj: h.matmul(
                    pb[:, :nt], lhsT=w[:, kt, 256 + jj * 128:256 + (jj + 1) * 128], rhs=xb[:, kt, :nt],
                    start=(kt == 0), stop=(kt == 7)), reads=[wb, xbbs[kt]], writes=[pbb])
            t, tb = self.tmp.next()
            P.op("act", lambda h, t=t, pa=pa: h.activation(out=t[:, :nt], in_=pa[:, :nt], func=AF.Silu),
                 reads=[pab], writes=[tb])
            P.op("dve", lambda h, t=t, pb=pb, j=j: h.tensor_tensor(out=self.h[:, j, :nt], in0=t[:, :nt],
                                                                 in1=pb[:, :nt], op=ALU.mult),
                 reads=[tb, pbb], writes=[self.hbuf[j]])
        if pending:
            pending.pop(0)()
    while pending:
        pending.pop(0)()


def _tp_ffn_out(self, x32, x32bs, nt, eps):
    P, c = self.P, self.c
    ones = c["ones_f32"]
    sqs = []
    for m in range(8):
        po, pob = self.po.next()
        for j in range(22):
            P.op("pe", lambda h, po=po, j=j, m=m: h.matmul(
                po[:, :nt], lhsT=self.wout[:, j, m * 128:(m + 1) * 128], rhs=self.h[:, j, :nt],
                start=(j == 0), stop=(j == 21)), reads=[self.woutb, self.hbuf[j]], writes=[pob])
        P.op("dve", lambda h, po=po, m=m: h.scalar_tensor_tensor(
            out=x32[:, m, :nt], in0=po[:, :nt], scalar=0.5 / ALPHA, in1=x32[:, m, :nt],
            op0=ALU.mult, op1=ALU.add), reads=[pob, x32bs[m]], writes=[x32bs[m]])
        sq, sqb = self.sq.next()
        P.op("act", lambda h, sq=sq, m=m: h.activation(out=sq[:, :nt], in_=x32[:, m, :nt], func=AF.Square),
             reads=[x32bs[m]], writes=[sqb])
        sqs.append((m, sq, sqb))
        if len(sqs) > 1:
            self._stat_mm(x32, x32bs, nt, *sqs.pop(0))
    while sqs:
        self._stat_mm(x32, x32bs, nt, *sqs.pop(0))
    P.op("dve", lambda h: h.tensor_scalar(out=self.mean[:, :nt], in0=self.ps1[:, :nt], scalar1=1.0 / D,
                                          scalar2=None, op0=ALU.mult), reads=[self.ps1b], writes=[self.meanb])
    P.op("dve", lambda h: h.tensor_tensor(out=self.msq[:, :nt], in0=self.mean[:, :nt], in1=self.mean[:, :nt],
                                          op=ALU.mult), reads=[self.meanb], writes=[self.msqb])
    P.op("dve", lambda h: h.scalar_tensor_tensor(out=self.rstd[:, :nt], in0=self.ps2[:, :nt], scalar=1.0 / D,
                                                 in1=self.msq[:, :nt], op0=ALU.mult, op1=ALU.subtract),
         reads=[self.ps2b, self.msqb], writes=[self.rstdb])
    P.op("dve", lambda h: h.tensor_scalar(out=self.rstd[:, :nt], in0=self.rstd[:, :nt], scalar1=eps,
                                          scalar2=None, op0=ALU.add), reads=[self.rstdb], writes=[self.rstdb])
    P.op("act", lambda h: h.activation(out=self.rstd[:, :nt], in_=self.rstd[:, :nt], func=AF.Sqrt),
         reads=[self.rstdb], writes=[self.rstdb])
    P.op("dve", lambda h: h.reciprocal(out=self.rstd[:, :nt], in_=self.rstd[:, :nt]),
         reads=[self.rstdb], writes=[self.rstdb])
    P.op("dve", lambda h: h.scalar_tensor_tensor(out=self.msq[:, :nt], in0=self.mean[:, :nt], scalar=-1.0,
                                                 in1=self.rstd[:, :nt], op0=ALU.mult, op1=ALU.mult),
         reads=[self.meanb, self.rstdb], writes=[self.msqb])


def _tp_stat_mm(self, x32, x32bs, nt, m, sq, sqb):
    P, c = self.P, self.c
    ones = c["ones_f32"]
    P.op("pe", lambda h: h.matmul(self.ps1[:, :nt], lhsT=ones[:, :], rhs=x32[:, m, :nt], start=(m == 0), stop=(m == 7)),
         reads=[x32bs[m], c["constb"]], writes=[self.ps1b])
    P.op("pe", lambda h: h.matmul(self.ps2[:, :nt], lhsT=ones[:, :], rhs=sq[:, :nt], start=(m == 0), stop=(m == 7)),
         reads=[sqb, c["constb"]], writes=[self.ps2b])


def _tp_norm_items(self, x32, x32bs, xb, xbbs, nt, g_ap, b_ap):
    P, c = self.P, self.c
    items = []
    for m in range(8):
        def item(m=m):
            t, tb = self.tmp.next()
            P.op("pool", lambda h: h.tensor_tensor(out=t[:, :nt], in0=x32[:, m, :nt], in1=self.rstd[:, :nt], op=ALU.mult),
                 reads=[x32bs[m], self.rstdb], writes=[tb])
            P.op("dve", lambda h: h.tensor_tensor(out=t[:, :nt], in0=t[:, :nt], in1=self.msq[:, :nt], op=ALU.add),
                 reads=[tb, self.msqb], writes=[tb])
            P.op("act", lambda h: h.activation(out=x32[:, m, :nt], in_=t[:, :nt], func=AF.Identity,
                                               scale=g_ap[:, m:m + 1], bias=b_ap[:, m:m + 1]),
                 reads=[tb, c["constb"]], writes=[x32bs[m]])
            P.op("pool", lambda h: h.tensor_copy(out=xb[:, m, :nt], in_=x32[:, m, :nt]), reads=[x32bs[m]],
                 writes=[xbbs[m]])
        items.append(item)
    return items


TPhase.ffn_in = _tp_ffn_in
LNBase._stat_mm = _tp_stat_mm
LNBase.norm_items = _tp_norm_items
TPhase.ffn_out = _tp_ffn_out


def _ln_stats(self, x32, x32bs, nt, eps):
    P = self.P
    for m in range(8):
        sq, sqb = self.sq.next()
        P.op("act", lambda h, sq=sq, m=m: h.activation(out=sq[:, :nt], in_=x32[:, m, :nt], func=AF.Square),
             reads=[x32bs[m]], writes=[sqb])
        self._stat_mm(x32, x32bs, nt, m, sq, sqb)
    P.op("dve", lambda h: h.tensor_scalar(out=self.mean[:, :nt], in0=self.ps1[:, :nt], scalar1=1.0 / D,
                                          scalar2=None, op0=ALU.mult), reads=[self.ps1b], writes=[self.meanb])
    P.op("dve", lambda h: h.tensor_tensor(out=self.msq[:, :nt], in0=self.mean[:, :nt], in1=self.mean[:, :nt],
                                          op=ALU.mult), reads=[self.meanb], writes=[self.msqb])
    P.op("dve", lambda h: h.scalar_tensor_tensor(out=self.rstd[:, :nt], in0=self.ps2[:, :nt], scalar=1.0 / D,
                                                 in1=self.msq[:, :nt], op0=ALU.mult, op1=ALU.subtract),
         reads=[self.ps2b, self.msqb], writes=[self.rstdb])
    P.op("dve", lambda h: h.tensor_scalar(out=self.rstd[:, :nt], in0=self.rstd[:, :nt], scalar1=eps,
                                          scalar2=None, op0=ALU.add), reads=[self.rstdb], writes=[self.rstdb])
    P.op("act", lambda h: h.activation(out=self.rstd[:, :nt], in_=self.rstd[:, :nt], func=AF.Sqrt),
         reads=[self.rstdb], writes=[self.rstdb])
    P.op("dve", lambda h: h.reciprocal(out=self.rstd[:, :nt], in_=self.rstd[:, :nt]),
         reads=[self.rstdb], writes=[self.rstdb])
    P.op("dve", lambda h: h.scalar_tensor_tensor(out=self.msq[:, :nt], in0=self.mean[:, :nt], scalar=-1.0,
                                                 in1=self.rstd[:, :nt], op0=ALU.mult, op1=ALU.mult),
         reads=[self.meanb, self.rstdb], writes=[self.msqb])


LNBase.ln_stats = _ln_stats
TPhase._stat_mm = _tp_stat_mm
TPhase.norm_items = _tp_norm_items


C_ONES, C_ID, C_MASK, C_SEL64, C_E, C_MASKT = 0, 128, 256, 384, 448, 520
NCST = 520 + 128
MASKNEG = -30000.0


def make_cst():
    cst = np.zeros((128, NCST), np.float32)
    cst[:, C_ONES:C_ONES + 128] = 1.0
    cst[:, C_ID:C_ID + 128] = np.eye(128, dtype=np.float32)
    k = np.arange(128)[:, None]
    q = np.arange(128)[None, :]
    cst[:, C_MASK:C_MASK + 128] = np.where(k > q, MASKNEG, 0.0)
    cst[64, C_SEL64:C_SEL64 + 64] = 1.0
    cst[:, C_MASKT:C_MASKT + 128] = np.where(q > k, MASKNEG, 0.0)
    for h in range(8):
        for x in range(3):
            cst[h, C_E + (h * 3 + x) * 3 + x] = 1.0
    return cst


class StopPhase(Exception):
    pass


def ck(n):
    if DEBUG.get("stopat", 10 ** 9) <= n:
        raise StopPhase()


class Consts:
    def __init__(self, P, nc, cst_ap):
        self.b = Buf("const")
        self.f32 = P.sb([128, NCST], F32, "cst_f32")
        self.bf = P.sb([128, NCST], BF16, "cst_bf")
        P.dma("sp", lambda h: h.dma_start(out=self.f32[:, :], in_=cst_ap[:, :]), writes=[self.b])
        P.op("dve", lambda h: h.tensor_copy(out=self.bf[:, :], in_=self.f32[:, :]), reads=[self.b], writes=[self.b])


class Fox:
    def __init__(self, P, C, H, NK, NQ, wf, nbf):
        self.P, self.C, self.H, self.NK, self.NQ = P, C, H, NK, NQ
        self.NKB = (NK + 127) // 128
        self.wf, self.nbf = wf, nbf
        self.KT = [P.sb([67, NK], BF16, f"KT{h}") for h in range(H)]
        self.KTb = [Buf(f"KT{h}") for h in range(H)]
        self.VA = P.sb([128, self.NKB, H, 65], BF16, "VA")
        self.VAb = Buf("VA")
        self.CK = P.sb([128, self.NKB, H], F32, "CK")
        self.CKb = Buf("CK")
        self.QA = [P.sb([67, NQ], BF16, f"QA{h}") for h in range(H)]
        self.QAb = [Buf(f"QA{h}") for h in range(H)]
        self.carry = P.sb([H, 1], F32, "fcarry")
        self.carryb = Buf("fcarry")
        self.onesr = P.sb([H, 512], F32, "onesr")
        self.sp_ = P.sb([H, 512], F32, "fsp")
        self.spb = Buf("fsp")
        self.cp = P.sb([H, 512], F32, "fcp")
        self.cpb = Buf("fcp")
        self.v8 = P.sb([H, 512], F32, "fv8")
        self.v8b = Buf("fv8")
        self.hml = [P.sb([H, 512], BF16, f"fhml{x}") for x in range(3)]
        self.hmlb = Buf("fhml")
        self.pt = Rot(P, 6, [128, 512], BF16, "fpt_")
        self.osb = Rot(P, 2, [65, 512], F32, "fosb_")
        self.rec = Rot(P, 2, [64, 512], F32, "frec_")
        self.omix = Rot(P, 2, [64, 512], BF16, "fomix_")
        for h in range(H):
            P.op("pool", lambda hh, h=h: hh.memset(self.KT[h][64:67, :], 1.0), writes=[self.KTb[h]])
        P.op("pool", lambda hh: hh.memset(self.VA[:, :, :, 64:65], 1.0), writes=[self.VAb])
        P.op("pool", lambda hh: hh.memset(self.onesr[:, :], 1.0), writes=[self.spb])
        P.op("pool", lambda hh: hh.memset(self.carry[:, :], 0.0), writes=[self.carryb])

    def logf_chain(self, pff, pffb, n, first, logf_out=None):
        P, H = self.P, self.H
        P.op("act", lambda h: h.activation(out=self.sp_[:, :n], in_=pff[0:H, :n], func=AF.Exp, scale=-1.0,
                                           bias=self.nbf[:, 0:1]), reads=[pffb], writes=[self.spb])
        P.op("act", lambda h: h.activation(out=self.sp_[:, :n], in_=self.sp_[:, :n], func=AF.Ln, bias=1.0),
             reads=[self.spb], writes=[self.spb])
        self.scan(n)

    def scan(self, n):
        P, H = self.P, self.H
        P.op("dve", lambda h: h.tensor_tensor_scan(out=self.cp[:, :n], data0=self.onesr[:, :n], data1=self.sp_[:, :n],
                                                   initial=self.carry[:, 0:1], op0=ALU.mult, op1=ALU.add),
             reads=[self.spb, self.carryb], writes=[self.cpb])
        P.op("dve", lambda h: h.tensor_copy(out=self.carry[:, 0:1], in_=self.cp[:, n - 1:n]),
             reads=[self.cpb], writes=[self.carryb])

    def split_q(self, n, off=0):
        P = self.P
        P.op("dve", lambda h: h.tensor_scalar(out=self.v8[:, :n], in0=self.cp[:, off:off + n], scalar1=-8.0,
                                              scalar2=None, op0=ALU.mult), reads=[self.cpb], writes=[self.v8b])
        for x in range(3):
            P.op("dve", lambda h, x=x: h.tensor_copy(out=self.hml[x][:, :n], in_=self.v8[:, :n]),
                 reads=[self.v8b], writes=[self.hmlb])
            if x < 2:
                P.op("dve", lambda h, x=x: h.tensor_tensor(out=self.v8[:, :n], in0=self.v8[:, :n],
                                                           in1=self.hml[x][:, :n], op=ALU.subtract),
                     reads=[self.v8b, self.hmlb], writes=[self.v8b])

    def ck_block(self, pst, pstb, kb, col0, nk):
        P, H, C = self.P, self.H, self.C
        P.op("pe", lambda h: h.matmul(pst[0:nk, 0:H], lhsT=self.cp[:, col0:col0 + nk],
                                      rhs=C.f32[0:H, C_ID:C_ID + H], start=True, stop=True),
             reads=[self.cpb, C.b], writes=[pstb])
        P.op("dve", lambda h: h.tensor_copy(out=self.CK[0:nk, kb, :], in_=pst[0:nk, 0:H]),
             reads=[pstb], writes=[self.CKb])

    def q_aug(self, pq, pqb, h, n):
        P, C = self.P, self.C
        for x in range(3):
            c0 = C_E + (h * 3 + x) * 3
            P.op("pe", lambda hh, x=x, c0=c0: hh.matmul(pq[64:67, :n], lhsT=C.bf[0:self.H, c0:c0 + 3],
                                                        rhs=self.hml[x][:, :n], start=(x == 0), stop=(x == 2)),
                 reads=[self.hmlb, C.b], writes=[pqb])
        qa, qab = self.QA[h], self.QAb[h]
        P.op("act", lambda hh: hh.activation(out=qa[:, :n], in_=pq[0:67, :n], func=AF.Copy),
             reads=[pqb], writes=[qab])

    def attend_multi(self, heads, nq, kblocks, ps_rot, pos, pden, pdenb, stores, LA=2):
        P, C = self.P, self.C
        nb = len(kblocks)
        tasks = [(h, i) for i in range(nb) for h in heads]
        pend = []

        def stage_a(h, i):
            kb, nk, col0, masked = kblocks[i]
            ps, psb = ps_rot.next()
            qa, qab = self.QA[h], self.QAb[h]
            P.op("pe", lambda hh: hh.matmul(
                ps[0:nk, col0:nq], lhsT=self.KT[h][0:67, kb * 128:kb * 128 + nk], rhs=qa[0:67, col0:nq],
                start=True, stop=(not masked)), reads=[self.KTb[h], qab], writes=[psb])
            if masked:
                w = min(128, nq - col0)
                P.op("pe", lambda hh: hh.matmul(
                    ps[0:nk, col0:col0 + w], lhsT=C.bf[0:nk, C_ID:C_ID + nk], rhs=C.bf[0:nk, C_MASK:C_MASK + w],
                    start=False, stop=True), reads=[C.b], writes=[psb])
            pt, ptb = self.pt.next()
            P.op("act", lambda hh: hh.activation(
                out=pt[0:nk, col0:nq], in_=ps[0:nk, col0:nq], func=AF.Exp, scale=0.125,
                bias=self.CK[0:nk, kb, h:h + 1]), reads=[psb, self.CKb], writes=[ptb])
            return (h, i, pt, ptb)

        def stage_b(h, i, pt, ptb):
            kb, nk, col0, masked = kblocks[i]
            po, pob = pos[h]
            P.op("pe", lambda hh: hh.matmul(
                po[0:65, col0:nq], lhsT=self.VA[0:nk, kb, h, :], rhs=pt[0:nk, col0:nq],
                start=(i == 0), stop=(i == nb - 1)), reads=[ptb, self.VAb], writes=[pob])

        for (h, i) in tasks:
            pend.append(stage_a(h, i))
            if len(pend) > LA:
                stage_b(*pend.pop(0))
        while pend:
            stage_b(*pend.pop(0))
        for h in heads:
            po, pob = pos[h]
            osb, osbb = self.osb.next()
            P.op("dve", lambda hh, osb=osb, po=po: hh.tensor_copy(out=osb[:, :nq], in_=po[0:65, :nq]), reads=[pob],
                 writes=[osbb])
            P.op("pe", lambda hh, osb=osb: hh.matmul(pden[0:64, :nq], lhsT=C.f32[0:65, C_SEL64:C_SEL64 + 64],
                                                     rhs=osb[0:65, :nq], start=True, stop=True),
                 reads=[osbb, C.b], writes=[pdenb])
            rec, recb = self.rec.next()
            P.op("dve", lambda hh, rec=rec: hh.reciprocal(out=rec[:, :nq], in_=pden[0:64, :nq]), reads=[pdenb],
                 writes=[recb])
            om, omb = self.omix.next()
            P.op("dve", lambda hh, om=om, osb=osb, rec=rec: hh.tensor_tensor(out=om[:, :nq], in0=osb[0:64, :nq],
                                                                            in1=rec[:, :nq], op=ALU.mult),
                 reads=[osbb, recb], writes=[omb])
            stores[h](om, omb)


def phase_fox_prompt(P, C, io):
    from contextlib import ExitStack
    with ExitStack() as ph:
        P.stack = ph
        wf = P.sb([128, 8, 386], BF16, "wfox")
        wfb = Buf("wfox")
        wf32 = P.sb([128, 8, 386], F32, "wfox32")
        P.dma("sp", lambda h: h.dma_start(out=wf32[:, :, :], in_=io["wfox"].rearrange("p (k c) -> p k c", c=386)),
              writes=[wfb])
        P.op("pool", lambda h: h.tensor_copy(out=wf[:, :, :], in_=wf32[:, :, :]), reads=[wfb], writes=[wfb])
        bf = P.sb([2, 1], F32, "bf")
        bfb = Buf("bf")
        P.dma("sp", lambda h: h.dma_start(out=bf[:, :], in_=io["bfx"][:, :]), writes=[bfb])
        P.op("dve", lambda h: h.tensor_scalar(out=bf[:, :], in0=bf[:, :], scalar1=-1.0, scalar2=None, op0=ALU.mult),
             reads=[bfb], writes=[bfb])
        F = Fox(P, C, 2, SEQ, 512, wf, bf)
        xb = Rot(P, 2, [128, 8, 512], BF16, "hxb_")
        pproj = Rot(P, 1, [128, 512], F32, "fpp_", psum=True)
        ps_rot = Rot(P, 5, [128, 512], F32, "fps_", psum=True)
        po = [P.ps([128, 512], F32, f"fpo{h}") for h in range(2)]
        pob = [Buf(f"fpo{h}", excl=True) for h in range(2)]
        pden, pdenb = pproj.items[0]
        kst = Rot(P, 2, [64, 512], F32, "kst_")
        vst = Rot(P, 2, [128, 4, 128], F32, "vst_")
        lst = Rot(P, 2, [2, 512], F32, "lst_")
        XG = None
        try:
            ck(1)
            _fox_loop(P, C, io, F, xb, pproj, ps_rot, po, pob, pden, pdenb, kst, vst, lst, wf, wfb, bf, bfb, XG)
        except StopPhase:
            pass
        P.barrier()
        P.emit()


def _fox_loop(P, C, io, F, xb, pproj, ps_rot, po, pob, pden, pdenb, kst, vst, lst, wf, wfb, bf, bfb, XG):
    QA2 = [[F.QA[h], P.sb([67, 512], BF16, f"QAx{h}")] for h in range(2)]
    QAb2 = [[F.QAb[h], Buf(f"QAx{h}")] for h in range(2)]
    if True:
        def prologue(tb):
            F.QA = [QA2[h][tb % 2] for h in range(2)]
            F.QAb = [QAb2[h][tb % 2] for h in range(2)]
            rk, cb = tb // (TPC // 512), (tb % (TPC // 512)) * 512
            x, xbuf = xb.next()
            P.dma("sp", lambda h, x=x, rk=rk, cb=cb: h.dma_start(
                out=x[:, :, :], in_=io["XGp"][cb // 512][rk * 1024:(rk + 1) * 1024, :].rearrange("(k p) n -> p k n", p=128)),
                reads=[io["XGpb"][cb // 512]], writes=[xbuf])
            ck(2)
            pf, pfb = pproj.next()
            for kt in range(8):
                P.op("pe", lambda h, pf=pf, kt=kt, x=x: h.matmul(pf[0:2, :], lhsT=wf[:, kt, 384:386], rhs=x[:, kt, :],
                                                              start=(kt == 0), stop=(kt == 7)),
                     reads=[wfb, xbuf], writes=[pfb])
            ck(3)
            F.nbf = bf
            P.op("act", lambda h, pf=pf: h.activation(out=F.sp_[:, :], in_=pf[0:2, :], func=AF.Exp, scale=-1.0,
                                                      bias=bf[:, 0:1]), reads=[pfb, bfb], writes=[F.spb])
            P.op("act", lambda h: h.activation(out=F.sp_[:, :], in_=F.sp_[:, :], func=AF.Ln, bias=1.0),
                 reads=[F.spb], writes=[F.spb])
            ck(4)
            F.scan(512)
            ck(5)
            F.split_q(512)
            ck(6)
            ls, lsb = lst.next()
            P.op("pool", lambda h, ls=ls: h.tensor_scalar(out=ls[:, :], in0=F.sp_[:, :], scalar1=-1.0, scalar2=None,
                                                          op0=ALU.mult), reads=[F.spb], writes=[lsb])
            P.dma("sp", lambda h, ls=ls, tb=tb: h.dma_start(out=io["o_logf"][:, tb * 512:(tb + 1) * 512], in_=ls[:, :]),
                  reads=[lsb])
            for sub in range(4):
                if DEBUG.get("nock"):
                    break
                pst, pstb = pproj.next()
                F.ck_block(pst, pstb, tb * 4 + sub, sub * 128, 128)
            ck(7)
            for hl in range(2):
                pk, pkb = pproj.next()
                for kt in range(8):
                    P.op("pe", lambda h, pk=pk, kt=kt, x=x, hl=hl: h.matmul(
                        pk[0:64, :], lhsT=wf[:, kt, 128 + hl * 64:128 + (hl + 1) * 64], rhs=x[:, kt, :],
                        start=(kt == 0), stop=(kt == 7)), reads=[wfb, xbuf], writes=[pkb])
                if not DEBUG.get("noktcopy"):
                    P.op("act", lambda h, pk=pk, hl=hl, tb=tb: h.activation(
                        out=F.KT[hl][0:64, tb * 512:(tb + 1) * 512], in_=pk[0:64, :], func=AF.Copy),
                        reads=[pkb], writes=[F.KTb[hl]])
                ks, ksb = kst.next()
                if not DEBUG.get("nokst"):
                    P.op("dve", lambda h, pk=pk, ks=ks: h.tensor_copy(out=ks[:, :], in_=pk[0:64, :]), reads=[pkb],
                         writes=[ksb])
                if not DEBUG.get("nokdma") and not (DEBUG.get("kdma0") and (hl != 0 or tb >= DEBUG["kdma0"])):
                    P.dma(DEBUG.get("kq", "sp") if isinstance(DEBUG.get("kq", "sp"), str) else "act", lambda h, ks=ks, hl=hl, tb=tb: h.dma_start(
                        out=io["o_foxk"][hl * 64:(hl + 1) * 64, tb * 512:(tb + 1) * 512], in_=ks[:, :]), reads=[ksb])
                pq, pqb = pproj.next()
                for kt in range(8):
                    P.op("pe", lambda h, pq=pq, kt=kt, x=x, hl=hl: h.matmul(
                        pq[0:64, :], lhsT=wf[:, kt, hl * 64:(hl + 1) * 64], rhs=x[:, kt, :],
                        start=(kt == 0), stop=(kt == 7)), reads=[wfb, xbuf], writes=[pqb])
                if not DEBUG.get("noaug"):
                    F.q_aug(pq, pqb, hl, 512)
            ck(8)
            vs, vsb = vst.next()
            for sub in range(4):
                pv, pvb = pproj.next()
                for kt in range(8):
                    P.op("pe", lambda h, pv=pv, kt=kt, x=x, sub=sub: h.matmul(
                        pv[:, 0:128], lhsT=x[:, kt, sub * 128:(sub + 1) * 128], rhs=wf[:, kt, 256:384],
                        start=(kt == 0), stop=(kt == 7)), reads=[wfb, xbuf], writes=[pvb])
                P.op("act", lambda h, pv=pv, sub=sub, tb=tb: h.activation(
                    out=F.VA[:, tb * 4 + sub, :, 0:64], in_=pv[:, 0:128].rearrange("p (h d) -> p h d", d=64),
                    func=AF.Copy), reads=[pvb], writes=[F.VAb])
                P.op("dve", lambda h, pv=pv, sub=sub, vs=vs: h.tensor_copy(out=vs[:, sub, :], in_=pv[:, 0:128]),
                     reads=[pvb], writes=[vsb])
            P.dma("sp", lambda h, vs=vs, tb=tb: h.dma_start(
                out=io["o_foxv"][tb * 512:(tb + 1) * 512, :].rearrange("(s p) f -> p s f", p=128), in_=vs[:, :, :]),
                reads=[vsb])

        def attention(tb):
            F.QA = [QA2[h][tb % 2] for h in range(2)]
            F.QAb = [QAb2[h][tb % 2] for h in range(2)]
            kbl = [(kb, 128, 0, False) for kb in range(4 * tb)]
            kbl += [(4 * tb + i, 128, 128 * i, True) for i in range(4)]
            stores = {}
            for hl in range(2):
                def store(om, omb, hl=hl, tb=tb):
                    P.dma("sp", lambda h: h.dma_start(
                        out=io["MXinp"][tb // 4][hl * 64:(hl + 1) * 64, (tb % 4) * 512:(tb % 4 + 1) * 512],
                        in_=om[:, :]), reads=[omb], writes=[io["MXinb"][tb]])
                stores[hl] = store
            F.attend_multi([0, 1], 512, kbl, ps_rot, {0: (po[0], pob[0]), 1: (po[1], pob[1])}, pden, pdenb, stores, LA=4)


        nb_ = SEQ // 512
        prologue(0)
        for tb in range(nb_):
            if tb + 1 < nb_:
                prologue(tb + 1)
            ck(9)
            attention(tb)

class Gdn:
    def __init__(self, P, C, n, L, banks):
        self.P, self.C, self.n, self.L = P, C, n, L
        self.nch = n // L
        self.B = banks
        sb = P.sb
        self.wg = sb([128, 8, 514], BF16, "wg")
        self.wgb = Buf("wg")
        self.cw = sb([128, 3, 4], F32, "cw")
        self.sc = sb([1, 2], F32, "gsc")
        self.ng = sb([128, 1], F32, "gng")
        self.parb = Buf("gpar")
        self.XC = [sb([128, 3 + n], F32, f"XC{i}") for i in range(3)]
        self.XCb = [Buf(f"XC{i}") for i in range(3)]
        self.acc = [sb([128, n], F32, f"gacc{i}") for i in range(3)]
        self.accb = [Buf(f"gacc{i}") for i in range(3)]
        self.sqt = sb([128, n], BF16, "gsq")
        self.sqb = Buf("gsq")
        self.rn = sb([128, n], F32, "grn")
        self.rnb = Buf("grn")
        self.kT = sb([128, n], BF16, "gkT")
        self.qT = sb([128, n], BF16, "gqT")
        self.kbT = sb([128, n], BF16, "gkbT")
        self.kgT = sb([128, n], BF16, "gkgT")
        self.qgT = sb([128, n], BF16, "gqgT")
        self.vT = sb([128, n], BF16, "gvT")
        self.kTb, self.qTb, self.kbTb, self.kgTb, self.qgTb, self.vTb = [Buf(x) for x in "kT qT kbT kgT qgT vT".split()]
        self.zs = sb([128, n], F32, "gzs")
        self.zsb = Buf("gzs")
        self.EG = sb([128, n], F32, "gEG")
        self.EGb = Buf("gEG")
        self.BE = sb([128, n], F32, "gBE")
        self.BEb = Buf("gBE")
        self.rows = {k: sb([1, n], F32, "gr_" + k) for k in ("g", "Gl", "nGl", "eg", "kd", "beta", "one")}
        self.rowb = {k: Buf("gr_" + k) for k in self.rows}
        self.kdT = sb([L, self.nch], F32, "gkdT")
        self.kdTb = Buf("gkdT")
        self.vtok = sb([L, self.nch, 128], BF16, "gvtok")
        self.vtokb = Buf("gvtok")
        self.ktok = sb([L, self.nch, 128], BF16, "gktok")
        self.ktokb = Buf("gktok")
        self.dT = sb([L, n], F32, "gdT")
        self.dTb = Buf("gdT")
        self.X = [sb([L, n], F32, f"gX{i}") for i in range(2)]
        self.Y = [sb([L, n], F32, f"gY{i}") for i in range(2)]
        self.Pm = sb([L, n], F32, "gPm")
        self.Xb = [Buf("gX0"), Buf("gX1")]
        self.Yb = [Buf("gY0"), Buf("gY1")]
        self.Pmb = Buf("gPm")
        self.AT = sb([L, n], BF16, "gAT")
        self.ATb = Buf("gAT")
        self.TT = sb([L, n], BF16, "gTT")
        self.TTb = Buf("gTT")
        self.idt = sb([L, n], F32, "gidt")
        self.strict = sb([L, n], F32, "gstrict")
        self.cb = Buf("gconstl")
        self.S32 = sb([128, 128], F32, "gS32")
        self.S32b = Buf("gS32")
        self.Sbf = sb([128, 128], BF16, "gSbf")
        self.Sbfb = Buf("gSbf")
        self.R = Rot(P, 2, [L, 128], BF16, "gR_")
        self.vn = Rot(P, 2, [L, 128], BF16, "gvn_")
        self.vkd = Rot(P, 2, [L, 128], BF16, "gvkd_")
        self.og = sb([128, n], F32, "gog")
        self.ogb = Buf("gog")
        self.om = Rot(P, 2, [128, n], BF16, "gom_")
        self.coef = sb([1, 2], F32, "gcoef")
        self.coefb = Buf("gcoef")
        self.rot = 0
        for c in range(self.nch):
            P.op("pool", lambda h, c=c: h.tensor_copy(out=self.idt[:, c * L:(c + 1) * L], in_=C.f32[0:L, C_ID:C_ID + L]),
                 reads=[C.b], writes=[self.cb])
            P.op("pool", lambda h, c=c: h.tensor_scalar(out=self.strict[:, c * L:(c + 1) * L],
                                                        in0=C.f32[0:L, C_MASKT:C_MASKT + L], scalar1=-1.0 / 30000.0,
                                                        scalar2=None, op0=ALU.mult), reads=[C.b], writes=[self.cb])
        P.op("pool", lambda h: h.memset(self.rows["one"][:, :], 1.0), writes=[self.rowb["one"]])

    def gb(self):
        it = self.B[self.rot % 2]
        self.rot += 1
        return it

    def load_params(self, wg_ap, cw_ap, sc_ap, ng_ap, stage):
        P = self.P
        P.dma("sp", lambda h: h.dma_start(out=stage[:, :, :], in_=wg_ap.rearrange("p (k c) -> p k c", c=514)),
              writes=[self.wgb])
        P.op("pool", lambda h: h.tensor_copy(out=self.wg[:, :, :], in_=stage[:, :, :]), reads=[self.wgb],
             writes=[self.wgb])
        P.dma("sp", lambda h: h.dma_start(out=self.cw[:, :, :], in_=cw_ap), writes=[self.parb])
        P.dma("sp", lambda h: h.dma_start(out=self.sc[:, :], in_=sc_ap), writes=[self.parb])
        P.dma("sp", lambda h: h.dma_start(out=self.ng[:, :], in_=ng_ap), writes=[self.parb])
        P.op("act", lambda h: h.activation(out=self.coef[:, 0:1], in_=self.sc[:, 0:1], func=AF.Exp),
             reads=[self.parb], writes=[self.coefb])
        P.op("dve", lambda h: h.tensor_scalar(out=self.coef[:, 0:1], in0=self.coef[:, 0:1], scalar1=-1.0,
                                              scalar2=None, op0=ALU.mult), reads=[self.coefb], writes=[self.coefb])

    def block(self, x, xbuf, first, store_o):
        P, C, n, L, nch, B = self.P, self.C, self.n, self.L, self.nch, self.B
        wg, wgb = self.wg, self.wgb
        ones_bf = C.bf[:, C_ONES:C_ONES + 128]
        for s_ in range(3):
            pp, ppb = self.gb()
            for kt in range(8):
                P.op("pe", lambda h, pp=pp, kt=kt, s_=s_: h.matmul(pp[:, :n], lhsT=wg[:, kt, s_ * 128:(s_ + 1) * 128],
                                                                 rhs=x[:, kt, :n], start=(kt == 0), stop=(kt == 7)),
                     reads=[wgb, xbuf], writes=[ppb])
            P.op("act", lambda h, pp=pp, s_=s_: h.activation(out=self.XC[s_][:, 3:3 + n], in_=pp[:, :n], func=AF.Copy),
                 reads=[ppb], writes=[self.XCb[s_]])
        pp, ppb = self.gb()
        for kt in range(8):
            P.op("pe", lambda h, pp=pp, kt=kt: h.matmul(pp[:, :n], lhsT=wg[:, kt, 384:512], rhs=x[:, kt, :n],
                                                        start=(kt == 0), stop=(kt == 7)), reads=[wgb, xbuf], writes=[ppb])
        P.op("act", lambda h, pp=pp: h.activation(out=self.zs[:, :], in_=pp[:, :n], func=AF.Silu), reads=[ppb],
             writes=[self.zsb])
        r = self.rows
        rb = self.rowb
        pg, pgb = self.gb()
        for kt in range(8):
            P.op("pe", lambda h, pg=pg, kt=kt: h.matmul(pg[0:1, :n], lhsT=wg[:, kt, 512:513], rhs=x[:, kt, :n],
                                                        start=(kt == 0), stop=(kt == 7)), reads=[wgb, xbuf], writes=[pgb])
        P.op("act", lambda h, pg=pg: h.activation(out=r["g"][:, :], in_=pg[0:1, :n], func=AF.Exp, bias=self.sc[:, 1:2]),
             reads=[pgb, self.parb], writes=[rb["g"]])
        P.op("act", lambda h: h.activation(out=r["g"][:, :], in_=r["g"][:, :], func=AF.Ln, bias=1.0),
             reads=[rb["g"]], writes=[rb["g"]])
        P.op("dve", lambda h: h.tensor_scalar(out=r["g"][:, :], in0=r["g"][:, :], scalar1=self.coef[:, 0:1],
                                              scalar2=None, op0=ALU.mult), reads=[rb["g"], self.coefb], writes=[rb["g"]])
        pg2, pg2b = self.gb()
        for kt in range(8):
            P.op("pe", lambda h, pg2=pg2, kt=kt: h.matmul(pg2[0:1, :n], lhsT=wg[:, kt, 513:514], rhs=x[:, kt, :n],
                                                          start=(kt == 0), stop=(kt == 7)), reads=[wgb, xbuf], writes=[pg2b])
        P.op("act", lambda h, pg2=pg2: h.activation(out=r["beta"][:, :], in_=pg2[0:1, :n], func=AF.Sigmoid),
             reads=[pg2b], writes=[rb["beta"]])
        for c in range(nch):
            P.op("dve", lambda h, c=c: h.tensor_tensor_scan(out=r["Gl"][:, c * L:(c + 1) * L],
                                                            data0=r["one"][:, c * L:(c + 1) * L],
                                                            data1=r["g"][:, c * L:(c + 1) * L], initial=0.0,
                                                            op0=ALU.mult, op1=ALU.add),
                 reads=[rb["g"], rb["one"]], writes=[rb["Gl"]])
        P.op("act", lambda h: h.activation(out=r["eg"][:, :], in_=r["Gl"][:, :], func=AF.Exp), reads=[rb["Gl"]],
             writes=[rb["eg"]])
        P.op("dve", lambda h: h.tensor_scalar(out=r["nGl"][:, :], in0=r["Gl"][:, :], scalar1=-1.0, scalar2=None,
                                              op0=ALU.mult), reads=[rb["Gl"]], writes=[rb["nGl"]])
        for c in range(nch):
            P.op("act", lambda h, c=c: h.activation(out=r["kd"][:, c * L:(c + 1) * L], in_=r["Gl"][:, c * L:(c + 1) * L],
                                                    func=AF.Exp, scale=-1.0, bias=r["Gl"][:, (c + 1) * L - 1:(c + 1) * L]),
                 reads=[rb["Gl"]], writes=[rb["kd"]])
        onesrow = C.f32[0:1, C_ONES:C_ONES + 128]
        pe_, peb = self.gb()
        P.op("pe", lambda h, pe_=pe_: h.matmul(pe_[:, :n], lhsT=onesrow, rhs=r["eg"][:, :], start=True, stop=True),
             reads=[rb["eg"], C.b], writes=[peb])
        P.op("act", lambda h, pe_=pe_: h.activation(out=self.EG[:, :], in_=pe_[:, :n], func=AF.Copy), reads=[peb],
             writes=[self.EGb])
        pb_, pbb = self.gb()
        P.op("pe", lambda h, pb_=pb_: h.matmul(pb_[:, :n], lhsT=onesrow, rhs=r["beta"][:, :], start=True, stop=True),
             reads=[rb["beta"], C.b], writes=[pbb])
        P.op("act", lambda h, pb_=pb_: h.activation(out=self.BE[:, :], in_=pb_[:, :n], func=AF.Copy), reads=[pbb],
             writes=[self.BEb])
        pk_, pkb_ = self.gb()
        for c in range(nch):
            P.op("pe", lambda h, pk_=pk_, c=c: h.matmul(pk_[0:L, c:c + 1], lhsT=r["kd"][:, c * L:(c + 1) * L],
                                                        rhs=C.f32[0:1, C_ONES:C_ONES + 1], start=True, stop=True),
                 reads=[rb["kd"], C.b], writes=[pkb_])
        P.op("dve", lambda h, pk_=pk_: h.tensor_copy(out=self.kdT[:, :], in_=pk_[0:L, 0:nch]), reads=[pkb_],
             writes=[self.kdTb])
        for s_ in range(3):
            xc, xcb = self.XC[s_], self.XCb[s_]
            acc, accb = self.acc[s_], self.accb[s_]
            P.op("dve", lambda h, xc=xc, acc=acc, s_=s_: h.tensor_scalar(out=acc[:, :], in0=xc[:, 0:n],
                                                                       scalar1=self.cw[:, s_, 0:1], scalar2=None,
                                                                       op0=ALU.mult), reads=[xcb, self.parb], writes=[accb])
            for i in range(1, 4):
                P.op("dve", lambda h, xc=xc, acc=acc, s_=s_, i=i: h.scalar_tensor_tensor(
                    out=acc[:, :], in0=xc[:, i:i + n], scalar=self.cw[:, s_, i:i + 1], in1=acc[:, :],
                    op0=ALU.mult, op1=ALU.add), reads=[xcb, self.parb, accb], writes=[accb])
            P.op("act", lambda h, acc=acc: h.activation(out=acc[:, :], in_=acc[:, :], func=AF.Silu), reads=[accb],
                 writes=[accb])
            P.op("pool", lambda h, xc=xc: h.tensor_copy(out=xc[:, 0:3], in_=xc[:, n:n + 3]), reads=[xcb], writes=[xcb])
        for s_ in range(2):
            acc, accb = self.acc[s_], self.accb[s_]
            P.op("act", lambda h, acc=acc: h.activation(out=self.sqt[:, :], in_=acc[:, :], func=AF.Square), reads=[accb],
                 writes=[self.sqb])
            pn, pnb = self.gb()
            P.op("pe", lambda h, pn=pn: h.matmul(pn[:, :n], lhsT=ones_bf, rhs=self.sqt[:, :], start=True, stop=True),
                 reads=[self.sqb, C.b], writes=[pnb])
            P.op("dve", lambda h, pn=pn: h.tensor_scalar(out=self.rn[:, :], in0=pn[:, :n], scalar1=NORM_EPS, scalar2=None,
                                                         op0=ALU.add), reads=[pnb], writes=[self.rnb])
            P.op("act", lambda h: h.activation(out=self.rn[:, :], in_=self.rn[:, :], func=AF.Sqrt), reads=[self.rnb],
                 writes=[self.rnb])
            P.op("dve", lambda h: h.reciprocal(out=self.rn[:, :], in_=self.rn[:, :]), reads=[self.rnb], writes=[self.rnb])
            if s_ == 0:
                P.op("dve", lambda h, acc=acc: h.scalar_tensor_tensor(out=acc[:, :], in0=acc[:, :], scalar=128.0 ** -0.5,
                                                                     in1=self.rn[:, :], op0=ALU.mult, op1=ALU.mult),
                     reads=[accb, self.rnb], writes=[accb])
            else:
                P.op("dve", lambda h, acc=acc: h.tensor_tensor(out=acc[:, :], in0=acc[:, :], in1=self.rn[:, :],
                                                              op=ALU.mult), reads=[accb, self.rnb], writes=[accb])
        qf, kf, vf = self.acc
        qfb, kfb, vfb = self.accb
        P.op("act", lambda h: h.activation(out=self.qT[:, :], in_=qf[:, :], func=AF.Copy), reads=[qfb], writes=[self.qTb])
        P.op("act", lambda h: h.activation(out=self.kT[:, :], in_=kf[:, :], func=AF.Copy), reads=[kfb], writes=[self.kTb])
        P.op("act", lambda h: h.activation(out=self.vT[:, :], in_=vf[:, :], func=AF.Copy), reads=[vfb], writes=[self.vTb])
        P.op("dve", lambda h: h.tensor_tensor(out=self.kbT[:, :], in0=kf[:, :], in1=self.BE[:, :], op=ALU.mult),
             reads=[kfb, self.BEb], writes=[self.kbTb])
        P.op("dve", lambda h: h.tensor_tensor(out=self.kgT[:, :], in0=kf[:, :], in1=self.EG[:, :], op=ALU.mult),
             reads=[kfb, self.EGb], writes=[self.kgTb])
        P.op("pool", lambda h: h.tensor_tensor(out=self.qgT[:, :], in0=qf[:, :], in1=self.EG[:, :], op=ALU.mult),
             reads=[qfb, self.EGb], writes=[self.qgTb])
        idb = C.bf[:, C_ID:C_ID + 128]
        for (src, srcb, dst, dstb) in ((self.kT, self.kTb, self.ktok, self.ktokb), (self.vT, self.vTb, self.vtok, self.vtokb)):
            for c0 in range(0, nch, 4):
                pt_, ptb_ = self.gb()
                m = min(4, nch - c0)
                for c in range(c0, c0 + m):
                    P.op("pe", lambda h, pt_=pt_, c=c, c0=c0, src=src: h.matmul(
                        pt_[0:L, (c - c0) * 128:(c - c0 + 1) * 128], lhsT=src[:, c * L:(c + 1) * L], rhs=idb,
                        start=True, stop=True), reads=[srcb, C.b], writes=[ptb_])
                P.op("dve", lambda h, pt_=pt_, c0=c0, m=m, dst=dst: h.tensor_copy(
                    out=dst[:, c0:c0 + m, :], in_=pt_[0:L, 0:m * 128].rearrange("p (c d) -> p c d", d=128)),
                    reads=[ptb_], writes=[dstb])
        (a1, a1b), (a2, a2b), (dm, dmb) = B[2], B[3], B[4]
        for c in range(nch):
            J = slice(c * L, (c + 1) * L)
            P.op("pe", lambda h, J=J: h.matmul(a1[0:L, J], lhsT=self.kbT[:, J], rhs=self.kT[:, J], start=True, stop=True),
                 reads=[self.kbTb, self.kTb], writes=[a1b])
            P.op("pe", lambda h, J=J: h.matmul(a2[0:L, J], lhsT=self.kT[:, J], rhs=self.qT[:, J], start=True, stop=True),
                 reads=[self.kTb, self.qTb], writes=[a2b])
            P.op("pe", lambda h, J=J: h.matmul(dm[0:L, J], lhsT=C.f32[0:1, C_ONES:C_ONES + L], rhs=r["Gl"][:, J],
                                               start=True, stop=False), reads=[rb["Gl"], C.b], writes=[dmb])
            P.op("pe", lambda h, J=J: h.matmul(dm[0:L, J], lhsT=r["nGl"][:, J], rhs=C.f32[0:1, C_ONES:C_ONES + L],
                                               start=False, stop=False), reads=[rb["nGl"], C.b], writes=[dmb])
            P.op("pe", lambda h, J=J: h.matmul(dm[0:L, J], lhsT=C.f32[0:L, C_ID:C_ID + L], rhs=C.f32[0:L, C_MASK:C_MASK + L],
                                               start=False, stop=True), reads=[C.b], writes=[dmb])
        P.op("act", lambda h: h.activation(out=self.dT[:, :], in_=dm[0:L, :n], func=AF.Exp), reads=[dmb], writes=[self.dTb])
        X, Xb_, Y, Yb_ = self.X, self.Xb, self.Y, self.Yb
        P.op("dve", lambda h: h.tensor_tensor(out=X[0][:, :], in0=a1[0:L, :n], in1=self.dT[:, :], op=ALU.mult),
             reads=[a1b, self.dTb], writes=[Xb_[0]])
        P.op("dve", lambda h: h.tensor_tensor(out=X[0][:, :], in0=X[0][:, :], in1=self.strict[:, :], op=ALU.mult),
             reads=[Xb_[0], self.cb], writes=[Xb_[0]])
        P.op("dve", lambda h: h.tensor_tensor(out=self.AT[:, :], in0=a2[0:L, :n], in1=self.dT[:, :], op=ALU.mult),
             reads=[a2b, self.dTb], writes=[self.ATb])
        P.op("pool", lambda h: h.tensor_tensor(out=self.Pm[:, :], in0=self.idt[:, :], in1=X[0][:, :], op=ALU.subtract),
             reads=[Xb_[0], self.cb], writes=[self.Pmb])
        (px, pxb), (py, pyb), (pp_, ppb_) = B[2], B[3], B[4]
        for c in range(nch):
            J = slice(c * L, (c + 1) * L)
            P.op("pe", lambda h, J=J: h.matmul(py[0:L, J], lhsT=X[0][:, J], rhs=C.f32[0:L, C_ID:C_ID + L], start=True,
                                               stop=True), reads=[Xb_[0], C.b], writes=[pyb])
        P.op("act", lambda h: h.activation(out=Y[0][:, :], in_=py[0:L, :n], func=AF.Copy), reads=[pyb], writes=[Yb_[0]])
        nlev = {64: 5, 32: 4}[L]
        cur = 0
        for lev in range(nlev):
            nxt = 1 - cur
            last = (lev == nlev - 1)
            for c in range(nch):
                J = slice(c * L, (c + 1) * L)
                if not last:
                    P.op("pe", lambda h, J=J, cur=cur: h.matmul(px[0:L, J], lhsT=Y[cur][:, J], rhs=X[cur][:, J],
                                                                start=True, stop=True),
                         reads=[Xb_[cur], Yb_[cur]], writes=[pxb])
                P.op("pe", lambda h, J=J, cur=cur: h.matmul(py[0:L, J], lhsT=X[cur][:, J], rhs=Y[cur][:, J],
                                                            start=True, stop=True),
                     reads=[Xb_[cur], Yb_[cur]], writes=[pyb])
            if not last:
                P.op("dve", lambda h, nxt=nxt: h.tensor_copy(out=X[nxt][:, :], in_=px[0:L, :n]), reads=[pxb],
                     writes=[Xb_[nxt]])
            P.op("act", lambda h, nxt=nxt: h.activation(out=Y[nxt][:, :], in_=py[0:L, :n], func=AF.Copy), reads=[pyb],
                 writes=[Yb_[nxt]])
            for c in range(nch):
                J = slice(c * L, (c + 1) * L)
                P.op("pe", lambda h, J=J, nxt=nxt: h.matmul(pp_[0:L, J], lhsT=Y[nxt][:, J], rhs=self.Pm[:, J],
                                                            start=True, stop=True),
                     reads=[Yb_[nxt], self.Pmb], writes=[ppb_])
            P.op("dve", lambda h: h.tensor_tensor(out=self.Pm[:, :], in0=self.Pm[:, :], in1=pp_[0:L, :n], op=ALU.add),
                 reads=[self.Pmb, ppb_], writes=[self.Pmb])
            cur = nxt
        P.op("dve", lambda h: h.tensor_tensor(out=self.TT[:, :], in0=self.Pm[:, :], in1=self.BE[0:L, :], op=ALU.mult),
             reads=[self.Pmb, self.BEb], writes=[self.TTb])
        (r1, r1b), (r2, r2b), (ro, rob) = B[5], B[6], B[7]
        for c in range(nch):
            J = slice(c * L, (c + 1) * L)
            P.op("pe", lambda h, J=J: h.matmul(r1[0:L, 0:128], lhsT=self.kgT[:, J], rhs=self.Sbf[:, :], start=True,
                                               stop=True), reads=[self.kgTb, self.Sbfb], writes=[r1b])
            R, Rb = self.R.next()
            P.op("dve", lambda h, R=R, c=c: h.tensor_tensor(out=R[:, :], in0=self.vtok[:, c, :], in1=r1[0:L, 0:128],
                                                           op=ALU.subtract), reads=[self.vtokb, r1b], writes=[Rb])
            P.op("pe", lambda h, J=J, R=R: h.matmul(r2[0:L, 0:128], lhsT=self.TT[:, J], rhs=R[:, :], start=True, stop=True),
                 reads=[self.TTb, Rb], writes=[r2b])
            vn, vnb = self.vn.next()
            vkd, vkdb = self.vkd.next()
            P.op("act", lambda h, vn=vn: h.activation(out=vn[:, :], in_=r2[0:L, 0:128], func=AF.Copy), reads=[r2b],
                 writes=[vnb])
            P.op("dve", lambda h, vkd=vkd, c=c: h.tensor_scalar(out=vkd[:, :], in0=r2[0:L, 0:128],
                                                               scalar1=self.kdT[:, c:c + 1], scalar2=None, op0=ALU.mult),
                 reads=[r2b, self.kdTb], writes=[vkdb])
            P.op("pe", lambda h, J=J: h.matmul(ro[:, J], lhsT=self.Sbf[:, :], rhs=self.qgT[:, J], start=True, stop=False),
                 reads=[self.Sbfb, self.qgTb], writes=[rob])
            P.op("pe", lambda h, J=J, vn=vn: h.matmul(ro[:, J], lhsT=vn[:, :], rhs=self.AT[:, J], start=False, stop=True),
                 reads=[vnb, self.ATb], writes=[rob])
            P.op("pe", lambda h, c=c, vkd=vkd: h.matmul(r1[:, 128:256], lhsT=self.ktok[:, c, :], rhs=vkd[:, :], start=True,
                                                        stop=True), reads=[self.ktokb, vkdb], writes=[r1b])
            col = (c + 1) * L - 1
            P.op("dve", lambda h, col=col: h.scalar_tensor_tensor(out=self.Sbf[:, :], in0=self.S32[:, :],
                                                                  scalar=self.EG[:, col:col + 1], in1=r1[:, 128:256],
                                                                  op0=ALU.mult, op1=ALU.add),
                 reads=[self.S32b, self.EGb, r1b], writes=[self.Sbfb])
            P.op("dve", lambda h, col=col: h.scalar_tensor_tensor(out=self.S32[:, :], in0=self.S32[:, :],
                                                                  scalar=self.EG[:, col:col + 1], in1=r1[:, 128:256],
                                                                  op0=ALU.mult, op1=ALU.add),
                 reads=[self.S32b, self.EGb, r1b], writes=[self.S32b])
        P.op("act", lambda h: h.activation(out=self.sqt[:, :], in_=ro[:, :n], func=AF.Square), reads=[rob],
             writes=[self.sqb])
        pn, pnb = self.gb()
        P.op("pe", lambda h, pn=pn: h.matmul(pn[:, :n], lhsT=ones_bf, rhs=self.sqt[:, :], start=True, stop=True),
             reads=[self.sqb, C.b], writes=[pnb])
        P.op("dve", lambda h, pn=pn: h.tensor_scalar(out=self.rn[:, :], in0=pn[:, :n], scalar1=1.0 / 128.0,
                                                     scalar2=NORM_EPS, op0=ALU.mult, op1=ALU.add), reads=[pnb],
             writes=[self.rnb])
        P.op("act", lambda h: h.activation(out=self.rn[:, :], in_=self.rn[:, :], func=AF.Sqrt), reads=[self.rnb],
             writes=[self.rnb])
        P.op("dve", lambda h: h.reciprocal(out=self.rn[:, :], in_=self.rn[:, :]), reads=[self.rnb], writes=[self.rnb])
        P.op("dve", lambda h: h.tensor_tensor(out=self.og[:, :], in0=ro[:, :n], in1=self.rn[:, :], op=ALU.mult),
             reads=[rob, self.rnb], writes=[self.ogb])
        om, omb = self.om.next()
        P.op("dve", lambda h, om=om: h.scalar_tensor_tensor(out=om[:, :], in0=self.og[:, :], scalar=self.ng[:, 0:1],
                                                           in1=self.zs[:, :], op0=ALU.mult, op1=ALU.mult),
             reads=[self.ogb, self.parb, self.zsb], writes=[omb])
        store_o(om, omb)


def phase_gdn_prompt(P, C, io):
    from contextlib import ExitStack
    with ExitStack() as ph:
        P.stack = ph
        banks = [(P.ps([128, 512], F32, f"gbank{i}"), Buf(f"gbank{i}", excl=True)) for i in range(8)]
        G = Gdn(P, C, 512, 64, banks)
        stage = P.sb([128, 8, 514], F32, "wgst")
        G.load_params(io["wgdn"], io["gcw"], io["gsc"], io["gng"], stage)
        for i in range(3):
            P.op("pool", lambda h, i=i: h.memset(G.XC[i][:, 0:3], 0.0), writes=[G.XCb[i]])
        P.op("pool", lambda h: h.memset(G.S32[:, :], 0.0), writes=[G.S32b])
        P.op("pool", lambda h: h.memset(G.Sbf[:, :], 0.0), writes=[G.Sbfb])
        xb = Rot(P, 2, [128, 8, 512], BF16, "gxb_")
        XG = None
        nb = SEQ // 512
        for tb in range(nb):
            rk, cb = tb // (TPC // 512), (tb % (TPC // 512)) * 512
            x, xbuf = xb.next()
            P.dma("sp", lambda h, x=x, rk=rk, cb=cb: h.dma_start(
                out=x[:, :, :], in_=io["XGp"][cb // 512][rk * 1024:(rk + 1) * 1024, :].rearrange("(k p) n -> p k n", p=128)),
                reads=[io["XGpb"][cb // 512]], writes=[xbuf])

            def store(om, omb, tb=tb):
                P.dma("sp", lambda h: h.dma_start(
                    out=io["MXinp"][tb // 4][128:256, (tb % 4) * 512:(tb % 4 + 1) * 512], in_=om[:, :]),
                      reads=[omb], writes=[io["MXinb"][tb]])
            G.block(x, xbuf, tb == 0, store)
            if tb % (PIECE // 512) == PIECE // 512 - 1:
                gather_one(P, io, "MX", tb // (PIECE // 512))
        P.dma("sp", lambda h: h.dma_start(out=io["o_gstate"][:, :], in_=G.S32[:, :]), reads=[G.S32b])
        for i in range(3):
            P.dma("sp", lambda h, i=i: h.dma_start(out=io["o_gconv"][i], in_=G.XC[i][:, 0:3]), reads=[G.XCb[i]])
        P.barrier()
        P.emit()


def phase_sample_even(P, C, io):
    from contextlib import ExitStack
    n = NS
    with ExitStack() as ph:
        P.stack = ph
        banks = [(P.ps([128, 512], F32, f"sbank{i}"), Buf(f"sbank{i}", excl=True)) for i in range(8)]
        x32 = P.sb([128, 8, n], F32, "sx32")
        xs = P.sb([128, 8, n], BF16, "sxb")
        xsb = Buf("sxb")
        P.dma("sp", lambda h: h.dma_start(out=x32[:, :, :], in_=io["X1v"][:, :, TPC:TPC + n]), reads=[io["X1b"][-1]],
              writes=[xsb])
        P.op("act", lambda h: h.activation(out=xs[:, :, :], in_=x32[:, :, :], func=AF.Copy), reads=[xsb], writes=[xsb])
        wfs = P.sb([128, 8, 1544], BF16, "wfs")
        wfsb = Buf("wfs")
        wst = Rot(P, 2, [128, 1544], F32, "wfst_")
        for kt in range(8):
            st, stb = wst.next()
            P.dma("act", lambda h, st=st, kt=kt: h.dma_start(out=st[:, :], in_=io["wfox_s"][:, kt * 1544:(kt + 1) * 1544]),
                  writes=[stb])
            P.op("pool", lambda h, st=st, kt=kt: h.tensor_copy(out=wfs[:, kt, :], in_=st[:, :]), reads=[stb], writes=[wfsb])
        bf = P.sb([8, 1], F32, "sbf")
        bfb = Buf("sbf")
        P.dma("sp", lambda h: h.dma_start(out=bf[:, :], in_=io["bf_s"][:, :]), writes=[bfb])
        P.op("dve", lambda h: h.tensor_scalar(out=bf[:, :], in0=bf[:, :], scalar1=-1.0, scalar2=None, op0=ALU.mult),
             reads=[bfb], writes=[bfb])
        F = Fox(P, C, 8, 1024 + n, n, wfs, bf)
        prot = [banks[0], banks[1]]
        pi = [0]

        def nb():
            it = prot[pi[0] % 2]
            pi[0] += 1
            return it
        kst = Rot(P, 2, [64, 1024], F32, "skst_")
        for hh in range(8):
            st, stb = kst.next()
            P.dma("sp", lambda h, st=st, hh=hh: h.dma_start(out=st[:, :], in_=io["pastk"][hh]), writes=[stb])
            P.op("act", lambda h, st=st, hh=hh: h.activation(out=F.KT[hh][0:64, 0:1024], in_=st[:, :], func=AF.Copy),
                 reads=[stb], writes=[F.KTb[hh]])
        vst = P.sb([128, 8, 512], F32, "svst")
        vstb = Buf("svst")
        P.dma("sp", lambda h: h.dma_start(out=vst[:, :, :], in_=io["pastv"].rearrange("(kb p) f -> p kb f", p=128)),
              writes=[vstb])
        for kb in range(8):
            P.op("pool", lambda h, kb=kb: h.tensor_copy(out=F.VA[:, kb, :, 0:64],
                                                        in_=vst[:, kb, :].rearrange("p (h d) -> p h d", d=64)),
                 reads=[vstb], writes=[F.VAb])
        lp = P.sb([8, 1024], F32, "slp")
        lpb = Buf("slp")
        P.dma("sp", lambda h: h.dma_start(out=lp[:, :], in_=io["pastlf"][:, :]), writes=[lpb])
        for half in range(2):
            P.op("dve", lambda h, half=half: h.tensor_scalar(out=F.sp_[:, :], in0=lp[:, half * 512:(half + 1) * 512],
                                                            scalar1=-1.0, scalar2=None, op0=ALU.mult),
                 reads=[lpb], writes=[F.spb])
            F.scan(512)
            for sub in range(4):
                pst, pstb = nb()
                F.ck_block(pst, pstb, half * 4 + sub, sub * 128, 128)
        pf, pfb = nb()
        for kt in range(8):
            P.op("pe", lambda h, kt=kt: h.matmul(pf[0:8, :n], lhsT=wfs[:, kt, 1536:1544], rhs=xs[:, kt, :],
                                                 start=(kt == 0), stop=(kt == 7)), reads=[wfsb, xsb], writes=[pfb])
        P.op("act", lambda h: h.activation(out=F.sp_[:, :n], in_=pf[0:8, :n], func=AF.Exp, scale=-1.0, bias=bf[:, 0:1]),
             reads=[pfb, bfb], writes=[F.spb])
        P.op("act", lambda h: h.activation(out=F.sp_[:, :n], in_=F.sp_[:, :n], func=AF.Ln, bias=1.0), reads=[F.spb],
             writes=[F.spb])
        F.scan(n)
        F.split_q(n)
        ls = P.sb([8, n], F32, "sls")
        lsb = Buf("sls")
        P.op("pool", lambda h: h.tensor_scalar(out=ls[:, :], in0=F.sp_[:, :n], scalar1=-1.0, scalar2=None, op0=ALU.mult),
             reads=[F.spb], writes=[lsb])
        P.dma("sp", lambda h: h.dma_start(out=io["o_slogf"][:, :], in_=ls[:, :]), reads=[lsb])
        pst, pstb = nb()
        F.ck_block(pst, pstb, 8, 0, n)
        kso = P.sb([64, 8, n], F32, "skso")
        ksob = Buf("skso")
        for hh in range(8):
            pk, pkb = nb()
            for kt in range(8):
                P.op("pe", lambda h, pk=pk, kt=kt, hh=hh: h.matmul(
                    pk[0:64, :n], lhsT=wfs[:, kt, 512 + hh * 64:512 + (hh + 1) * 64], rhs=xs[:, kt, :],
                    start=(kt == 0), stop=(kt == 7)), reads=[wfsb, xsb], writes=[pkb])
            P.op("act", lambda h, pk=pk, hh=hh: h.activation(out=F.KT[hh][0:64, 1024:1024 + n], in_=pk[0:64, :n],
                                                             func=AF.Copy), reads=[pkb], writes=[F.KTb[hh]])
            P.op("dve", lambda h, pk=pk, hh=hh: h.tensor_copy(out=kso[:, hh, :], in_=pk[0:64, :n]), reads=[pkb],
                 writes=[ksob])
            pq, pqb = nb()
            for kt in range(8):
                P.op("pe", lambda h, pq=pq, kt=kt, hh=hh: h.matmul(
                    pq[0:64, :n], lhsT=wfs[:, kt, hh * 64:(hh + 1) * 64], rhs=xs[:, kt, :],
                    start=(kt == 0), stop=(kt == 7)), reads=[wfsb, xsb], writes=[pqb])
            F.q_aug(pq, pqb, hh, n)
        P.dma("sp", lambda h: h.dma_start(out=io["o_sfoxk"].rearrange("(h d) n -> d h n", d=64), in_=kso[:, :, :]),
              reads=[ksob])
        pv, pvb = nb()
        for kt in range(8):
            P.op("pe", lambda h, kt=kt: h.matmul(pv[0:n, 0:512], lhsT=xs[:, kt, :], rhs=wfs[:, kt, 1024:1536],
                                                 start=(kt == 0), stop=(kt == 7)), reads=[wfsb, xsb], writes=[pvb])
        P.op("act", lambda h: h.activation(out=F.VA[0:n, 8, :, 0:64],
                                           in_=pv[0:n, 0:512].rearrange("p (h d) -> p h d", d=64), func=AF.Copy),
             reads=[pvb], writes=[F.VAb])
        vso = P.sb([n, 512], F32, "svso")
        vsob = Buf("svso")
        P.op("dve", lambda h: h.tensor_copy(out=vso[:, :], in_=pv[0:n, 0:512]), reads=[pvb], writes=[vsob])
        P.dma("sp", lambda h: h.dma_start(out=io["o_sfoxv"][:, :], in_=vso[:, :]), reads=[vsob])
        ps_rot = Rot(P, 1, [128, 512], F32, "unused", psum=False)
        ps_rot.items = [banks[2], banks[3], banks[4]]
        kbl = [(kb, 128, 0, False) for kb in range(8)] + [(8, n, 0, True)]
        for h0_ in range(0, 8, 2):
            stores = {}
            for hh in (h0_, h0_ + 1):
                def store(om, omb, hh=hh):
                    P.dma("sp", lambda h: h.dma_start(out=io["SMX"][hh * 64:(hh + 1) * 64, :], in_=om[:, :n]),
                          reads=[omb], writes=[io["SMXb"]])
                stores[hh] = store
            F.attend_multi([h0_, h0_ + 1], n, kbl, ps_rot, {h0_: banks[5], h0_ + 1: banks[6]}, banks[7][0], banks[7][1],
                           stores)
        G = Gdn(P, C, n, n, banks)
        stage = P.sb([128, 8, 514], F32, "swgst")
        for hd in range(4):
            G.load_params(io["wgdn_s"][hd], io["gcw_s"][hd], io["gsc_s"][hd], io["gng"], stage)
            for i in range(3):
                P.dma("sp", lambda h, i=i, hd=hd: h.dma_start(out=G.XC[i][:, 0:3], in_=io["sconv"][hd, i]),
                      writes=[G.XCb[i]])
            P.dma("sp", lambda h, hd=hd: h.dma_start(out=G.S32[:, :], in_=io["sstate"][hd]), writes=[G.S32b])
            P.op("act", lambda h: h.activation(out=G.Sbf[:, :], in_=G.S32[:, :], func=AF.Copy), reads=[G.S32b],
                 writes=[G.Sbfb])

            def store(om, omb, hd=hd):
                P.dma("sp", lambda h: h.dma_start(out=io["SMX"][512 + hd * 128:512 + (hd + 1) * 128, :], in_=om[:, :]),
                      reads=[omb], writes=[io["SMXb"]])
            G.block(xs, xsb, True, store)
            P.dma("sp", lambda h, hd=hd: h.dma_start(out=io["o_sgstate"][hd], in_=G.S32[:, :]), reads=[G.S32b])
            for i in range(3):
                P.dma("sp", lambda h, i=i, hd=hd: h.dma_start(out=io["o_sgconv"][hd, i], in_=G.XC[i][:, 0:3]),
                      reads=[G.XCb[i]])
        P.barrier()
        P.emit()


def groups_():
    return [(i * 512, 512) for i in range(TPC // 512)] + [(TPC, NS)]


def ffn_pass(P, C, io, consts, lnp_sb, fidx, lnidx, src_v, src_b, dst_v, dst_b, xg=None, out_final=None):
    from contextlib import ExitStack
    with ExitStack() as ph:
        P.stack = ph
        T = TPhase(P, consts)
        T.load_wout(io["w_out"][fidx])
        xbufs = {}
        pending = []
        eps = LN_EPS / (ALPHA * ALPHA)
        grp = groups_()
        WS = io["WBF"]
        WSb = io["WBFb"]
        sched = [(gi, jp) for gi in range(len(grp)) for jp in range(11)]
        issued = [0]
        fifo = []
        wbq = []
        LA = 3

        def issue_one():
            gi, jp = sched[issued[0]]
            issued[0] += 1
            w, wb = T.win.next()
            if gi == 0:
                ws, wsb = T.wstage.next()
                P.dma("sp", lambda h: h.dma_start(out=ws[:, :, :], in_=io["w_in"][fidx][jp].rearrange("p (k c) -> p k c", c=512)),
                      writes=[wsb])
                P.op("pool", lambda h: h.tensor_copy(out=w[:, :, :], in_=ws[:, :, :]), reads=[wsb], writes=[wb])
                while wbq:
                    wbq.pop(0)()
                wbq.append(lambda: P.dma("sp", lambda h: h.dma_start(
                    out=WS[jp].rearrange("p (k c) -> p k c", c=512), in_=w[:, :, :]), reads=[wb], writes=[WSb[jp]]))
            else:
                while wbq:
                    wbq.pop(0)()
                P.dma("sp", lambda h: h.dma_start(out=w[:, :, :], in_=WS[jp].rearrange("p (k c) -> p k c", c=512)),
                      reads=[WSb[jp]], writes=[wb])
            fifo.append((w, wb))

        def provider():
            while issued[0] < len(sched) and len(fifo) < LA:
                issue_one()
            return fifo.pop(0)

        def load_x(gi):
            t0, nt = grp[gi]
            x32, x32b0 = T.x32.next()
            xb, xbb0 = T.xb.next()
            if id(x32b0) not in xbufs:
                xbufs[id(x32b0)] = ([Buf(f"x32m{m}") for m in range(8)], [Buf(f"xbm{m}") for m in range(8)])
            x32bs, xbbs = xbufs[id(x32b0)]
            P.dma("sp", lambda h: h.dma_start(out=x32[:, :, :nt], in_=src_v[:, :, t0:t0 + nt]),
                  reads=[src_b[gi]] if src_b else [], writes=x32bs)
            P.op("act", lambda h: h.activation(out=xb[:, :, :nt], in_=x32[:, :, :nt], func=AF.Copy),
                 reads=x32bs, writes=xbbs)
            return x32, x32bs, xb, xbbs

        nxt = load_x(0)
        for gi, (t0, nt) in enumerate(grp):
            x32, x32bs, xb, xbbs = nxt
            T.ffn_in(xb, xbbs, nt, provider, pending)
            if gi + 1 < len(grp):
                nxt = load_x(gi + 1)
            T.ffn_out(x32, x32bs, nt, eps)
            pending = T.norm_items(x32, x32bs, xb, xbbs, nt, lnp_sb[:, 0, lnidx, :], lnp_sb[:, 1, lnidx, :])

            def stores(gi=gi, t0=t0, nt=nt, x32=x32, xb=xb, x32bs=x32bs, xbbs=xbbs):
                if out_final is not None:
                    if gi < TPC // 512:
                        P.dma("sp", lambda h: h.dma_start(
                            out=out_final[0].rearrange("(k p) n -> p k n", p=128)[:, :, t0:t0 + nt], in_=x32[:, :, :nt]),
                            reads=x32bs)
                    else:
                        P.dma("sp", lambda h: h.dma_start(
                            out=out_final[1].rearrange("(k p) n -> p k n", p=128), in_=x32[:, :, :nt]), reads=x32bs)
                else:
                    P.dma("sp", lambda h: h.dma_start(out=dst_v[:, :, t0:t0 + nt], in_=x32[:, :, :nt]),
                          reads=x32bs, writes=[dst_b[gi]])
                if xg and gi < TPC // 512:
                    P.dma("sp", lambda h: h.dma_start(
                        out=io["XGinp"][gi].rearrange("(k p) n -> p k n", p=128), in_=xb[:, :, :nt]),
                        reads=xbbs, writes=io["XGinpb"][gi])
                    gather_one(P, io, "XG", gi)
            pending.append(stores)
        while pending:
            pending.pop(0)()
        P.barrier()
        P.emit()


RG_ = [[0, 1, 2, 3], [4, 5, 6, 7]]
PIECE = 2048


def gather_one(P, io, name, j):
    inp, outp = io[name + "inp"][j], io[name + "p"][j]
    P.coll(lambda h: h.collective_compute("AllGather", ALU.bypass, replica_groups=RG_, ins=[inp.opt()],
                                          outs=[outp.opt()]),
           reads=io[name + "inpb"][j], writes=[io[name + "pb"][j]])


def gather_pieces(P, io, name):
    for j in range(len(io[name + "p"])):
        gather_one(P, io, name, j)


class ProjPass(LNBase):
    def __init__(self, P, consts, nw):
        self._ln_alloc(P, consts)
        self.x32 = Rot(P, 2, [128, 8, 512], F32, "px32_")
        self.xb = Rot(P, 2, [128, 8, 512], BF16, "pxb_")
        self.A = Rot(P, 2, [128, 8, 512], BF16, "pA_")
        self.Bq = Rot(P, 4, [128, 8, 512], BF16, "pB_")
        self.w = [P.sb([128, 8, 1024], BF16, f"pw{i}") for i in range(nw)]
        self.wb = [Buf(f"pw{i}") for i in range(nw)]
        self.wst = Rot(P, 2, [128, 2, 1024], F32, "pwst_")
        self.po = Rot(P, 2, [128, 512], F32, "ppo_", psum=True)
        self.pg = Rot(P, 2, [128, 512], F32, "ppg_", psum=True)
        self.rm = P.sb([128, 4], F32, "rmask")
        self.rmb = Buf("rmask")
        self.tmp = Rot(P, 4, [128, 512], F32, "ptmp_")
        self.xbufs = {}

    def bufs(self, x32b0):
        if id(x32b0) not in self.xbufs:
            self.xbufs[id(x32b0)] = ([Buf(f"px32m{m}") for m in range(8)], [Buf(f"pxbm{m}") for m in range(8)])
        return self.xbufs[id(x32b0)]

    def ln_now(self, x32, x32bs, xb, xbbs, nt, g_ap, b_ap):
        self.ln_stats(x32, x32bs, nt, LN_EPS / (ALPHA * ALPHA))
        for it in self.norm_items(x32, x32bs, xb, xbbs, nt, g_ap, b_ap):
            it()

    def load_w(self, i, ap):
        P = self.P
        for q in range(4):
            st, stb = self.wst.next()
            P.dma("act", lambda h, st=st, q=q: h.dma_start(
                out=st[:, :, :], in_=ap.rearrange("(k p) c -> p k c", p=128)[:, 2 * q:2 * q + 2, :]), writes=[stb])
            P.op("pool", lambda h, st=st, q=q: h.tensor_copy(out=self.w[i][:, 2 * q:2 * q + 2, :], in_=st[:, :, :]),
                 reads=[stb], writes=[self.wb[i]])

    def combine(self, gathered, gb, t0, nt, A, Ab):
        P = self.P
        for q in range(4):
            Bt, Bb = self.Bq.next()
            tok = q * TPC + t0
            pj, pc = tok // PIECE, tok % PIECE
            P.dma("sp", lambda h, Bt=Bt, pj=pj, pc=pc: h.dma_start(
                out=Bt[:, :, :nt], in_=gathered[pj].rearrange("(k p) n -> p k n", p=128)[:, :, pc:pc + nt]),
                reads=[gb[pj]], writes=[Bb])
            if q == 0:
                P.op("pool", lambda h, Bt=Bt: h.tensor_scalar(out=A[:, :, :nt], in0=Bt[:, :, :nt],
                                                              scalar1=self.rm[:, 0:1], scalar2=None, op0=ALU.mult),
                     reads=[Bb, self.rmb], writes=[Ab])
            else:
                P.op("dve", lambda h, Bt=Bt, q=q: h.scalar_tensor_tensor(out=A[:, :, :nt], in0=Bt[:, :, :nt],
                                                                        scalar=self.rm[:, q:q + 1], in1=A[:, :, :nt],
                                                                        op0=ALU.mult, op1=ALU.add),
                     reads=[Bb, self.rmb, Ab], writes=[Ab])


def proj_pass_even(P, C, io, consts, lnp_sb, src_v, src_b, dst_v, dst_b):
    from contextlib import ExitStack
    with ExitStack() as ph:
        P.stack = ph
        T = ProjPass(P, consts, 1)
        P.dma("sp", lambda h: h.dma_start(out=T.rm[:, :], in_=io["rmask"][:, :]), writes=[T.rmb])
        T.load_w(0, io["even_w_out"])
        perm = [0, 4, 1, 5, 2, 6, 3, 7]
        for gi, (t0, nt) in enumerate(groups_()):
            x32, x32b0 = T.x32.next()
            xb, xbb = T.xb.next()
            x32bs, xbbs = T.bufs(x32b0)
            A, Ab = T.A.next()
            P.dma("sp", lambda h, x32=x32, t0=t0, nt=nt: h.dma_start(out=x32[:, :, :nt], in_=src_v[:, :, t0:t0 + nt]),
                  reads=[src_b[gi]], writes=x32bs)
            if gi < TPC // 512:
                T.combine(io["MXp"], io["MXpb"], t0, nt, A, Ab)
                kmap = perm
            else:
                P.dma("sp", lambda h, A=A, nt=nt: h.dma_start(out=A[:, :, :nt],
                                                              in_=io["SMX"].rearrange("(k p) n -> p k n", p=128)),
                      reads=[io["SMXb"]], writes=[Ab])
                kmap = list(range(8))
            for m in range(8):
                po, pob = T.po.next()
                for kt in range(8):
                    P.op("pe", lambda h, po=po, kt=kt, m=m, A=A, kmap=kmap, nt=nt: h.matmul(
                        po[:, :nt], lhsT=T.w[0][:, kmap[kt], m * 128:(m + 1) * 128], rhs=A[:, kt, :nt],
                        start=(kt == 0), stop=(kt == 7)), reads=[T.wb[0], Ab], writes=[pob])
                P.op("dve", lambda h, po=po, m=m, x32=x32, nt=nt: h.scalar_tensor_tensor(
                    out=x32[:, m, :nt], in0=po[:, :nt], scalar=1.0 / ALPHA, in1=x32[:, m, :nt], op0=ALU.mult,
                    op1=ALU.add), reads=[pob, x32bs[m]], writes=[x32bs[m]])
            T.ln_now(x32, x32bs, xb, xbbs, nt, lnp_sb[:, 0, 1, :], lnp_sb[:, 1, 1, :])
            P.dma("sp", lambda h, x32=x32, t0=t0, nt=nt: h.dma_start(out=dst_v[:, :, t0:t0 + nt], in_=x32[:, :, :nt]),
                  reads=x32bs, writes=[dst_b[gi]])
        P.barrier()
        P.emit()


def proj_pass_odd(P, C, io, consts, lnp_sb, src_v, src_b, dst_v, dst_b):
    from contextlib import ExitStack
    with ExitStack() as ph:
        P.stack = ph
        T = ProjPass(P, consts, 2)
        P.dma("sp", lambda h: h.dma_start(out=T.rm[:, :], in_=io["rmask"][:, :]), writes=[T.rmb])
        T.load_w(0, io["glu_w"])
        T.load_w(1, io["odd_w_out"])
        gbias = P.sb([128, 8], F32, "glub")
        gbb = Buf("glub")
        P.dma("sp", lambda h: h.dma_start(out=gbias[:, :], in_=io["glu_b"][:, :]), writes=[gbb])
        zz32 = Rot(P, 1, [128, 8, 512], F32, "zz32_")
        zzb = Rot(P, 2, [128, 8, 512], BF16, "zzb_")
        sg = Rot(P, 2, [128, 512], F32, "sg_")
        for gi, (t0, nt) in enumerate(groups_()):
            x32, x32b0 = T.x32.next()
            xb, xbb = T.xb.next()
            x32bs, xbbs = T.bufs(x32b0)
            A, Ab = T.A.next()
            P.dma("sp", lambda h, x32=x32, t0=t0, nt=nt: h.dma_start(out=x32[:, :, :nt], in_=src_v[:, :, t0:t0 + nt]),
                  reads=[src_b[gi]], writes=x32bs)
            if gi < TPC // 512:
                T.combine(io["YSp"], io["YSpb"], t0, nt, A, Ab)
            else:
                P.dma("sp", lambda h, A=A, nt=nt: h.dma_start(out=A[:, :, :nt],
                                                              in_=io["SYS"].rearrange("(k p) n -> p k n", p=128)),
                      reads=[io["SYSb"]], writes=[Ab])
            z32, z32b = zz32.next()
            zb, zbb = zzb.next()
            P.op("act", lambda h, A=A, z32=z32, nt=nt: h.activation(out=z32[:, :, :nt], in_=A[:, :, :nt], func=AF.Square),
                 reads=[Ab], writes=[z32b])
            P.op("dve", lambda h, z32=z32, nt=nt: h.tensor_scalar(out=z32[:, :, :nt], in0=z32[:, :, :nt], scalar1=0.044715,
                                                                scalar2=1.0, op0=ALU.mult, op1=ALU.add),
                 reads=[z32b], writes=[z32b])
            P.op("dve", lambda h, A=A, z32=z32, nt=nt: h.tensor_tensor(out=z32[:, :, :nt], in0=z32[:, :, :nt],
                                                                      in1=A[:, :, :nt], op=ALU.mult),
                 reads=[z32b, Ab], writes=[z32b])
            P.op("act", lambda h, z32=z32, nt=nt: h.activation(out=z32[:, :, :nt], in_=z32[:, :, :nt], func=AF.Sigmoid,
                                                              scale=1.5957691216057308), reads=[z32b], writes=[z32b])
            P.op("dve", lambda h, A=A, z32=z32, nt=nt: h.tensor_tensor(out=z32[:, :, :nt], in0=z32[:, :, :nt],
                                                                      in1=A[:, :, :nt], op=ALU.mult),
                 reads=[z32b, Ab], writes=[z32b])
            P.op("pool", lambda h, zb=zb, z32=z32, nt=nt: h.tensor_copy(out=zb[:, :, :nt], in_=z32[:, :, :nt]),
                 reads=[z32b], writes=[zbb])
            for m in range(8):
                pg, pgb = T.pg.next()
                for kt in range(8):
                    P.op("pe", lambda h, pg=pg, kt=kt, m=m, zb=zb, nt=nt: h.matmul(
                        pg[:, :nt], lhsT=T.w[0][:, kt, m * 128:(m + 1) * 128], rhs=zb[:, kt, :nt],
                        start=(kt == 0), stop=(kt == 7)), reads=[T.wb[0], zbb], writes=[pgb])
                s_, sb_ = sg.next()
                P.op("act", lambda h, pg=pg, s_=s_, m=m, nt=nt: h.activation(out=s_[:, :nt], in_=pg[:, :nt],
                                                                            func=AF.Sigmoid, bias=gbias[:, m:m + 1]),
                     reads=[pgb, gbb], writes=[sb_])
                P.op("dve", lambda h, s_=s_, m=m, A=A, z32=z32, nt=nt: h.tensor_tensor(
                    out=A[:, m, :nt], in0=z32[:, m, :nt], in1=s_[:, :nt], op=ALU.mult), reads=[z32b, sb_, Ab],
                    writes=[Ab])
            for m in range(8):
                po, pob = T.po.next()
                for kt in range(8):
                    P.op("pe", lambda h, po=po, kt=kt, m=m, A=A, nt=nt: h.matmul(
                        po[:, :nt], lhsT=T.w[1][:, kt, m * 128:(m + 1) * 128], rhs=A[:, kt, :nt],
                        start=(kt == 0), stop=(kt == 7)), reads=[T.wb[1], Ab], writes=[pob])
                P.op("dve", lambda h, po=po, m=m, x32=x32, nt=nt: h.scalar_tensor_tensor(
                    out=x32[:, m, :nt], in0=po[:, :nt], scalar=1.0 / ALPHA, in1=x32[:, m, :nt], op0=ALU.mult,
                    op1=ALU.add), reads=[pob, x32bs[m]], writes=[x32bs[m]])
            T.ln_now(x32, x32bs, xb, xbbs, nt, lnp_sb[:, 0, 4, :], lnp_sb[:, 1, 4, :])
            P.dma("sp", lambda h, x32=x32, t0=t0, nt=nt: h.dma_start(out=dst_v[:, :, t0:t0 + nt], in_=x32[:, :, :nt]),
                  reads=x32bs, writes=[dst_b[gi]])
        P.barrier()
        P.emit()


class S5:
    def __init__(self, P, C, NSt, nct, NB, banks):
        self.P, self.C, self.NSt, self.nct, self.NB, self.B = P, C, NSt, nct, NB, banks
        sb = P.sb
        self.wu = sb([128, 8, nct * 128], BF16, "s5wu")
        self.wub = Buf("s5wu")
        self.par = sb([128, 3, NSt], F32, "s5par")
        self.BB = [sb([128, nct, 128], BF16, f"s5BB{i}") for i in range(2)]
        self.CC = [sb([128, NSt, 32], BF16, f"s5CC{i}") for i in range(2)]
        self.dv = sb([128, nct], F32, "s5d")
        self.pb = Buf("s5params")
        self.tab = {k: sb([128, NSt, NB], F32, "s5t_" + k) for k in ("Er", "Ei", "PRr", "PRi")}
        self.tabb = Buf("s5tab")
        self.sm = {k: sb([128, NSt], F32, "s5s_" + k) for k in
                   ("dl", "rho", "th", "t1", "t2", "sn", "cs", "ckr", "cki", "kr", "ki", "nr", "ni", "den",
                    "inr", "ini", "wlr", "wli", "hr", "hi")}
        self.smb = Buf("s5small")
        self.initb = Buf("s5init")
        self.wlb = Buf("s5wl")
        self.tmp = [sb([128, NSt, max(NB // 2, 1)], F32, f"s5tmp{i}") for i in range(2)]
        self.tmpb = Buf("s5tmp")
        self.u32 = sb([128, nct, NB], F32, "s5u32")
        self.ub = sb([128, nct, NB], BF16, "s5ub")
        self.ubuf = Buf("s5u")
        self.t = Rot(P, 16, [128, NB], F32, "s5w_")
        self.bp = Rot(P, 8, [128, NB], F32, "s5bp_")
        self.wv = Rot(P, 8, [128, NB], F32, "s5wv_")
        self.xv = Rot(P, 8, [128, NB], BF16, "s5xv_")
        self.yo = Rot(P, 2, [128, NB], BF16, "s5yo_")
        self.rot = 0

    def load(self, wu_ap, par_ap, bbr_ap, bbi_ap, ccr_ap, cci_ap, d_ap, h0=None):
        P, NSt, nct, NB = self.P, self.NSt, self.nct, self.NB
        wst = Rot(P, 2, [128, nct * 128], F32, "s5wst_")
        for kt in range(8):
            st, stb = wst.next()
            P.dma("act", lambda h, st=st, kt=kt: h.dma_start(out=st[:, :], in_=wu_ap[:, kt * nct * 128:(kt + 1) * nct * 128]),
                  writes=[stb])
            P.op("pool", lambda h, st=st, kt=kt: h.tensor_copy(out=self.wu[:, kt, :], in_=st[:, :]), reads=[stb],
                 writes=[self.wub])
        P.dma("sp", lambda h: h.dma_start(out=self.par[:, :, :], in_=par_ap), writes=[self.pb])
        P.dma("sp", lambda h: h.dma_start(out=self.dv[:, :], in_=d_ap), writes=[self.pb])
        bst = P.sb([128, nct, 128], F32, "s5bst")
        cst_ = P.sb([128, NSt, 32], F32, "s5cst")
        for i, ap in enumerate((bbr_ap, bbi_ap)):
            P.dma("sp", lambda h, ap=ap: h.dma_start(out=bst[:, :, :], in_=ap), writes=[self.pb])
            P.op("pool", lambda h, i=i: h.tensor_copy(out=self.BB[i][:, :, :], in_=bst[:, :, :]), reads=[self.pb],
                 writes=[self.pb])
        for i, ap in enumerate((ccr_ap, cci_ap)):
            P.dma("sp", lambda h, ap=ap: h.dma_start(out=cst_[:, :, :], in_=ap), writes=[self.pb])
            if i == 0:
                P.op("pool", lambda h: h.tensor_copy(out=self.CC[0][:, :, :], in_=cst_[:, :, :]), reads=[self.pb],
                     writes=[self.pb])
            else:
                P.op("pool", lambda h: h.tensor_scalar(out=self.CC[1][:, :, :], in0=cst_[:, :, :], scalar1=-1.0,
                                                       scalar2=None, op0=ALU.mult), reads=[self.pb], writes=[self.pb])
        sm, smb = self.sm, self.smb
        lre, lim, lst = self.par[:, 0, :], self.par[:, 1, :], self.par[:, 2, :]
        PI = float(np.pi)

        def v(fn, reads=(), writes=()):
            P.op("dve", fn, reads=[self.pb, smb] + list(reads), writes=[smb] + list(writes))

        P.op("act", lambda h: h.activation(out=sm["dl"][:, :], in_=lst, func=AF.Exp), reads=[self.pb], writes=[smb])
        v(lambda h: h.tensor_tensor(out=sm["t1"][:, :], in0=sm["dl"][:, :], in1=lre, op=ALU.mult))
        P.op("act", lambda h: h.activation(out=sm["rho"][:, :], in_=sm["t1"][:, :], func=AF.Exp), reads=[smb], writes=[smb])
        v(lambda h: h.tensor_tensor(out=sm["th"][:, :], in0=sm["dl"][:, :], in1=lim, op=ALU.mult))
        for (dst, shift) in (("sn", 0.0), ("cs", PI / 2)):
            v(lambda h, shift=shift: h.tensor_scalar(out=sm["t1"][:, :], in0=sm["th"][:, :], scalar1=shift, scalar2=None,
                                                     op0=ALU.add))
            v(lambda h: h.tensor_copy(out=sm["t2"][:, :], in_=sm["t1"][:, :]))
            for thr in (PI, 3 * PI, 5 * PI, 7 * PI):
                v(lambda h, thr=thr: h.tensor_scalar(out=sm["nr"][:, :], in0=sm["t1"][:, :], scalar1=thr,
                                                     scalar2=-2.0 * PI, op0=ALU.is_gt, op1=ALU.mult))
                v(lambda h: h.tensor_tensor(out=sm["t2"][:, :], in0=sm["t2"][:, :], in1=sm["nr"][:, :], op=ALU.add))
            P.op("act", lambda h, dst=dst: h.activation(out=sm[dst][:, :], in_=sm["t2"][:, :], func=AF.Sin), reads=[smb],
                 writes=[smb])
        v(lambda h: h.tensor_tensor(out=sm["nr"][:, :], in0=sm["rho"][:, :], in1=sm["cs"][:, :], op=ALU.mult))
        v(lambda h: h.tensor_scalar(out=sm["nr"][:, :], in0=sm["nr"][:, :], scalar1=-1.0, scalar2=None, op0=ALU.add))
        v(lambda h: h.tensor_tensor(out=sm["ni"][:, :], in0=sm["rho"][:, :], in1=sm["sn"][:, :], op=ALU.mult))
        v(lambda h: h.tensor_tensor(out=sm["den"][:, :], in0=lre, in1=lre, op=ALU.mult))
        v(lambda h: h.tensor_tensor(out=sm["t1"][:, :], in0=lim, in1=lim, op=ALU.mult))
        v(lambda h: h.tensor_tensor(out=sm["den"][:, :], in0=sm["den"][:, :], in1=sm["t1"][:, :], op=ALU.add))
        v(lambda h: h.reciprocal(out=sm["den"][:, :], in_=sm["den"][:, :]))
        v(lambda h: h.tensor_tensor(out=sm["t1"][:, :], in0=sm["nr"][:, :], in1=lre, op=ALU.mult))
        v(lambda h: h.tensor_tensor(out=sm["t2"][:, :], in0=sm["ni"][:, :], in1=lim, op=ALU.mult))
        v(lambda h: h.tensor_tensor(out=sm["kr"][:, :], in0=sm["t1"][:, :], in1=sm["t2"][:, :], op=ALU.add))
        v(lambda h: h.tensor_tensor(out=sm["kr"][:, :], in0=sm["kr"][:, :], in1=sm["den"][:, :], op=ALU.mult))
        v(lambda h: h.tensor_tensor(out=sm["t1"][:, :], in0=sm["ni"][:, :], in1=lre, op=ALU.mult))
        v(lambda h: h.tensor_tensor(out=sm["t2"][:, :], in0=sm["nr"][:, :], in1=lim, op=ALU.mult))
        v(lambda h: h.tensor_tensor(out=sm["ki"][:, :], in0=sm["t1"][:, :], in1=sm["t2"][:, :], op=ALU.subtract))
        v(lambda h: h.tensor_tensor(out=sm["ki"][:, :], in0=sm["ki"][:, :], in1=sm["den"][:, :], op=ALU.mult))
        v(lambda h: h.tensor_copy(out=sm["ckr"][:, :], in_=sm["cs"][:, :]))
        v(lambda h: h.tensor_copy(out=sm["cki"][:, :], in_=sm["sn"][:, :]))
        Er, Ei, PRr, PRi = (self.tab[k] for k in ("Er", "Ei", "PRr", "PRi"))
        tb = self.tabb
        t0_, t1_ = self.tmp

        def bc(name, w):
            return sm[name][:, :].unsqueeze(2).to_broadcast([128, NSt, w])

        def tv(fn):
            P.op("dve", fn, reads=[smb, tb, self.tmpb], writes=[tb, self.tmpb])

        tv(lambda h: h.memset(Er[:, :, 0:1], 1.0))
        tv(lambda h: h.memset(Ei[:, :, 0:1], 0.0))
        w = 1
        while w < NB:
            tv(lambda h, w=w: h.tensor_tensor(out=t0_[:, :, 0:w], in0=Er[:, :, 0:w], in1=bc("ckr", w), op=ALU.mult))
            tv(lambda h, w=w: h.tensor_tensor(out=t1_[:, :, 0:w], in0=Ei[:, :, 0:w], in1=bc("cki", w), op=ALU.mult))
            tv(lambda h, w=w: h.tensor_tensor(out=Er[:, :, w:2 * w], in0=t0_[:, :, 0:w], in1=t1_[:, :, 0:w],
                                              op=ALU.subtract))
            tv(lambda h, w=w: h.tensor_tensor(out=t0_[:, :, 0:w], in0=Er[:, :, 0:w], in1=bc("cki", w), op=ALU.mult))
            tv(lambda h, w=w: h.tensor_tensor(out=t1_[:, :, 0:w], in0=Ei[:, :, 0:w], in1=bc("ckr", w), op=ALU.mult))
            tv(lambda h, w=w: h.tensor_tensor(out=Ei[:, :, w:2 * w], in0=t0_[:, :, 0:w], in1=t1_[:, :, 0:w], op=ALU.add))
            v(lambda h: h.tensor_tensor(out=sm["t1"][:, :], in0=sm["ckr"][:, :], in1=sm["ckr"][:, :], op=ALU.mult))
            v(lambda h: h.tensor_tensor(out=sm["t2"][:, :], in0=sm["cki"][:, :], in1=sm["cki"][:, :], op=ALU.mult))
            v(lambda h: h.tensor_tensor(out=sm["cki"][:, :], in0=sm["ckr"][:, :], in1=sm["cki"][:, :], op=ALU.mult))
            v(lambda h: h.tensor_scalar(out=sm["cki"][:, :], in0=sm["cki"][:, :], scalar1=2.0, scalar2=None, op0=ALU.mult))
            v(lambda h: h.tensor_tensor(out=sm["ckr"][:, :], in0=sm["t1"][:, :], in1=sm["t2"][:, :], op=ALU.subtract))
            w *= 2
        hw_ = max(NB // 2, 1)
        for lo in range(0, NB, hw_):
            sl = slice(lo, lo + hw_)
            tv(lambda h, sl=sl: h.tensor_tensor(out=t0_[:, :, 0:hw_], in0=Er[:, :, sl], in1=bc("kr", hw_), op=ALU.mult))
            tv(lambda h, sl=sl: h.tensor_tensor(out=t1_[:, :, 0:hw_], in0=Ei[:, :, sl], in1=bc("ki", hw_), op=ALU.mult))
            tv(lambda h, sl=sl: h.tensor_tensor(out=PRr[:, :, sl], in0=t0_[:, :, 0:hw_], in1=t1_[:, :, 0:hw_], op=ALU.add))
            tv(lambda h, sl=sl: h.tensor_tensor(out=t0_[:, :, 0:hw_], in0=Er[:, :, sl], in1=bc("ki", hw_), op=ALU.mult))
            tv(lambda h, sl=sl: h.tensor_tensor(out=t1_[:, :, 0:hw_], in0=Ei[:, :, sl], in1=bc("kr", hw_), op=ALU.mult))
            tv(lambda h, sl=sl: h.tensor_tensor(out=PRi[:, :, sl], in0=t0_[:, :, 0:hw_], in1=t1_[:, :, 0:hw_],
                                                op=ALU.subtract))
        ib = self.initb
        if h0 is None:
            P.op("dve", lambda h: h.memset(sm["inr"][:, :], 0.0), writes=[ib])
            P.op("dve", lambda h: h.memset(sm["ini"][:, :], 0.0), writes=[ib])
        else:
            P.dma("sp", lambda h: h.dma_start(out=sm["hr"][:, :], in_=h0[0]), writes=[ib])
            P.dma("sp", lambda h: h.dma_start(out=sm["hi"][:, :], in_=h0[1]), writes=[ib])
            self.cmul(sm["inr"], sm["ini"], sm["hr"], sm["hi"], sm["cs"], sm["sn"], 1.0, [ib, smb], [ib])

    def cmul(self, outr, outi, ar, ai, br, bi, sgn, reads, writes):
        P, sm, smb = self.P, self.sm, self.smb

        def v(fn):
            P.op("dve", fn, reads=list(reads) + [smb], writes=list(writes) + [smb])
        v(lambda h: h.tensor_tensor(out=sm["t1"][:, :], in0=ar[:, :], in1=br[:, :], op=ALU.mult))
        v(lambda h: h.tensor_tensor(out=sm["t2"][:, :], in0=ai[:, :], in1=bi[:, :], op=ALU.mult))
        v(lambda h: h.tensor_tensor(out=sm["nr"][:, :], in0=ar[:, :], in1=bi[:, :], op=ALU.mult))
        v(lambda h: h.tensor_tensor(out=sm["ni"][:, :], in0=ai[:, :], in1=br[:, :], op=ALU.mult))
        if sgn > 0:
            v(lambda h: h.tensor_tensor(out=outr[:, :], in0=sm["t1"][:, :], in1=sm["t2"][:, :], op=ALU.subtract))
            v(lambda h: h.tensor_tensor(out=outi[:, :], in0=sm["nr"][:, :], in1=sm["ni"][:, :], op=ALU.add))
        else:
            v(lambda h: h.tensor_tensor(out=outr[:, :], in0=sm["t1"][:, :], in1=sm["t2"][:, :], op=ALU.add))
            v(lambda h: h.tensor_tensor(out=outi[:, :], in0=sm["ni"][:, :], in1=sm["nr"][:, :], op=ALU.subtract))

    def gb(self):
        it = self.B[self.rot % 4]
        self.rot += 1
        return it

    def block(self, x, xbuf, store_y):
        P, C, NSt, nct, n = self.P, self.C, self.NSt, self.nct, self.NB
        sm, smb = self.sm, self.smb
        Er, Ei, PRr, PRi = (self.tab[k] for k in ("Er", "Ei", "PRr", "PRi"))
        for ct in range(nct):
            pu, pub = self.gb()
            for kt in range(8):
                P.op("pe", lambda h, pu=pu, kt=kt, ct=ct: h.matmul(pu[:, :n], lhsT=self.wu[:, kt, ct * 128:(ct + 1) * 128],
                                                                 rhs=x[:, kt, :n], start=(kt == 0), stop=(kt == 7)),
                     reads=[self.wub, xbuf], writes=[pub])
            P.op("act", lambda h, pu=pu, ct=ct: h.activation(out=self.u32[:, ct, :], in_=pu[:, :n], func=AF.Copy),
                 reads=[pub], writes=[self.ubuf])
            P.op("pool", lambda h, ct=ct: h.tensor_copy(out=self.ub[:, ct, :], in_=self.u32[:, ct, :]), reads=[self.ubuf],
                 writes=[self.ubuf])
        for ct in range(nct):
            py, pyb = self.B[4 + ct % 2]
            for j in range(4):
                m = ct * 4 + j
                pr, prb = self.gb()
                pi_, pib = self.gb()
                P.op("pe", lambda h, pr=pr, j=j, ct=ct: h.matmul(pr[:, :n], lhsT=self.BB[0][j * 32:(j + 1) * 32, ct, :],
                                                                rhs=self.ub[j * 32:(j + 1) * 32, ct, :], start=True,
                                                                stop=True, tile_position=(j * 32, 0)),
                     reads=[self.pb, self.ubuf], writes=[prb])
                P.op("pe", lambda h, pi_=pi_, j=j, ct=ct: h.matmul(pi_[:, :n], lhsT=self.BB[1][j * 32:(j + 1) * 32, ct, :],
                                                                  rhs=self.ub[j * 32:(j + 1) * 32, ct, :], start=True,
                                                                  stop=True, tile_position=(j * 32, 0)),
                     reads=[self.pb, self.ubuf], writes=[pib])
                ts = [self.t.next() for _ in range(4)]
                for (tt, ttb), (src, srcb), tabn in zip(ts, ((pr, prb), (pi_, pib), (pr, prb), (pi_, pib)),
                                                       (PRr, PRi, PRi, PRr)):
                    P.op("dve", lambda h, tt=tt, src=src, tabn=tabn, m=m: h.tensor_tensor(
                        out=tt[:, :], in0=src[:, :n], in1=tabn[:, m, :], op=ALU.mult), reads=[srcb, self.tabb],
                        writes=[ttb])
                (bpr, bprb), (bpi, bpib) = self.bp.next(), self.bp.next()
                P.op("pool", lambda h, bpr=bpr, a=ts[0][0], b=ts[1][0]: h.tensor_tensor(out=bpr[:, :], in0=a[:, :], in1=b[:, :],
                                                                                       op=ALU.subtract),
                     reads=[ts[0][1], ts[1][1]], writes=[bprb])
                P.op("pool", lambda h, bpi=bpi, a=ts[2][0], b=ts[3][0]: h.tensor_tensor(out=bpi[:, :], in0=a[:, :], in1=b[:, :],
                                                                                       op=ALU.add),
                     reads=[ts[2][1], ts[3][1]], writes=[bpib])
                (wr, wrb), (wi, wib) = self.wv.next(), self.wv.next()
                rho_bc = sm["rho"][:, m:m + 1].to_broadcast([128, n])
                P.op("dve", lambda h, wr=wr, bpr=bpr, m=m, rho_bc=rho_bc: h.tensor_tensor_scan(
                    out=wr[:, :], data0=rho_bc, data1=bpr[:, :], initial=sm["inr"][:, m:m + 1], op0=ALU.mult,
                    op1=ALU.add), reads=[bprb, smb, self.initb], writes=[wrb])
                P.op("dve", lambda h, wi=wi, bpi=bpi, m=m, rho_bc=rho_bc: h.tensor_tensor_scan(
                    out=wi[:, :], data0=rho_bc, data1=bpi[:, :], initial=sm["ini"][:, m:m + 1], op0=ALU.mult,
                    op1=ALU.add), reads=[bpib, smb, self.initb], writes=[wib])
                P.op("act", lambda h, wr=wr, m=m: h.activation(out=sm["wlr"][:, m:m + 1], in_=wr[:, n - 1:n], func=AF.Copy),
                     reads=[wrb], writes=[self.wlb])
                P.op("act", lambda h, wi=wi, m=m: h.activation(out=sm["wli"][:, m:m + 1], in_=wi[:, n - 1:n], func=AF.Copy),
                     reads=[wib], writes=[self.wlb])
                ts = [self.t.next() for _ in range(4)]
                for ii, ((tt, ttb), (src, srcb), tabn) in enumerate(zip(ts, ((wr, wrb), (wi, wib), (wr, wrb), (wi, wib)),
                                                                     (Er, Ei, Ei, Er))):
                    P.op("dve" if ii < 2 else "pool", lambda h, tt=tt, src=src, tabn=tabn, m=m: h.tensor_tensor(
                        out=tt[:, :], in0=src[:, :], in1=tabn[:, m, :], op=ALU.mult), reads=[srcb, self.tabb],
                        writes=[ttb])
                (xr, xrb), (xi, xib) = self.xv.next(), self.xv.next()
                P.op("pool", lambda h, xr=xr, a=ts[0][0], b=ts[1][0]: h.tensor_tensor(out=xr[:, :], in0=a[:, :], in1=b[:, :],
                                                                                     op=ALU.subtract),
                     reads=[ts[0][1], ts[1][1]], writes=[xrb])
                P.op("pool", lambda h, xi=xi, a=ts[2][0], b=ts[3][0]: h.tensor_tensor(out=xi[:, :], in0=a[:, :], in1=b[:, :],
                                                                                     op=ALU.add),
                     reads=[ts[2][1], ts[3][1]], writes=[xib])
                P.op("pe", lambda h, xr=xr, m=m, j=j: h.matmul(py[j * 32:(j + 1) * 32, :n], lhsT=self.CC[0][:, m, :],
                                                              rhs=xr[:, :], start=True, stop=False,
                                                              tile_position=(0, j * 32)),
                     reads=[self.pb, xrb], writes=[pyb])
                P.op("pe", lambda h, xi=xi, m=m, j=j: h.matmul(py[j * 32:(j + 1) * 32, :n], lhsT=self.CC[1][:, m, :],
                                                              rhs=xi[:, :], start=False, stop=True,
                                                              tile_position=(0, j * 32)),
                     reads=[self.pb, xib], writes=[pyb])
            yo, yob = self.yo.next()
            P.op("dve", lambda h, yo=yo, ct=ct: h.scalar_tensor_tensor(out=yo[:, :], in0=self.u32[:, ct, :],
                                                                      scalar=self.dv[:, ct:ct + 1], in1=py[:, :n],
                                                                      op0=ALU.mult, op1=ALU.add),
                 reads=[self.ubuf, self.pb, pyb], writes=[yob])
            store_y(ct, yo, yob)
        self.cmul(sm["inr"], sm["ini"], sm["wlr"], sm["wli"], sm["ckr"], sm["cki"], 1.0, [self.wlb, self.initb],
                  [self.initb])

    def final_state(self, out_r, out_i):
        P, sm = self.P, self.sm
        self.cmul(sm["hr"], sm["hi"], sm["inr"], sm["ini"], sm["cs"], sm["sn"], -1.0, [self.initb], [self.initb])
        P.dma("sp", lambda h: h.dma_start(out=out_r, in_=sm["hr"][:, :]), reads=[self.initb, self.smb])
        P.dma("sp", lambda h: h.dma_start(out=out_i, in_=sm["hi"][:, :]), reads=[self.initb, self.smb])


def phase_s5_prompt(P, C, io):
    from contextlib import ExitStack
    with ExitStack() as ph:
        P.stack = ph
        banks = [(P.ps([128, 512], F32, f"s5bank{i}"), Buf(f"s5bank{i}", excl=True)) for i in range(6)]
        S = S5(P, C, 8, 2, 512, banks)
        S.load(io["s5wu"], io["s5par"], io["s5bbr"], io["s5bbi"], io["s5ccr"], io["s5cci"], io["s5d"])
        xb = Rot(P, 2, [128, 8, 512], BF16, "s5xb_")
        XG = None
        for tb in range(SEQ // 512):
            rk, cb = tb // (TPC // 512), (tb % (TPC // 512)) * 512
            x, xbuf = xb.next()
            P.dma("sp", lambda h, x=x, rk=rk, cb=cb: h.dma_start(
                out=x[:, :, :], in_=io["XGp"][cb // 512][rk * 1024:(rk + 1) * 1024, :].rearrange("(k p) n -> p k n", p=128)),
                reads=[io["XGpb"][cb // 512]], writes=[xbuf])

            def store(ct, yo, yob, tb=tb):
                P.dma("sp", lambda h: h.dma_start(
                    out=io["YSinp"][tb // 4][ct * 128:(ct + 1) * 128, (tb % 4) * 512:(tb % 4 + 1) * 512],
                    in_=yo[:, :]), reads=[yob], writes=[io["YSinb"][tb]])
            S.block(x, xbuf, store)
            if tb % (PIECE // 512) == PIECE // 512 - 1:
                gather_one(P, io, "YS", tb // (PIECE // 512))
        S.final_state(io["o_s5re"], io["o_s5im"])
        P.barrier()
        P.emit()


def phase_s5_sample(P, C, io):
    from contextlib import ExitStack
    n = NS
    with ExitStack() as ph:
        P.stack = ph
        banks = [(P.ps([128, 512], F32, f"s5sbank{i}"), Buf(f"s5sbank{i}", excl=True)) for i in range(6)]
        S = S5(P, C, 32, 8, n, banks)
        S.load(io["s5wu_s"], io["s5par_s"], io["s5bbr_s"], io["s5bbi_s"], io["s5ccr_s"], io["s5cci_s"], io["s5d_s"],
               h0=(io["s5h0r"], io["s5h0i"]))
        x32 = P.sb([128, 8, n], F32, "s5sx32")
        xs = P.sb([128, 8, n], BF16, "s5sxb")
        xsb = Buf("s5sxb")
        P.dma("sp", lambda h: h.dma_start(out=x32[:, :, :], in_=io["X4v"][:, :, TPC:TPC + n]), reads=[io["X4b"][-1]],
              writes=[xsb])
        P.op("act", lambda h: h.activation(out=xs[:, :, :], in_=x32[:, :, :], func=AF.Copy), reads=[xsb], writes=[xsb])

        def store(ct, yo, yob):
            P.dma("sp", lambda h: h.dma_start(out=io["SYS"][ct * 128:(ct + 1) * 128, :], in_=yo[:, :]), reads=[yob],
                  writes=[io["SYSb"]])
        S.block(xs, xsb, store)
        S.final_state(io["o_ss5re"], io["o_ss5im"])
        P.barrier()
        P.emit()


def build(stop="all"):
    from contextlib import ExitStack
    nc = bass.Bass("TRN2", target_bir_lowering=False)
    NTOK = TPC + NS
    io = {}

    def din(name, shape):
        io[name] = nc.dram_tensor(name, list(shape), F32, kind="ExternalInput").ap()
        return io[name]

    def dout(name, shape):
        io[name] = nc.dram_tensor(name, list(shape), F32, kind="ExternalOutput").ap()
        return io[name]

    xT = din("xT", [D, NTOK])
    din("w_in", [4, 11, 128, 4096])
    din("w_out", [4, 128, 22 * 1024])
    lnp = din("lnp", [128, 2, 6, 8])
    cst = din("cst", [128, NCST])
    din("rmask", [128, 4])
    din("wfox", [128, 8 * 386]); din("bfx", [2, 1])
    din("wgdn", [128, 8 * 514]); din("gcw", [128, 3, 4]); din("gsc", [1, 2]); din("gng", [128, 1])
    din("wfox_s", [128, 8 * 1544]); din("bf_s", [8, 1]); din("pastk", [8, 64, 1024]); din("pastv", [1024, 512])
    din("pastlf", [8, 1024]); din("wgdn_s", [4, 128, 8 * 514]); din("gcw_s", [4, 128, 3, 4]); din("gsc_s", [4, 1, 2])
    din("sconv", [4, 3, 128, 3]); din("sstate", [4, 128, 128])
    din("even_w_out", [D, D]); din("glu_w", [D, D]); din("odd_w_out", [D, D]); din("glu_b", [128, 8])
    din("s5wu", [128, 8 * 256]); din("s5par", [128, 3, 8]); din("s5bbr", [128, 2, 128]); din("s5bbi", [128, 2, 128])
    din("s5ccr", [128, 8, 32]); din("s5cci", [128, 8, 32]); din("s5d", [128, 2])
    din("s5wu_s", [128, 8 * 1024]); din("s5par_s", [128, 3, 32]); din("s5bbr_s", [128, 8, 128])
    din("s5bbi_s", [128, 8, 128]); din("s5ccr_s", [128, 32, 32]); din("s5cci_s", [128, 32, 32]); din("s5d_s", [128, 8])
    din("s5h0r", [128, 32]); din("s5h0i", [128, 32])
    dout("o_y", [D, TPC]); dout("o_ys", [D, NS])
    dout("o_logf", [2, SEQ]); dout("o_foxk", [128, SEQ]); dout("o_foxv", [SEQ, 128])
    dout("o_gstate", [128, 128]); dout("o_gconv", [3, 128, 3])
    dout("o_slogf", [8, NS]); dout("o_sfoxk", [512, NS]); dout("o_sfoxv", [NS, 512])
    dout("o_sgstate", [4, 128, 128]); dout("o_sgconv", [4, 3, 128, 3])
    dout("o_s5re", [128, 8]); dout("o_s5im", [128, 8]); dout("o_ss5re", [128, 32]); dout("o_ss5im", [128, 32])
    dbg = dout("dbg", [256, SEQ]) if (stop != "all" and not DEBUG.get("nodump")) else None
    XA = nc.dram_tensor("XA", [D, NTOK], F32).ap()
    XB = nc.dram_tensor("XB", [D, NTOK], F32).ap()
    io["WBF"] = [nc.dram_tensor(f"WBF{j}", [128, 4096], BF16).ap() for j in range(11)]
    io["WBFb"] = [Buf(f"WBF{j}") for j in range(11)]
    ngp = TPC // 512
    io["XGinp"] = [nc.dram_tensor(f"XGin{g}", [D, 512], BF16).ap() for g in range(ngp)]
    io["XGp"] = [nc.dram_tensor(f"XG{g}", [4 * D, 512], BF16).ap() for g in range(ngp)]
    io["XGinpb"] = [[Buf(f"XGin{g}")] for g in range(ngp)]
    io["XGpb"] = [Buf(f"XG{g}") for g in range(ngp)]
    npc = SEQ // PIECE
    for nm in ("MX", "YS"):
        io[nm + "inp"] = [nc.dram_tensor(f"{nm}in{j}", [256, PIECE], BF16).ap() for j in range(npc)]
        io[nm + "p"] = [nc.dram_tensor(f"{nm}{j}", [4 * 256, PIECE], BF16).ap() for j in range(npc)]
        io[nm + "inb"] = [Buf(f"{nm}in{i}") for i in range(SEQ // 512)]
        io[nm + "inpb"] = [io[nm + "inb"][j * (PIECE // 512):(j + 1) * (PIECE // 512)] for j in range(npc)]
        io[nm + "pb"] = [Buf(f"{nm}{j}") for j in range(npc)]
    io["SMX"] = nc.dram_tensor("SMX", [1024, NS], BF16).ap()
    io["SMXb"] = Buf("SMX")
    io["SYS"] = nc.dram_tensor("SYS", [1024, NS], BF16).ap()
    io["SYSb"] = Buf("SYS")
    ng = len(groups_())

    with ExitStack() as gstack:
        P = Prog(nc, gstack)
        C = Consts(P, nc, cst)
        lnp_sb = P.sb([128, 2, 6, 8], F32, "lnp_sb")
        P.dma("sp", lambda h: h.dma_start(out=lnp_sb[:, :, :, :], in_=lnp[:, :, :, :]), writes=[C.b])
        consts = {"constb": C.b, "ones_f32": C.f32[:, C_ONES:C_ONES + 128]}
        xTv = xT.rearrange("(k p) n -> p k n", p=128)
        XAv = XA.rearrange("(k p) n -> p k n", p=128)
        XBv = XB.rearrange("(k p) n -> p k n", p=128)
        XAb = [Buf(f"XA_{i}") for i in range(ng)]
        XBb = [Buf(f"XB_{i}") for i in range(ng)]
        io["X1v"], io["X1b"] = XAv, XAb
        io["X4v"], io["X4b"] = XBv, XBb
        xg = True

        def done():
            P.stack = gstack
            P.finish()
            P.emit()
            return nc

        def dump(src_ap, rows0, deps):
            if DEBUG.get("nodump"):
                return
            with ExitStack() as ph:
                P.stack = ph
                t16 = P.sb([128, 2048], BF16, "d16")
                t32 = P.sb([128, 2048], F32, "d32")
                tb_ = Buf("d")
                for i in range(SEQ // 2048):
                    P.dma("sp", lambda h, i=i: h.dma_start(out=t16[:, :], in_=src_ap[rows0:rows0 + 128, i * 2048:(i + 1) * 2048]),
                          reads=deps, writes=[tb_])
                    P.op("dve", lambda h: h.tensor_copy(out=t32[:, :], in_=t16[:, :]), reads=[tb_], writes=[tb_])
                    P.dma("sp", lambda h, i=i: h.dma_start(out=dbg[0:128, i * 2048:(i + 1) * 2048], in_=t32[:, :]),
                          reads=[tb_], writes=[tb_])
                P.barrier()
                P.emit()

        ffn_pass(P, C, io, consts, lnp_sb, 0, 0, xTv, None, XAv, XAb, xg=xg)
        if stop == "p0":
            return done()
        if not DEBUG.get("nofox"):
            phase_fox_prompt(P, C, io)
        if stop == "h0f":
            dump(io["MXinp"][0], 0, io["MXinb"])
            return done()
        if not DEBUG.get("nogdn"):
            phase_gdn_prompt(P, C, io)
        if stop == "h0g":
            dump(io["MXinp"][0], 128, io["MXinb"])
            return done()
        phase_sample_even(P, C, io)
        if stop == "h0s" and DEBUG.get("nodump"):
            return done()
        if stop == "h0s":
            with ExitStack() as ph:
                P.stack = ph
                t16 = P.sb([128, 8, NS], BF16, "d16")
                t32 = P.sb([128, 8, NS], F32, "d32")
                tb_ = Buf("d")
                P.dma("sp", lambda h: h.dma_start(out=t16[:, :, :], in_=io["SMX"].rearrange("(k p) n -> p k n", p=128)),
                      reads=[io["SMXb"]], writes=[tb_])
                P.op("dve", lambda h: h.tensor_copy(out=t32[:, :, :], in_=t16[:, :, :]), reads=[tb_], writes=[tb_])
                P.dma("sp", lambda h: h.dma_start(out=dbg[:, 0:8 * NS].rearrange("p (k n) -> p k n", n=NS)[0:128],
                                                  in_=t32[:, :, :]), reads=[tb_], writes=[tb_])
                P.barrier()
                P.emit()
            return done()
        proj_pass_even(P, C, io, consts, lnp_sb, XAv, XAb, XBv, XBb)
        if stop == "p1a":
            return done()
        ffn_pass(P, C, io, consts, lnp_sb, 1, 2, XBv, XBb, XAv, XAb)
        ffn_pass(P, C, io, consts, lnp_sb, 2, 3, XAv, XAb, XBv, XBb, xg=xg)
        phase_s5_prompt(P, C, io)
        if stop == "h1":
            return done()
        phase_s5_sample(P, C, io)
        if stop == "h1s":
            return done()
        proj_pass_odd(P, C, io, consts, lnp_sb, XBv, XBb, XAv, XAb)
        if stop == "p3a":
            return done()
        ffn_pass(P, C, io, consts, lnp_sb, 3, 5, XAv, XAb, None, None, out_final=(io["o_y"], io["o_ys"]))
        return done()


def _s5_layouts(inputs, groups):
    ng = len(groups)
    nst, nct = ng // 2, ng // 8
    lre, lim, lst = inputs["s5_lam_re"][0], inputs["s5_lam_im"][0], inputs["s5_log_step"][0]
    bre, bim = inputs["s5_b_re"][0], inputs["s5_b_im"][0]
    cre, cim = inputs["s5_c_re"][0], inputs["s5_c_im"][0]
    par = np.zeros((128, 3, nst), np.float32)
    bbr = np.zeros((128, nct, 128), np.float32); bbi = np.zeros((128, nct, 128), np.float32)
    ccr = np.zeros((128, nst, 32), np.float32); cci = np.zeros((128, nst, 32), np.float32)
    for m in range(nst):
        for hh in range(2):
            g = groups[2 * m + hh]
            sl = slice(hh * 64, (hh + 1) * 64)
            par[sl, 0, m] = lre[g]; par[sl, 1, m] = lim[g]; par[sl, 2, m] = lst[g]
            prow = (m % 4) * 32 + hh * 16
            bbr[prow:prow + 16, m // 4, sl] = bre[g].T
            bbi[prow:prow + 16, m // 4, sl] = bim[g].T
            ccr[sl, m, hh * 16:(hh + 1) * 16] = cre[g].T
            cci[sl, m, hh * 16:(hh + 1) * 16] = cim[g].T
    return par, bbr, bbi, ccr, cci


def _state_tiles(h):
    ng = h.shape[0]
    return np.ascontiguousarray(h.reshape(ng // 2, 128).T)


def host_prep(inputs, c):
    b, r = c // 4, c % 4
    xp = inputs["x_prompt"][b, r * TPC:(r + 1) * TPC, :]
    xs = inputs["x_sample"][c]
    xT = np.ascontiguousarray(np.concatenate([xp, xs], 0).T)
    ew = inputs["even_w_in"][0]
    cols = np.concatenate([np.arange(r * 128, (r + 1) * 128), 512 + np.arange(r * 128, (r + 1) * 128),
                           1024 + np.arange(r * 128, (r + 1) * 128), 1536 + np.arange(2 * r, 2 * r + 2)])
    wfox = np.ascontiguousarray(ew[:, cols].reshape(8, 128, 386).transpose(1, 0, 2)).reshape(128, 8 * 386)
    bfx = np.ascontiguousarray(inputs["fox_b_f"][0, 2 * r:2 * r + 2].reshape(2, 1))
    gc = np.concatenate([1544 + np.arange(r * 128, (r + 1) * 128), 1544 + 512 + np.arange(r * 128, (r + 1) * 128),
                         1544 + 1024 + np.arange(r * 128, (r + 1) * 128), 3088 + np.arange(r * 128, (r + 1) * 128),
                         [3080 + r], [3084 + r]])
    wgdn = np.ascontiguousarray(ew[:, gc].reshape(8, 128, 514).transpose(1, 0, 2)).reshape(128, 8 * 514)
    cwf = inputs["gdn_conv_w"][0]
    gcw = np.ascontiguousarray(np.stack([cwf[:, s_ * 512 + r * 128:s_ * 512 + (r + 1) * 128].T for s_ in range(3)], 1))
    gsc = np.array([[inputs["gdn_a_log"][0, r], inputs["gdn_dt_bias"][0, r]]], np.float32)
    pastk = np.ascontiguousarray(inputs["cache_fox_k"][0, c].transpose(1, 2, 0))
    pastv = np.ascontiguousarray(inputs["cache_fox_v"][0, c].reshape(1024, 512))
    pastlf = np.ascontiguousarray(inputs["cache_fox_logf"][0, c].T)
    sconv = np.ascontiguousarray(inputs["state_gdn_conv"][0, c].reshape(3, 3, 4, 128).transpose(2, 1, 3, 0))
    sstate = np.ascontiguousarray(inputs["state_gdn"][0, c])
    rmask = np.zeros((128, 4), np.float32)
    rmask[:, r] = 1.0
    ow = inputs["odd_w_in"][0]
    s5wu = np.ascontiguousarray(ow[:, r * 256:(r + 1) * 256].reshape(8, 128, 256).transpose(1, 0, 2)).reshape(128, 2048)
    par, bbr, bbi, ccr, cci = _s5_layouts(inputs, list(range(16 * r, 16 * r + 16)))
    s5d = np.ascontiguousarray(inputs["s5_d"][0, r * 256:(r + 1) * 256].reshape(2, 128).T)
    return {"xT": xT, "wfox": wfox, "bfx": bfx, "wgdn": wgdn, "gcw": gcw, "gsc": gsc,
            "pastk": pastk, "pastv": pastv, "pastlf": pastlf, "sconv": sconv, "sstate": sstate, "rmask": rmask,
            "s5wu": s5wu, "s5par": par, "s5bbr": bbr, "s5bbi": bbi, "s5ccr": ccr, "s5cci": cci, "s5d": s5d,
            "s5h0r": _state_tiles(inputs["state_s5_re"][0, c]), "s5h0i": _state_tiles(inputs["state_s5_im"][0, c])}


def shared_prep(inputs):
    w_in = inputs["ffn_w_in"].reshape(4, 8, 128, 2, 11, 256)
    w_in = np.ascontiguousarray(w_in.transpose(0, 4, 2, 1, 3, 5)).reshape(4, 11, 128, 4096)
    w_out = inputs["ffn_w_out"].reshape(4, 22, 128, 1024)
    w_out = np.ascontiguousarray(w_out.transpose(0, 2, 1, 3)).reshape(4, 128, 22 * 1024)
    g = inputs["ln_g"].reshape(6, 8, 128)
    bb = inputs["ln_b"].reshape(6, 8, 128)
    lnp = np.ascontiguousarray(np.stack([g, bb], 0).transpose(3, 0, 1, 2))
    ew = inputs["even_w_in"][0]
    wfox_s = np.ascontiguousarray(ew[:, :1544].reshape(8, 128, 1544).transpose(1, 0, 2)).reshape(128, 8 * 1544)
    bf_s = np.ascontiguousarray(inputs["fox_b_f"][0].reshape(8, 1))
    cwf = inputs["gdn_conv_w"][0]
    wg_l, cw_l, sc_l = [], [], []
    for r in range(4):
        gc = np.concatenate([1544 + np.arange(r * 128, (r + 1) * 128), 1544 + 512 + np.arange(r * 128, (r + 1) * 128),
                             1544 + 1024 + np.arange(r * 128, (r + 1) * 128), 3088 + np.arange(r * 128, (r + 1) * 128),
                             [3080 + r], [3084 + r]])
        wg_l.append(np.ascontiguousarray(ew[:, gc].reshape(8, 128, 514).transpose(1, 0, 2)).reshape(128, 8 * 514))
        cw_l.append(np.stack([cwf[:, s_ * 512 + r * 128:s_ * 512 + (r + 1) * 128].T for s_ in range(3)], 1))
        sc_l.append(np.array([[inputs["gdn_a_log"][0, r], inputs["gdn_dt_bias"][0, r]]], np.float32))
    ow = inputs["odd_w_in"][0]
    s5wu_s = np.ascontiguousarray(ow.reshape(8, 128, 1024).transpose(1, 0, 2)).reshape(128, 8192)
    par, bbr, bbi, ccr, cci = _s5_layouts(inputs, list(range(64)))
    return {"w_in": w_in, "w_out": w_out, "lnp": lnp, "cst": make_cst(), "wfox_s": wfox_s, "bf_s": bf_s,
            "wgdn_s": np.ascontiguousarray(np.stack(wg_l)), "gcw_s": np.ascontiguousarray(np.stack(cw_l)),
            "gsc_s": np.ascontiguousarray(np.stack(sc_l)),
            "gng": np.ascontiguousarray(inputs["gdn_norm_g"][0].reshape(128, 1)),
            "even_w_out": np.ascontiguousarray(inputs["even_w_out"][0]), "glu_w": np.ascontiguousarray(inputs["s5_glu_w"][0]),
            "odd_w_out": np.ascontiguousarray(inputs["odd_w_out"][0]),
            "glu_b": np.ascontiguousarray(inputs["s5_glu_b"][0].reshape(8, 128).T),
            "s5wu_s": s5wu_s, "s5par_s": par, "s5bbr_s": bbr, "s5bbi_s": bbi, "s5ccr_s": ccr, "s5cci_s": cci,
            "s5d_s": np.ascontiguousarray(inputs["s5_d"][0].reshape(8, 128).T)}


def assemble(results):
    B, T = 2, SEQ
    f = np.float32
    y_p = np.zeros((B, T, D), f); y_s = np.zeros((8, NS, D), f)
    fk = np.zeros((1, B, T, 8, 64), f); fv = np.zeros((1, B, T, 8, 64), f); fl = np.zeros((1, B, T, 8), f)
    gs = np.zeros((1, B, 4, 128, 128), f); gcv = np.zeros((1, B, 3, 1536), f)
    sre = np.zeros((1, B, 64, 64), f); sim_ = np.zeros((1, B, 64, 64), f)
    sk = np.zeros((1, 8, NS, 8, 64), f); sv = np.zeros((1, 8, NS, 8, 64), f); sl = np.zeros((1, 8, NS, 8), f)
    sgs = np.zeros((1, 8, 4, 128, 128), f); sgc = np.zeros((1, 8, 3, 1536), f)
    ssre = np.zeros((1, 8, 64, 64), f); ssim = np.zeros((1, 8, 64, 64), f)
    for c in range(NCORE):
        d = results[c]
        b, r = c // 4, c % 4
        y_p[b, r * TPC:(r + 1) * TPC] = d["o_y"].T
        y_s[c] = d["o_ys"].T
        for hl in range(2):
            fk[0, b, :, 2 * r + hl, :] = d["o_foxk"][hl * 64:(hl + 1) * 64].T
            fv[0, b, :, 2 * r + hl, :] = d["o_foxv"][:, hl * 64:(hl + 1) * 64]
            fl[0, b, :, 2 * r + hl] = d["o_logf"][hl]
        gs[0, b, r] = d["o_gstate"]
        for s_ in range(3):
            gcv[0, b, :, s_ * 512 + r * 128:s_ * 512 + (r + 1) * 128] = d["o_gconv"][s_].T
        sre[0, b, 16 * r:16 * r + 16] = d["o_s5re"].T.reshape(16, 64)
        sim_[0, b, 16 * r:16 * r + 16] = d["o_s5im"].T.reshape(16, 64)
        sk[0, c] = d["o_sfoxk"].T.reshape(NS, 8, 64)
        sv[0, c] = d["o_sfoxv"].reshape(NS, 8, 64)
        sl[0, c] = d["o_slogf"].T
        sgs[0, c] = d["o_sgstate"]
        sgc[0, c] = d["o_sgconv"].transpose(3, 1, 0, 2).reshape(3, 1536)
        ssre[0, c] = d["o_ss5re"].T.reshape(64, 64)
        ssim[0, c] = d["o_ss5im"].T.reshape(64, 64)
    return (y_p, y_s, fk, fv, fl, gs, gcv, sre, sim_, sk, sv, sl, sgs, sgc, ssre, ssim)


def kernel(**inputs):
    inputs = {k: np.asarray(v) for k, v in inputs.items()}
    nc = build()
    sh = shared_prep(inputs)
    in_maps = []
    for c in range(NCORE):
        m = dict(sh)
        m.update(host_prep(inputs, c))
        in_maps.append(m)
    res = run_bass_kernel_spmd(nc, in_maps, core_ids=list(range(NCORE)))
    return assemble(res.results)
```

```python
import numpy as np
import ml_dtypes
import concourse.bass as bass
import concourse.mybir as mybir
from concourse.bass_utils import run_bass_kernel_spmd

F32 = mybir.dt.float32
BF16 = mybir.dt.bfloat16
AF = mybir.ActivationFunctionType
ALU = mybir.AluOpType
AX = mybir.AxisListType

D = 1024
SEQ = 16384
NCORE = 8
TPC = 4096
NS = 32
DFF = 2816
ALPHA = 4.0 ** 0.25
LN_EPS = 1e-5
NORM_EPS = 1e-6
DEBUG = {}


class Buf:
    __slots__ = ("name", "w", "r", "excl")

    def __init__(self, name, excl=False):
        self.name = name
        self.w = {}
        self.r = {}
        self.excl = excl


class Eng:
    def __init__(self, name, sem, step):
        self.name = name
        self.sem = sem
        self.step = step
        self.count = 0
        self.waited = {}
        self.prog = []


class Prog:
    def __init__(self, nc, stack):
        self.nc = nc
        self.stack = stack
        self.eng = {}
        for n in ("pe", "act", "dve", "pool", "sp"):
            self.eng[n] = Eng(n, stack.enter_context(nc.semaphore("s_" + n)), 1)
        self.nslot = 6
        self.slots = {}
        self.slot_rr = {}
        for q in ("sp", "pool", "act"):
            self.slots[q] = [Eng(f"dma_{q}{i}", stack.enter_context(nc.semaphore(f"d_{q}{i}")), 16)
                             for i in range(self.nslot)]
            self.slot_rr[q] = 0
        self.ntile = 0
        self.cc = Eng("cc", stack.enter_context(nc.semaphore("s_cc")), 1)

    def sb(self, shape, dt, name=None):
        self.ntile += 1
        name = f"{name or 't'}_s{self.ntile}"
        t = self.stack.enter_context(self.nc.sbuf_tensor(name, list(shape), dt))
        return t

    def ps(self, shape, dt, name=None):
        self.ntile += 1
        name = f"{name or 'p'}_p{self.ntile}"
        t = self.stack.enter_context(self.nc.psum_tensor(name, list(shape), dt))
        return t

    def _wait(self, e, deps):
        for src, val in deps.items():
            if src is e and e.name == "pe":
                continue
            if e.waited.get(src, 0) < val:
                e.prog.append(("w", src.sem, val))
                e.waited[src] = val

    @staticmethod
    def _deps(reads, writes):
        deps = {}
        for b in reads:
            for s, v in b.w.items():
                if deps.get(s, 0) < v:
                    deps[s] = v
            if b.excl:
                for s, v in b.r.items():
                    if deps.get(s, 0) < v:
                        deps[s] = v
        for b in writes:
            for s, v in b.w.items():
                if deps.get(s, 0) < v:
                    deps[s] = v
            for s, v in b.r.items():
                if deps.get(s, 0) < v:
                    deps[s] = v
        return deps

    def op(self, en, fn, reads=(), writes=()):
        e = self.eng[en]
        self._wait(e, self._deps(reads, writes))
        e.count += 1
        e.prog.append(("o", fn, e.sem, 1))
        for b in reads:
            if b.r.get(e, 0) < e.count:
                b.r[e] = e.count
        for b in writes:
            b.w = {e: e.count}
            b.r = {}

    def dma(self, q, fn, reads=(), writes=()):
        e = self.eng[q]
        sl = self.slots[q][self.slot_rr[q] % self.nslot]
        self.slot_rr[q] += 1
        deps = self._deps(reads, writes)
        if sl.count:
            deps[sl] = max(deps.get(sl, 0), sl.count)
        self._wait(e, deps)
        sl.count += 16
        e.prog.append(("o", fn, sl.sem, 16))
        for b in reads:
            b.r[sl] = sl.count
        for b in writes:
            b.w = {sl: sl.count}
            b.r = {}

    def wait_all(self, en, bufs):
        e = self.eng[en]
        deps = {}
        for b in bufs:
            for s, v in list(b.w.items()) + list(b.r.items()):
                if deps.get(s, 0) < v:
                    deps[s] = v
        self._wait(e, deps)

    def finish(self):
        e = self.eng["sp"]
        deps = {}
        for x in self.eng.values():
            if x.count:
                deps[x] = x.count
        for q in self.slots:
            for sl in self.slots[q]:
                if sl.count:
                    deps[sl] = sl.count
        self._wait(e, deps)

    def barrier(self):
        deps = {}
        for x in list(self.eng.values()) + [self.cc]:
            if x.count:
                deps[x] = x.count
        for q in self.slots:
            for sl in self.slots[q]:
                if sl.count:
                    deps[sl] = sl.count
        for e in self.eng.values():
            self._wait(e, dict(deps))

    def coll(self, fn, reads=(), writes=()):
        e = self.eng["pool"]
        self._wait(e, self._deps(reads, writes))
        self.cc.count += 1
        e.prog.append(("o", fn, self.cc.sem, 1))
        for b in reads:
            b.r[self.cc] = self.cc.count
        for b in writes:
            b.w = {self.cc: self.cc.count}
            b.r = {}

    def emit(self):
        nc = self.nc
        progs = {k: Eng(k, None, 1) for k in self.eng}
        for k in self.eng:
            progs[k].prog = self.eng[k].prog
            self.eng[k].prog = []

        def run(h, prog):
            for it in prog:
                if it[0] == "w":
                    h.wait_ge(it[1], it[2])
                else:
                    ins = it[1](h)
                    ins.then_inc(it[2], it[3])

        with nc.Block() as block:
            @block.sync
            def _(h):
                run(h, progs["sp"].prog)

            @block.tensor
            def _(h):
                run(h, progs["pe"].prog)

            @block.scalar
            def _(h):
                run(h, progs["act"].prog)

            @block.vector
            def _(h):
                run(h, progs["dve"].prog)

            @block.gpsimd
            def _(h):
                run(h, progs["pool"].prog)


class Rot:
    def __init__(self, P, n, shape, dt, name, psum=False):
        self.items = []
        for i in range(n):
            t = P.ps(shape, dt, f"{name}{i}") if psum else P.sb(shape, dt, f"{name}{i}")
            self.items.append((t, Buf(f"{name}{i}", excl=psum)))
        self.i = 0

    def next(self):
        it = self.items[self.i % len(self.items)]
        self.i += 1
        return it


class LNBase:
    def _ln_alloc(self, P, consts, NT=512):
        self.P = P
        self.c = consts
        self.sq = Rot(P, 2, [128, NT], F32, "sq_")
        self.ps1 = P.ps([128, NT], F32, "ps1")
        self.ps1b = Buf("ps1", excl=True)
        self.ps2 = P.ps([128, NT], F32, "ps2")
        self.ps2b = Buf("ps2", excl=True)
        self.mean = P.sb([128, NT], F32, "mean")
        self.meanb = Buf("mean")
        self.rstd = P.sb([128, NT], F32, "rstd")
        self.rstdb = Buf("rstd")
        self.msq = P.sb([128, NT], F32, "msq")
        self.msqb = Buf("msq")


class TPhase(LNBase):
    def __init__(self, P, consts):
        NT = 512
        self.NT = NT
        self._ln_alloc(P, consts, NT)
        self.x32 = Rot(P, 2, [128, 8, NT], F32, "x32_")
        self.xb = Rot(P, 2, [128, 8, NT], BF16, "xb_")
        self.h = P.sb([128, 22, NT], BF16, "h")
        self.hbuf = [Buf(f"h{j}") for j in range(22)]
        self.win = Rot(P, 4, [128, 8, 512], BF16, "win_")
        self.wstage = Rot(P, 2, [128, 8, 512], F32, "wst_")
        self.wout = P.sb([128, 22, 1024], BF16, "wout")
        self.woutb = Buf("wout")
        self.tmp = Rot(P, 4, [128, NT], F32, "silu_")
        self.pa = Rot(P, 2, [128, NT], F32, "pa_", psum=True)
        self.pb = Rot(P, 2, [128, NT], F32, "pb_", psum=True)
        self.po = Rot(P, 2, [128, NT], F32, "po_", psum=True)

    def load_wout(self, w_out_ap):
        P = self.P
        for i in range(11):
            ws, wsb = self.wstage.next()
            P.dma("act", lambda h, i=i, ws=ws: h.dma_start(
                out=ws[:, 0:4, :], in_=w_out_ap[:, i * 2048:(i + 1) * 2048].rearrange("p (j c) -> p j c", c=512)),
                writes=[wsb])
            P.op("pool", lambda h, i=i, ws=ws: h.tensor_copy(
                out=self.wout[:, 2 * i:2 * i + 2, :].rearrange("p j (a c) -> p (j a) c", c=512), in_=ws[:, 0:4, :]),
                reads=[wsb], writes=[self.woutb])

    def layer_norm(self, r32, r32b, xob, xobb, nt, g_ap, b_ap, eps):
        P = self.P
        c = self.c
        ones = c["ones_f32"]
        for m in range(8):
            sq, sqb = self.sq.next()
            P.op("act", lambda h, sq=sq, m=m: h.activation(out=sq[:, :nt], in_=r32[:, m, :nt], func=AF.Square),
                 reads=[r32b], writes=[sqb])
            P.op("pe", lambda h, m=m: h.matmul(self.ps1[:, :nt], lhsT=ones[:, :], rhs=r32[:, m, :nt],
                                               start=(m == 0), stop=(m == 7)),
                 reads=[r32b, c["constb"]], writes=[self.ps1b])
            P.op("pe", lambda h, m=m, sq=sq: h.matmul(self.ps2[:, :nt], lhsT=ones[:, :], rhs=sq[:, :nt],
                                                      start=(m == 0), stop=(m == 7)),
                 reads=[sqb, c["constb"]], writes=[self.ps2b])
        P.op("dve", lambda h: h.tensor_scalar(out=self.mean[:, :nt], in0=self.ps1[:, :nt], scalar1=1.0 / D,
                                              scalar2=None, op0=ALU.mult),
             reads=[self.ps1b], writes=[self.meanb])
        P.op("dve", lambda h: h.tensor_tensor(out=self.msq[:, :nt], in0=self.mean[:, :nt], in1=self.mean[:, :nt],
                                              op=ALU.mult),
             reads=[self.meanb], writes=[self.msqb])
        P.op("dve", lambda h: h.scalar_tensor_tensor(out=self.rstd[:, :nt], in0=self.ps2[:, :nt], scalar=1.0 / D,
                                                     in1=self.msq[:, :nt], op0=ALU.mult, op1=ALU.subtract),
             reads=[self.ps2b, self.msqb], writes=[self.rstdb])
        P.op("dve", lambda h: h.tensor_scalar(out=self.rstd[:, :nt], in0=self.rstd[:, :nt], scalar1=eps,
                                              scalar2=None, op0=ALU.add),
             reads=[self.rstdb], writes=[self.rstdb])
        P.op("act", lambda h: h.activation(out=self.rstd[:, :nt], in_=self.rstd[:, :nt], func=AF.Sqrt),
             reads=[self.rstdb], writes=[self.rstdb])
        P.op("dve", lambda h: h.reciprocal(out=self.rstd[:, :nt], in_=self.rstd[:, :nt]),
             reads=[self.rstdb], writes=[self.rstdb])
        xo32, xo32b = r32, r32b
        for m in range(8):
            P.op("pool", lambda h, m=m: h.tensor_tensor(out=r32[:, m, :nt], in0=r32[:, m, :nt], in1=self.mean[:, :nt],
                                                        op=ALU.subtract),
                 reads=[r32b, self.meanb], writes=[r32b])
            P.op("dve", lambda h, m=m: h.tensor_tensor(out=r32[:, m, :nt], in0=r32[:, m, :nt], in1=self.rstd[:, :nt],
                                                       op=ALU.mult),
                 reads=[r32b, self.rstdb], writes=[r32b])
            P.op("act", lambda h, m=m: h.activation(out=xo32[:, m, :nt], in_=r32[:, m, :nt], func=AF.Identity,
                                                    scale=g_ap[:, m:m + 1], bias=b_ap[:, m:m + 1]),
                 reads=[r32b, c["constb"]], writes=[xo32b])
            P.op("pool", lambda h, m=m: h.tensor_copy(out=xob[:, m, :nt], in_=xo32[:, m, :nt]),
                 reads=[xo32b], writes=[xobb])
        return xo32, xo32b, xob, xobb

    def ffn_ln(self, x32, x32b, xb, xbb, nt, w_in_ap, g_ap, b_ap):
        P = self.P
        jq = []
        for jp in range(11):
            w, wb = self.win.next()
            ws, wsb = self.wstage.next()
            P.dma("act", lambda h, ws=ws, jp=jp: h.dma_start(
                out=ws[:, :, :], in_=w_in_ap[jp].rearrange("p (k c) -> p k c", c=512)), writes=[wsb])
            P.op("pool", lambda h, w=w, ws=ws: h.tensor_copy(out=w[:, :, :], in_=ws[:, :, :]), reads=[wsb],
                 writes=[wb])
            for jj in range(2):
                j = jp * 2 + jj
                pa, pab = self.pa.next()
                pb, pbb = self.pb.next()
                for kt in range(8):
                    P.op("pe", lambda h, pa=pa, w=w, kt=kt, jj=jj: h.matmul(
                        pa[:, :nt], lhsT=w[:, kt, jj * 128:(jj + 1) * 128], rhs=xb[:, kt, :nt],
                        start=(kt == 0), stop=(kt == 7)), reads=[wb, xbb], writes=[pab])
                for kt in range(8):
                    P.op("pe", lambda h, pb=pb, w=w, kt=kt, jj=jj: h.matmul(
                        pb[:, :nt], lhsT=w[:, kt, 256 + jj * 128:256 + (jj + 1) * 128], rhs=xb[:, kt, :nt],
                        start=(kt == 0), stop=(kt == 7)), reads=[wb, xbb], writes=[pbb])
                t, tb = self.tmp.next()
                P.op("act", lambda h, t=t, pa=pa: h.activation(out=t[:, :nt], in_=pa[:, :nt], func=AF.Silu),
                     reads=[pab], writes=[tb])
                P.op("dve", lambda h, t=t, pb=pb, j=j: h.tensor_tensor(out=self.h[:, j, :nt], in0=t[:, :nt],
                                                                     in1=pb[:, :nt], op=ALU.mult),
                     reads=[tb, pbb], writes=[self.hbuf[j]])
        for m in range(8):
            po, pob = self.po.next()
            for j in range(22):
                P.op("pe", lambda h, po=po, j=j, m=m: h.matmul(
                    po[:, :nt], lhsT=self.wout[:, j, m * 128:(m + 1) * 128], rhs=self.h[:, j, :nt],
                    start=(j == 0), stop=(j == 21)), reads=[self.woutb, self.hbuf[j]], writes=[pob])
            P.op("dve", lambda h, po=po, m=m: h.scalar_tensor_tensor(
                out=x32[:, m, :nt], in0=po[:, :nt], scalar=0.5 / ALPHA, in1=x32[:, m, :nt],
                op0=ALU.mult, op1=ALU.add), reads=[pob, x32b], writes=[x32b])
        return self.layer_norm(x32, x32b, xb, xbb, nt, g_ap, b_ap, LN_EPS / (ALPHA * ALPHA))


LNBase.layer_norm = TPhase.layer_norm

def _tp_ffn_in(self, xb, xbbs, nt, w_in_ap, pending):
    P = self.P
    for jp in range(11):
        w, wb = w_in_ap()
        for jj in range(2):
            j = jp * 2 + jj
            pa, pab = self.pa.next()
            pb, pbb = self.pb.next()
            for kt in range(8):
                P.op("pe", lambda h, pa=pa, w=w, kt=kt, jj=jj: h.matmul(
                    pa[:, :nt], lhsT=w[:, kt, jj * 128:(jj + 1) * 128], rhs=xb[:, kt, :nt],
                    start=(kt == 0), stop=(kt == 7)), reads=[wb, xbbs[kt]], writes=[pab])
            for kt in range(8):
                P.op("pe", lambda h, pb=pb, w=w, kt=kt, jj=jj: h.matmul(
                    pb[:, :nt], lhsT=w[:, kt, 256 + jj * 128:256 + (jj + 1) * 128], rhs=xb[:, kt, :nt],
                    start=(kt == 0), stop=(kt == 7)), reads=[wb, xbbs[kt]], writes=[pbb])
            t, tb = self.tmp.next()
            P.op("act", lambda h, t=t, pa=pa: h.activation(out=t[:, :nt], in_=pa[:, :nt], func=AF.Silu),
                 reads=[pab], writes=[tb])
            P.op("dve", lambda h, t=t, pb=pb, j=j: h.tensor_tensor(out=self.h[:, j, :nt], in0=t[:, :nt],
                                                                 in1=pb[:, :nt], op=ALU.mult),
                 reads=[tb, pbb], writes=[self.hbuf[j]])
        if pending:
            pending.pop(0)()
    while pending:
        pending.pop(0)()


def _tp_ffn_out(self, x32, x32bs, nt, eps):
    P, c = self.P, self.c
    ones = c["ones_f32"]
    sqs = []
    for m in range(8):
        po, pob = self.po.next()
        for j in range(22):
            P.op("pe", lambda h, po=po, j=j, m=m: h.matmul(
                po[:, :nt], lhsT=self.wout[:, j, m * 128:(m + 1) * 128], rhs=self.h[:, j, :nt],
                start=(j == 0), stop=(j == 21)), reads=[self.woutb, self.hbuf[j]], writes=[pob])
        P.op("dve", lambda h, po=po, m=m: h.scalar_tensor_tensor(
            out=x32[:, m, :nt], in0=po[:, :nt], scalar=0.5 / ALPHA, in1=x32[:, m, :nt],
            op0=ALU.mult, op1=ALU.add), reads=[pob, x32bs[m]], writes=[x32bs[m]])
        sq, sqb = self.sq.next()
        P.op("act", lambda h, sq=sq, m=m: h.activation(out=sq[:, :nt], in_=x32[:, m, :nt], func=AF.Square),
             reads=[x32bs[m]], writes=[sqb])
        sqs.append((m, sq, sqb))
        if len(sqs) > 1:
            self._stat_mm(x32, x32bs, nt, *sqs.pop(0))
    while sqs:
        self._stat_mm(x32, x32bs, nt, *sqs.pop(0))
    P.op("dve", lambda h: h.tensor_scalar(out=self.mean[:, :nt], in0=self.ps1[:, :nt], scalar1=1.0 / D,
                                          scalar2=None, op0=ALU.mult), reads=[self.ps1b], writes=[self.meanb])
    P.op("dve", lambda h: h.tensor_tensor(out=self.msq[:, :nt], in0=self.mean[:, :nt], in1=self.mean[:, :nt],
                                          op=ALU.mult), reads=[self.meanb], writes=[self.msqb])
    P.op("dve", lambda h: h.scalar_tensor_tensor(out=self.rstd[:, :nt], in0=self.ps2[:, :nt], scalar=1.0 / D,
                                                 in1=self.msq[:, :nt], op0=ALU.mult, op1=ALU.subtract),
         reads=[self.ps2b, self.msqb], writes=[self.rstdb])
    P.op("dve", lambda h: h.tensor_scalar(out=self.rstd[:, :nt], in0=self.rstd[:, :nt], scalar1=eps,
                                          scalar2=None, op0=ALU.add), reads=[self.rstdb], writes=[self.rstdb])
    P.op("act", lambda h: h.activation(out=self.rstd[:, :nt], in_=self.rstd[:, :nt], func=AF.Sqrt),
         reads=[self.rstdb], writes=[self.rstdb])
    P.op("dve", lambda h: h.reciprocal(out=self.rstd[:, :nt], in_=self.rstd[:, :nt]),
         reads=[self.rstdb], writes=[self.rstdb])
    P.op("dve", lambda h: h.scalar_tensor_tensor(out=self.msq[:, :nt], in0=self.mean[:, :nt], scalar=-1.0,
                                                 in1=self.rstd[:, :nt], op0=ALU.mult, op1=ALU.mult),
         reads=[self.meanb, self.rstdb], writes=[self.msqb])


def _tp_stat_mm(self, x32, x32bs, nt, m, sq, sqb):
    P, c = self.P, self.c
    ones = c["ones_f32"]
    P.op("pe", lambda h: h.matmul(self.ps1[:, :nt], lhsT=ones[:, :], rhs=x32[:, m, :nt], start=(m == 0), stop=(m == 7)),
         reads=[x32bs[m], c["constb"]], writes=[self.ps1b])
    P.op("pe", lambda h: h.matmul(self.ps2[:, :nt], lhsT=ones[:, :], rhs=sq[:, :nt], start=(m == 0), stop=(m == 7)),
         reads=[sqb, c["constb"]], writes=[self.ps2b])


def _tp_norm_items(self, x32, x32bs, xb, xbbs, nt, g_ap, b_ap):
    P, c = self.P, self.c
    items = []
    for m in range(8):
        def item(m=m):
            t, tb = self.tmp.next()
            P.op("pool", lambda h: h.tensor_tensor(out=t[:, :nt], in0=x32[:, m, :nt], in1=self.rstd[:, :nt], op=ALU.mult),
                 reads=[x32bs[m], self.rstdb], writes=[tb])
            P.op("dve", lambda h: h.tensor_tensor(out=t[:, :nt], in0=t[:, :nt], in1=self.msq[:, :nt], op=ALU.add),
                 reads=[tb, self.msqb], writes=[tb])
            P.op("act", lambda h: h.activation(out=x32[:, m, :nt], in_=t[:, :nt], func=AF.Identity,
                                               scale=g_ap[:, m:m + 1], bias=b_ap[:, m:m + 1]),
                 reads=[tb, c["constb"]], writes=[x32bs[m]])
            P.op("pool", lambda h: h.tensor_copy(out=xb[:, m, :nt], in_=x32[:, m, :nt]), reads=[x32bs[m]],
                 writes=[xbbs[m]])
        items.append(item)
    return items


TPhase.ffn_in = _tp_ffn_in
LNBase._stat_mm = _tp_stat_mm
LNBase.norm_items = _tp_norm_items
TPhase.ffn_out = _tp_ffn_out


def _ln_stats(self, x32, x32bs, nt, eps):
    P = self.P
    for m in range(8):
        sq, sqb = self.sq.next()
        P.op("act", lambda h, sq=sq, m=m: h.activation(out=sq[:, :nt], in_=x32[:, m, :nt], func=AF.Square),
             reads=[x32bs[m]], writes=[sqb])
        self._stat_mm(x32, x32bs, nt, m, sq, sqb)
    P.op("dve", lambda h: h.tensor_scalar(out=self.mean[:, :nt], in0=self.ps1[:, :nt], scalar1=1.0 / D,
                                          scalar2=None, op0=ALU.mult), reads=[self.ps1b], writes=[self.meanb])
    P.op("dve", lambda h: h.tensor_tensor(out=self.msq[:, :nt], in0=self.mean[:, :nt], in1=self.mean[:, :nt],
                                          op=ALU.mult), reads=[self.meanb], writes=[self.msqb])
    P.op("dve", lambda h: h.scalar_tensor_tensor(out=self.rstd[:, :nt], in0=self.ps2[:, :nt], scalar=1.0 / D,
                                                 in1=self.msq[:, :nt], op0=ALU.mult, op1=ALU.subtract),
         reads=[self.ps2b, self.msqb], writes=[self.rstdb])
    P.op("dve", lambda h: h.tensor_scalar(out=self.rstd[:, :nt], in0=self.rstd[:, :nt], scalar1=eps,
                                          scalar2=None, op0=ALU.add), reads=[self.rstdb], writes=[self.rstdb])
    P.op("act", lambda h: h.activation(out=self.rstd[:, :nt], in_=self.rstd[:, :nt], func=AF.Sqrt),
         reads=[self.rstdb], writes=[self.rstdb])
    P.op("dve", lambda h: h.reciprocal(out=self.rstd[:, :nt], in_=self.rstd[:, :nt]),
         reads=[self.rstdb], writes=[self.rstdb])
    P.op("dve", lambda h: h.scalar_tensor_tensor(out=self.msq[:, :nt], in0=self.mean[:, :nt], scalar=-1.0,
                                                 in1=self.rstd[:, :nt], op0=ALU.mult, op1=ALU.mult),
         reads=[self.meanb, self.rstdb], writes=[self.msqb])


LNBase.ln_stats = _ln_stats
TPhase._stat_mm = _tp_stat_mm
TPhase.norm_items = _tp_norm_items


C_ONES, C_ID, C_MASK, C_SEL64, C_E, C_MASKT = 0, 128, 256, 384, 448, 520
NCST = 520 + 128
MASKNEG = -30000.0


def make_cst():
    cst = np.zeros((128, NCST), np.float32)
    cst[:, C_ONES:C_ONES + 128] = 1.0
    cst[:, C_ID:C_ID + 128] = np.eye(128, dtype=np.float32)
    k = np.arange(128)[:, None]
    q = np.arange(128)[None, :]
    cst[:, C_MASK:C_MASK + 128] = np.where(k > q, MASKNEG, 0.0)
    cst[64, C_SEL64:C_SEL64 + 64] = 1.0
    cst[:, C_MASKT:C_MASKT + 128] = np.where(q > k, MASKNEG, 0.0)
    for h in range(8):
        for x in range(3):
            cst[h, C_E + (h * 3 + x) * 3 + x] = 1.0
    return cst


class StopPhase(Exception):
    pass


def ck(n):
    if DEBUG.get("stopat", 10 ** 9) <= n:
        raise StopPhase()


class Consts:
    def __init__(self, P, nc, cst_ap):
        self.b = Buf("const")
        self.f32 = P.sb([128, NCST], F32, "cst_f32")
        self.bf = P.sb([128, NCST], BF16, "cst_bf")
        P.dma("sp", lambda h: h.dma_start(out=self.f32[:, :], in_=cst_ap[:, :]), writes=[self.b])
        P.op("dve", lambda h: h.tensor_copy(out=self.bf[:, :], in_=self.f32[:, :]), reads=[self.b], writes=[self.b])


class Fox:
    def __init__(self, P, C, H, NK, NQ, wf, nbf):
        self.P, self.C, self.H, self.NK, self.NQ = P, C, H, NK, NQ
        self.NKB = (NK + 127) // 128
        self.wf, self.nbf = wf, nbf
        self.KT = [P.sb([67, NK], BF16, f"KT{h}") for h in range(H)]
        self.KTb = [Buf(f"KT{h}") for h in range(H)]
        self.VA = P.sb([128, self.NKB, H, 65], BF16, "VA")
        self.VAb = Buf("VA")
        self.CK = P.sb([128, self.NKB, H], F32, "CK")
        self.CKb = Buf("CK")
        self.QA = [P.sb([67, NQ], BF16, f"QA{h}") for h in range(H)]
        self.QAb = [Buf(f"QA{h}") for h in range(H)]
        self.carry = P.sb([H, 1], F32, "fcarry")
        self.carryb = Buf("fcarry")
        self.onesr = P.sb([H, 512], F32, "onesr")
        self.sp_ = P.sb([H, 512], F32, "fsp")
        self.spb = Buf("fsp")
        self.cp = P.sb([H, 512], F32, "fcp")
        self.cpb = Buf("fcp")
        self.v8 = P.sb([H, 512], F32, "fv8")
        self.v8b = Buf("fv8")
        self.hml = [P.sb([H, 512], BF16, f"fhml{x}") for x in range(3)]
        self.hmlb = Buf("fhml")
        self.pt = Rot(P, 4, [128, 512], BF16, "fpt_")
        self.osb = Rot(P, 2, [65, 512], F32, "fosb_")
        self.rec = Rot(P, 2, [64, 512], F32, "frec_")
        self.omix = Rot(P, 2, [64, 512], BF16, "fomix_")
        for h in range(H):
            P.op("pool", lambda hh, h=h: hh.memset(self.KT[h][64:67, :], 1.0), writes=[self.KTb[h]])
        P.op("pool", lambda hh: hh.memset(self.VA[:, :, :, 64:65], 1.0), writes=[self.VAb])
        P.op("pool", lambda hh: hh.memset(self.onesr[:, :], 1.0), writes=[self.spb])
        P.op("pool", lambda hh: hh.memset(self.carry[:, :], 0.0), writes=[self.carryb])

    def logf_chain(self, pff, pffb, n, first, logf_out=None):
        P, H = self.P, self.H
        P.op("act", lambda h: h.activation(out=self.sp_[:, :n], in_=pff[0:H, :n], func=AF.Exp, scale=-1.0,
                                           bias=self.nbf[:, 0:1]), reads=[pffb], writes=[self.spb])
        P.op("act", lambda h: h.activation(out=self.sp_[:, :n], in_=self.sp_[:, :n], func=AF.Ln, bias=1.0),
             reads=[self.spb], writes=[self.spb])
        self.scan(n)

    def scan(self, n):
        P, H = self.P, self.H
        P.op("dve", lambda h: h.tensor_tensor_scan(out=self.cp[:, :n], data0=self.onesr[:, :n], data1=self.sp_[:, :n],
                                                   initial=self.carry[:, 0:1], op0=ALU.mult, op1=ALU.add),
             reads=[self.spb, self.carryb], writes=[self.cpb])
        P.op("dve", lambda h: h.tensor_copy(out=self.carry[:, 0:1], in_=self.cp[:, n - 1:n]),
             reads=[self.cpb], writes=[self.carryb])

    def split_q(self, n, off=0):
        P = self.P
        P.op("dve", lambda h: h.tensor_scalar(out=self.v8[:, :n], in0=self.cp[:, off:off + n], scalar1=-8.0,
                                              scalar2=None, op0=ALU.mult), reads=[self.cpb], writes=[self.v8b])
        for x in range(3):
            P.op("dve", lambda h, x=x: h.tensor_copy(out=self.hml[x][:, :n], in_=self.v8[:, :n]),
                 reads=[self.v8b], writes=[self.hmlb])
            if x < 2:
                P.op("dve", lambda h, x=x: h.tensor_tensor(out=self.v8[:, :n], in0=self.v8[:, :n],
                                                           in1=self.hml[x][:, :n], op=ALU.subtract),
                     reads=[self.v8b, self.hmlb], writes=[self.v8b])

    def ck_block(self, pst, pstb, kb, col0, nk):
        P, H, C = self.P, self.H, self.C
        P.op("pe", lambda h: h.matmul(pst[0:nk, 0:H], lhsT=self.cp[:, col0:col0 + nk],
                                      rhs=C.f32[0:H, C_ID:C_ID + H], start=True, stop=True),
             reads=[self.cpb, C.b], writes=[pstb])
        P.op("dve", lambda h: h.tensor_copy(out=self.CK[0:nk, kb, :], in_=pst[0:nk, 0:H]),
             reads=[pstb], writes=[self.CKb])

    def q_aug(self, pq, pqb, h, n):
        P, C = self.P, self.C
        for x in range(3):
            c0 = C_E + (h * 3 + x) * 3
            P.op("pe", lambda hh, x=x, c0=c0: hh.matmul(pq[64:67, :n], lhsT=C.bf[0:self.H, c0:c0 + 3],
                                                        rhs=self.hml[x][:, :n], start=(x == 0), stop=(x == 2)),
                 reads=[self.hmlb, C.b], writes=[pqb])
        qa, qab = self.QA[h], self.QAb[h]
        P.op("act", lambda hh: hh.activation(out=qa[:, :n], in_=pq[0:67, :n], func=AF.Copy),
             reads=[pqb], writes=[qab])

    def attend_multi(self, heads, nq, kblocks, ps_rot, pos, pden, pdenb, stores, LA=2):
        P, C = self.P, self.C
        nb = len(kblocks)
        tasks = [(h, i) for i in range(nb) for h in heads]
        pend = []

        def stage_a(h, i):
            kb, nk, col0, masked = kblocks[i]
            ps, psb = ps_rot.next()
            qa, qab = self.QA[h], self.QAb[h]
            P.op("pe", lambda hh: hh.matmul(
                ps[0:nk, col0:nq], lhsT=self.KT[h][0:67, kb * 128:kb * 128 + nk], rhs=qa[0:67, col0:nq],
                start=True, stop=(not masked)), reads=[self.KTb[h], qab], writes=[psb])
            if masked:
                w = min(128, nq - col0)
                P.op("pe", lambda hh: hh.matmul(
                    ps[0:nk, col0:col0 + w], lhsT=C.bf[0:nk, C_ID:C_ID + nk], rhs=C.bf[0:nk, C_MASK:C_MASK + w],
                    start=False, stop=True), reads=[C.b], writes=[psb])
            pt, ptb = self.pt.next()
            P.op("act", lambda hh: hh.activation(
                out=pt[0:nk, col0:nq], in_=ps[0:nk, col0:nq], func=AF.Exp, scale=0.125,
                bias=self.CK[0:nk, kb, h:h + 1]), reads=[psb, self.CKb], writes=[ptb])
            return (h, i, pt, ptb)

        def stage_b(h, i, pt, ptb):
            kb, nk, col0, masked = kblocks[i]
            po, pob = pos[h]
            P.op("pe", lambda hh: hh.matmul(
                po[0:65, col0:nq], lhsT=self.VA[0:nk, kb, h, :], rhs=pt[0:nk, col0:nq],
                start=(i == 0), stop=(i == nb - 1)), reads=[ptb, self.VAb], writes=[pob])

        for (h, i) in tasks:
            pend.append(stage_a(h, i))
            if len(pend) > LA:
                stage_b(*pend.pop(0))
        while pend:
            stage_b(*pend.pop(0))
        for h in heads:
            po, pob = pos[h]
            osb, osbb = self.osb.next()
            P.op("dve", lambda hh, osb=osb, po=po: hh.tensor_copy(out=osb[:, :nq], in_=po[0:65, :nq]), reads=[pob],
                 writes=[osbb])
            P.op("pe", lambda hh, osb=osb: hh.matmul(pden[0:64, :nq], lhsT=C.f32[0:65, C_SEL64:C_SEL64 + 64],
                                                     rhs=osb[0:65, :nq], start=True, stop=True),
                 reads=[osbb, C.b], writes=[pdenb])
            rec, recb = self.rec.next()
            P.op("dve", lambda hh, rec=rec: hh.reciprocal(out=rec[:, :nq], in_=pden[0:64, :nq]), reads=[pdenb],
                 writes=[recb])
            om, omb = self.omix.next()
            P.op("dve", lambda hh, om=om, osb=osb, rec=rec: hh.tensor_tensor(out=om[:, :nq], in0=osb[0:64, :nq],
                                                                            in1=rec[:, :nq], op=ALU.mult),
                 reads=[osbb, recb], writes=[omb])
            stores[h](om, omb)


def phase_fox_prompt(P, C, io):
    from contextlib import ExitStack
    with ExitStack() as ph:
        P.stack = ph
        wf = P.sb([128, 8, 386], BF16, "wfox")
        wfb = Buf("wfox")
        wf32 = P.sb([128, 8, 386], F32, "wfox32")
        P.dma("sp", lambda h: h.dma_start(out=wf32[:, :, :], in_=io["wfox"].rearrange("p (k c) -> p k c", c=386)),
              writes=[wfb])
        P.op("pool", lambda h: h.tensor_copy(out=wf[:, :, :], in_=wf32[:, :, :]), reads=[wfb], writes=[wfb])
        bf = P.sb([2, 1], F32, "bf")
        bfb = Buf("bf")
        P.dma("sp", lambda h: h.dma_start(out=bf[:, :], in_=io["bfx"][:, :]), writes=[bfb])
        P.op("dve", lambda h: h.tensor_scalar(out=bf[:, :], in0=bf[:, :], scalar1=-1.0, scalar2=None, op0=ALU.mult),
             reads=[bfb], writes=[bfb])
        F = Fox(P, C, 2, SEQ, 512, wf, bf)
        xb = Rot(P, 2, [128, 8, 512], BF16, "hxb_")
        pproj = Rot(P, 2, [128, 512], F32, "fpp_", psum=True)
        ps_rot = Rot(P, 3, [128, 512], F32, "fps_", psum=True)
        po = [P.ps([128, 512], F32, f"fpo{h}") for h in range(2)]
        pob = [Buf(f"fpo{h}", excl=True) for h in range(2)]
        pden = P.ps([128, 512], F32, "fpden")
        pdenb = Buf("fpden", excl=True)
        kst = Rot(P, 2, [64, 512], F32, "kst_")
        vst = Rot(P, 2, [128, 4, 128], F32, "vst_")
        lst = Rot(P, 2, [2, 512], F32, "lst_")
        XG = None
        try:
            ck(1)
            _fox_loop(P, C, io, F, xb, pproj, ps_rot, po, pob, pden, pdenb, kst, vst, lst, wf, wfb, bf, bfb, XG)
        except StopPhase:
            pass
        P.barrier()
        P.emit()


def _fox_loop(P, C, io, F, xb, pproj, ps_rot, po, pob, pden, pdenb, kst, vst, lst, wf, wfb, bf, bfb, XG):
    QA2 = [[F.QA[h], P.sb([67, 512], BF16, f"QAx{h}")] for h in range(2)]
    QAb2 = [[F.QAb[h], Buf(f"QAx{h}")] for h in range(2)]
    if True:
        def prologue(tb):
            F.QA = [QA2[h][tb % 2] for h in range(2)]
            F.QAb = [QAb2[h][tb % 2] for h in range(2)]
            rk, cb = tb // (TPC // 512), (tb % (TPC // 512)) * 512
            x, xbuf = xb.next()
            P.dma("sp", lambda h, x=x, rk=rk, cb=cb: h.dma_start(
                out=x[:, :, :], in_=io["XGp"][cb // 512][rk * 1024:(rk + 1) * 1024, :].rearrange("(k p) n -> p k n", p=128)),
                reads=[io["XGpb"][cb // 512]], writes=[xbuf])
            ck(2)
            pf, pfb = pproj.next()
            for kt in range(8):
                P.op("pe", lambda h, pf=pf, kt=kt, x=x: h.matmul(pf[0:2, :], lhsT=wf[:, kt, 384:386], rhs=x[:, kt, :],
                                                              start=(kt == 0), stop=(kt == 7)),
                     reads=[wfb, xbuf], writes=[pfb])
            ck(3)
            F.nbf = bf
            P.op("act", lambda h, pf=pf: h.activation(out=F.sp_[:, :], in_=pf[0:2, :], func=AF.Exp, scale=-1.0,
                                                      bias=bf[:, 0:1]), reads=[pfb, bfb], writes=[F.spb])
            P.op("act", lambda h: h.activation(out=F.sp_[:, :], in_=F.sp_[:, :], func=AF.Ln, bias=1.0),
                 reads=[F.spb], writes=[F.spb])
            ck(4)
            F.scan(512)
            ck(5)
            F.split_q(512)
            ck(6)
            ls, lsb = lst.next()
            P.op("pool", lambda h, ls=ls: h.tensor_scalar(out=ls[:, :], in0=F.sp_[:, :], scalar1=-1.0, scalar2=None,
                                                          op0=ALU.mult), reads=[F.spb], writes=[lsb])
            P.dma("sp", lambda h, ls=ls, tb=tb: h.dma_start(out=io["o_logf"][:, tb * 512:(tb + 1) * 512], in_=ls[:, :]),
                  reads=[lsb])
            for sub in range(4):
                if DEBUG.get("nock"):
                    break
                pst, pstb = pproj.next()
                F.ck_block(pst, pstb, tb * 4 + sub, sub * 128, 128)
            ck(7)
            for hl in range(2):
                pk, pkb = pproj.next()
                for kt in range(8):
                    P.op("pe", lambda h, pk=pk, kt=kt, x=x, hl=hl: h.matmul(
                        pk[0:64, :], lhsT=wf[:, kt, 128 + hl * 64:128 + (hl + 1) * 64], rhs=x[:, kt, :],
                        start=(kt == 0), stop=(kt == 7)), reads=[wfb, xbuf], writes=[pkb])
                if not DEBUG.get("noktcopy"):
                    P.op("act", lambda h, pk=pk, hl=hl, tb=tb: h.activation(
                        out=F.KT[hl][0:64, tb * 512:(tb + 1) * 512], in_=pk[0:64, :], func=AF.Copy),
                        reads=[pkb], writes=[F.KTb[hl]])
                ks, ksb = kst.next()
                if not DEBUG.get("nokst"):
                    P.op("dve", lambda h, pk=pk, ks=ks: h.tensor_copy(out=ks[:, :], in_=pk[0:64, :]), reads=[pkb],
                         writes=[ksb])
                if not DEBUG.get("nokdma") and not (DEBUG.get("kdma0") and (hl != 0 or tb >= DEBUG["kdma0"])):
                    P.dma(DEBUG.get("kq", "sp") if isinstance(DEBUG.get("kq", "sp"), str) else "act", lambda h, ks=ks, hl=hl, tb=tb: h.dma_start(
                        out=io["o_foxk"][hl * 64:(hl + 1) * 64, tb * 512:(tb + 1) * 512], in_=ks[:, :]), reads=[ksb])
                pq, pqb = pproj.next()
                for kt in range(8):
                    P.op("pe", lambda h, pq=pq, kt=kt, x=x, hl=hl: h.matmul(
                        pq[0:64, :], lhsT=wf[:, kt, hl * 64:(hl + 1) * 64], rhs=x[:, kt, :],
                        start=(kt == 0), stop=(kt == 7)), reads=[wfb, xbuf], writes=[pqb])
                if not DEBUG.get("noaug"):
                    F.q_aug(pq, pqb, hl, 512)
            ck(8)
            vs, vsb = vst.next()
            for sub in range(4):
                pv, pvb = pproj.next()
                for kt in range(8):
                    P.op("pe", lambda h, pv=pv, kt=kt, x=x, sub=sub: h.matmul(
                        pv[:, 0:128], lhsT=x[:, kt, sub * 128:(sub + 1) * 128], rhs=wf[:, kt, 256:384],
                        start=(kt == 0), stop=(kt == 7)), reads=[wfb, xbuf], writes=[pvb])
                P.op("act", lambda h, pv=pv, sub=sub, tb=tb: h.activation(
                    out=F.VA[:, tb * 4 + sub, :, 0:64], in_=pv[:, 0:128].rearrange("p (h d) -> p h d", d=64),
                    func=AF.Copy), reads=[pvb], writes=[F.VAb])
                P.op("dve", lambda h, pv=pv, sub=sub, vs=vs: h.tensor_copy(out=vs[:, sub, :], in_=pv[:, 0:128]),
                     reads=[pvb], writes=[vsb])
            P.dma("sp", lambda h, vs=vs, tb=tb: h.dma_start(
                out=io["o_foxv"][tb * 512:(tb + 1) * 512, :].rearrange("(s p) f -> p s f", p=128), in_=vs[:, :, :]),
                reads=[vsb])

        def attention(tb):
            F.QA = [QA2[h][tb % 2] for h in range(2)]
            F.QAb = [QAb2[h][tb % 2] for h in range(2)]
            kbl = [(kb, 128, 0, False) for kb in range(4 * tb)]
            kbl += [(4 * tb + i, 128, 128 * i, True) for i in range(4)]
            stores = {}
            for hl in range(2):
                def store(om, omb, hl=hl, tb=tb):
                    P.dma("sp", lambda h: h.dma_start(
                        out=io["MXinp"][tb // 4][hl * 64:(hl + 1) * 64, (tb % 4) * 512:(tb % 4 + 1) * 512],
                        in_=om[:, :]), reads=[omb], writes=[io["MXinb"][tb]])
                stores[hl] = store
            F.attend_multi([0, 1], 512, kbl, ps_rot, {0: (po[0], pob[0]), 1: (po[1], pob[1])}, pden, pdenb, stores)


        nb_ = SEQ // 512
        prologue(0)
        for tb in range(nb_):
            if tb + 1 < nb_:
                prologue(tb + 1)
            ck(9)
            attention(tb)

class Gdn:
    def __init__(self, P, C, n, L, banks):
        self.P, self.C, self.n, self.L = P, C, n, L
        self.nch = n // L
        self.B = banks
        sb = P.sb
        self.wg = sb([128, 8, 514], BF16, "wg")
        self.wgb = Buf("wg")
        self.cw = sb([128, 3, 4], F32, "cw")
        self.sc = sb([1, 2], F32, "gsc")
        self.ng = sb([128, 1], F32, "gng")
        self.parb = Buf("gpar")
        self.XC = [sb([128, 3 + n], F32, f"XC{i}") for i in range(3)]
        self.XCb = [Buf(f"XC{i}") for i in range(3)]
        self.acc = [sb([128, n], F32, f"gacc{i}") for i in range(3)]
        self.accb = [Buf(f"gacc{i}") for i in range(3)]
        self.sqt = sb([128, n], BF16, "gsq")
        self.sqb = Buf("gsq")
        self.rn = sb([128, n], F32, "grn")
        self.rnb = Buf("grn")
        self.kT = sb([128, n], BF16, "gkT")
        self.qT = sb([128, n], BF16, "gqT")
        self.kbT = sb([128, n], BF16, "gkbT")
        self.kgT = sb([128, n], BF16, "gkgT")
        self.qgT = sb([128, n], BF16, "gqgT")
        self.vT = sb([128, n], BF16, "gvT")
        self.kTb, self.qTb, self.kbTb, self.kgTb, self.qgTb, self.vTb = [Buf(x) for x in "kT qT kbT kgT qgT vT".split()]
        self.zs = sb([128, n], F32, "gzs")
        self.zsb = Buf("gzs")
        self.EG = sb([128, n], F32, "gEG")
        self.EGb = Buf("gEG")
        self.BE = sb([128, n], F32, "gBE")
        self.BEb = Buf("gBE")
        self.rows = {k: sb([1, n], F32, "gr_" + k) for k in ("g", "Gl", "nGl", "eg", "kd", "beta", "one")}
        self.rowb = {k: Buf("gr_" + k) for k in self.rows}
        self.kdT = sb([L, self.nch], F32, "gkdT")
        self.kdTb = Buf("gkdT")
        self.vtok = sb([L, self.nch, 128], BF16, "gvtok")
        self.vtokb = Buf("gvtok")
        self.ktok = sb([L, self.nch, 128], BF16, "gktok")
        self.ktokb = Buf("gktok")
        self.dT = sb([L, n], F32, "gdT")
        self.dTb = Buf("gdT")
        self.X = [sb([L, n], F32, f"gX{i}") for i in range(2)]
        self.Y = [sb([L, n], F32, f"gY{i}") for i in range(2)]
        self.Pm = sb([L, n], F32, "gPm")
        self.Xb = [Buf("gX0"), Buf("gX1")]
        self.Yb = [Buf("gY0"), Buf("gY1")]
        self.Pmb = Buf("gPm")
        self.AT = sb([L, n], BF16, "gAT")
        self.ATb = Buf("gAT")
        self.TT = sb([L, n], BF16, "gTT")
        self.TTb = Buf("gTT")
        self.idt = sb([L, n], F32, "gidt")
        self.strict = sb([L, n], F32, "gstrict")
        self.cb = Buf("gconstl")
        self.S32 = sb([128, 128], F32, "gS32")
        self.S32b = Buf("gS32")
        self.Sbf = sb([128, 128], BF16, "gSbf")
        self.Sbfb = Buf("gSbf")
        self.R = Rot(P, 2, [L, 128], BF16, "gR_")
        self.vn = Rot(P, 2, [L, 128], BF16, "gvn_")
        self.vkd = Rot(P, 2, [L, 128], BF16, "gvkd_")
        self.og = sb([128, n], F32, "gog")
        self.ogb = Buf("gog")
        self.om = Rot(P, 2, [128, n], BF16, "gom_")
        self.coef = sb([1, 2], F32, "gcoef")
        self.coefb = Buf("gcoef")
        self.rot = 0
        for c in range(self.nch):
            P.op("pool", lambda h, c=c: h.tensor_copy(out=self.idt[:, c * L:(c + 1) * L], in_=C.f32[0:L, C_ID:C_ID + L]),
                 reads=[C.b], writes=[self.cb])
            P.op("pool", lambda h, c=c: h.tensor_scalar(out=self.strict[:, c * L:(c + 1) * L],
                                                        in0=C.f32[0:L, C_MASKT:C_MASKT + L], scalar1=-1.0 / 30000.0,
                                                        scalar2=None, op0=ALU.mult), reads=[C.b], writes=[self.cb])
        P.op("pool", lambda h: h.memset(self.rows["one"][:, :], 1.0), writes=[self.rowb["one"]])

    def gb(self):
        it = self.B[self.rot % 2]
        self.rot += 1
        return it

    def load_params(self, wg_ap, cw_ap, sc_ap, ng_ap, stage):
        P = self.P
        P.dma("sp", lambda h: h.dma_start(out=stage[:, :, :], in_=wg_ap.rearrange("p (k c) -> p k c", c=514)),
              writes=[self.wgb])
        P.op("pool", lambda h: h.tensor_copy(out=self.wg[:, :, :], in_=stage[:, :, :]), reads=[self.wgb],
             writes=[self.wgb])
        P.dma("sp", lambda h: h.dma_start(out=self.cw[:, :, :], in_=cw_ap), writes=[self.parb])
        P.dma("sp", lambda h: h.dma_start(out=self.sc[:, :], in_=sc_ap), writes=[self.parb])
        P.dma("sp", lambda h: h.dma_start(out=self.ng[:, :], in_=ng_ap), writes=[self.parb])
        P.op("act", lambda h: h.activation(out=self.coef[:, 0:1], in_=self.sc[:, 0:1], func=AF.Exp),
             reads=[self.parb], writes=[self.coefb])
        P.op("dve", lambda h: h.tensor_scalar(out=self.coef[:, 0:1], in0=self.coef[:, 0:1], scalar1=-1.0,
                                              scalar2=None, op0=ALU.mult), reads=[self.coefb], writes=[self.coefb])

    def block(self, x, xbuf, first, store_o):
        P, C, n, L, nch, B = self.P, self.C, self.n, self.L, self.nch, self.B
        wg, wgb = self.wg, self.wgb
        ones_bf = C.bf[:, C_ONES:C_ONES + 128]
        for s_ in range(3):
            pp, ppb = self.gb()
            for kt in range(8):
                P.op("pe", lambda h, pp=pp, kt=kt, s_=s_: h.matmul(pp[:, :n], lhsT=wg[:, kt, s_ * 128:(s_ + 1) * 128],
                                                                 rhs=x[:, kt, :n], start=(kt == 0), stop=(kt == 7)),
                     reads=[wgb, xbuf], writes=[ppb])
            P.op("act", lambda h, pp=pp, s_=s_: h.activation(out=self.XC[s_][:, 3:3 + n], in_=pp[:, :n], func=AF.Copy),
                 reads=[ppb], writes=[self.XCb[s_]])
        pp, ppb = self.gb()
        for kt in range(8):
            P.op("pe", lambda h, pp=pp, kt=kt: h.matmul(pp[:, :n], lhsT=wg[:, kt, 384:512], rhs=x[:, kt, :n],
                                                        start=(kt == 0), stop=(kt == 7)), reads=[wgb, xbuf], writes=[ppb])
        P.op("act", lambda h, pp=pp: h.activation(out=self.zs[:, :], in_=pp[:, :n], func=AF.Silu), reads=[ppb],
             writes=[self.zsb])
        r = self.rows
        rb = self.rowb
        pg, pgb = self.gb()
        for kt in range(8):
            P.op("pe", lambda h, pg=pg, kt=kt: h.matmul(pg[0:1, :n], lhsT=wg[:, kt, 512:513], rhs=x[:, kt, :n],
                                                        start=(kt == 0), stop=(kt == 7)), reads=[wgb, xbuf], writes=[pgb])
        P.op("act", lambda h, pg=pg: h.activation(out=r["g"][:, :], in_=pg[0:1, :n], func=AF.Exp, bias=self.sc[:, 1:2]),
             reads=[pgb, self.parb], writes=[rb["g"]])
        P.op("act", lambda h: h.activation(out=r["g"][:, :], in_=r["g"][:, :], func=AF.Ln, bias=1.0),
             reads=[rb["g"]], writes=[rb["g"]])
        P.op("dve", lambda h: h.tensor_scalar(out=r["g"][:, :], in0=r["g"][:, :], scalar1=self.coef[:, 0:1],
                                              scalar2=None, op0=ALU.mult), reads=[rb["g"], self.coefb], writes=[rb["g"]])
        pg2, pg2b = self.gb()
        for kt in range(8):
            P.op("pe", lambda h, pg2=pg2, kt=kt: h.matmul(pg2[0:1, :n], lhsT=wg[:, kt, 513:514], rhs=x[:, kt, :n],
                                                          start=(kt == 0), stop=(kt == 7)), reads=[wgb, xbuf], writes=[pg2b])
        P.op("act", lambda h, pg2=pg2: h.activation(out=r["beta"][:, :], in_=pg2[0:1, :n], func=AF.Sigmoid),
             reads=[pg2b], writes=[rb["beta"]])
        for c in range(nch):
            P.op("dve", lambda h, c=c: h.tensor_tensor_scan(out=r["Gl"][:, c * L:(c + 1) * L],
                                                            data0=r["one"][:, c * L:(c + 1) * L],
                                                            data1=r["g"][:, c * L:(c + 1) * L], initial=0.0,
                                                            op0=ALU.mult, op1=ALU.add),
                 reads=[rb["g"], rb["one"]], writes=[rb["Gl"]])
        P.op("act", lambda h: h.activation(out=r["eg"][:, :], in_=r["Gl"][:, :], func=AF.Exp), reads=[rb["Gl"]],
             writes=[rb["eg"]])
        P.op("dve", lambda h: h.tensor_scalar(out=r["nGl"][:, :], in0=r["Gl"][:, :], scalar1=-1.0, scalar2=None,
                                              op0=ALU.mult), reads=[rb["Gl"]], writes=[rb["nGl"]])
        for c in range(nch):
            P.op("act", lambda h, c=c: h.activation(out=r["kd"][:, c * L:(c + 1) * L], in_=r["Gl"][:, c * L:(c + 1) * L],
                                                    func=AF.Exp, scale=-1.0, bias=r["Gl"][:, (c + 1) * L - 1:(c + 1) * L]),
                 reads=[rb["Gl"]], writes=[rb["kd"]])
        onesrow = C.f32[0:1, C_ONES:C_ONES + 128]
        pe_, peb = self.gb()
        P.op("pe", lambda h, pe_=pe_: h.matmul(pe_[:, :n], lhsT=onesrow, rhs=r["eg"][:, :], start=True, stop=True),
             reads=[rb["eg"], C.b], writes=[peb])
        P.op("act", lambda h, pe_=pe_: h.activation(out=self.EG[:, :], in_=pe_[:, :n], func=AF.Copy), reads=[peb],
             writes=[self.EGb])
        pb_, pbb = self.gb()
        P.op("pe", lambda h, pb_=pb_: h.matmul(pb_[:, :n], lhsT=onesrow, rhs=r["beta"][:, :], start=True, stop=True),
             reads=[rb["beta"], C.b], writes=[pbb])
        P.op("act", lambda h, pb_=pb_: h.activation(out=self.BE[:, :], in_=pb_[:, :n], func=AF.Copy), reads=[pbb],
             writes=[self.BEb])
        pk_, pkb_ = self.gb()
        for c in range(nch):
            P.op("pe", lambda h, pk_=pk_, c=c: h.matmul(pk_[0:L, c:c + 1], lhsT=r["kd"][:, c * L:(c + 1) * L],
                                                        rhs=C.f32[0:1, C_ONES:C_ONES + 1], start=True, stop=True),
                 reads=[rb["kd"], C.b], writes=[pkb_])
        P.op("dve", lambda h, pk_=pk_: h.tensor_copy(out=self.kdT[:, :], in_=pk_[0:L, 0:nch]), reads=[pkb_],
             writes=[self.kdTb])
        for s_ in range(3):
            xc, xcb = self.XC[s_], self.XCb[s_]
            acc, accb = self.acc[s_], self.accb[s_]
            P.op("dve", lambda h, xc=xc, acc=acc, s_=s_: h.tensor_scalar(out=acc[:, :], in0=xc[:, 0:n],
                                                                       scalar1=self.cw[:, s_, 0:1], scalar2=None,
                                                                       op0=ALU.mult), reads=[xcb, self.parb], writes=[accb])
            for i in range(1, 4):
                P.op("dve", lambda h, xc=xc, acc=acc, s_=s_, i=i: h.scalar_tensor_tensor(
                    out=acc[:, :], in0=xc[:, i:i + n], scalar=self.cw[:, s_, i:i + 1], in1=acc[:, :],
                    op0=ALU.mult, op1=ALU.add), reads=[xcb, self.parb, accb], writes=[accb])
            P.op("act", lambda h, acc=acc: h.activation(out=acc[:, :], in_=acc[:, :], func=AF.Silu), reads=[accb],
                 writes=[accb])
            P.op("pool", lambda h, xc=xc: h.tensor_copy(out=xc[:, 0:3], in_=xc[:, n:n + 3]), reads=[xcb], writes=[xcb])
        for s_ in range(2):
            acc, accb = self.acc[s_], self.accb[s_]
            P.op("act", lambda h, acc=acc: h.activation(out=self.sqt[:, :], in_=acc[:, :], func=AF.Square), reads=[accb],
                 writes=[self.sqb])
            pn, pnb = self.gb()
            P.op("pe", lambda h, pn=pn: h.matmul(pn[:, :n], lhsT=ones_bf, rhs=self.sqt[:, :], start=True, stop=True),
                 reads=[self.sqb, C.b], writes=[pnb])
            P.op("dve", lambda h, pn=pn: h.tensor_scalar(out=self.rn[:, :], in0=pn[:, :n], scalar1=NORM_EPS, scalar2=None,
                                                         op0=ALU.add), reads=[pnb], writes=[self.rnb])
            P.op("act", lambda h: h.activation(out=self.rn[:, :], in_=self.rn[:, :], func=AF.Sqrt), reads=[self.rnb],
                 writes=[self.rnb])
            P.op("dve", lambda h: h.reciprocal(out=self.rn[:, :], in_=self.rn[:, :]), reads=[self.rnb], writes=[self.rnb])
            if s_ == 0:
                P.op("dve", lambda h, acc=acc: h.scalar_tensor_tensor(out=acc[:, :], in0=acc[:, :], scalar=128.0 ** -0.5,
                                                                     in1=self.rn[:, :], op0=ALU.mult, op1=ALU.mult),
                     reads=[accb, self.rnb], writes=[accb])
            else:
                P.op("dve", lambda h, acc=acc: h.tensor_tensor(out=acc[:, :], in0=acc[:, :], in1=self.rn[:, :],
                                                              op=ALU.mult), reads=[accb, self.rnb], writes=[accb])
        qf, kf, vf = self.acc
        qfb, kfb, vfb = self.accb
        P.op("act", lambda h: h.activation(out=self.qT[:, :], in_=qf[:, :], func=AF.Copy), reads=[qfb], writes=[self.qTb])
        P.op("act", lambda h: h.activation(out=self.kT[:, :], in_=kf[:, :], func=AF.Copy), reads=[kfb], writes=[self.kTb])
        P.op("act", lambda h: h.activation(out=self.vT[:, :], in_=vf[:, :], func=AF.Copy), reads=[vfb], writes=[self.vTb])
        P.op("dve", lambda h: h.tensor_tensor(out=self.kbT[:, :], in0=kf[:, :], in1=self.BE[:, :], op=ALU.mult),
             reads=[kfb, self.BEb], writes=[self.kbTb])
        P.op("dve", lambda h: h.tensor_tensor(out=self.kgT[:, :], in0=kf[:, :], in1=self.EG[:, :], op=ALU.mult),
             reads=[kfb, self.EGb], writes=[self.kgTb])
        P.op("pool", lambda h: h.tensor_tensor(out=self.qgT[:, :], in0=qf[:, :], in1=self.EG[:, :], op=ALU.mult),
             reads=[qfb, self.EGb], writes=[self.qgTb])
        idb = C.bf[:, C_ID:C_ID + 128]
        for (src, srcb, dst, dstb) in ((self.kT, self.kTb, self.ktok, self.ktokb), (self.vT, self.vTb, self.vtok, self.vtokb)):
            for c0 in range(0, nch, 4):
                pt_, ptb_ = self.gb()
                m = min(4, nch - c0)
                for c in range(c0, c0 + m):
                    P.op("pe", lambda h, pt_=pt_, c=c, c0=c0, src=src: h.matmul(
                        pt_[0:L, (c - c0) * 128:(c - c0 + 1) * 128], lhsT=src[:, c * L:(c + 1) * L], rhs=idb,
                        start=True, stop=True), reads=[srcb, C.b], writes=[ptb_])
                P.op("dve", lambda h, pt_=pt_, c0=c0, m=m, dst=dst: h.tensor_copy(
                    out=dst[:, c0:c0 + m, :], in_=pt_[0:L, 0:m * 128].rearrange("p (c d) -> p c d", d=128)),
                    reads=[ptb_], writes=[dstb])
        (a1, a1b), (a2, a2b), (dm, dmb) = B[2], B[3], B[4]
        for c in range(nch):
            J = slice(c * L, (c + 1) * L)
            P.op("pe", lambda h, J=J: h.matmul(a1[0:L, J], lhsT=self.kbT[:, J], rhs=self.kT[:, J], start=True, stop=True),
                 reads=[self.kbTb, self.kTb], writes=[a1b])
            P.op("pe", lambda h, J=J: h.matmul(a2[0:L, J], lhsT=self.kT[:, J], rhs=self.qT[:, J], start=True, stop=True),
                 reads=[self.kTb, self.qTb], writes=[a2b])
            P.op("pe", lambda h, J=J: h.matmul(dm[0:L, J], lhsT=C.f32[0:1, C_ONES:C_ONES + L], rhs=r["Gl"][:, J],
                                               start=True, stop=False), reads=[rb["Gl"], C.b], writes=[dmb])
            P.op("pe", lambda h, J=J: h.matmul(dm[0:L, J], lhsT=r["nGl"][:, J], rhs=C.f32[0:1, C_ONES:C_ONES + L],
                                               start=False, stop=False), reads=[rb["nGl"], C.b], writes=[dmb])
            P.op("pe", lambda h, J=J: h.matmul(dm[0:L, J], lhsT=C.f32[0:L, C_ID:C_ID + L], rhs=C.f32[0:L, C_MASK:C_MASK + L],
                                               start=False, stop=True), reads=[C.b], writes=[dmb])
        P.op("act", lambda h: h.activation(out=self.dT[:, :], in_=dm[0:L, :n], func=AF.Exp), reads=[dmb], writes=[self.dTb])
        X, Xb_, Y, Yb_ = self.X, self.Xb, self.Y, self.Yb
        P.op("dve", lambda h: h.tensor_tensor(out=X[0][:, :], in0=a1[0:L, :n], in1=self.dT[:, :], op=ALU.mult),
             reads=[a1b, self.dTb], writes=[Xb_[0]])
        P.op("dve", lambda h: h.tensor_tensor(out=X[0][:, :], in0=X[0][:, :], in1=self.strict[:, :], op=ALU.mult),
             reads=[Xb_[0], self.cb], writes=[Xb_[0]])
        P.op("dve", lambda h: h.tensor_tensor(out=self.AT[:, :], in0=a2[0:L, :n], in1=self.dT[:, :], op=ALU.mult),
             reads=[a2b, self.dTb], writes=[self.ATb])
        P.op("pool", lambda h: h.tensor_tensor(out=self.Pm[:, :], in0=self.idt[:, :], in1=X[0][:, :], op=ALU.subtract),
             reads=[Xb_[0], self.cb], writes=[self.Pmb])
        (px, pxb), (py, pyb), (pp_, ppb_) = B[2], B[3], B[4]
        for c in range(nch):
            J = slice(c * L, (c + 1) * L)
            P.op("pe", lambda h, J=J: h.matmul(py[0:L, J], lhsT=X[0][:, J], rhs=C.f32[0:L, C_ID:C_ID + L], start=True,
                                               stop=True), reads=[Xb_[0], C.b], writes=[pyb])
        P.op("act", lambda h: h.activation(out=Y[0][:, :], in_=py[0:L, :n], func=AF.Copy), reads=[pyb], writes=[Yb_[0]])
        nlev = {64: 5, 32: 4}[L]
        cur = 0
        for lev in range(nlev):
            nxt = 1 - cur
            last = (lev == nlev - 1)
            for c in range(nch):
                J = slice(c * L, (c + 1) * L)
                if not last:
                    P.op("pe", lambda h, J=J, cur=cur: h.matmul(px[0:L, J], lhsT=Y[cur][:, J], rhs=X[cur][:, J],
                                                                start=True, stop=True),
                         reads=[Xb_[cur], Yb_[cur]], writes=[pxb])
                P.op("pe", lambda h, J=J, cur=cur: h.matmul(py[0:L, J], lhsT=X[cur][:, J], rhs=Y[cur][:, J],
                                                            start=True, stop=True),
                     reads=[Xb_[cur], Yb_[cur]], writes=[pyb])
            if not last:
                P.op("dve", lambda h, nxt=nxt: h.tensor_copy(out=X[nxt][:, :], in_=px[0:L, :n]), reads=[pxb],
                     writes=[Xb_[nxt]])
            P.op("act", lambda h, nxt=nxt: h.activation(out=Y[nxt][:, :], in_=py[0:L, :n], func=AF.Copy), reads=[pyb],
                 writes=[Yb_[nxt]])
            for c in range(nch):
                J = slice(c * L, (c + 1) * L)
                P.op("pe", lambda h, J=J, nxt=nxt: h.matmul(pp_[0:L, J], lhsT=Y[nxt][:, J], rhs=self.Pm[:, J],
                                                            start=True, stop=True),
                     reads=[Yb_[nxt], self.Pmb], writes=[ppb_])
            P.op("dve", lambda h: h.tensor_tensor(out=self.Pm[:, :], in0=self.Pm[:, :], in1=pp_[0:L, :n], op=ALU.add),
                 reads=[self.Pmb, ppb_], writes=[self.Pmb])
            cur = nxt
        P.op("dve", lambda h: h.tensor_tensor(out=self.TT[:, :], in0=self.Pm[:, :], in1=self.BE[0:L, :], op=ALU.mult),
             reads=[self.Pmb, self.BEb], writes=[self.TTb])
        (r1, r1b), (r2, r2b), (ro, rob) = B[5], B[6], B[7]
        for c in range(nch):
            J = slice(c * L, (c + 1) * L)
            P.op("pe", lambda h, J=J: h.matmul(r1[0:L, 0:128], lhsT=self.kgT[:, J], rhs=self.Sbf[:, :], start=True,
                                               stop=True), reads=[self.kgTb, self.Sbfb], writes=[r1b])
            R, Rb = self.R.next()
            P.op("dve", lambda h, R=R, c=c: h.tensor_tensor(out=R[:, :], in0=self.vtok[:, c, :], in1=r1[0:L, 0:128],
                                                           op=ALU.subtract), reads=[self.vtokb, r1b], writes=[Rb])
            P.op("pe", lambda h, J=J, R=R: h.matmul(r2[0:L, 0:128], lhsT=self.TT[:, J], rhs=R[:, :], start=True, stop=True),
                 reads=[self.TTb, Rb], writes=[r2b])
            vn, vnb = self.vn.next()
            vkd, vkdb = self.vkd.next()
            P.op("dve", lambda h, vkd=vkd, c=c: h.tensor_scalar(out=vkd[:, :], in0=r2[0:L, 0:128],
                                                               scalar1=self.kdT[:, c:c + 1], scalar2=None, op0=ALU.mult),
                 reads=[r2b, self.kdTb], writes=[vkdb])
            P.op("act", lambda h, vn=vn: h.activation(out=vn[:, :], in_=r2[0:L, 0:128], func=AF.Copy), reads=[r2b],
                 writes=[vnb])
            P.op("pe", lambda h, J=J: h.matmul(ro[:, J], lhsT=self.Sbf[:, :], rhs=self.qgT[:, J], start=True, stop=False),
                 reads=[self.Sbfb, self.qgTb], writes=[rob])
            P.op("pe", lambda h, c=c, vkd=vkd: h.matmul(r1[:, 128:256], lhsT=self.ktok[:, c, :], rhs=vkd[:, :], start=True,
                                                        stop=True), reads=[self.ktokb, vkdb], writes=[r1b])
            P.op("pe", lambda h, J=J, vn=vn: h.matmul(ro[:, J], lhsT=vn[:, :], rhs=self.AT[:, J], start=False, stop=True),
                 reads=[vnb, self.ATb], writes=[rob])
            col = (c + 1) * L - 1
            P.op("dve", lambda h, col=col: h.scalar_tensor_tensor(out=self.Sbf[:, :], in0=self.S32[:, :],
                                                                  scalar=self.EG[:, col:col + 1], in1=r1[:, 128:256],
                                                                  op0=ALU.mult, op1=ALU.add),
                 reads=[self.S32b, self.EGb, r1b], writes=[self.Sbfb])
            P.op("dve", lambda h, col=col: h.scalar_tensor_tensor(out=self.S32[:, :], in0=self.S32[:, :],
                                                                  scalar=self.EG[:, col:col + 1], in1=r1[:, 128:256],
                                                                  op0=ALU.mult, op1=ALU.add),
                 reads=[self.S32b, self.EGb, r1b], writes=[self.S32b])
        P.op("act", lambda h: h.activation(out=self.sqt[:, :], in_=ro[:, :n], func=AF.Square), reads=[rob],
             writes=[self.sqb])
        pn, pnb = self.gb()
        P.op("pe", lambda h, pn=pn: h.matmul(pn[:, :n], lhsT=ones_bf, rhs=self.sqt[:, :], start=True, stop=True),
             reads=[self.sqb, C.b], writes=[pnb])
        P.op("dve", lambda h, pn=pn: h.tensor_scalar(out=self.rn[:, :], in0=pn[:, :n], scalar1=1.0 / 128.0,
                                                     scalar2=NORM_EPS, op0=ALU.mult, op1=ALU.add), reads=[pnb],
             writes=[self.rnb])
        P.op("act", lambda h: h.activation(out=self.rn[:, :], in_=self.rn[:, :], func=AF.Sqrt), reads=[self.rnb],
             writes=[self.rnb])
        P.op("dve", lambda h: h.reciprocal(out=self.rn[:, :], in_=self.rn[:, :]), reads=[self.rnb], writes=[self.rnb])
        P.op("dve", lambda h: h.tensor_tensor(out=self.og[:, :], in0=ro[:, :n], in1=self.rn[:, :], op=ALU.mult),
             reads=[rob, self.rnb], writes=[self.ogb])
        om, omb = self.om.next()
        P.op("dve", lambda h, om=om: h.scalar_tensor_tensor(out=om[:, :], in0=self.og[:, :], scalar=self.ng[:, 0:1],
                                                           in1=self.zs[:, :], op0=ALU.mult, op1=ALU.mult),
             reads=[self.ogb, self.parb, self.zsb], writes=[omb])
        store_o(om, omb)


def phase_gdn_prompt(P, C, io):
    from contextlib import ExitStack
    with ExitStack() as ph:
        P.stack = ph
        banks = [(P.ps([128, 512], F32, f"gbank{i}"), Buf(f"gbank{i}", excl=True)) for i in range(8)]
        G = Gdn(P, C, 512, 64, banks)
        stage = P.sb([128, 8, 514], F32, "wgst")
        G.load_params(io["wgdn"], io["gcw"], io["gsc"], io["gng"], stage)
        for i in range(3):
            P.op("pool", lambda h, i=i: h.memset(G.XC[i][:, 0:3], 0.0), writes=[G.XCb[i]])
        P.op("pool", lambda h: h.memset(G.S32[:, :], 0.0), writes=[G.S32b])
        P.op("pool", lambda h: h.memset(G.Sbf[:, :], 0.0), writes=[G.Sbfb])
        xb = Rot(P, 2, [128, 8, 512], BF16, "gxb_")
        XG = None
        nb = SEQ // 512
        for tb in range(nb):
            rk, cb = tb // (TPC // 512), (tb % (TPC // 512)) * 512
            x, xbuf = xb.next()
            P.dma("sp", lambda h, x=x, rk=rk, cb=cb: h.dma_start(
                out=x[:, :, :], in_=io["XGp"][cb // 512][rk * 1024:(rk + 1) * 1024, :].rearrange("(k p) n -> p k n", p=128)),
                reads=[io["XGpb"][cb // 512]], writes=[xbuf])

            def store(om, omb, tb=tb):
                P.dma("sp", lambda h: h.dma_start(
                    out=io["MXinp"][tb // 4][128:256, (tb % 4) * 512:(tb % 4 + 1) * 512], in_=om[:, :]),
                      reads=[omb], writes=[io["MXinb"][tb]])
            G.block(x, xbuf, tb == 0, store)
            if tb % (PIECE // 512) == PIECE // 512 - 1:
                gather_one(P, io, "MX", tb // (PIECE // 512))
        P.dma("sp", lambda h: h.dma_start(out=io["o_gstate"][:, :], in_=G.S32[:, :]), reads=[G.S32b])
        for i in range(3):
            P.dma("sp", lambda h, i=i: h.dma_start(out=io["o_gconv"][i], in_=G.XC[i][:, 0:3]), reads=[G.XCb[i]])
        P.barrier()
        P.emit()


def phase_sample_even(P, C, io):
    from contextlib import ExitStack
    n = NS
    with ExitStack() as ph:
        P.stack = ph
        banks = [(P.ps([128, 512], F32, f"sbank{i}"), Buf(f"sbank{i}", excl=True)) for i in range(8)]
        x32 = P.sb([128, 8, n], F32, "sx32")
        xs = P.sb([128, 8, n], BF16, "sxb")
        xsb = Buf("sxb")
        P.dma("sp", lambda h: h.dma_start(out=x32[:, :, :], in_=io["X1v"][:, :, TPC:TPC + n]), reads=[io["X1b"][-1]],
              writes=[xsb])
        P.op("act", lambda h: h.activation(out=xs[:, :, :], in_=x32[:, :, :], func=AF.Copy), reads=[xsb], writes=[xsb])
        wfs = P.sb([128, 8, 1544], BF16, "wfs")
        wfsb = Buf("wfs")
        wst = Rot(P, 2, [128, 1544], F32, "wfst_")
        for kt in range(8):
            st, stb = wst.next()
            P.dma("act", lambda h, st=st, kt=kt: h.dma_start(out=st[:, :], in_=io["wfox_s"][:, kt * 1544:(kt + 1) * 1544]),
                  writes=[stb])
            P.op("pool", lambda h, st=st, kt=kt: h.tensor_copy(out=wfs[:, kt, :], in_=st[:, :]), reads=[stb], writes=[wfsb])
        bf = P.sb([8, 1], F32, "sbf")
        bfb = Buf("sbf")
        P.dma("sp", lambda h: h.dma_start(out=bf[:, :], in_=io["bf_s"][:, :]), writes=[bfb])
        P.op("dve", lambda h: h.tensor_scalar(out=bf[:, :], in0=bf[:, :], scalar1=-1.0, scalar2=None, op0=ALU.mult),
             reads=[bfb], writes=[bfb])
        F = Fox(P, C, 8, 1024 + n, n, wfs, bf)
        prot = [banks[0], banks[1]]
        pi = [0]

        def nb():
            it = prot[pi[0] % 2]
            pi[0] += 1
            return it
        kst = Rot(P, 2, [64, 1024], F32, "skst_")
        for hh in range(8):
            st, stb = kst.next()
            P.dma("sp", lambda h, st=st, hh=hh: h.dma_start(out=st[:, :], in_=io["pastk"][hh]), writes=[stb])
            P.op("act", lambda h, st=st, hh=hh: h.activation(out=F.KT[hh][0:64, 0:1024], in_=st[:, :], func=AF.Copy),
                 reads=[stb], writes=[F.KTb[hh]])
        vst = P.sb([128, 8, 512], F32, "svst")
        vstb = Buf("svst")
        P.dma("sp", lambda h: h.dma_start(out=vst[:, :, :], in_=io["pastv"].rearrange("(kb p) f -> p kb f", p=128)),
              writes=[vstb])
        for kb in range(8):
            P.op("pool", lambda h, kb=kb: h.tensor_copy(out=F.VA[:, kb, :, 0:64],
                                                        in_=vst[:, kb, :].rearrange("p (h d) -> p h d", d=64)),
                 reads=[vstb], writes=[F.VAb])
        lp = P.sb([8, 1024], F32, "slp")
        lpb = Buf("slp")
        P.dma("sp", lambda h: h.dma_start(out=lp[:, :], in_=io["pastlf"][:, :]), writes=[lpb])
        for half in range(2):
            P.op("dve", lambda h, half=half: h.tensor_scalar(out=F.sp_[:, :], in0=lp[:, half * 512:(half + 1) * 512],
                                                            scalar1=-1.0, scalar2=None, op0=ALU.mult),
                 reads=[lpb], writes=[F.spb])
            F.scan(512)
            for sub in range(4):
                pst, pstb = nb()
                F.ck_block(pst, pstb, half * 4 + sub, sub * 128, 128)
        pf, pfb = nb()
        for kt in range(8):
            P.op("pe", lambda h, kt=kt: h.matmul(pf[0:8, :n], lhsT=wfs[:, kt, 1536:1544], rhs=xs[:, kt, :],
                                                 start=(kt == 0), stop=(kt == 7)), reads=[wfsb, xsb], writes=[pfb])
        P.op("act", lambda h: h.activation(out=F.sp_[:, :n], in_=pf[0:8, :n], func=AF.Exp, scale=-1.0, bias=bf[:, 0:1]),
             reads=[pfb, bfb], writes=[F.spb])
        P.op("act", lambda h: h.activation(out=F.sp_[:, :n], in_=F.sp_[:, :n], func=AF.Ln, bias=1.0), reads=[F.spb],
             writes=[F.spb])
        F.scan(n)
        F.split_q(n)
        ls = P.sb([8, n], F32, "sls")
        lsb = Buf("sls")
        P.op("pool", lambda h: h.tensor_scalar(out=ls[:, :], in0=F.sp_[:, :n], scalar1=-1.0, scalar2=None, op0=ALU.mult),
             reads=[F.spb], writes=[lsb])
        P.dma("sp", lambda h: h.dma_start(out=io["o_slogf"][:, :], in_=ls[:, :]), reads=[lsb])
        pst, pstb = nb()
        F.ck_block(pst, pstb, 8, 0, n)
        kso = P.sb([64, 8, n], F32, "skso")
        ksob = Buf("skso")
        for hh in range(8):
            pk, pkb = nb()
            for kt in range(8):
                P.op("pe", lambda h, pk=pk, kt=kt, hh=hh: h.matmul(
                    pk[0:64, :n], lhsT=wfs[:, kt, 512 + hh * 64:512 + (hh + 1) * 64], rhs=xs[:, kt, :],
                    start=(kt == 0), stop=(kt == 7)), reads=[wfsb, xsb], writes=[pkb])
            P.op("act", lambda h, pk=pk, hh=hh: h.activation(out=F.KT[hh][0:64, 1024:1024 + n], in_=pk[0:64, :n],
                                                             func=AF.Copy), reads=[pkb], writes=[F.KTb[hh]])
            P.op("dve", lambda h, pk=pk, hh=hh: h.tensor_copy(out=kso[:, hh, :], in_=pk[0:64, :n]), reads=[pkb],
                 writes=[ksob])
            pq, pqb = nb()
            for kt in range(8):
                P.op("pe", lambda h, pq=pq, kt=kt, hh=hh: h.matmul(
                    pq[0:64, :n], lhsT=wfs[:, kt, hh * 64:(hh + 1) * 64], rhs=xs[:, kt, :],
                    start=(kt == 0), stop=(kt == 7)), reads=[wfsb, xsb], writes=[pqb])
            F.q_aug(pq, pqb, hh, n)
        P.dma("sp", lambda h: h.dma_start(out=io["o_sfoxk"].rearrange("(h d) n -> d h n", d=64), in_=kso[:, :, :]),
              reads=[ksob])
        pv, pvb = nb()
        for kt in range(8):
            P.op("pe", lambda h, kt=kt: h.matmul(pv[0:n, 0:512], lhsT=xs[:, kt, :], rhs=wfs[:, kt, 1024:1536],
                                                 start=(kt == 0), stop=(kt == 7)), reads=[wfsb, xsb], writes=[pvb])
        P.op("act", lambda h: h.activation(out=F.VA[0:n, 8, :, 0:64],
                                           in_=pv[0:n, 0:512].rearrange("p (h d) -> p h d", d=64), func=AF.Copy),
             reads=[pvb], writes=[F.VAb])
        vso = P.sb([n, 512], F32, "svso")
        vsob = Buf("svso")
        P.op("dve", lambda h: h.tensor_copy(out=vso[:, :], in_=pv[0:n, 0:512]), reads=[pvb], writes=[vsob])
        P.dma("sp", lambda h: h.dma_start(out=io["o_sfoxv"][:, :], in_=vso[:, :]), reads=[vsob])
        ps_rot = Rot(P, 1, [128, 512], F32, "unused", psum=False)
        ps_rot.items = [banks[2], banks[3], banks[4]]
        kbl = [(kb, 128, 0, False) for kb in range(8)] + [(8, n, 0, True)]
        for h0_ in range(0, 8, 2):
            stores = {}
            for hh in (h0_, h0_ + 1):
                def store(om, omb, hh=hh):
                    P.dma("sp", lambda h: h.dma_start(out=io["SMX"][hh * 64:(hh + 1) * 64, :], in_=om[:, :n]),
                          reads=[omb], writes=[io["SMXb"]])
                stores[hh] = store
            F.attend_multi([h0_, h0_ + 1], n, kbl, ps_rot, {h0_: banks[5], h0_ + 1: banks[6]}, banks[7][0], banks[7][1],
                           stores)
        G = Gdn(P, C, n, n, banks)
        stage = P.sb([128, 8, 514], F32, "swgst")
        for hd in range(4):
            G.load_params(io["wgdn_s"][hd], io["gcw_s"][hd], io["gsc_s"][hd], io["gng"], stage)
            for i in range(3):
                P.dma("sp", lambda h, i=i, hd=hd: h.dma_start(out=G.XC[i][:, 0:3], in_=io["sconv"][hd, i]),
                      writes=[G.XCb[i]])
            P.dma("sp", lambda h, hd=hd: h.dma_start(out=G.S32[:, :], in_=io["sstate"][hd]), writes=[G.S32b])
            P.op("act", lambda h: h.activation(out=G.Sbf[:, :], in_=G.S32[:, :], func=AF.Copy), reads=[G.S32b],
                 writes=[G.Sbfb])

            def store(om, omb, hd=hd):
                P.dma("sp", lambda h: h.dma_start(out=io["SMX"][512 + hd * 128:512 + (hd + 1) * 128, :], in_=om[:, :]),
                      reads=[omb], writes=[io["SMXb"]])
            G.block(xs, xsb, True, store)
            P.dma("sp", lambda h, hd=hd: h.dma_start(out=io["o_sgstate"][hd], in_=G.S32[:, :]), reads=[G.S32b])
            for i in range(3):
                P.dma("sp", lambda h, i=i, hd=hd: h.dma_start(out=io["o_sgconv"][hd, i], in_=G.XC[i][:, 0:3]),
                      reads=[G.XCb[i]])
        P.barrier()
        P.emit()


def groups_():
    return [(i * 512, 512) for i in range(TPC // 512)] + [(TPC, NS)]


def ffn_pass(P, C, io, consts, lnp_sb, fidx, lnidx, src_v, src_b, dst_v, dst_b, xg=None, out_final=None):
    from contextlib import ExitStack
    with ExitStack() as ph:
        P.stack = ph
        T = TPhase(P, consts)
        T.load_wout(io["w_out"][fidx])
        xbufs = {}
        pending = []
        eps = LN_EPS / (ALPHA * ALPHA)
        grp = groups_()
        WS = io["WBF"]
        WSb = io["WBFb"]
        sched = [(gi, jp) for gi in range(len(grp)) for jp in range(11)]
        issued = [0]
        fifo = []
        wbq = []
        LA = 3

        def issue_one():
            gi, jp = sched[issued[0]]
            issued[0] += 1
            w, wb = T.win.next()
            if gi == 0:
                ws, wsb = T.wstage.next()
                P.dma("sp", lambda h: h.dma_start(out=ws[:, :, :], in_=io["w_in"][fidx][jp].rearrange("p (k c) -> p k c", c=512)),
                      writes=[wsb])
                P.op("pool", lambda h: h.tensor_copy(out=w[:, :, :], in_=ws[:, :, :]), reads=[wsb], writes=[wb])
                while wbq:
                    wbq.pop(0)()
                wbq.append(lambda: P.dma("sp", lambda h: h.dma_start(
                    out=WS[jp].rearrange("p (k c) -> p k c", c=512), in_=w[:, :, :]), reads=[wb], writes=[WSb[jp]]))
            else:
                while wbq:
                    wbq.pop(0)()
                P.dma("sp", lambda h: h.dma_start(out=w[:, :, :], in_=WS[jp].rearrange("p (k c) -> p k c", c=512)),
                      reads=[WSb[jp]], writes=[wb])
            fifo.append((w, wb))

        def provider():
            while issued[0] < len(sched) and len(fifo) < LA:
                issue_one()
            return fifo.pop(0)

        def load_x(gi):
            t0, nt = grp[gi]
            x32, x32b0 = T.x32.next()
            xb, xbb0 = T.xb.next()
            if id(x32b0) not in xbufs:
                xbufs[id(x32b0)] = ([Buf(f"x32m{m}") for m in range(8)], [Buf(f"xbm{m}") for m in range(8)])
            x32bs, xbbs = xbufs[id(x32b0)]
            P.dma("sp", lambda h: h.dma_start(out=x32[:, :, :nt], in_=src_v[:, :, t0:t0 + nt]),
                  reads=[src_b[gi]] if src_b else [], writes=x32bs)
            P.op("act", lambda h: h.activation(out=xb[:, :, :nt], in_=x32[:, :, :nt], func=AF.Copy),
                 reads=x32bs, writes=xbbs)
            return x32, x32bs, xb, xbbs

        nxt = load_x(0)
        for gi, (t0, nt) in enumerate(grp):
            x32, x32bs, xb, xbbs = nxt
            T.ffn_in(xb, xbbs, nt, provider, pending)
            if gi + 1 < len(grp):
                nxt = load_x(gi + 1)
            T.ffn_out(x32, x32bs, nt, eps)
            pending = T.norm_items(x32, x32bs, xb, xbbs, nt, lnp_sb[:, 0, lnidx, :], lnp_sb[:, 1, lnidx, :])

            def stores(gi=gi, t0=t0, nt=nt, x32=x32, xb=xb, x32bs=x32bs, xbbs=xbbs):
                if out_final is not None:
                    if gi < TPC // 512:
                        P.dma("sp", lambda h: h.dma_start(
                            out=out_final[0].rearrange("(k p) n -> p k n", p=128)[:, :, t0:t0 + nt], in_=x32[:, :, :nt]),
                            reads=x32bs)
                    else:
                        P.dma("sp", lambda h: h.dma_start(
                            out=out_final[1].rearrange("(k p) n -> p k n", p=128), in_=x32[:, :, :nt]), reads=x32bs)
                else:
                    P.dma("sp", lambda h: h.dma_start(out=dst_v[:, :, t0:t0 + nt], in_=x32[:, :, :nt]),
                          reads=x32bs, writes=[dst_b[gi]])
                if xg and gi < TPC // 512:
                    P.dma("sp", lambda h: h.dma_start(
                        out=io["XGinp"][gi].rearrange("(k p) n -> p k n", p=128), in_=xb[:, :, :nt]),
                        reads=xbbs, writes=io["XGinpb"][gi])
                    gather_one(P, io, "XG", gi)
            pending.append(stores)
        while pending:
            pending.pop(0)()
        P.barrier()
        P.emit()


RG_ = [[0, 1, 2, 3], [4, 5, 6, 7]]
PIECE = 2048


def gather_one(P, io, name, j):
    inp, outp = io[name + "inp"][j], io[name + "p"][j]
    P.coll(lambda h: h.collective_compute("AllGather", ALU.bypass, replica_groups=RG_, ins=[inp.opt()],
                                          outs=[outp.opt()]),
           reads=io[name + "inpb"][j], writes=[io[name + "pb"][j]])


def gather_pieces(P, io, name):
    for j in range(len(io[name + "p"])):
        gather_one(P, io, name, j)


class ProjPass(LNBase):
    def __init__(self, P, consts, nw):
        self._ln_alloc(P, consts)
        self.x32 = Rot(P, 2, [128, 8, 512], F32, "px32_")
        self.xb = Rot(P, 2, [128, 8, 512], BF16, "pxb_")
        self.A = Rot(P, 2, [128, 8, 512], BF16, "pA_")
        self.Bq = Rot(P, 4, [128, 8, 512], BF16, "pB_")
        self.w = [P.sb([128, 8, 1024], BF16, f"pw{i}") for i in range(nw)]
        self.wb = [Buf(f"pw{i}") for i in range(nw)]
        self.wst = Rot(P, 2, [128, 2, 1024], F32, "pwst_")
        self.po = Rot(P, 2, [128, 512], F32, "ppo_", psum=True)
        self.pg = Rot(P, 2, [128, 512], F32, "ppg_", psum=True)
        self.rm = P.sb([128, 4], F32, "rmask")
        self.rmb = Buf("rmask")
        self.tmp = Rot(P, 4, [128, 512], F32, "ptmp_")
        self.xbufs = {}

    def bufs(self, x32b0):
        if id(x32b0) not in self.xbufs:
            self.xbufs[id(x32b0)] = ([Buf(f"px32m{m}") for m in range(8)], [Buf(f"pxbm{m}") for m in range(8)])
        return self.xbufs[id(x32b0)]

    def ln_now(self, x32, x32bs, xb, xbbs, nt, g_ap, b_ap):
        self.ln_stats(x32, x32bs, nt, LN_EPS / (ALPHA * ALPHA))
        for it in self.norm_items(x32, x32bs, xb, xbbs, nt, g_ap, b_ap):
            it()

    def load_w(self, i, ap):
        P = self.P
        for q in range(4):
            st, stb = self.wst.next()
            P.dma("act", lambda h, st=st, q=q: h.dma_start(
                out=st[:, :, :], in_=ap.rearrange("(k p) c -> p k c", p=128)[:, 2 * q:2 * q + 2, :]), writes=[stb])
            P.op("pool", lambda h, st=st, q=q: h.tensor_copy(out=self.w[i][:, 2 * q:2 * q + 2, :], in_=st[:, :, :]),
                 reads=[stb], writes=[self.wb[i]])

    def combine(self, gathered, gb, t0, nt, A, Ab):
        P = self.P
        for q in range(4):
            Bt, Bb = self.Bq.next()
            tok = q * TPC + t0
            pj, pc = tok // PIECE, tok % PIECE
            P.dma("sp", lambda h, Bt=Bt, pj=pj, pc=pc: h.dma_start(
                out=Bt[:, :, :nt], in_=gathered[pj].rearrange("(k p) n -> p k n", p=128)[:, :, pc:pc + nt]),
                reads=[gb[pj]], writes=[Bb])
            if q == 0:
                P.op("pool", lambda h, Bt=Bt: h.tensor_scalar(out=A[:, :, :nt], in0=Bt[:, :, :nt],
                                                              scalar1=self.rm[:, 0:1], scalar2=None, op0=ALU.mult),
                     reads=[Bb, self.rmb], writes=[Ab])
            else:
                P.op("dve", lambda h, Bt=Bt, q=q: h.scalar_tensor_tensor(out=A[:, :, :nt], in0=Bt[:, :, :nt],
                                                                        scalar=self.rm[:, q:q + 1], in1=A[:, :, :nt],
                                                                        op0=ALU.mult, op1=ALU.add),
                     reads=[Bb, self.rmb, Ab], writes=[Ab])


def proj_pass_even(P, C, io, consts, lnp_sb, src_v, src_b, dst_v, dst_b):
    from contextlib import ExitStack
    with ExitStack() as ph:
        P.stack = ph
        T = ProjPass(P, consts, 1)
        P.dma("sp", lambda h: h.dma_start(out=T.rm[:, :], in_=io["rmask"][:, :]), writes=[T.rmb])
        T.load_w(0, io["even_w_out"])
        perm = [0, 4, 1, 5, 2, 6, 3, 7]
        for gi, (t0, nt) in enumerate(groups_()):
            x32, x32b0 = T.x32.next()
            xb, xbb = T.xb.next()
            x32bs, xbbs = T.bufs(x32b0)
            A, Ab = T.A.next()
            P.dma("sp", lambda h, x32=x32, t0=t0, nt=nt: h.dma_start(out=x32[:, :, :nt], in_=src_v[:, :, t0:t0 + nt]),
                  reads=[src_b[gi]], writes=x32bs)
            if gi < TPC // 512:
                T.combine(io["MXp"], io["MXpb"], t0, nt, A, Ab)
                kmap = perm
            else:
                P.dma("sp", lambda h, A=A, nt=nt: h.dma_start(out=A[:, :, :nt],
                                                              in_=io["SMX"].rearrange("(k p) n -> p k n", p=128)),
                      reads=[io["SMXb"]], writes=[Ab])
                kmap = list(range(8))
            for m in range(8):
                po, pob = T.po.next()
                for kt in range(8):
                    P.op("pe", lambda h, po=po, kt=kt, m=m, A=A, kmap=kmap, nt=nt: h.matmul(
                        po[:, :nt], lhsT=T.w[0][:, kmap[kt], m * 128:(m + 1) * 128], rhs=A[:, kt, :nt],
                        start=(kt == 0), stop=(kt == 7)), reads=[T.wb[0], Ab], writes=[pob])
                P.op("dve", lambda h, po=po, m=m, x32=x32, nt=nt: h.scalar_tensor_tensor(
                    out=x32[:, m, :nt], in0=po[:, :nt], scalar=1.0 / ALPHA, in1=x32[:, m, :nt], op0=ALU.mult,
                    op1=ALU.add), reads=[pob, x32bs[m]], writes=[x32bs[m]])
            T.ln_now(x32, x32bs, xb, xbbs, nt, lnp_sb[:, 0, 1, :], lnp_sb[:, 1, 1, :])
            P.dma("sp", lambda h, x32=x32, t0=t0, nt=nt: h.dma_start(out=dst_v[:, :, t0:t0 + nt], in_=x32[:, :, :nt]),
                  reads=x32bs, writes=[dst_b[gi]])
        P.barrier()
        P.emit()


def proj_pass_odd(P, C, io, consts, lnp_sb, src_v, src_b, dst_v, dst_b):
    from contextlib import ExitStack
    with ExitStack() as ph:
        P.stack = ph
        T = ProjPass(P, consts, 2)
        P.dma("sp", lambda h: h.dma_start(out=T.rm[:, :], in_=io["rmask"][:, :]), writes=[T.rmb])
        T.load_w(0, io["glu_w"])
        T.load_w(1, io["odd_w_out"])
        gbias = P.sb([128, 8], F32, "glub")
        gbb = Buf("glub")
        P.dma("sp", lambda h: h.dma_start(out=gbias[:, :], in_=io["glu_b"][:, :]), writes=[gbb])
        zz32 = Rot(P, 1, [128, 8, 512], F32, "zz32_")
        zzb = Rot(P, 2, [128, 8, 512], BF16, "zzb_")
        sg = Rot(P, 2, [128, 512], F32, "sg_")
        for gi, (t0, nt) in enumerate(groups_()):
            x32, x32b0 = T.x32.next()
            xb, xbb = T.xb.next()
            x32bs, xbbs = T.bufs(x32b0)
            A, Ab = T.A.next()
            P.dma("sp", lambda h, x32=x32, t0=t0, nt=nt: h.dma_start(out=x32[:, :, :nt], in_=src_v[:, :, t0:t0 + nt]),
                  reads=[src_b[gi]], writes=x32bs)
            if gi < TPC // 512:
                T.combine(io["YSp"], io["YSpb"], t0, nt, A, Ab)
            else:
                P.dma("sp", lambda h, A=A, nt=nt: h.dma_start(out=A[:, :, :nt],
                                                              in_=io["SYS"].rearrange("(k p) n -> p k n", p=128)),
                      reads=[io["SYSb"]], writes=[Ab])
            z32, z32b = zz32.next()
            zb, zbb = zzb.next()
            P.op("act", lambda h, A=A, z32=z32, nt=nt: h.activation(out=z32[:, :, :nt], in_=A[:, :, :nt], func=AF.Square),
                 reads=[Ab], writes=[z32b])
            P.op("dve", lambda h, z32=z32, nt=nt: h.tensor_scalar(out=z32[:, :, :nt], in0=z32[:, :, :nt], scalar1=0.044715,
                                                                scalar2=1.0, op0=ALU.mult, op1=ALU.add),
                 reads=[z32b], writes=[z32b])
            P.op("dve", lambda h, A=A, z32=z32, nt=nt: h.tensor_tensor(out=z32[:, :, :nt], in0=z32[:, :, :nt],
                                                                      in1=A[:, :, :nt], op=ALU.mult),
                 reads=[z32b, Ab], writes=[z32b])
            P.op("act", lambda h, z32=z32, nt=nt: h.activation(out=z32[:, :, :nt], in_=z32[:, :, :nt], func=AF.Sigmoid,
                                                              scale=1.5957691216057308), reads=[z32b], writes=[z32b])
            P.op("dve", lambda h, A=A, z32=z32, nt=nt: h.tensor_tensor(out=z32[:, :, :nt], in0=z32[:, :, :nt],
                                                                      in1=A[:, :, :nt], op=ALU.mult),
                 reads=[z32b, Ab], writes=[z32b])
            P.op("pool", lambda h, zb=zb, z32=z32, nt=nt: h.tensor_copy(out=zb[:, :, :nt], in_=z32[:, :, :nt]),
                 reads=[z32b], writes=[zbb])
            for m in range(8):
                pg, pgb = T.pg.next()
                for kt in range(8):
                    P.op("pe", lambda h, pg=pg, kt=kt, m=m, zb=zb, nt=nt: h.matmul(
                        pg[:, :nt], lhsT=T.w[0][:, kt, m * 128:(m + 1) * 128], rhs=zb[:, kt, :nt],
                        start=(kt == 0), stop=(kt == 7)), reads=[T.wb[0], zbb], writes=[pgb])
                s_, sb_ = sg.next()
                P.op("act", lambda h, pg=pg, s_=s_, m=m, nt=nt: h.activation(out=s_[:, :nt], in_=pg[:, :nt],
                                                                            func=AF.Sigmoid, bias=gbias[:, m:m + 1]),
                     reads=[pgb, gbb], writes=[sb_])
                P.op("dve", lambda h, s_=s_, m=m, A=A, z32=z32, nt=nt: h.tensor_tensor(
                    out=A[:, m, :nt], in0=z32[:, m, :nt], in1=s_[:, :nt], op=ALU.mult), reads=[z32b, sb_, Ab],
                    writes=[Ab])
            for m in range(8):
                po, pob = T.po.next()
                for kt in range(8):
                    P.op("pe", lambda h, po=po, kt=kt, m=m, A=A, nt=nt: h.matmul(
                        po[:, :nt], lhsT=T.w[1][:, kt, m * 128:(m + 1) * 128], rhs=A[:, kt, :nt],
                        start=(kt == 0), stop=(kt == 7)), reads=[T.wb[1], Ab], writes=[pob])
                P.op("dve", lambda h, po=po, m=m, x32=x32, nt=nt: h.scalar_tensor_tensor(
                    out=x32[:, m, :nt], in0=po[:, :nt], scalar=1.0 / ALPHA, in1=x32[:, m, :nt], op0=ALU.mult,
                    op1=ALU.add), reads=[pob, x32bs[m]], writes=[x32bs[m]])
            T.ln_now(x32, x32bs, xb, xbbs, nt, lnp_sb[:, 0, 4, :], lnp_sb[:, 1, 4, :])
            P.dma("sp", lambda h, x32=x32, t0=t0, nt=nt: h.dma_start(out=dst_v[:, :, t0:t0 + nt], in_=x32[:, :, :nt]),
                  reads=x32bs, writes=[dst_b[gi]])
        P.barrier()
        P.emit()


class S5:
    def __init__(self, P, C, NSt, nct, NB, banks):
        self.P, self.C, self.NSt, self.nct, self.NB, self.B = P, C, NSt, nct, NB, banks
        sb = P.sb
        self.wu = sb([128, 8, nct * 128], BF16, "s5wu")
        self.wub = Buf("s5wu")
        self.par = sb([128, 3, NSt], F32, "s5par")
        self.BB = [sb([128, nct, 128], BF16, f"s5BB{i}") for i in range(2)]
        self.CC = [sb([128, NSt, 32], BF16, f"s5CC{i}") for i in range(2)]
        self.dv = sb([128, nct], F32, "s5d")
        self.pb = Buf("s5params")
        self.tab = {k: sb([128, NSt, NB], F32, "s5t_" + k) for k in ("Er", "Ei", "PRr", "PRi")}
        self.tabb = Buf("s5tab")
        self.sm = {k: sb([128, NSt], F32, "s5s_" + k) for k in
                   ("dl", "rho", "th", "t1", "t2", "sn", "cs", "ckr", "cki", "kr", "ki", "nr", "ni", "den",
                    "inr", "ini", "wlr", "wli", "hr", "hi")}
        self.smb = Buf("s5small")
        self.initb = Buf("s5init")
        self.wlb = Buf("s5wl")
        self.tmp = [sb([128, NSt, max(NB // 2, 1)], F32, f"s5tmp{i}") for i in range(2)]
        self.tmpb = Buf("s5tmp")
        self.u32 = sb([128, nct, NB], F32, "s5u32")
        self.ub = sb([128, nct, NB], BF16, "s5ub")
        self.ubuf = Buf("s5u")
        self.t = Rot(P, 16, [128, NB], F32, "s5w_")
        self.bp = Rot(P, 8, [128, NB], F32, "s5bp_")
        self.wv = Rot(P, 8, [128, NB], F32, "s5wv_")
        self.xv = Rot(P, 8, [128, NB], BF16, "s5xv_")
        self.yo = Rot(P, 2, [128, NB], BF16, "s5yo_")
        self.rot = 0

    def load(self, wu_ap, par_ap, bbr_ap, bbi_ap, ccr_ap, cci_ap, d_ap, h0=None):
        P, NSt, nct, NB = self.P, self.NSt, self.nct, self.NB
        wst = Rot(P, 2, [128, nct * 128], F32, "s5wst_")
        for kt in range(8):
            st, stb = wst.next()
            P.dma("act", lambda h, st=st, kt=kt: h.dma_start(out=st[:, :], in_=wu_ap[:, kt * nct * 128:(kt + 1) * nct * 128]),
                  writes=[stb])
            P.op("pool", lambda h, st=st, kt=kt: h.tensor_copy(out=self.wu[:, kt, :], in_=st[:, :]), reads=[stb],
                 writes=[self.wub])
        P.dma("sp", lambda h: h.dma_start(out=self.par[:, :, :], in_=par_ap), writes=[self.pb])
        P.dma("sp", lambda h: h.dma_start(out=self.dv[:, :], in_=d_ap), writes=[self.pb])
        bst = P.sb([128, nct, 128], F32, "s5bst")
        cst_ = P.sb([128, NSt, 32], F32, "s5cst")
        for i, ap in enumerate((bbr_ap, bbi_ap)):
            P.dma("sp", lambda h, ap=ap: h.dma_start(out=bst[:, :, :], in_=ap), writes=[self.pb])
            P.op("pool", lambda h, i=i: h.tensor_copy(out=self.BB[i][:, :, :], in_=bst[:, :, :]), reads=[self.pb],
                 writes=[self.pb])
        for i, ap in enumerate((ccr_ap, cci_ap)):
            P.dma("sp", lambda h, ap=ap: h.dma_start(out=cst_[:, :, :], in_=ap), writes=[self.pb])
            if i == 0:
                P.op("pool", lambda h: h.tensor_copy(out=self.CC[0][:, :, :], in_=cst_[:, :, :]), reads=[self.pb],
                     writes=[self.pb])
            else:
                P.op("pool", lambda h: h.tensor_scalar(out=self.CC[1][:, :, :], in0=cst_[:, :, :], scalar1=-1.0,
                                                       scalar2=None, op0=ALU.mult), reads=[self.pb], writes=[self.pb])
        sm, smb = self.sm, self.smb
        lre, lim, lst = self.par[:, 0, :], self.par[:, 1, :], self.par[:, 2, :]
        PI = float(np.pi)

        def v(fn, reads=(), writes=()):
            P.op("dve", fn, reads=[self.pb, smb] + list(reads), writes=[smb] + list(writes))

        P.op("act", lambda h: h.activation(out=sm["dl"][:, :], in_=lst, func=AF.Exp), reads=[self.pb], writes=[smb])
        v(lambda h: h.tensor_tensor(out=sm["t1"][:, :], in0=sm["dl"][:, :], in1=lre, op=ALU.mult))
        P.op("act", lambda h: h.activation(out=sm["rho"][:, :], in_=sm["t1"][:, :], func=AF.Exp), reads=[smb], writes=[smb])
        v(lambda h: h.tensor_tensor(out=sm["th"][:, :], in0=sm["dl"][:, :], in1=lim, op=ALU.mult))
        for (dst, shift) in (("sn", 0.0), ("cs", PI / 2)):
            v(lambda h, shift=shift: h.tensor_scalar(out=sm["t1"][:, :], in0=sm["th"][:, :], scalar1=shift, scalar2=None,
                                                     op0=ALU.add))
            v(lambda h: h.tensor_copy(out=sm["t2"][:, :], in_=sm["t1"][:, :]))
            for thr in (PI, 3 * PI, 5 * PI, 7 * PI):
                v(lambda h, thr=thr: h.tensor_scalar(out=sm["nr"][:, :], in0=sm["t1"][:, :], scalar1=thr,
                                                     scalar2=-2.0 * PI, op0=ALU.is_gt, op1=ALU.mult))
                v(lambda h: h.tensor_tensor(out=sm["t2"][:, :], in0=sm["t2"][:, :], in1=sm["nr"][:, :], op=ALU.add))
            P.op("act", lambda h, dst=dst: h.activation(out=sm[dst][:, :], in_=sm["t2"][:, :], func=AF.Sin), reads=[smb],
                 writes=[smb])
        v(lambda h: h.tensor_tensor(out=sm["nr"][:, :], in0=sm["rho"][:, :], in1=sm["cs"][:, :], op=ALU.mult))
        v(lambda h: h.tensor_scalar(out=sm["nr"][:, :], in0=sm["nr"][:, :], scalar1=-1.0, scalar2=None, op0=ALU.add))
        v(lambda h: h.tensor_tensor(out=sm["ni"][:, :], in0=sm["rho"][:, :], in1=sm["sn"][:, :], op=ALU.mult))
        v(lambda h: h.tensor_tensor(out=sm["den"][:, :], in0=lre, in1=lre, op=ALU.mult))
        v(lambda h: h.tensor_tensor(out=sm["t1"][:, :], in0=lim, in1=lim, op=ALU.mult))
        v(lambda h: h.tensor_tensor(out=sm["den"][:, :], in0=sm["den"][:, :], in1=sm["t1"][:, :], op=ALU.add))
        v(lambda h: h.reciprocal(out=sm["den"][:, :], in_=sm["den"][:, :]))
        v(lambda h: h.tensor_tensor(out=sm["t1"][:, :], in0=sm["nr"][:, :], in1=lre, op=ALU.mult))
        v(lambda h: h.tensor_tensor(out=sm["t2"][:, :], in0=sm["ni"][:, :], in1=lim, op=ALU.mult))
        v(lambda h: h.tensor_tensor(out=sm["kr"][:, :], in0=sm["t1"][:, :], in1=sm["t2"][:, :], op=ALU.add))
        v(lambda h: h.tensor_tensor(out=sm["kr"][:, :], in0=sm["kr"][:, :], in1=sm["den"][:, :], op=ALU.mult))
        v(lambda h: h.tensor_tensor(out=sm["t1"][:, :], in0=sm["ni"][:, :], in1=lre, op=ALU.mult))
        v(lambda h: h.tensor_tensor(out=sm["t2"][:, :], in0=sm["nr"][:, :], in1=lim, op=ALU.mult))
        v(lambda h: h.tensor_tensor(out=sm["ki"][:, :], in0=sm["t1"][:, :], in1=sm["t2"][:, :], op=ALU.subtract))
        v(lambda h: h.tensor_tensor(out=sm["ki"][:, :], in0=sm["ki"][:, :], in1=sm["den"][:, :], op=ALU.mult))
        v(lambda h: h.tensor_copy(out=sm["ckr"][:, :], in_=sm["cs"][:, :]))
        v(lambda h: h.tensor_copy(out=sm["cki"][:, :], in_=sm["sn"][:, :]))
        Er, Ei, PRr, PRi = (self.tab[k] for k in ("Er", "Ei", "PRr", "PRi"))
        tb = self.tabb
        t0_, t1_ = self.tmp

        def bc(name, w):
            return sm[name][:, :].unsqueeze(2).to_broadcast([128, NSt, w])

        def tv(fn):
            P.op("dve", fn, reads=[smb, tb, self.tmpb], writes=[tb, self.tmpb])

        tv(lambda h: h.memset(Er[:, :, 0:1], 1.0))
        tv(lambda h: h.memset(Ei[:, :, 0:1], 0.0))
        w = 1
        while w < NB:
            tv(lambda h, w=w: h.tensor_tensor(out=t0_[:, :, 0:w], in0=Er[:, :, 0:w], in1=bc("ckr", w), op=ALU.mult))
            tv(lambda h, w=w: h.tensor_tensor(out=t1_[:, :, 0:w], in0=Ei[:, :, 0:w], in1=bc("cki", w), op=ALU.mult))
            tv(lambda h, w=w: h.tensor_tensor(out=Er[:, :, w:2 * w], in0=t0_[:, :, 0:w], in1=t1_[:, :, 0:w],
                                              op=ALU.subtract))
            tv(lambda h, w=w: h.tensor_tensor(out=t0_[:, :, 0:w], in0=Er[:, :, 0:w], in1=bc("cki", w), op=ALU.mult))
            tv(lambda h, w=w: h.tensor_tensor(out=t1_[:, :, 0:w], in0=Ei[:, :, 0:w], in1=bc("ckr", w), op=ALU.mult))
            tv(lambda h, w=w: h.tensor_tensor(out=Ei[:, :, w:2 * w], in0=t0_[:, :, 0:w], in1=t1_[:, :, 0:w], op=ALU.add))
            v(lambda h: h.tensor_tensor(out=sm["t1"][:, :], in0=sm["ckr"][:, :], in1=sm["ckr"][:, :], op=ALU.mult))
            v(lambda h: h.tensor_tensor(out=sm["t2"][:, :], in0=sm["cki"][:, :], in1=sm["cki"][:, :], op=ALU.mult))
            v(lambda h: h.tensor_tensor(out=sm["cki"][:, :], in0=sm["ckr"][:, :], in1=sm["cki"][:, :], op=ALU.mult))
            v(lambda h: h.tensor_scalar(out=sm["cki"][:, :], in0=sm["cki"][:, :], scalar1=2.0, scalar2=None, op0=ALU.mult))
            v(lambda h: h.tensor_tensor(out=sm["ckr"][:, :], in0=sm["t1"][:, :], in1=sm["t2"][:, :], op=ALU.subtract))
            w *= 2
        hw_ = max(NB // 2, 1)
        for lo in range(0, NB, hw_):
            sl = slice(lo, lo + hw_)
            tv(lambda h, sl=sl: h.tensor_tensor(out=t0_[:, :, 0:hw_], in0=Er[:, :, sl], in1=bc("kr", hw_), op=ALU.mult))
            tv(lambda h, sl=sl: h.tensor_tensor(out=t1_[:, :, 0:hw_], in0=Ei[:, :, sl], in1=bc("ki", hw_), op=ALU.mult))
            tv(lambda h, sl=sl: h.tensor_tensor(out=PRr[:, :, sl], in0=t0_[:, :, 0:hw_], in1=t1_[:, :, 0:hw_], op=ALU.add))
            tv(lambda h, sl=sl: h.tensor_tensor(out=t0_[:, :, 0:hw_], in0=Er[:, :, sl], in1=bc("ki", hw_), op=ALU.mult))
            tv(lambda h, sl=sl: h.tensor_tensor(out=t1_[:, :, 0:hw_], in0=Ei[:, :, sl], in1=bc("kr", hw_), op=ALU.mult))
            tv(lambda h, sl=sl: h.tensor_tensor(out=PRi[:, :, sl], in0=t0_[:, :, 0:hw_], in1=t1_[:, :, 0:hw_],
                                                op=ALU.subtract))
        ib = self.initb
        if h0 is None:
            P.op("dve", lambda h: h.memset(sm["inr"][:, :], 0.0), writes=[ib])
            P.op("dve", lambda h: h.memset(sm["ini"][:, :], 0.0), writes=[ib])
        else:
            P.dma("sp", lambda h: h.dma_start(out=sm["hr"][:, :], in_=h0[0]), writes=[ib])
            P.dma("sp", lambda h: h.dma_start(out=sm["hi"][:, :], in_=h0[1]), writes=[ib])
            self.cmul(sm["inr"], sm["ini"], sm["hr"], sm["hi"], sm["cs"], sm["sn"], 1.0, [ib, smb], [ib])

    def cmul(self, outr, outi, ar, ai, br, bi, sgn, reads, writes):
        P, sm, smb = self.P, self.sm, self.smb

        def v(fn):
            P.op("dve", fn, reads=list(reads) + [smb], writes=list(writes) + [smb])
        v(lambda h: h.tensor_tensor(out=sm["t1"][:, :], in0=ar[:, :], in1=br[:, :], op=ALU.mult))
        v(lambda h: h.tensor_tensor(out=sm["t2"][:, :], in0=ai[:, :], in1=bi[:, :], op=ALU.mult))
        v(lambda h: h.tensor_tensor(out=sm["nr"][:, :], in0=ar[:, :], in1=bi[:, :], op=ALU.mult))
        v(lambda h: h.tensor_tensor(out=sm["ni"][:, :], in0=ai[:, :], in1=br[:, :], op=ALU.mult))
        if sgn > 0:
            v(lambda h: h.tensor_tensor(out=outr[:, :], in0=sm["t1"][:, :], in1=sm["t2"][:, :], op=ALU.subtract))
            v(lambda h: h.tensor_tensor(out=outi[:, :], in0=sm["nr"][:, :], in1=sm["ni"][:, :], op=ALU.add))
        else:
            v(lambda h: h.tensor_tensor(out=outr[:, :], in0=sm["t1"][:, :], in1=sm["t2"][:, :], op=ALU.add))
            v(lambda h: h.tensor_tensor(out=outi[:, :], in0=sm["ni"][:, :], in1=sm["nr"][:, :], op=ALU.subtract))

    def gb(self):
        it = self.B[self.rot % 4]
        self.rot += 1
        return it

    def block(self, x, xbuf, store_y):
        P, C, NSt, nct, n = self.P, self.C, self.NSt, self.nct, self.NB
        sm, smb = self.sm, self.smb
        Er, Ei, PRr, PRi = (self.tab[k] for k in ("Er", "Ei", "PRr", "PRi"))
        for ct in range(nct):
            pu, pub = self.gb()
            for kt in range(8):
                P.op("pe", lambda h, pu=pu, kt=kt, ct=ct: h.matmul(pu[:, :n], lhsT=self.wu[:, kt, ct * 128:(ct + 1) * 128],
                                                                 rhs=x[:, kt, :n], start=(kt == 0), stop=(kt == 7)),
                     reads=[self.wub, xbuf], writes=[pub])
            P.op("act", lambda h, pu=pu, ct=ct: h.activation(out=self.u32[:, ct, :], in_=pu[:, :n], func=AF.Copy),
                 reads=[pub], writes=[self.ubuf])
            P.op("pool", lambda h, ct=ct: h.tensor_copy(out=self.ub[:, ct, :], in_=self.u32[:, ct, :]), reads=[self.ubuf],
                 writes=[self.ubuf])
        for ct in range(nct):
            py, pyb = self.B[4 + ct % 2]
            for j in range(4):
                m = ct * 4 + j
                pr, prb = self.gb()
                pi_, pib = self.gb()
                P.op("pe", lambda h, pr=pr, j=j, ct=ct: h.matmul(pr[:, :n], lhsT=self.BB[0][j * 32:(j + 1) * 32, ct, :],
                                                                rhs=self.ub[j * 32:(j + 1) * 32, ct, :], start=True,
                                                                stop=True, tile_position=(j * 32, 0)),
                     reads=[self.pb, self.ubuf], writes=[prb])
                P.op("pe", lambda h, pi_=pi_, j=j, ct=ct: h.matmul(pi_[:, :n], lhsT=self.BB[1][j * 32:(j + 1) * 32, ct, :],
                                                                  rhs=self.ub[j * 32:(j + 1) * 32, ct, :], start=True,
                                                                  stop=True, tile_position=(j * 32, 0)),
                     reads=[self.pb, self.ubuf], writes=[pib])
                ts = [self.t.next() for _ in range(4)]
                for (tt, ttb), (src, srcb), tabn in zip(ts, ((pr, prb), (pi_, pib), (pr, prb), (pi_, pib)),
                                                       (PRr, PRi, PRi, PRr)):
                    P.op("dve", lambda h, tt=tt, src=src, tabn=tabn, m=m: h.tensor_tensor(
                        out=tt[:, :], in0=src[:, :n], in1=tabn[:, m, :], op=ALU.mult), reads=[srcb, self.tabb],
                        writes=[ttb])
                (bpr, bprb), (bpi, bpib) = self.bp.next(), self.bp.next()
                P.op("pool", lambda h, bpr=bpr, a=ts[0][0], b=ts[1][0]: h.tensor_tensor(out=bpr[:, :], in0=a[:, :], in1=b[:, :],
                                                                                       op=ALU.subtract),
                     reads=[ts[0][1], ts[1][1]], writes=[bprb])
                P.op("pool", lambda h, bpi=bpi, a=ts[2][0], b=ts[3][0]: h.tensor_tensor(out=bpi[:, :], in0=a[:, :], in1=b[:, :],
                                                                                       op=ALU.add),
                     reads=[ts[2][1], ts[3][1]], writes=[bpib])
                (wr, wrb), (wi, wib) = self.wv.next(), self.wv.next()
                rho_bc = sm["rho"][:, m:m + 1].to_broadcast([128, n])
                P.op("dve", lambda h, wr=wr, bpr=bpr, m=m, rho_bc=rho_bc: h.tensor_tensor_scan(
                    out=wr[:, :], data0=rho_bc, data1=bpr[:, :], initial=sm["inr"][:, m:m + 1], op0=ALU.mult,
                    op1=ALU.add), reads=[bprb, smb, self.initb], writes=[wrb])
                P.op("dve", lambda h, wi=wi, bpi=bpi, m=m, rho_bc=rho_bc: h.tensor_tensor_scan(
                    out=wi[:, :], data0=rho_bc, data1=bpi[:, :], initial=sm["ini"][:, m:m + 1], op0=ALU.mult,
                    op1=ALU.add), reads=[bpib, smb, self.initb], writes=[wib])
                P.op("act", lambda h, wr=wr, m=m: h.activation(out=sm["wlr"][:, m:m + 1], in_=wr[:, n - 1:n], func=AF.Copy),
                     reads=[wrb], writes=[self.wlb])
                P.op("act", lambda h, wi=wi, m=m: h.activation(out=sm["wli"][:, m:m + 1], in_=wi[:, n - 1:n], func=AF.Copy),
                     reads=[wib], writes=[self.wlb])
                ts = [self.t.next() for _ in range(4)]
                for ii, ((tt, ttb), (src, srcb), tabn) in enumerate(zip(ts, ((wr, wrb), (wi, wib), (wr, wrb), (wi, wib)),
                                                                     (Er, Ei, Ei, Er))):
                    P.op("dve" if ii < 2 else "pool", lambda h, tt=tt, src=src, tabn=tabn, m=m: h.tensor_tensor(
                        out=tt[:, :], in0=src[:, :], in1=tabn[:, m, :], op=ALU.mult), reads=[srcb, self.tabb],
                        writes=[ttb])
                (xr, xrb), (xi, xib) = self.xv.next(), self.xv.next()
                P.op("pool", lambda h, xr=xr, a=ts[0][0], b=ts[1][0]: h.tensor_tensor(out=xr[:, :], in0=a[:, :], in1=b[:, :],
                                                                                     op=ALU.subtract),
                     reads=[ts[0][1], ts[1][1]], writes=[xrb])
                P.op("pool", lambda h, xi=xi, a=ts[2][0], b=ts[3][0]: h.tensor_tensor(out=xi[:, :], in0=a[:, :], in1=b[:, :],
                                                                                     op=ALU.add),
                     reads=[ts[2][1], ts[3][1]], writes=[xib])
                P.op("pe", lambda h, xr=xr, m=m, j=j: h.matmul(py[j * 32:(j + 1) * 32, :n], lhsT=self.CC[0][:, m, :],
                                                              rhs=xr[:, :], start=True, stop=False,
                                                              tile_position=(0, j * 32)),
                     reads=[self.pb, xrb], writes=[pyb])
                P.op("pe", lambda h, xi=xi, m=m, j=j: h.matmul(py[j * 32:(j + 1) * 32, :n], lhsT=self.CC[1][:, m, :],
                                                              rhs=xi[:, :], start=False, stop=True,
                                                              tile_position=(0, j * 32)),
                     reads=[self.pb, xib], writes=[pyb])
            yo, yob = self.yo.next()
            P.op("dve", lambda h, yo=yo, ct=ct: h.scalar_tensor_tensor(out=yo[:, :], in0=self.u32[:, ct, :],
                                                                      scalar=self.dv[:, ct:ct + 1], in1=py[:, :n],
                                                                      op0=ALU.mult, op1=ALU.add),
                 reads=[self.ubuf, self.pb, pyb], writes=[yob])
            store_y(ct, yo, yob)
        self.cmul(sm["inr"], sm["ini"], sm["wlr"], sm["wli"], sm["ckr"], sm["cki"], 1.0, [self.wlb, self.initb],
                  [self.initb])

    def final_state(self, out_r, out_i):
        P, sm = self.P, self.sm
        self.cmul(sm["hr"], sm["hi"], sm["inr"], sm["ini"], sm["cs"], sm["sn"], -1.0, [self.initb], [self.initb])
        P.dma("sp", lambda h: h.dma_start(out=out_r, in_=sm["hr"][:, :]), reads=[self.initb, self.smb])
        P.dma("sp", lambda h: h.dma_start(out=out_i, in_=sm["hi"][:, :]), reads=[self.initb, self.smb])


def phase_s5_prompt(P, C, io):
    from contextlib import ExitStack
    with ExitStack() as ph:
        P.stack = ph
        banks = [(P.ps([128, 512], F32, f"s5bank{i}"), Buf(f"s5bank{i}", excl=True)) for i in range(6)]
        S = S5(P, C, 8, 2, 512, banks)
        S.load(io["s5wu"], io["s5par"], io["s5bbr"], io["s5bbi"], io["s5ccr"], io["s5cci"], io["s5d"])
        xb = Rot(P, 2, [128, 8, 512], BF16, "s5xb_")
        XG = None
        for tb in range(SEQ // 512):
            rk, cb = tb // (TPC // 512), (tb % (TPC // 512)) * 512
            x, xbuf = xb.next()
            P.dma("sp", lambda h, x=x, rk=rk, cb=cb: h.dma_start(
                out=x[:, :, :], in_=io["XGp"][cb // 512][rk * 1024:(rk + 1) * 1024, :].rearrange("(k p) n -> p k n", p=128)),
                reads=[io["XGpb"][cb // 512]], writes=[xbuf])

            def store(ct, yo, yob, tb=tb):
                P.dma("sp", lambda h: h.dma_start(
                    out=io["YSinp"][tb // 4][ct * 128:(ct + 1) * 128, (tb % 4) * 512:(tb % 4 + 1) * 512],
                    in_=yo[:, :]), reads=[yob], writes=[io["YSinb"][tb]])
            S.block(x, xbuf, store)
            if tb % (PIECE // 512) == PIECE // 512 - 1:
                gather_one(P, io, "YS", tb // (PIECE // 512))
        S.final_state(io["o_s5re"], io["o_s5im"])
        P.barrier()
        P.emit()


def phase_s5_sample(P, C, io):
    from contextlib import ExitStack
    n = NS
    with ExitStack() as ph:
        P.stack = ph
        banks = [(P.ps([128, 512], F32, f"s5sbank{i}"), Buf(f"s5sbank{i}", excl=True)) for i in range(6)]
        S = S5(P, C, 32, 8, n, banks)
        S.load(io["s5wu_s"], io["s5par_s"], io["s5bbr_s"], io["s5bbi_s"], io["s5ccr_s"], io["s5cci_s"], io["s5d_s"],
               h0=(io["s5h0r"], io["s5h0i"]))
        x32 = P.sb([128, 8, n], F32, "s5sx32")
        xs = P.sb([128, 8, n], BF16, "s5sxb")
        xsb = Buf("s5sxb")
        P.dma("sp", lambda h: h.dma_start(out=x32[:, :, :], in_=io["X4v"][:, :, TPC:TPC + n]), reads=[io["X4b"][-1]],
              writes=[xsb])
        P.op("act", lambda h: h.activation(out=xs[:, :, :], in_=x32[:, :, :], func=AF.Copy), reads=[xsb], writes=[xsb])

        def store(ct, yo, yob):
            P.dma("sp", lambda h: h.dma_start(out=io["SYS"][ct * 128:(ct + 1) * 128, :], in_=yo[:, :]), reads=[yob],
                  writes=[io["SYSb"]])
        S.block(xs, xsb, store)
        S.final_state(io["o_ss5re"], io["o_ss5im"])
        P.barrier()
        P.emit()


def build(stop="all"):
    from contextlib import ExitStack
    nc = bass.Bass("TRN2", target_bir_lowering=False)
    NTOK = TPC + NS
    io = {}

    def din(name, shape):
        io[name] = nc.dram_tensor(name, list(shape), F32, kind="ExternalInput").ap()
        return io[name]

    def dout(name, shape):
        io[name] = nc.dram_tensor(name, list(shape), F32, kind="ExternalOutput").ap()
        return io[name]

    xT = din("xT", [D, NTOK])
    din("w_in", [4, 11, 128, 4096])
    din("w_out", [4, 128, 22 * 1024])
    lnp = din("lnp", [128, 2, 6, 8])
    cst = din("cst", [128, NCST])
    din("rmask", [128, 4])
    din("wfox", [128, 8 * 386]); din("bfx", [2, 1])
    din("wgdn", [128, 8 * 514]); din("gcw", [128, 3, 4]); din("gsc", [1, 2]); din("gng", [128, 1])
    din("wfox_s", [128, 8 * 1544]); din("bf_s", [8, 1]); din("pastk", [8, 64, 1024]); din("pastv", [1024, 512])
    din("pastlf", [8, 1024]); din("wgdn_s", [4, 128, 8 * 514]); din("gcw_s", [4, 128, 3, 4]); din("gsc_s", [4, 1, 2])
    din("sconv", [4, 3, 128, 3]); din("sstate", [4, 128, 128])
    din("even_w_out", [D, D]); din("glu_w", [D, D]); din("odd_w_out", [D, D]); din("glu_b", [128, 8])
    din("s5wu", [128, 8 * 256]); din("s5par", [128, 3, 8]); din("s5bbr", [128, 2, 128]); din("s5bbi", [128, 2, 128])
    din("s5ccr", [128, 8, 32]); din("s5cci", [128, 8, 32]); din("s5d", [128, 2])
    din("s5wu_s", [128, 8 * 1024]); din("s5par_s", [128, 3, 32]); din("s5bbr_s", [128, 8, 128])
    din("s5bbi_s", [128, 8, 128]); din("s5ccr_s", [128, 32, 32]); din("s5cci_s", [128, 32, 32]); din("s5d_s", [128, 8])
    din("s5h0r", [128, 32]); din("s5h0i", [128, 32])
    dout("o_y", [D, TPC]); dout("o_ys", [D, NS])
    dout("o_logf", [2, SEQ]); dout("o_foxk", [128, SEQ]); dout("o_foxv", [SEQ, 128])
    dout("o_gstate", [128, 128]); dout("o_gconv", [3, 128, 3])
    dout("o_slogf", [8, NS]); dout("o_sfoxk", [512, NS]); dout("o_sfoxv", [NS, 512])
    dout("o_sgstate", [4, 128, 128]); dout("o_sgconv", [4, 3, 128, 3])
    dout("o_s5re", [128, 8]); dout("o_s5im", [128, 8]); dout("o_ss5re", [128, 32]); dout("o_ss5im", [128, 32])
    dbg = dout("dbg", [256, SEQ]) if (stop != "all" and not DEBUG.get("nodump")) else None
    XA = nc.dram_tensor("XA", [D, NTOK], F32).ap()
    XB = nc.dram_tensor("XB", [D, NTOK], F32).ap()
    io["WBF"] = [nc.dram_tensor(f"WBF{j}", [128, 4096], BF16).ap() for j in range(11)]
    io["WBFb"] = [Buf(f"WBF{j}") for j in range(11)]
    ngp = TPC // 512
    io["XGinp"] = [nc.dram_tensor(f"XGin{g}", [D, 512], BF16).ap() for g in range(ngp)]
    io["XGp"] = [nc.dram_tensor(f"XG{g}", [4 * D, 512], BF16).ap() for g in range(ngp)]
    io["XGinpb"] = [[Buf(f"XGin{g}")] for g in range(ngp)]
    io["XGpb"] = [Buf(f"XG{g}") for g in range(ngp)]
    npc = SEQ // PIECE
    for nm in ("MX", "YS"):
        io[nm + "inp"] = [nc.dram_tensor(f"{nm}in{j}", [256, PIECE], BF16).ap() for j in range(npc)]
        io[nm + "p"] = [nc.dram_tensor(f"{nm}{j}", [4 * 256, PIECE], BF16).ap() for j in range(npc)]
        io[nm + "inb"] = [Buf(f"{nm}in{i}") for i in range(SEQ // 512)]
        io[nm + "inpb"] = [io[nm + "inb"][j * (PIECE // 512):(j + 1) * (PIECE // 512)] for j in range(npc)]
        io[nm + "pb"] = [Buf(f"{nm}{j}") for j in range(npc)]
    io["SMX"] = nc.dram_tensor("SMX", [1024, NS], BF16).ap()
    io["SMXb"] = Buf("SMX")
    io["SYS"] = nc.dram_tensor("SYS", [1024, NS], BF16).ap()
    io["SYSb"] = Buf("SYS")
    ng = len(groups_())

    with ExitStack() as gstack:
        P = Prog(nc, gstack)
        C = Consts(P, nc, cst)
        lnp_sb = P.sb([128, 2, 6, 8], F32, "lnp_sb")
        P.dma("sp", lambda h: h.dma_start(out=lnp_sb[:, :, :, :], in_=lnp[:, :, :, :]), writes=[C.b])
        consts = {"constb": C.b, "ones_f32": C.f32[:, C_ONES:C_ONES + 128]}
        xTv = xT.rearrange("(k p) n -> p k n", p=128)
        XAv = XA.rearrange("(k p) n -> p k n", p=128)
        XBv = XB.rearrange("(k p) n -> p k n", p=128)
        XAb = [Buf(f"XA_{i}") for i in range(ng)]
        XBb = [Buf(f"XB_{i}") for i in range(ng)]
        io["X1v"], io["X1b"] = XAv, XAb
        io["X4v"], io["X4b"] = XBv, XBb
        xg = True

        def done():
            P.stack = gstack
            P.finish()
            P.emit()
            return nc

        def dump(src_ap, rows0, deps):
            if DEBUG.get("nodump"):
                return
            with ExitStack() as ph:
                P.stack = ph
                t16 = P.sb([128, 2048], BF16, "d16")
                t32 = P.sb([128, 2048], F32, "d32")
                tb_ = Buf("d")
                for i in range(SEQ // 2048):
                    P.dma("sp", lambda h, i=i: h.dma_start(out=t16[:, :], in_=src_ap[rows0:rows0 + 128, i * 2048:(i + 1) * 2048]),
                          reads=deps, writes=[tb_])
                    P.op("dve", lambda h: h.tensor_copy(out=t32[:, :], in_=t16[:, :]), reads=[tb_], writes=[tb_])
                    P.dma("sp", lambda h, i=i: h.dma_start(out=dbg[0:128, i * 2048:(i + 1) * 2048], in_=t32[:, :]),
                          reads=[tb_], writes=[tb_])
                P.barrier()
                P.emit()

        ffn_pass(P, C, io, consts, lnp_sb, 0, 0, xTv, None, XAv, XAb, xg=xg)
        if stop == "p0":
            return done()
        if not DEBUG.get("nofox"):
            phase_fox_prompt(P, C, io)
        if stop == "h0f":
            dump(io["MXinp"][0], 0, io["MXinb"])
            return done()
        if not DEBUG.get("nogdn"):
            phase_gdn_prompt(P, C, io)
        if stop == "h0g":
            dump(io["MXinp"][0], 128, io["MXinb"])
            return done()
        phase_sample_even(P, C, io)
        if stop == "h0s" and DEBUG.get("nodump"):
            return done()
        if stop == "h0s":
            with ExitStack() as ph:
                P.stack = ph
                t16 = P.sb([128, 8, NS], BF16, "d16")
                t32 = P.sb([128, 8, NS], F32, "d32")
                tb_ = Buf("d")
                P.dma("sp", lambda h: h.dma_start(out=t16[:, :, :], in_=io["SMX"].rearrange("(k p) n -> p k n", p=128)),
                      reads=[io["SMXb"]], writes=[tb_])
                P.op("dve", lambda h: h.tensor_copy(out=t32[:, :, :], in_=t16[:, :, :]), reads=[tb_], writes=[tb_])
                P.dma("sp", lambda h: h.dma_start(out=dbg[:, 0:8 * NS].rearrange("p (k n) -> p k n", n=NS)[0:128],
                                                  in_=t32[:, :, :]), reads=[tb_], writes=[tb_])
                P.barrier()
                P.emit()
            return done()
        proj_pass_even(P, C, io, consts, lnp_sb, XAv, XAb, XBv, XBb)
        if stop == "p1a":
            return done()
        ffn_pass(P, C, io, consts, lnp_sb, 1, 2, XBv, XBb, XAv, XAb)
        ffn_pass(P, C, io, consts, lnp_sb, 2, 3, XAv, XAb, XBv, XBb, xg=xg)
        phase_s5_prompt(P, C, io)
        if stop == "h1":
            return done()
        phase_s5_sample(P, C, io)
        if stop == "h1s":
            return done()
        proj_pass_odd(P, C, io, consts, lnp_sb, XBv, XBb, XAv, XAb)
        if stop == "p3a":
            return done()
        ffn_pass(P, C, io, consts, lnp_sb, 3, 5, XAv, XAb, None, None, out_final=(io["o_y"], io["o_ys"]))
        return done()


def _s5_layouts(inputs, groups):
    ng = len(groups)
    nst, nct = ng // 2, ng // 8
    lre, lim, lst = inputs["s5_lam_re"][0], inputs["s5_lam_im"][0], inputs["s5_log_step"][0]
    bre, bim = inputs["s5_b_re"][0], inputs["s5_b_im"][0]
    cre, cim = inputs["s5_c_re"][0], inputs["s5_c_im"][0]
    par = np.zeros((128, 3, nst), np.float32)
    bbr = np.zeros((128, nct, 128), np.float32); bbi = np.zeros((128, nct, 128), np.float32)
    ccr = np.zeros((128, nst, 32), np.float32); cci = np.zeros((128, nst, 32), np.float32)
    for m in range(nst):
        for hh in range(2):
            g = groups[2 * m + hh]
            sl = slice(hh * 64, (hh + 1) * 64)
            par[sl, 0, m] = lre[g]; par[sl, 1, m] = lim[g]; par[sl, 2, m] = lst[g]
            prow = (m % 4) * 32 + hh * 16
            bbr[prow:prow + 16, m // 4, sl] = bre[g].T
            bbi[prow:prow + 16, m // 4, sl] = bim[g].T
            ccr[sl, m, hh * 16:(hh + 1) * 16] = cre[g].T
            cci[sl, m, hh * 16:(hh + 1) * 16] = cim[g].T
    return par, bbr, bbi, ccr, cci


def _state_tiles(h):
    ng = h.shape[0]
    return np.ascontiguousarray(h.reshape(ng // 2, 128).T)


def host_prep(inputs, c):
    b, r = c // 4, c % 4
    xp = inputs["x_prompt"][b, r * TPC:(r + 1) * TPC, :]
    xs = inputs["x_sample"][c]
    xT = np.ascontiguousarray(np.concatenate([xp, xs], 0).T)
    ew = inputs["even_w_in"][0]
    cols = np.concatenate([np.arange(r * 128, (r + 1) * 128), 512 + np.arange(r * 128, (r + 1) * 128),
                           1024 + np.arange(r * 128, (r + 1) * 128), 1536 + np.arange(2 * r, 2 * r + 2)])
    wfox = np.ascontiguousarray(ew[:, cols].reshape(8, 128, 386).transpose(1, 0, 2)).reshape(128, 8 * 386)
    bfx = np.ascontiguousarray(inputs["fox_b_f"][0, 2 * r:2 * r + 2].reshape(2, 1))
    gc = np.concatenate([1544 + np.arange(r * 128, (r + 1) * 128), 1544 + 512 + np.arange(r * 128, (r + 1) * 128),
                         1544 + 1024 + np.arange(r * 128, (r + 1) * 128), 3088 + np.arange(r * 128, (r + 1) * 128),
                         [3080 + r], [3084 + r]])
    wgdn = np.ascontiguousarray(ew[:, gc].reshape(8, 128, 514).transpose(1, 0, 2)).reshape(128, 8 * 514)
    cwf = inputs["gdn_conv_w"][0]
    gcw = np.ascontiguousarray(np.stack([cwf[:, s_ * 512 + r * 128:s_ * 512 + (r + 1) * 128].T for s_ in range(3)], 1))
    gsc = np.array([[inputs["gdn_a_log"][0, r], inputs["gdn_dt_bias"][0, r]]], np.float32)
    pastk = np.ascontiguousarray(inputs["cache_fox_k"][0, c].transpose(1, 2, 0))
    pastv = np.ascontiguousarray(inputs["cache_fox_v"][0, c].reshape(1024, 512))
    pastlf = np.ascontiguousarray(inputs["cache_fox_logf"][0, c].T)
    sconv = np.ascontiguousarray(inputs["state_gdn_conv"][0, c].reshape(3, 3, 4, 128).transpose(2, 1, 3, 0))
    sstate = np.ascontiguousarray(inputs["state_gdn"][0, c])
    rmask = np.zeros((128, 4), np.float32)
    rmask[:, r] = 1.0
    ow = inputs["odd_w_in"][0]
    s5wu = np.ascontiguousarray(ow[:, r * 256:(r + 1) * 256].reshape(8, 128, 256).transpose(1, 0, 2)).reshape(128, 2048)
    par, bbr, bbi, ccr, cci = _s5_layouts(inputs, list(range(16 * r, 16 * r + 16)))
    s5d = np.ascontiguousarray(inputs["s5_d"][0, r * 256:(r + 1) * 256].reshape(2, 128).T)
    return {"xT": xT, "wfox": wfox, "bfx": bfx, "wgdn": wgdn, "gcw": gcw, "gsc": gsc,
            "pastk": pastk, "pastv": pastv, "pastlf": pastlf, "sconv": sconv, "sstate": sstate, "rmask": rmask,
            "s5wu": s5wu, "s5par": par, "s5bbr": bbr, "s5bbi": bbi, "s5ccr": ccr, "s5cci": cci, "s5d": s5d,
            "s5h0r": _state_tiles(inputs["state_s5_re"][0, c]), "s5h0i": _state_tiles(inputs["state_s5_im"][0, c])}


def shared_prep(inputs):
    w_in = inputs["ffn_w_in"].reshape(4, 8, 128, 2, 11, 256)
    w_in = np.ascontiguousarray(w_in.transpose(0, 4, 2, 1, 3, 5)).reshape(4, 11, 128, 4096)
    w_out = inputs["ffn_w_out"].reshape(4, 22, 128, 1024)
    w_out = np.ascontiguousarray(w_out.transpose(0, 2, 1, 3)).reshape(4, 128, 22 * 1024)
    g = inputs["ln_g"].reshape(6, 8, 128)
    bb = inputs["ln_b"].reshape(6, 8, 128)
    lnp = np.ascontiguousarray(np.stack([g, bb], 0).transpose(3, 0, 1, 2))
    ew = inputs["even_w_in"][0]
    wfox_s = np.ascontiguousarray(ew[:, :1544].reshape(8, 128, 1544).transpose(1, 0, 2)).reshape(128, 8 * 1544)
    bf_s = np.ascontiguousarray(inputs["fox_b_f"][0].reshape(8, 1))
    cwf = inputs["gdn_conv_w"][0]
    wg_l, cw_l, sc_l = [], [], []
    for r in range(4):
        gc = np.concatenate([1544 + np.arange(r * 128, (r + 1) * 128), 1544 + 512 + np.arange(r * 128, (r + 1) * 128),
                             1544 + 1024 + np.arange(r * 128, (r + 1) * 128), 3088 + np.arange(r * 128, (r + 1) * 128),
                             [3080 + r], [3084 + r]])
        wg_l.append(np.ascontiguousarray(ew[:, gc].reshape(8, 128, 514).transpose(1, 0, 2)).reshape(128, 8 * 514))
        cw_l.append(np.stack([cwf[:, s_ * 512 + r * 128:s_ * 512 + (r + 1) * 128].T for s_ in range(3)], 1))
        sc_l.append(np.array([[inputs["gdn_a_log"][0, r], inputs["gdn_dt_bias"][0, r]]], np.float32))
    ow = inputs["odd_w_in"][0]
    s5wu_s = np.ascontiguousarray(ow.reshape(8, 128, 1024).transpose(1, 0, 2)).reshape(128, 8192)
    par, bbr, bbi, ccr, cci = _s5_layouts(inputs, list(range(64)))
    return {"w_in": w_in, "w_out": w_out, "lnp": lnp, "cst": make_cst(), "wfox_s": wfox_s, "bf_s": bf_s,
            "wgdn_s": np.ascontiguousarray(np.stack(wg_l)), "gcw_s": np.ascontiguousarray(np.stack(cw_l)),
            "gsc_s": np.ascontiguousarray(np.stack(sc_l)),
            "gng": np.ascontiguousarray(inputs["gdn_norm_g"][0].reshape(128, 1)),
            "even_w_out": np.ascontiguousarray(inputs["even_w_out"][0]), "glu_w": np.ascontiguousarray(inputs["s5_glu_w"][0]),
            "odd_w_out": np.ascontiguousarray(inputs["odd_w_out"][0]),
            "glu_b": np.ascontiguousarray(inputs["s5_glu_b"][0].reshape(8, 128).T),
            "s5wu_s": s5wu_s, "s5par_s": par, "s5bbr_s": bbr, "s5bbi_s": bbi, "s5ccr_s": ccr, "s5cci_s": cci,
            "s5d_s": np.ascontiguousarray(inputs["s5_d"][0].reshape(8, 128).T)}


def assemble(results):
    B, T = 2, SEQ
    f = np.float32
    y_p = np.zeros((B, T, D), f); y_s = np.zeros((8, NS, D), f)
    fk = np.zeros((1, B, T, 8, 64), f); fv = np.zeros((1, B, T, 8, 64), f); fl = np.zeros((1, B, T, 8), f)
    gs = np.zeros((1, B, 4, 128, 128), f); gcv = np.zeros((1, B, 3, 1536), f)
    sre = np.zeros((1, B, 64, 64), f); sim_ = np.zeros((1, B, 64, 64), f)
    sk = np.zeros((1, 8, NS, 8, 64), f); sv = np.zeros((1, 8, NS, 8, 64), f); sl = np.zeros((1, 8, NS, 8), f)
    sgs = np.zeros((1, 8, 4, 128, 128), f); sgc = np.zeros((1, 8, 3, 1536), f)
    ssre = np.zeros((1, 8, 64, 64), f); ssim = np.zeros((1, 8, 64, 64), f)
    for c in range(NCORE):
        d = results[c]
        b, r = c // 4, c % 4
        y_p[b, r * TPC:(r + 1) * TPC] = d["o_y"].T
        y_s[c] = d["o_ys"].T
        for hl in range(2):
            fk[0, b, :, 2 * r + hl, :] = d["o_foxk"][hl * 64:(hl + 1) * 64].T
            fv[0, b, :, 2 * r + hl, :] = d["o_foxv"][:, hl * 64:(hl + 1) * 64]
            fl[0, b, :, 2 * r + hl] = d["o_logf"][hl]
        gs[0, b, r] = d["o_gstate"]
        for s_ in range(3):
            gcv[0, b, :, s_ * 512 + r * 128:s_ * 512 + (r + 1) * 128] = d["o_gconv"][s_].T
        sre[0, b, 16 * r:16 * r + 16] = d["o_s5re"].T.reshape(16, 64)
        sim_[0, b, 16 * r:16 * r + 16] = d["o_s5im"].T.reshape(16, 64)
        sk[0, c] = d["o_sfoxk"].T.reshape(NS, 8, 64)
        sv[0, c] = d["o_sfoxv"].reshape(NS, 8, 64)
        sl[0, c] = d["o_slogf"].T
        sgs[0, c] = d["o_sgstate"]
        sgc[0, c] = d["o_sgconv"].transpose(3, 1, 0, 2).reshape(3, 1536)
        ssre[0, c] = d["o_ss5re"].T.reshape(64, 64)
        ssim[0, c] = d["o_ss5im"].T.reshape(64, 64)
    return (y_p, y_s, fk, fv, fl, gs, gcv, sre, sim_, sk, sv, sl, sgs, sgc, ssre, ssim)


def kernel(**inputs):
    inputs = {k: np.asarray(v) for k, v in inputs.items()}
    nc = build()
    sh = shared_prep(inputs)
    in_maps = []
    for c in range(NCORE):
        m = dict(sh)
        m.update(host_prep(inputs, c))
        in_maps.append(m)
    res = run_bass_kernel_spmd(nc, in_maps, core_ids=list(range(NCORE)))
    return assemble(res.results)
```

```python
import numpy as np
import ml_dtypes
import concourse.bass as bass
import concourse.mybir as mybir
from concourse.bass_utils import run_bass_kernel_spmd

F32 = mybir.dt.float32
BF16 = mybir.dt.bfloat16
AF = mybir.ActivationFunctionType
ALU = mybir.AluOpType
AX = mybir.AxisListType

D = 1024
SEQ = 16384
NCORE = 8
TPC = 4096
NS = 32
DFF = 2816
ALPHA = 4.0 ** 0.25
LN_EPS = 1e-5
NORM_EPS = 1e-6
DEBUG = {}


class Buf:
    __slots__ = ("name", "w", "r", "excl")

    def __init__(self, name, excl=False):
        self.name = name
        self.w = {}
        self.r = {}
        self.excl = excl


class Eng:
    def __init__(self, name, sem, step):
        self.name = name
        self.sem = sem
        self.step = step
        self.count = 0
        self.waited = {}
        self.prog = []


class Prog:
    def __init__(self, nc, stack):
        self.nc = nc
        self.stack = stack
        self.eng = {}
        for n in ("pe", "act", "dve", "pool", "sp"):
            self.eng[n] = Eng(n, stack.enter_context(nc.semaphore("s_" + n)), 1)
        self.nslot = 6
        self.slots = {}
        self.slot_rr = {}
        for q in ("sp", "pool", "act"):
            self.slots[q] = [Eng(f"dma_{q}{i}", stack.enter_context(nc.semaphore(f"d_{q}{i}")), 16)
                             for i in range(self.nslot)]
            self.slot_rr[q] = 0
        self.ntile = 0
        self.cc = Eng("cc", stack.enter_context(nc.semaphore("s_cc")), 1)

    def sb(self, shape, dt, name=None):
        self.ntile += 1
        name = f"{name or 't'}_s{self.ntile}"
        t = self.stack.enter_context(self.nc.sbuf_tensor(name, list(shape), dt))
        return t

    def ps(self, shape, dt, name=None):
        self.ntile += 1
        name = f"{name or 'p'}_p{self.ntile}"
        t = self.stack.enter_context(self.nc.psum_tensor(name, list(shape), dt))
        return t

    def _wait(self, e, deps):
        for src, val in deps.items():
            if src is e and e.name == "pe":
                continue
            if e.waited.get(src, 0) < val:
                e.prog.append(("w", src.sem, val))
                e.waited[src] = val

    @staticmethod
    def _deps(reads, writes):
        deps = {}
        for b in reads:
            for s, v in b.w.items():
                if deps.get(s, 0) < v:
                    deps[s] = v
            if b.excl:
                for s, v in b.r.items():
                    if deps.get(s, 0) < v:
                        deps[s] = v
        for b in writes:
            for s, v in b.w.items():
                if deps.get(s, 0) < v:
                    deps[s] = v
            for s, v in b.r.items():
                if deps.get(s, 0) < v:
                    deps[s] = v
        return deps

    def op(self, en, fn, reads=(), writes=()):
        e = self.eng[en]
        self._wait(e, self._deps(reads, writes))
        e.count += 1
        e.prog.append(("o", fn, e.sem, 1))
        for b in reads:
            if b.r.get(e, 0) < e.count:
                b.r[e] = e.count
        for b in writes:
            b.w = {e: e.count}
            b.r = {}

    def dma(self, q, fn, reads=(), writes=()):
        e = self.eng[q]
        sl = self.slots[q][self.slot_rr[q] % self.nslot]
        self.slot_rr[q] += 1
        deps = self._deps(reads, writes)
        if sl.count:
            deps[sl] = max(deps.get(sl, 0), sl.count)
        self._wait(e, deps)
        sl.count += 16
        e.prog.append(("o", fn, sl.sem, 16))
        for b in reads:
            b.r[sl] = sl.count
        for b in writes:
            b.w = {sl: sl.count}
            b.r = {}

    def wait_all(self, en, bufs):
        e = self.eng[en]
        deps = {}
        for b in bufs:
            for s, v in list(b.w.items()) + list(b.r.items()):
                if deps.get(s, 0) < v:
                    deps[s] = v
        self._wait(e, deps)

    def finish(self):
        e = self.eng["sp"]
        deps = {}
        for x in self.eng.values():
            if x.count:
                deps[x] = x.count
        for q in self.slots:
            for sl in self.slots[q]:
                if sl.count:
                    deps[sl] = sl.count
        self._wait(e, deps)

    def barrier(self):
        deps = {}
        for x in list(self.eng.values()) + [self.cc]:
            if x.count:
                deps[x] = x.count
        for q in self.slots:
            for sl in self.slots[q]:
                if sl.count:
                    deps[sl] = sl.count
        for e in self.eng.values():
            self._wait(e, dict(deps))

    def coll(self, fn, reads=(), writes=()):
        e = self.eng["pool"]
        self._wait(e, self._deps(reads, writes))
        self.cc.count += 1
        e.prog.append(("o", fn, self.cc.sem, 1))
        for b in reads:
            b.r[self.cc] = self.cc.count
        for b in writes:
            b.w = {self.cc: self.cc.count}
            b.r = {}

    def emit(self):
        nc = self.nc
        progs = {k: Eng(k, None, 1) for k in self.eng}
        for k in self.eng:
            progs[k].prog = self.eng[k].prog
            self.eng[k].prog = []

        def run(h, prog):
            for it in prog:
                if it[0] == "w":
                    h.wait_ge(it[1], it[2])
                else:
                    ins = it[1](h)
                    ins.then_inc(it[2], it[3])

        with nc.Block() as block:
            @block.sync
            def _(h):
                run(h, progs["sp"].prog)

            @block.tensor
            def _(h):
                run(h, progs["pe"].prog)

            @block.scalar
            def _(h):
                run(h, progs["act"].prog)

            @block.vector
            def _(h):
                run(h, progs["dve"].prog)

            @block.gpsimd
            def _(h):
                run(h, progs["pool"].prog)


class Rot:
    def __init__(self, P, n, shape, dt, name, psum=False):
        self.items = []
        for i in range(n):
            t = P.ps(shape, dt, f"{name}{i}") if psum else P.sb(shape, dt, f"{name}{i}")
            self.items.append((t, Buf(f"{name}{i}", excl=psum)))
        self.i = 0

    def next(self):
        it = self.items[self.i % len(self.items)]
        self.i += 1
        return it


class LNBase:
    def _ln_alloc(self, P, consts, NT=512):
        self.P = P
        self.c = consts
        self.sq = Rot(P, 2, [128, NT], F32, "sq_")
        self.ps1 = P.ps([128, NT], F32, "ps1")
        self.ps1b = Buf("ps1", excl=True)
        self.ps2 = P.ps([128, NT], F32, "ps2")
        self.ps2b = Buf("ps2", excl=True)
        self.mean = P.sb([128, NT], F32, "mean")
        self.meanb = Buf("mean")
        self.rstd = P.sb([128, NT], F32, "rstd")
        self.rstdb = Buf("rstd")
        self.msq = P.sb([128, NT], F32, "msq")
        self.msqb = Buf("msq")


class TPhase(LNBase):
    def __init__(self, P, consts):
        NT = 512
        self.NT = NT
        self._ln_alloc(P, consts, NT)
        self.x32 = Rot(P, 2, [128, 8, NT], F32, "x32_")
        self.xb = Rot(P, 2, [128, 8, NT], BF16, "xb_")
        self.h = P.sb([128, 22, NT], BF16, "h")
        self.hbuf = [Buf(f"h{j}") for j in range(22)]
        self.win = Rot(P, 4, [128, 8, 512], BF16, "win_")
        self.wstage = Rot(P, 2, [128, 8, 512], F32, "wst_")
        self.wout = P.sb([128, 22, 1024], BF16, "wout")
        self.woutb = Buf("wout")
        self.tmp = Rot(P, 4, [128, NT], F32, "silu_")
        self.pa = Rot(P, 2, [128, NT], F32, "pa_", psum=True)
        self.pb = Rot(P, 2, [128, NT], F32, "pb_", psum=True)
        self.po = Rot(P, 2, [128, NT], F32, "po_", psum=True)

    def load_wout(self, w_out_ap):
        P = self.P
        for i in range(11):
            ws, wsb = self.wstage.next()
            P.dma("act", lambda h, i=i, ws=ws: h.dma_start(
                out=ws[:, 0:4, :], in_=w_out_ap[:, i * 2048:(i + 1) * 2048].rearrange("p (j c) -> p j c", c=512)),
                writes=[wsb])
            P.op("pool", lambda h, i=i, ws=ws: h.tensor_copy(
                out=self.wout[:, 2 * i:2 * i + 2, :].rearrange("p j (a c) -> p (j a) c", c=512), in_=ws[:, 0:4, :]),
                reads=[wsb], writes=[self.woutb])

    def layer_norm(self, r32, r32b, xob, xobb, nt, g_ap, b_ap, eps):
        P = self.P
        c = self.c
        ones = c["ones_f32"]
        for m in range(8):
            sq, sqb = self.sq.next()
            P.op("act", lambda h, sq=sq, m=m: h.activation(out=sq[:, :nt], in_=r32[:, m, :nt], func=AF.Square),
                 reads=[r32b], writes=[sqb])
            P.op("pe", lambda h, m=m: h.matmul(self.ps1[:, :nt], lhsT=ones[:, :], rhs=r32[:, m, :nt],
                                               start=(m == 0), stop=(m == 7)),
                 reads=[r32b, c["constb"]], writes=[self.ps1b])
            P.op("pe", lambda h, m=m, sq=sq: h.matmul(self.ps2[:, :nt], lhsT=ones[:, :], rhs=sq[:, :nt],
                                                      start=(m == 0), stop=(m == 7)),
                 reads=[sqb, c["constb"]], writes=[self.ps2b])
        P.op("dve", lambda h: h.tensor_scalar(out=self.mean[:, :nt], in0=self.ps1[:, :nt], scalar1=1.0 / D,
                                              scalar2=None, op0=ALU.mult),
             reads=[self.ps1b], writes=[self.meanb])
        P.op("dve", lambda h: h.tensor_tensor(out=self.msq[:, :nt], in0=self.mean[:, :nt], in1=self.mean[:, :nt],
                                              op=ALU.mult),
             reads=[self.meanb], writes=[self.msqb])
        P.op("dve", lambda h: h.scalar_tensor_tensor(out=self.rstd[:, :nt], in0=self.ps2[:, :nt], scalar=1.0 / D,
                                                     in1=self.msq[:, :nt], op0=ALU.mult, op1=ALU.subtract),
             reads=[self.ps2b, self.msqb], writes=[self.rstdb])
        P.op("dve", lambda h: h.tensor_scalar(out=self.rstd[:, :nt], in0=self.rstd[:, :nt], scalar1=eps,
                                              scalar2=None, op0=ALU.add),
             reads=[self.rstdb], writes=[self.rstdb])
        P.op("act", lambda h: h.activation(out=self.rstd[:, :nt], in_=self.rstd[:, :nt], func=AF.Sqrt),
             reads=[self.rstdb], writes=[self.rstdb])
        P.op("dve", lambda h: h.reciprocal(out=self.rstd[:, :nt], in_=self.rstd[:, :nt]),
             reads=[self.rstdb], writes=[self.rstdb])
        xo32, xo32b = r32, r32b
        for m in range(8):
            P.op("pool", lambda h, m=m: h.tensor_tensor(out=r32[:, m, :nt], in0=r32[:, m, :nt], in1=self.mean[:, :nt],
                                                        op=ALU.subtract),
                 reads=[r32b, self.meanb], writes=[r32b])
            P.op("dve", lambda h, m=m: h.tensor_tensor(out=r32[:, m, :nt], in0=r32[:, m, :nt], in1=self.rstd[:, :nt],
                                                       op=ALU.mult),
                 reads=[r32b, self.rstdb], writes=[r32b])
            P.op("act", lambda h, m=m: h.activation(out=xo32[:, m, :nt], in_=r32[:, m, :nt], func=AF.Identity,
                                                    scale=g_ap[:, m:m + 1], bias=b_ap[:, m:m + 1]),
                 reads=[r32b, c["constb"]], writes=[xo32b])
            P.op("pool", lambda h, m=m: h.tensor_copy(out=xob[:, m, :nt], in_=xo32[:, m, :nt]),
                 reads=[xo32b], writes=[xobb])
        return xo32, xo32b, xob, xobb

    def ffn_ln(self, x32, x32b, xb, xbb, nt, w_in_ap, g_ap, b_ap):
        P = self.P
        jq = []
        for jp in range(11):
            w, wb = self.win.next()
            ws, wsb = self.wstage.next()
            P.dma("act", lambda h, ws=ws, jp=jp: h.dma_start(
                out=ws[:, :, :], in_=w_in_ap[jp].rearrange("p (k c) -> p k c", c=512)), writes=[wsb])
            P.op("pool", lambda h, w=w, ws=ws: h.tensor_copy(out=w[:, :, :], in_=ws[:, :, :]), reads=[wsb],
                 writes=[wb])
            for jj in range(2):
                j = jp * 2 + jj
                pa, pab = self.pa.next()
                pb, pbb = self.pb.next()
                for kt in range(8):
                    P.op("pe", lambda h, pa=pa, w=w, kt=kt, jj=jj: h.matmul(
                        pa[:, :nt], lhsT=w[:, kt, jj * 128:(jj + 1) * 128], rhs=xb[:, kt, :nt],
                        start=(kt == 0), stop=(kt == 7)), reads=[wb, xbb], writes=[pab])
                for kt in range(8):
                    P.op("pe", lambda h, pb=pb, w=w, kt=kt, jj=jj: h.matmul(
                        pb[:, :nt], lhsT=w[:, kt, 256 + jj * 128:256 + (jj + 1) * 128], rhs=xb[:, kt, :nt],
                        start=(kt == 0), stop=(kt == 7)), reads=[wb, xbb], writes=[pbb])
                t, tb = self.tmp.next()
                P.op("act", lambda h, t=t, pa=pa: h.activation(out=t[:, :nt], in_=pa[:, :nt], func=AF.Silu),
                     reads=[pab], writes=[tb])
                P.op("dve", lambda h, t=t, pb=pb, j=j: h.tensor_tensor(out=self.h[:, j, :nt], in0=t[:, :nt],
                                                                     in1=pb[:, :nt], op=ALU.mult),
                     reads=[tb, pbb], writes=[self.hbuf[j]])
        for m in range(8):
            po, pob = self.po.next()
            for j in range(22):
                P.op("pe", lambda h, po=po, j=j, m=m: h.matmul(
                    po[:, :nt], lhsT=self.wout[:, j, m * 128:(m + 1) * 128], rhs=self.h[:, j, :nt],
                    start=(j == 0), stop=(j == 21)), reads=[self.woutb, self.hbuf[j]], writes=[pob])
            P.op("dve", lambda h, po=po, m=m: h.scalar_tensor_tensor(
                out=x32[:, m, :nt], in0=po[:, :nt], scalar=0.5 / ALPHA, in1=x32[:, m, :nt],
                op0=ALU.mult, op1=ALU.add), reads=[pob, x32b], writes=[x32b])
        return self.layer_norm(x32, x32b, xb, xbb, nt, g_ap, b_ap, LN_EPS / (ALPHA * ALPHA))


LNBase.layer_norm = TPhase.layer_norm

def _tp_ffn_in(self, xb, xbbs, nt, w_in_ap, pending):
    P = self.P
    for jp in range(11):
        w, wb = w_in_ap()
        for jj in range(2):
            j = jp * 2 + jj
            pa, pab = self.pa.next()
            pb, pbb = self.pb.next()
            for kt in range(8):
                P.op("pe", lambda h, pa=pa, w=w, kt=kt, jj=jj: h.matmul(
                    pa[:, :nt], lhsT=w[:, kt, jj * 128:(jj + 1) * 128], rhs=xb[:, kt, :nt],
                    start=(kt == 0), stop=(kt == 7)), reads=[wb, xbbs[kt]], writes=[pab])
            for kt in range(8):
                P.op("pe", lambda h, pb=pb, w=w, kt=kt, jj=jj: h.matmul(
                    pb[:, :nt], lhsT=w[:, kt, 256 + jj * 128:256 + (jj + 1) * 128], rhs=xb[:, kt, :nt],
                    start=(kt == 0), stop=(kt == 7)), reads=[wb, xbbs[kt]], writes=[pbb])
            t, tb = self.tmp.next()
            P.op("act", lambda h, t=t, pa=pa: h.activation(out=t[:, :nt], in_=pa[:, :nt], func=AF.Silu),
                 reads=[pab], writes=[tb])
            P.op("dve", lambda h, t=t, pb=pb, j=j: h.tensor_tensor(out=self.h[:, j, :nt], in0=t[:, :nt],
                                                                 in1=pb[:, :nt], op=ALU.mult),
                 reads=[tb, pbb], writes=[self.hbuf[j]])
        if pending:
            pending.pop(0)()
    while pending:
        pending.pop(0)()


def _tp_ffn_out(self, x32, x32bs, nt, eps):
    P, c = self.P, self.c
    ones = c["ones_f32"]
    sqs = []
    for m in range(8):
        po, pob = self.po.next()
        for j in range(22):
            P.op("pe", lambda h, po=po, j=j, m=m: h.matmul(
                po[:, :nt], lhsT=self.wout[:, j, m * 128:(m + 1) * 128], rhs=self.h[:, j, :nt],
                start=(j == 0), stop=(j == 21)), reads=[self.woutb, self.hbuf[j]], writes=[pob])
        P.op("dve", lambda h, po=po, m=m: h.scalar_tensor_tensor(
            out=x32[:, m, :nt], in0=po[:, :nt], scalar=0.5 / ALPHA, in1=x32[:, m, :nt],
            op0=ALU.mult, op1=ALU.add), reads=[pob, x32bs[m]], writes=[x32bs[m]])
        sq, sqb = self.sq.next()
        P.op("act", lambda h, sq=sq, m=m: h.activation(out=sq[:, :nt], in_=x32[:, m, :nt], func=AF.Square),
             reads=[x32bs[m]], writes=[sqb])
        sqs.append((m, sq, sqb))
        if len(sqs) > 1:
            self._stat_mm(x32, x32bs, nt, *sqs.pop(0))
    while sqs:
        self._stat_mm(x32, x32bs, nt, *sqs.pop(0))
    P.op("dve", lambda h: h.tensor_scalar(out=self.mean[:, :nt], in0=self.ps1[:, :nt], scalar1=1.0 / D,
                                          scalar2=None, op0=ALU.mult), reads=[self.ps1b], writes=[self.meanb])
    P.op("dve", lambda h: h.tensor_tensor(out=self.msq[:, :nt], in0=self.mean[:, :nt], in1=self.mean[:, :nt],
                                          op=ALU.mult), reads=[self.meanb], writes=[self.msqb])
    P.op("dve", lambda h: h.scalar_tensor_tensor(out=self.rstd[:, :nt], in0=self.ps2[:, :nt], scalar=1.0 / D,
                                                 in1=self.msq[:, :nt], op0=ALU.mult, op1=ALU.subtract),
         reads=[self.ps2b, self.msqb], writes=[self.rstdb])
    P.op("dve", lambda h: h.tensor_scalar(out=self.rstd[:, :nt], in0=self.rstd[:, :nt], scalar1=eps,
                                          scalar2=None, op0=ALU.add), reads=[self.rstdb], writes=[self.rstdb])
    P.op("act", lambda h: h.activation(out=self.rstd[:, :nt], in_=self.rstd[:, :nt], func=AF.Sqrt),
         reads=[self.rstdb], writes=[self.rstdb])
    P.op("dve", lambda h: h.reciprocal(out=self.rstd[:, :nt], in_=self.rstd[:, :nt]),
         reads=[self.rstdb], writes=[self.rstdb])
    P.op("dve", lambda h: h.scalar_tensor_tensor(out=self.msq[:, :nt], in0=self.mean[:, :nt], scalar=-1.0,
                                                 in1=self.rstd[:, :nt], op0=ALU.mult, op1=ALU.mult),
         reads=[self.meanb, self.rstdb], writes=[self.msqb])


def _tp_stat_mm(self, x32, x32bs, nt, m, sq, sqb):
    P, c = self.P, self.c
    ones = c["ones_f32"]
    P.op("pe", lambda h: h.matmul(self.ps1[:, :nt], lhsT=ones[:, :], rhs=x32[:, m, :nt], start=(m == 0), stop=(m == 7)),
         reads=[x32bs[m], c["constb"]], writes=[self.ps1b])
    P.op("pe", lambda h: h.matmul(self.ps2[:, :nt], lhsT=ones[:, :], rhs=sq[:, :nt], start=(m == 0), stop=(m == 7)),
         reads=[sqb, c["constb"]], writes=[self.ps2b])


def _tp_norm_items(self, x32, x32bs, xb, xbbs, nt, g_ap, b_ap):
    P, c = self.P, self.c
    items = []
    for m in range(8):
        def item(m=m):
            t, tb = self.tmp.next()
            P.op("pool", lambda h: h.tensor_tensor(out=t[:, :nt], in0=x32[:, m, :nt], in1=self.rstd[:, :nt], op=ALU.mult),
                 reads=[x32bs[m], self.rstdb], writes=[tb])
            P.op("dve", lambda h: h.tensor_tensor(out=t[:, :nt], in0=t[:, :nt], in1=self.msq[:, :nt], op=ALU.add),
                 reads=[tb, self.msqb], writes=[tb])
            P.op("act", lambda h: h.activation(out=x32[:, m, :nt], in_=t[:, :nt], func=AF.Identity,
                                               scale=g_ap[:, m:m + 1], bias=b_ap[:, m:m + 1]),
                 reads=[tb, c["constb"]], writes=[x32bs[m]])
            P.op("pool", lambda h: h.tensor_copy(out=xb[:, m, :nt], in_=x32[:, m, :nt]), reads=[x32bs[m]],
                 writes=[xbbs[m]])
        items.append(item)
    return items


TPhase.ffn_in = _tp_ffn_in
LNBase._stat_mm = _tp_stat_mm
LNBase.norm_items = _tp_norm_items
TPhase.ffn_out = _tp_ffn_out


def _ln_stats(self, x32, x32bs, nt, eps):
    P = self.P
    for m in range(8):
        sq, sqb = self.sq.next()
        P.op("act", lambda h, sq=sq, m=m: h.activation(out=sq[:, :nt], in_=x32[:, m, :nt], func=AF.Square),
             reads=[x32bs[m]], writes=[sqb])
        self._stat_mm(x32, x32bs, nt, m, sq, sqb)
    P.op("dve", lambda h: h.tensor_scalar(out=self.mean[:, :nt], in0=self.ps1[:, :nt], scalar1=1.0 / D,
                                          scalar2=None, op0=ALU.mult), reads=[self.ps1b], writes=[self.meanb])
    P.op("dve", lambda h: h.tensor_tensor(out=self.msq[:, :nt], in0=self.mean[:, :nt], in1=self.mean[:, :nt],
                                          op=ALU.mult), reads=[self.meanb], writes=[self.msqb])
    P.op("dve", lambda h: h.scalar_tensor_tensor(out=self.rstd[:, :nt], in0=self.ps2[:, :nt], scalar=1.0 / D,
                                                 in1=self.msq[:, :nt], op0=ALU.mult, op1=ALU.subtract),
         reads=[self.ps2b, self.msqb], writes=[self.rstdb])
    P.op("dve", lambda h: h.tensor_scalar(out=self.rstd[:, :nt], in0=self.rstd[:, :nt], scalar1=eps,
                                          scalar2=None, op0=ALU.add), reads=[self.rstdb], writes=[self.rstdb])
    P.op("act", lambda h: h.activation(out=self.rstd[:, :nt], in_=self.rstd[:, :nt], func=AF.Sqrt),
         reads=[self.rstdb], writes=[self.rstdb])
    P.op("dve", lambda h: h.reciprocal(out=self.rstd[:, :nt], in_=self.rstd[:, :nt]),
         reads=[self.rstdb], writes=[self.rstdb])
    P.op("dve", lambda h: h.scalar_tensor_tensor(out=self.msq[:, :nt], in0=self.mean[:, :nt], scalar=-1.0,
                                                 in1=self.rstd[:, :nt], op0=ALU.mult, op1=ALU.mult),
         reads=[self.meanb, self.rstdb], writes=[self.msqb])


LNBase.ln_stats = _ln_stats
TPhase._stat_mm = _tp_stat_mm
TPhase.norm_items = _tp_norm_items


C_ONES, C_ID, C_MASK, C_SEL64, C_E, C_MASKT = 0, 128, 256, 384, 448, 520
NCST = 520 + 128
MASKNEG = -30000.0


def make_cst():
    cst = np.zeros((128, NCST), np.float32)
    cst[:, C_ONES:C_ONES + 128] = 1.0
    cst[:, C_ID:C_ID + 128] = np.eye(128, dtype=np.float32)
    k = np.arange(128)[:, None]
    q = np.arange(128)[None, :]
    cst[:, C_MASK:C_MASK + 128] = np.where(k > q, MASKNEG, 0.0)
    cst[64, C_SEL64:C_SEL64 + 64] = 1.0
    cst[:, C_MASKT:C_MASKT + 128] = np.where(q > k, MASKNEG, 0.0)
    for h in range(8):
        for x in range(3):
            cst[h, C_E + (h * 3 + x) * 3 + x] = 1.0
    return cst


class StopPhase(Exception):
    pass


def ck(n):
    if DEBUG.get("stopat", 10 ** 9) <= n:
        raise StopPhase()


class Consts:
    def __init__(self, P, nc, cst_ap):
        self.b = Buf("const")
        self.f32 = P.sb([128, NCST], F32, "cst_f32")
        self.bf = P.sb([128, NCST], BF16, "cst_bf")
        P.dma("sp", lambda h: h.dma_start(out=self.f32[:, :], in_=cst_ap[:, :]), writes=[self.b])
        P.op("dve", lambda h: h.tensor_copy(out=self.bf[:, :], in_=self.f32[:, :]), reads=[self.b], writes=[self.b])


class Fox:
    def __init__(self, P, C, H, NK, NQ, wf, nbf):
        self.P, self.C, self.H, self.NK, self.NQ = P, C, H, NK, NQ
        self.NKB = (NK + 127) // 128
        self.wf, self.nbf = wf, nbf
        self.KT = [P.sb([67, NK], BF16, f"KT{h}") for h in range(H)]
        self.KTb = [Buf(f"KT{h}") for h in range(H)]
        self.VA = P.sb([128, self.NKB, H, 65], BF16, "VA")
        self.VAb = Buf("VA")
        self.CK = P.sb([128, self.NKB, H], F32, "CK")
        self.CKb = Buf("CK")
        self.QA = [P.sb([67, NQ], BF16, f"QA{h}") for h in range(H)]
        self.QAb = [Buf(f"QA{h}") for h in range(H)]
        self.carry = P.sb([H, 1], F32, "fcarry")
        self.carryb = Buf("fcarry")
        self.onesr = P.sb([H, 512], F32, "onesr")
        self.sp_ = P.sb([H, 512], F32, "fsp")
        self.spb = Buf("fsp")
        self.cp = P.sb([H, 512], F32, "fcp")
        self.cpb = Buf("fcp")
        self.v8 = P.sb([H, 512], F32, "fv8")
        self.v8b = Buf("fv8")
        self.hml = [P.sb([H, 512], BF16, f"fhml{x}") for x in range(3)]
        self.hmlb = Buf("fhml")
        self.pt = Rot(P, 4, [128, 512], BF16, "fpt_")
        self.osb = Rot(P, 2, [65, 512], F32, "fosb_")
        self.rec = Rot(P, 2, [64, 512], F32, "frec_")
        self.omix = Rot(P, 2, [64, 512], BF16, "fomix_")
        for h in range(H):
            P.op("pool", lambda hh, h=h: hh.memset(self.KT[h][64:67, :], 1.0), writes=[self.KTb[h]])
        P.op("pool", lambda hh: hh.memset(self.VA[:, :, :, 64:65], 1.0), writes=[self.VAb])
        P.op("pool", lambda hh: hh.memset(self.onesr[:, :], 1.0), writes=[self.spb])
        P.op("pool", lambda hh: hh.memset(self.carry[:, :], 0.0), writes=[self.carryb])

    def logf_chain(self, pff, pffb, n, first, logf_out=None):
        P, H = self.P, self.H
        P.op("act", lambda h: h.activation(out=self.sp_[:, :n], in_=pff[0:H, :n], func=AF.Exp, scale=-1.0,
                                           bias=self.nbf[:, 0:1]), reads=[pffb], writes=[self.spb])
        P.op("act", lambda h: h.activation(out=self.sp_[:, :n], in_=self.sp_[:, :n], func=AF.Ln, bias=1.0),
             reads=[self.spb], writes=[self.spb])
        self.scan(n)

    def scan(self, n):
        P, H = self.P, self.H
        P.op("dve", lambda h: h.tensor_tensor_scan(out=self.cp[:, :n], data0=self.onesr[:, :n], data1=self.sp_[:, :n],
                                                   initial=self.carry[:, 0:1], op0=ALU.mult, op1=ALU.add),
             reads=[self.spb, self.carryb], writes=[self.cpb])
        P.op("dve", lambda h: h.tensor_copy(out=self.carry[:, 0:1], in_=self.cp[:, n - 1:n]),
             reads=[self.cpb], writes=[self.carryb])

    def split_q(self, n, off=0):
        P = self.P
        P.op("dve", lambda h: h.tensor_scalar(out=self.v8[:, :n], in0=self.cp[:, off:off + n], scalar1=-8.0,
                                              scalar2=None, op0=ALU.mult), reads=[self.cpb], writes=[self.v8b])
        for x in range(3):
            P.op("dve", lambda h, x=x: h.tensor_copy(out=self.hml[x][:, :n], in_=self.v8[:, :n]),
                 reads=[self.v8b], writes=[self.hmlb])
            if x < 2:
                P.op("dve", lambda h, x=x: h.tensor_tensor(out=self.v8[:, :n], in0=self.v8[:, :n],
                                                           in1=self.hml[x][:, :n], op=ALU.subtract),
                     reads=[self.v8b, self.hmlb], writes=[self.v8b])

    def ck_block(self, pst, pstb, kb, col0, nk):
        P, H, C = self.P, self.H, self.C
        P.op("pe", lambda h: h.matmul(pst[0:nk, 0:H], lhsT=self.cp[:, col0:col0 + nk],
                                      rhs=C.f32[0:H, C_ID:C_ID + H], start=True, stop=True),
             reads=[self.cpb, C.b], writes=[pstb])
        P.op("dve", lambda h: h.tensor_copy(out=self.CK[0:nk, kb, :], in_=pst[0:nk, 0:H]),
             reads=[pstb], writes=[self.CKb])

    def q_aug(self, pq, pqb, h, n):
        P, C = self.P, self.C
        for x in range(3):
            c0 = C_E + (h * 3 + x) * 3
            P.op("pe", lambda hh, x=x, c0=c0: hh.matmul(pq[64:67, :n], lhsT=C.bf[0:self.H, c0:c0 + 3],
                                                        rhs=self.hml[x][:, :n], start=(x == 0), stop=(x == 2)),
                 reads=[self.hmlb, C.b], writes=[pqb])
        qa, qab = self.QA[h], self.QAb[h]
        P.op("act", lambda hh: hh.activation(out=qa[:, :n], in_=pq[0:67, :n], func=AF.Copy),
             reads=[pqb], writes=[qab])

    def attend_multi(self, heads, nq, kblocks, ps_rot, pos, pden, pdenb, stores, LA=2):
        P, C = self.P, self.C
        nb = len(kblocks)
        tasks = [(h, i) for i in range(nb) for h in heads]
        pend = []

        def stage_a(h, i):
            kb, nk, col0, masked = kblocks[i]
            ps, psb = ps_rot.next()
            qa, qab = self.QA[h], self.QAb[h]
            P.op("pe", lambda hh: hh.matmul(
                ps[0:nk, col0:nq], lhsT=self.KT[h][0:67, kb * 128:kb * 128 + nk], rhs=qa[0:67, col0:nq],
                start=True, stop=(not masked)), reads=[self.KTb[h], qab], writes=[psb])
            if masked:
                w = min(128, nq - col0)
                P.op("pe", lambda hh: hh.matmul(
                    ps[0:nk, col0:col0 + w], lhsT=C.bf[0:nk, C_ID:C_ID + nk], rhs=C.bf[0:nk, C_MASK:C_MASK + w],
                    start=False, stop=True), reads=[C.b], writes=[psb])
            pt, ptb = self.pt.next()
            P.op("act", lambda hh: hh.activation(
                out=pt[0:nk, col0:nq], in_=ps[0:nk, col0:nq], func=AF.Exp, scale=0.125,
                bias=self.CK[0:nk, kb, h:h + 1]), reads=[psb, self.CKb], writes=[ptb])
            return (h, i, pt, ptb)

        def stage_b(h, i, pt, ptb):
            kb, nk, col0, masked = kblocks[i]
            po, pob = pos[h]
            P.op("pe", lambda hh: hh.matmul(
                po[0:65, col0:nq], lhsT=self.VA[0:nk, kb, h, :], rhs=pt[0:nk, col0:nq],
                start=(i == 0), stop=(i == nb - 1)), reads=[ptb, self.VAb], writes=[pob])

        for (h, i) in tasks:
            pend.append(stage_a(h, i))
            if len(pend) > LA:
                stage_b(*pend.pop(0))
        while pend:
            stage_b(*pend.pop(0))
        for h in heads:
            po, pob = pos[h]
            osb, osbb = self.osb.next()
            P.op("dve", lambda hh, osb=osb, po=po: hh.tensor_copy(out=osb[:, :nq], in_=po[0:65, :nq]), reads=[pob],
                 writes=[osbb])
            P.op("pe", lambda hh, osb=osb: hh.matmul(pden[0:64, :nq], lhsT=C.f32[0:65, C_SEL64:C_SEL64 + 64],
                                                     rhs=osb[0:65, :nq], start=True, stop=True),
                 reads=[osbb, C.b], writes=[pdenb])
            rec, recb = self.rec.next()
            P.op("dve", lambda hh, rec=rec: hh.reciprocal(out=rec[:, :nq], in_=pden[0:64, :nq]), reads=[pdenb],
                 writes=[recb])
            om, omb = self.omix.next()
            P.op("dve", lambda hh, om=om, osb=osb, rec=rec: hh.tensor_tensor(out=om[:, :nq], in0=osb[0:64, :nq],
                                                                            in1=rec[:, :nq], op=ALU.mult),
                 reads=[osbb, recb], writes=[omb])
            stores[h](om, omb)


def phase_fox_prompt(P, C, io):
    from contextlib import ExitStack
    with ExitStack() as ph:
        P.stack = ph
        wf = P.sb([128, 8, 386], BF16, "wfox")
        wfb = Buf("wfox")
        wf32 = P.sb([128, 8, 386], F32, "wfox32")
        P.dma("sp", lambda h: h.dma_start(out=wf32[:, :, :], in_=io["wfox"].rearrange("p (k c) -> p k c", c=386)),
              writes=[wfb])
        P.op("pool", lambda h: h.tensor_copy(out=wf[:, :, :], in_=wf32[:, :, :]), reads=[wfb], writes=[wfb])
        bf = P.sb([2, 1], F32, "bf")
        bfb = Buf("bf")
        P.dma("sp", lambda h: h.dma_start(out=bf[:, :], in_=io["bfx"][:, :]), writes=[bfb])
        P.op("dve", lambda h: h.tensor_scalar(out=bf[:, :], in0=bf[:, :], scalar1=-1.0, scalar2=None, op0=ALU.mult),
             reads=[bfb], writes=[bfb])
        F = Fox(P, C, 2, SEQ, 512, wf, bf)
        xb = Rot(P, 2, [128, 8, 512], BF16, "hxb_")
        pproj = Rot(P, 2, [128, 512], F32, "fpp_", psum=True)
        ps_rot = Rot(P, 3, [128, 512], F32, "fps_", psum=True)
        po = [P.ps([128, 512], F32, f"fpo{h}") for h in range(2)]
        pob = [Buf(f"fpo{h}", excl=True) for h in range(2)]
        pden = P.ps([128, 512], F32, "fpden")
        pdenb = Buf("fpden", excl=True)
        kst = Rot(P, 2, [64, 512], F32, "kst_")
        vst = Rot(P, 2, [128, 4, 128], F32, "vst_")
        lst = Rot(P, 2, [2, 512], F32, "lst_")
        XG = None
        try:
            ck(1)
            _fox_loop(P, C, io, F, xb, pproj, ps_rot, po, pob, pden, pdenb, kst, vst, lst, wf, wfb, bf, bfb, XG)
        except StopPhase:
            pass
        P.barrier()
        P.emit()


def _fox_loop(P, C, io, F, xb, pproj, ps_rot, po, pob, pden, pdenb, kst, vst, lst, wf, wfb, bf, bfb, XG):
    QA2 = [[F.QA[h], P.sb([67, 512], BF16, f"QAx{h}")] for h in range(2)]
    QAb2 = [[F.QAb[h], Buf(f"QAx{h}")] for h in range(2)]
    if True:
        def prologue(tb):
            F.QA = [QA2[h][tb % 2] for h in range(2)]
            F.QAb = [QAb2[h][tb % 2] for h in range(2)]
            rk, cb = tb // (TPC // 512), (tb % (TPC // 512)) * 512
            x, xbuf = xb.next()
            P.dma("sp", lambda h, x=x, rk=rk, cb=cb: h.dma_start(
                out=x[:, :, :], in_=io["XGp"][cb // 512][rk * 1024:(rk + 1) * 1024, :].rearrange("(k p) n -> p k n", p=128)),
                reads=[io["XGpb"][cb // 512]], writes=[xbuf])
            ck(2)
            pf, pfb = pproj.next()
            for kt in range(8):
                P.op("pe", lambda h, pf=pf, kt=kt, x=x: h.matmul(pf[0:2, :], lhsT=wf[:, kt, 384:386], rhs=x[:, kt, :],
                                                              start=(kt == 0), stop=(kt == 7)),
                     reads=[wfb, xbuf], writes=[pfb])
            ck(3)
            F.nbf = bf
            P.op("act", lambda h, pf=pf: h.activation(out=F.sp_[:, :], in_=pf[0:2, :], func=AF.Exp, scale=-1.0,
                                                      bias=bf[:, 0:1]), reads=[pfb, bfb], writes=[F.spb])
            P.op("act", lambda h: h.activation(out=F.sp_[:, :], in_=F.sp_[:, :], func=AF.Ln, bias=1.0),
                 reads=[F.spb], writes=[F.spb])
            ck(4)
            F.scan(512)
            ck(5)
            F.split_q(512)
            ck(6)
            ls, lsb = lst.next()
            P.op("pool", lambda h, ls=ls: h.tensor_scalar(out=ls[:, :], in0=F.sp_[:, :], scalar1=-1.0, scalar2=None,
                                                          op0=ALU.mult), reads=[F.spb], writes=[lsb])
            P.dma("sp", lambda h, ls=ls, tb=tb: h.dma_start(out=io["o_logf"][:, tb * 512:(tb + 1) * 512], in_=ls[:, :]),
                  reads=[lsb])
            for sub in range(4):
                if DEBUG.get("nock"):
                    break
                pst, pstb = pproj.next()
                F.ck_block(pst, pstb, tb * 4 + sub, sub * 128, 128)
            ck(7)
            for hl in range(2):
                pk, pkb = pproj.next()
                for kt in range(8):
                    P.op("pe", lambda h, pk=pk, kt=kt, x=x, hl=hl: h.matmul(
                        pk[0:64, :], lhsT=wf[:, kt, 128 + hl * 64:128 + (hl + 1) * 64], rhs=x[:, kt, :],
                        start=(kt == 0), stop=(kt == 7)), reads=[wfb, xbuf], writes=[pkb])
                if not DEBUG.get("noktcopy"):
                    P.op("act", lambda h, pk=pk, hl=hl, tb=tb: h.activation(
                        out=F.KT[hl][0:64, tb * 512:(tb + 1) * 512], in_=pk[0:64, :], func=AF.Copy),
                        reads=[pkb], writes=[F.KTb[hl]])
                ks, ksb = kst.next()
                if not DEBUG.get("nokst"):
                    P.op("dve", lambda h, pk=pk, ks=ks: h.tensor_copy(out=ks[:, :], in_=pk[0:64, :]), reads=[pkb],
                         writes=[ksb])
                if not DEBUG.get("nokdma") and not (DEBUG.get("kdma0") and (hl != 0 or tb >= DEBUG["kdma0"])):
                    P.dma(DEBUG.get("kq", "sp") if isinstance(DEBUG.get("kq", "sp"), str) else "act", lambda h, ks=ks, hl=hl, tb=tb: h.dma_start(
                        out=io["o_foxk"][hl * 64:(hl + 1) * 64, tb * 512:(tb + 1) * 512], in_=ks[:, :]), reads=[ksb])
                pq, pqb = pproj.next()
                for kt in range(8):
                    P.op("pe", lambda h, pq=pq, kt=kt, x=x, hl=hl: h.matmul(
                        pq[0:64, :], lhsT=wf[:, kt, hl * 64:(hl + 1) * 64], rhs=x[:, kt, :],
                        start=(kt == 0), stop=(kt == 7)), reads=[wfb, xbuf], writes=[pqb])
                if not DEBUG.get("noaug"):
                    F.q_aug(pq, pqb, hl, 512)
            ck(8)
            vs, vsb = vst.next()
            for sub in range(4):
                pv, pvb = pproj.next()
                for kt in range(8):
                    P.op("pe", lambda h, pv=pv, kt=kt, x=x, sub=sub: h.matmul(
                        pv[:, 0:128], lhsT=x[:, kt, sub * 128:(sub + 1) * 128], rhs=wf[:, kt, 256:384],
                        start=(kt == 0), stop=(kt == 7)), reads=[wfb, xbuf], writes=[pvb])
                P.op("act", lambda h, pv=pv, sub=sub, tb=tb: h.activation(
                    out=F.VA[:, tb * 4 + sub, :, 0:64], in_=pv[:, 0:128].rearrange("p (h d) -> p h d", d=64),
                    func=AF.Copy), reads=[pvb], writes=[F.VAb])
                P.op("dve", lambda h, pv=pv, sub=sub, vs=vs: h.tensor_copy(out=vs[:, sub, :], in_=pv[:, 0:128]),
                     reads=[pvb], writes=[vsb])
            P.dma("sp", lambda h, vs=vs, tb=tb: h.dma_start(
                out=io["o_foxv"][tb * 512:(tb + 1) * 512, :].rearrange("(s p) f -> p s f", p=128), in_=vs[:, :, :]),
                reads=[vsb])

        def attention(tb):
            F.QA = [QA2[h][tb % 2] for h in range(2)]
            F.QAb = [QAb2[h][tb % 2] for h in range(2)]
            kbl = [(kb, 128, 0, False) for kb in range(4 * tb)]
            kbl += [(4 * tb + i, 128, 128 * i, True) for i in range(4)]
            stores = {}
            for hl in range(2):
                def store(om, omb, hl=hl, tb=tb):
                    P.dma("sp", lambda h: h.dma_start(
                        out=io["MXinp"][tb // 4][hl * 64:(hl + 1) * 64, (tb % 4) * 512:(tb % 4 + 1) * 512],
                        in_=om[:, :]), reads=[omb], writes=[io["MXinb"][tb]])
                stores[hl] = store
            F.attend_multi([0, 1], 512, kbl, ps_rot, {0: (po[0], pob[0]), 1: (po[1], pob[1])}, pden, pdenb, stores)


        nb_ = SEQ // 512
        prologue(0)
        for tb in range(nb_):
            if tb + 1 < nb_:
                prologue(tb + 1)
            ck(9)
            attention(tb)

class Gdn:
    def __init__(self, P, C, n, L, banks):
        self.P, self.C, self.n, self.L = P, C, n, L
        self.nch = n // L
        self.B = banks
        sb = P.sb
        self.wg = sb([128, 8, 514], BF16, "wg")
        self.wgb = Buf("wg")
        self.cw = sb([128, 3, 4], F32, "cw")
        self.sc = sb([1, 2], F32, "gsc")
        self.ng = sb([128, 1], F32, "gng")
        self.parb = Buf("gpar")
        self.XC = [sb([128, 3 + n], F32, f"XC{i}") for i in range(3)]
        self.XCb = [Buf(f"XC{i}") for i in range(3)]
        self.acc = [sb([128, n], F32, f"gacc{i}") for i in range(3)]
        self.accb = [Buf(f"gacc{i}") for i in range(3)]
        self.sqt = sb([128, n], BF16, "gsq")
        self.sqb = Buf("gsq")
        self.rn = sb([128, n], F32, "grn")
        self.rnb = Buf("grn")
        self.kT = sb([128, n], BF16, "gkT")
        self.qT = sb([128, n], BF16, "gqT")
        self.kbT = sb([128, n], BF16, "gkbT")
        self.kgT = sb([128, n], BF16, "gkgT")
        self.qgT = sb([128, n], BF16, "gqgT")
        self.vT = sb([128, n], BF16, "gvT")
        self.kTb, self.qTb, self.kbTb, self.kgTb, self.qgTb, self.vTb = [Buf(x) for x in "kT qT kbT kgT qgT vT".split()]
        self.zs = sb([128, n], F32, "gzs")
        self.zsb = Buf("gzs")
        self.EG = sb([128, n], F32, "gEG")
        self.EGb = Buf("gEG")
        self.BE = sb([128, n], F32, "gBE")
        self.BEb = Buf("gBE")
        self.rows = {k: sb([1, n], F32, "gr_" + k) for k in ("g", "Gl", "nGl", "eg", "kd", "beta", "one")}
        self.rowb = {k: Buf("gr_" + k) for k in self.rows}
        self.kdT = sb([L, self.nch], F32, "gkdT")
        self.kdTb = Buf("gkdT")
        self.vtok = sb([L, self.nch, 128], BF16, "gvtok")
        self.vtokb = Buf("gvtok")
        self.ktok = sb([L, self.nch, 128], BF16, "gktok")
        self.ktokb = Buf("gktok")
        self.dT = sb([L, n], F32, "gdT")
        self.dTb = Buf("gdT")
        self.X = [sb([L, n], F32, f"gX{i}") for i in range(2)]
        self.Y = [sb([L, n], F32, f"gY{i}") for i in range(2)]
        self.Pm = sb([L, n], F32, "gPm")
        self.Xb = [Buf("gX0"), Buf("gX1")]
        self.Yb = [Buf("gY0"), Buf("gY1")]
        self.Pmb = Buf("gPm")
        self.AT = sb([L, n], BF16, "gAT")
        self.ATb = Buf("gAT")
        self.TT = sb([L, n], BF16, "gTT")
        self.TTb = Buf("gTT")
        self.idt = sb([L, n], F32, "gidt")
        self.strict = sb([L, n], F32, "gstrict")
        self.cb = Buf("gconstl")
        self.S32 = sb([128, 128], F32, "gS32")
        self.S32b = Buf("gS32")
        self.Sbf = sb([128, 128], BF16, "gSbf")
        self.Sbfb = Buf("gSbf")
        self.R = Rot(P, 2, [L, 128], BF16, "gR_")
        self.vn = Rot(P, 2, [L, 128], BF16, "gvn_")
        self.vkd = Rot(P, 2, [L, 128], BF16, "gvkd_")
        self.og = sb([128, n], F32, "gog")
        self.ogb = Buf("gog")
        self.om = Rot(P, 2, [128, n], BF16, "gom_")
        self.coef = sb([1, 2], F32, "gcoef")
        self.coefb = Buf("gcoef")
        self.rot = 0
        for c in range(self.nch):
            P.op("pool", lambda h, c=c: h.tensor_copy(out=self.idt[:, c * L:(c + 1) * L], in_=C.f32[0:L, C_ID:C_ID + L]),
                 reads=[C.b], writes=[self.cb])
            P.op("pool", lambda h, c=c: h.tensor_scalar(out=self.strict[:, c * L:(c + 1) * L],
                                                        in0=C.f32[0:L, C_MASKT:C_MASKT + L], scalar1=-1.0 / 30000.0,
                                                        scalar2=None, op0=ALU.mult), reads=[C.b], writes=[self.cb])
        P.op("pool", lambda h: h.memset(self.rows["one"][:, :], 1.0), writes=[self.rowb["one"]])

    def gb(self):
        it = self.B[self.rot % 2]
        self.rot += 1
        return it

    def load_params(self, wg_ap, cw_ap, sc_ap, ng_ap, stage):
        P = self.P
        P.dma("sp", lambda h: h.dma_start(out=stage[:, :, :], in_=wg_ap.rearrange("p (k c) -> p k c", c=514)),
              writes=[self.wgb])
        P.op("pool", lambda h: h.tensor_copy(out=self.wg[:, :, :], in_=stage[:, :, :]), reads=[self.wgb],
             writes=[self.wgb])
        P.dma("sp", lambda h: h.dma_start(out=self.cw[:, :, :], in_=cw_ap), writes=[self.parb])
        P.dma("sp", lambda h: h.dma_start(out=self.sc[:, :], in_=sc_ap), writes=[self.parb])
        P.dma("sp", lambda h: h.dma_start(out=self.ng[:, :], in_=ng_ap), writes=[self.parb])
        P.op("act", lambda h: h.activation(out=self.coef[:, 0:1], in_=self.sc[:, 0:1], func=AF.Exp),
             reads=[self.parb], writes=[self.coefb])
        P.op("dve", lambda h: h.tensor_scalar(out=self.coef[:, 0:1], in0=self.coef[:, 0:1], scalar1=-1.0,
                                              scalar2=None, op0=ALU.mult), reads=[self.coefb], writes=[self.coefb])

    def block(self, x, xbuf, first, store_o):
        P, C, n, L, nch, B = self.P, self.C, self.n, self.L, self.nch, self.B
        wg, wgb = self.wg, self.wgb
        ones_bf = C.bf[:, C_ONES:C_ONES + 128]
        for s_ in range(3):
            pp, ppb = self.gb()
            for kt in range(8):
                P.op("pe", lambda h, pp=pp, kt=kt, s_=s_: h.matmul(pp[:, :n], lhsT=wg[:, kt, s_ * 128:(s_ + 1) * 128],
                                                                 rhs=x[:, kt, :n], start=(kt == 0), stop=(kt == 7)),
                     reads=[wgb, xbuf], writes=[ppb])
            P.op("act", lambda h, pp=pp, s_=s_: h.activation(out=self.XC[s_][:, 3:3 + n], in_=pp[:, :n], func=AF.Copy),
                 reads=[ppb], writes=[self.XCb[s_]])
        pp, ppb = self.gb()
        for kt in range(8):
            P.op("pe", lambda h, pp=pp, kt=kt: h.matmul(pp[:, :n], lhsT=wg[:, kt, 384:512], rhs=x[:, kt, :n],
                                                        start=(kt == 0), stop=(kt == 7)), reads=[wgb, xbuf], writes=[ppb])
        P.op("act", lambda h, pp=pp: h.activation(out=self.zs[:, :], in_=pp[:, :n], func=AF.Silu), reads=[ppb],
             writes=[self.zsb])
        r = self.rows
        rb = self.rowb
        pg, pgb = self.gb()
        for kt in range(8):
            P.op("pe", lambda h, pg=pg, kt=kt: h.matmul(pg[0:1, :n], lhsT=wg[:, kt, 512:513], rhs=x[:, kt, :n],
                                                        start=(kt == 0), stop=(kt == 7)), reads=[wgb, xbuf], writes=[pgb])
        P.op("act", lambda h, pg=pg: h.activation(out=r["g"][:, :], in_=pg[0:1, :n], func=AF.Exp, bias=self.sc[:, 1:2]),
             reads=[pgb, self.parb], writes=[rb["g"]])
        P.op("act", lambda h: h.activation(out=r["g"][:, :], in_=r["g"][:, :], func=AF.Ln, bias=1.0),
             reads=[rb["g"]], writes=[rb["g"]])
        P.op("dve", lambda h: h.tensor_scalar(out=r["g"][:, :], in0=r["g"][:, :], scalar1=self.coef[:, 0:1],
                                              scalar2=None, op0=ALU.mult), reads=[rb["g"], self.coefb], writes=[rb["g"]])
        pg2, pg2b = self.gb()
        for kt in range(8):
            P.op("pe", lambda h, pg2=pg2, kt=kt: h.matmul(pg2[0:1, :n], lhsT=wg[:, kt, 513:514], rhs=x[:, kt, :n],
                                                          start=(kt == 0), stop=(kt == 7)), reads=[wgb, xbuf], writes=[pg2b])
        P.op("act", lambda h, pg2=pg2: h.activation(out=r["beta"][:, :], in_=pg2[0:1, :n], func=AF.Sigmoid),
             reads=[pg2b], writes=[rb["beta"]])
        for c in range(nch):
            P.op("dve", lambda h, c=c: h.tensor_tensor_scan(out=r["Gl"][:, c * L:(c + 1) * L],
                                                            data0=r["one"][:, c * L:(c + 1) * L],
                                                            data1=r["g"][:, c * L:(c + 1) * L], initial=0.0,
                                                            op0=ALU.mult, op1=ALU.add),
                 reads=[rb["g"], rb["one"]], writes=[rb["Gl"]])
        P.op("act", lambda h: h.activation(out=r["eg"][:, :], in_=r["Gl"][:, :], func=AF.Exp), reads=[rb["Gl"]],
             writes=[rb["eg"]])
        P.op("dve", lambda h: h.tensor_scalar(out=r["nGl"][:, :], in0=r["Gl"][:, :], scalar1=-1.0, scalar2=None,
                                              op0=ALU.mult), reads=[rb["Gl"]], writes=[rb["nGl"]])
        for c in range(nch):
            P.op("act", lambda h, c=c: h.activation(out=r["kd"][:, c * L:(c + 1) * L], in_=r["Gl"][:, c * L:(c + 1) * L],
                                                    func=AF.Exp, scale=-1.0, bias=r["Gl"][:, (c + 1) * L - 1:(c + 1) * L]),
                 reads=[rb["Gl"]], writes=[rb["kd"]])
        onesrow = C.f32[0:1, C_ONES:C_ONES + 128]
        pe_, peb = self.gb()
        P.op("pe", lambda h, pe_=pe_: h.matmul(pe_[:, :n], lhsT=onesrow, rhs=r["eg"][:, :], start=True, stop=True),
             reads=[rb["eg"], C.b], writes=[peb])
        P.op("act", lambda h, pe_=pe_: h.activation(out=self.EG[:, :], in_=pe_[:, :n], func=AF.Copy), reads=[peb],
             writes=[self.EGb])
        pb_, pbb = self.gb()
        P.op("pe", lambda h, pb_=pb_: h.matmul(pb_[:, :n], lhsT=onesrow, rhs=r["beta"][:, :], start=True, stop=True),
             reads=[rb["beta"], C.b], writes=[pbb])
        P.op("act", lambda h, pb_=pb_: h.activation(out=self.BE[:, :], in_=pb_[:, :n], func=AF.Copy), reads=[pbb],
             writes=[self.BEb])
        pk_, pkb_ = self.gb()
        for c in range(nch):
            P.op("pe", lambda h, pk_=pk_, c=c: h.matmul(pk_[0:L, c:c + 1], lhsT=r["kd"][:, c * L:(c + 1) * L],
                                                        rhs=C.f32[0:1, C_ONES:C_ONES + 1], start=True, stop=True),
                 reads=[rb["kd"], C.b], writes=[pkb_])
        P.op("dve", lambda h, pk_=pk_: h.tensor_copy(out=self.kdT[:, :], in_=pk_[0:L, 0:nch]), reads=[pkb_],
             writes=[self.kdTb])
        for s_ in range(3):
            xc, xcb = self.XC[s_], self.XCb[s_]
            acc, accb = self.acc[s_], self.accb[s_]
            P.op("dve", lambda h, xc=xc, acc=acc, s_=s_: h.tensor_scalar(out=acc[:, :], in0=xc[:, 0:n],
                                                                       scalar1=self.cw[:, s_, 0:1], scalar2=None,
                                                                       op0=ALU.mult), reads=[xcb, self.parb], writes=[accb])
            for i in range(1, 4):
                P.op("dve", lambda h, xc=xc, acc=acc, s_=s_, i=i: h.scalar_tensor_tensor(
                    out=acc[:, :], in0=xc[:, i:i + n], scalar=self.cw[:, s_, i:i + 1], in1=acc[:, :],
                    op0=ALU.mult, op1=ALU.add), reads=[xcb, self.parb, accb], writes=[accb])
            P.op("act", lambda h, acc=acc: h.activation(out=acc[:, :], in_=acc[:, :], func=AF.Silu), reads=[accb],
                 writes=[accb])
            P.op("pool", lambda h, xc=xc: h.tensor_copy(out=xc[:, 0:3], in_=xc[:, n:n + 3]), reads=[xcb], writes=[xcb])
        for s_ in range(2):
            acc, accb = self.acc[s_], self.accb[s_]
            P.op("act", lambda h, acc=acc: h.activation(out=self.sqt[:, :], in_=acc[:, :], func=AF.Square), reads=[accb],
                 writes=[self.sqb])
            pn, pnb = self.gb()
            P.op("pe", lambda h, pn=pn: h.matmul(pn[:, :n], lhsT=ones_bf, rhs=self.sqt[:, :], start=True, stop=True),
                 reads=[self.sqb, C.b], writes=[pnb])
            P.op("dve", lambda h, pn=pn: h.tensor_scalar(out=self.rn[:, :], in0=pn[:, :n], scalar1=NORM_EPS, scalar2=None,
                                                         op0=ALU.add), reads=[pnb], writes=[self.rnb])
            P.op("act", lambda h: h.activation(out=self.rn[:, :], in_=self.rn[:, :], func=AF.Sqrt), reads=[self.rnb],
                 writes=[self.rnb])
            P.op("dve", lambda h: h.reciprocal(out=self.rn[:, :], in_=self.rn[:, :]), reads=[self.rnb], writes=[self.rnb])
            if s_ == 0:
                P.op("dve", lambda h, acc=acc: h.scalar_tensor_tensor(out=acc[:, :], in0=acc[:, :], scalar=128.0 ** -0.5,
                                                                     in1=self.rn[:, :], op0=ALU.mult, op1=ALU.mult),
                     reads=[accb, self.rnb], writes=[accb])
            else:
                P.op("dve", lambda h, acc=acc: h.tensor_tensor(out=acc[:, :], in0=acc[:, :], in1=self.rn[:, :],
                                                              op=ALU.mult), reads=[accb, self.rnb], writes=[accb])
        qf, kf, vf = self.acc
        qfb, kfb, vfb = self.accb
        P.op("act", lambda h: h.activation(out=self.qT[:, :], in_=qf[:, :], func=AF.Copy), reads=[qfb], writes=[self.qTb])
        P.op("act", lambda h: h.activation(out=self.kT[:, :], in_=kf[:, :], func=AF.Copy), reads=[kfb], writes=[self.kTb])
        P.op("act", lambda h: h.activation(out=self.vT[:, :], in_=vf[:, :], func=AF.Copy), reads=[vfb], writes=[self.vTb])
        P.op("dve", lambda h: h.tensor_tensor(out=self.kbT[:, :], in0=kf[:, :], in1=self.BE[:, :], op=ALU.mult),
             reads=[kfb, self.BEb], writes=[self.kbTb])
        P.op("dve", lambda h: h.tensor_tensor(out=self.kgT[:, :], in0=kf[:, :], in1=self.EG[:, :], op=ALU.mult),
             reads=[kfb, self.EGb], writes=[self.kgTb])
        P.op("pool", lambda h: h.tensor_tensor(out=self.qgT[:, :], in0=qf[:, :], in1=self.EG[:, :], op=ALU.mult),
             reads=[qfb, self.EGb], writes=[self.qgTb])
        idb = C.bf[:, C_ID:C_ID + 128]
        for (src, srcb, dst, dstb) in ((self.kT, self.kTb, self.ktok, self.ktokb), (self.vT, self.vTb, self.vtok, self.vtokb)):
            for c0 in range(0, nch, 4):
                pt_, ptb_ = self.gb()
                m = min(4, nch - c0)
                for c in range(c0, c0 + m):
                    P.op("pe", lambda h, pt_=pt_, c=c, c0=c0, src=src: h.matmul(
                        pt_[0:L, (c - c0) * 128:(c - c0 + 1) * 128], lhsT=src[:, c * L:(c + 1) * L], rhs=idb,
                        start=True, stop=True), reads=[srcb, C.b], writes=[ptb_])
                P.op("dve", lambda h, pt_=pt_, c0=c0, m=m, dst=dst: h.tensor_copy(
                    out=dst[:, c0:c0 + m, :], in_=pt_[0:L, 0:m * 128].rearrange("p (c d) -> p c d", d=128)),
                    reads=[ptb_], writes=[dstb])
        (a1, a1b), (a2, a2b), (dm, dmb) = B[2], B[3], B[4]
        for c in range(nch):
            J = slice(c * L, (c + 1) * L)
            P.op("pe", lambda h, J=J: h.matmul(a1[0:L, J], lhsT=self.kbT[:, J], rhs=self.kT[:, J], start=True, stop=True),
                 reads=[self.kbTb, self.kTb], writes=[a1b])
            P.op("pe", lambda h, J=J: h.matmul(a2[0:L, J], lhsT=self.kT[:, J], rhs=self.qT[:, J], start=True, stop=True),
                 reads=[self.kTb, self.qTb], writes=[a2b])
            P.op("pe", lambda h, J=J: h.matmul(dm[0:L, J], lhsT=C.f32[0:1, C_ONES:C_ONES + L], rhs=r["Gl"][:, J],
                                               start=True, stop=False), reads=[rb["Gl"], C.b], writes=[dmb])
            P.op("pe", lambda h, J=J: h.matmul(dm[0:L, J], lhsT=r["nGl"][:, J], rhs=C.f32[0:1, C_ONES:C_ONES + L],
                                               start=False, stop=False), reads=[rb["nGl"], C.b], writes=[dmb])
            P.op("pe", lambda h, J=J: h.matmul(dm[0:L, J], lhsT=C.f32[0:L, C_ID:C_ID + L], rhs=C.f32[0:L, C_MASK:C_MASK + L],
                                               start=False, stop=True), reads=[C.b], writes=[dmb])
        P.op("act", lambda h: h.activation(out=self.dT[:, :], in_=dm[0:L, :n], func=AF.Exp), reads=[dmb], writes=[self.dTb])
        X, Xb_, Y, Yb_ = self.X, self.Xb, self.Y, self.Yb
        P.op("dve", lambda h: h.tensor_tensor(out=X[0][:, :], in0=a1[0:L, :n], in1=self.dT[:, :], op=ALU.mult),
             reads=[a1b, self.dTb], writes=[Xb_[0]])
        P.op("dve", lambda h: h.tensor_tensor(out=X[0][:, :], in0=X[0][:, :], in1=self.strict[:, :], op=ALU.mult),
             reads=[Xb_[0], self.cb], writes=[Xb_[0]])
        P.op("dve", lambda h: h.tensor_tensor(out=self.AT[:, :], in0=a2[0:L, :n], in1=self.dT[:, :], op=ALU.mult),
             reads=[a2b, self.dTb], writes=[self.ATb])
        P.op("pool", lambda h: h.tensor_tensor(out=self.Pm[:, :], in0=self.idt[:, :], in1=X[0][:, :], op=ALU.subtract),
             reads=[Xb_[0], self.cb], writes=[self.Pmb])
        (px, pxb), (py, pyb), (pp_, ppb_) = B[2], B[3], B[4]
        for c in range(nch):
            J = slice(c * L, (c + 1) * L)
            P.op("pe", lambda h, J=J: h.matmul(py[0:L, J], lhsT=X[0][:, J], rhs=C.f32[0:L, C_ID:C_ID + L], start=True,
                                               stop=True), reads=[Xb_[0], C.b], writes=[pyb])
        P.op("act", lambda h: h.activation(out=Y[0][:, :], in_=py[0:L, :n], func=AF.Copy), reads=[pyb], writes=[Yb_[0]])
        nlev = {64: 5, 32: 4}[L]
        cur = 0
        for lev in range(nlev):
            nxt = 1 - cur
            last = (lev == nlev - 1)
            for c in range(nch):
                J = slice(c * L, (c + 1) * L)
                if not last:
                    P.op("pe", lambda h, J=J, cur=cur: h.matmul(px[0:L, J], lhsT=Y[cur][:, J], rhs=X[cur][:, J],
                                                                start=True, stop=True),
                         reads=[Xb_[cur], Yb_[cur]], writes=[pxb])
                P.op("pe", lambda h, J=J, cur=cur: h.matmul(py[0:L, J], lhsT=X[cur][:, J], rhs=Y[cur][:, J],
                                                            start=True, stop=True),
                     reads=[Xb_[cur], Yb_[cur]], writes=[pyb])
            if not last:
                P.op("dve", lambda h, nxt=nxt: h.tensor_copy(out=X[nxt][:, :], in_=px[0:L, :n]), reads=[pxb],
                     writes=[Xb_[nxt]])
            P.op("act", lambda h, nxt=nxt: h.activation(out=Y[nxt][:, :], in_=py[0:L, :n], func=AF.Copy), reads=[pyb],
                 writes=[Yb_[nxt]])
            for c in range(nch):
                J = slice(c * L, (c + 1) * L)
                P.op("pe", lambda h, J=J, nxt=nxt: h.matmul(pp_[0:L, J], lhsT=Y[nxt][:, J], rhs=self.Pm[:, J],
                                                            start=True, stop=True),
                     reads=[Yb_[nxt], self.Pmb], writes=[ppb_])
            P.op("dve", lambda h: h.tensor_tensor(out=self.Pm[:, :], in0=self.Pm[:, :], in1=pp_[0:L, :n], op=ALU.add),
                 reads=[self.Pmb, ppb_], writes=[self.Pmb])
            cur = nxt
        P.op("dve", lambda h: h.tensor_tensor(out=self.TT[:, :], in0=self.Pm[:, :], in1=self.BE[0:L, :], op=ALU.mult),
             reads=[self.Pmb, self.BEb], writes=[self.TTb])
        (r1, r1b), (r2, r2b), (ro, rob) = B[5], B[6], B[7]
        for c in range(nch):
            J = slice(c * L, (c + 1) * L)
            P.op("pe", lambda h, J=J: h.matmul(r1[0:L, 0:128], lhsT=self.kgT[:, J], rhs=self.Sbf[:, :], start=True,
                                               stop=True), reads=[self.kgTb, self.Sbfb], writes=[r1b])
            R, Rb = self.R.next()
            P.op("dve", lambda h, R=R, c=c: h.tensor_tensor(out=R[:, :], in0=self.vtok[:, c, :], in1=r1[0:L, 0:128],
                                                           op=ALU.subtract), reads=[self.vtokb, r1b], writes=[Rb])
            P.op("pe", lambda h, J=J, R=R: h.matmul(r2[0:L, 0:128], lhsT=self.TT[:, J], rhs=R[:, :], start=True, stop=True),
                 reads=[self.TTb, Rb], writes=[r2b])
            vn, vnb = self.vn.next()
            vkd, vkdb = self.vkd.next()
            P.op("dve", lambda h, vkd=vkd, c=c: h.tensor_scalar(out=vkd[:, :], in0=r2[0:L, 0:128],
                                                               scalar1=self.kdT[:, c:c + 1], scalar2=None, op0=ALU.mult),
                 reads=[r2b, self.kdTb], writes=[vkdb])
            P.op("act", lambda h, vn=vn: h.activation(out=vn[:, :], in_=r2[0:L, 0:128], func=AF.Copy), reads=[r2b],
                 writes=[vnb])
            P.op("pe", lambda h, J=J: h.matmul(ro[:, J], lhsT=self.Sbf[:, :], rhs=self.qgT[:, J], start=True, stop=False),
                 reads=[self.Sbfb, self.qgTb], writes=[rob])
            P.op("pe", lambda h, c=c, vkd=vkd: h.matmul(r1[:, 128:256], lhsT=self.ktok[:, c, :], rhs=vkd[:, :], start=True,
                                                        stop=True), reads=[self.ktokb, vkdb], writes=[r1b])
            P.op("pe", lambda h, J=J, vn=vn: h.matmul(ro[:, J], lhsT=vn[:, :], rhs=self.AT[:, J], start=False, stop=True),
                 reads=[vnb, self.ATb], writes=[rob])
            col = (c + 1) * L - 1
            P.op("dve", lambda h, col=col: h.scalar_tensor_tensor(out=self.Sbf[:, :], in0=self.S32[:, :],
                                                                  scalar=self.EG[:, col:col + 1], in1=r1[:, 128:256],
                                                                  op0=ALU.mult, op1=ALU.add),
                 reads=[self.S32b, self.EGb, r1b], writes=[self.Sbfb])
            P.op("dve", lambda h, col=col: h.scalar_tensor_tensor(out=self.S32[:, :], in0=self.S32[:, :],
                                                                  scalar=self.EG[:, col:col + 1], in1=r1[:, 128:256],
                                                                  op0=ALU.mult, op1=ALU.add),
                 reads=[self.S32b, self.EGb, r1b], writes=[self.S32b])
        P.op("act", lambda h: h.activation(out=self.sqt[:, :], in_=ro[:, :n], func=AF.Square), reads=[rob],
             writes=[self.sqb])
        pn, pnb = self.gb()
        P.op("pe", lambda h, pn=pn: h.matmul(pn[:, :n], lhsT=ones_bf, rhs=self.sqt[:, :], start=True, stop=True),
             reads=[self.sqb, C.b], writes=[pnb])
        P.op("dve", lambda h, pn=pn: h.tensor_scalar(out=self.rn[:, :], in0=pn[:, :n], scalar1=1.0 / 128.0,
                                                     scalar2=NORM_EPS, op0=ALU.mult, op1=ALU.add), reads=[pnb],
             writes=[self.rnb])
        P.op("act", lambda h: h.activation(out=self.rn[:, :], in_=self.rn[:, :], func=AF.Sqrt), reads=[self.rnb],
             writes=[self.rnb])
        P.op("dve", lambda h: h.reciprocal(out=self.rn[:, :], in_=self.rn[:, :]), reads=[self.rnb], writes=[self.rnb])
        P.op("dve", lambda h: h.tensor_tensor(out=self.og[:, :], in0=ro[:, :n], in1=self.rn[:, :], op=ALU.mult),
             reads=[rob, self.rnb], writes=[self.ogb])
        om, omb = self.om.next()
        P.op("dve", lambda h, om=om: h.scalar_tensor_tensor(out=om[:, :], in0=self.og[:, :], scalar=self.ng[:, 0:1],
                                                           in1=self.zs[:, :], op0=ALU.mult, op1=ALU.mult),
             reads=[self.ogb, self.parb, self.zsb], writes=[omb])
        store_o(om, omb)


def phase_gdn_prompt(P, C, io):
    from contextlib import ExitStack
    with ExitStack() as ph:
        P.stack = ph
        banks = [(P.ps([128, 512], F32, f"gbank{i}"), Buf(f"gbank{i}", excl=True)) for i in range(8)]
        G = Gdn(P, C, 512, 64, banks)
        stage = P.sb([128, 8, 514], F32, "wgst")
        G.load_params(io["wgdn"], io["gcw"], io["gsc"], io["gng"], stage)
        for i in range(3):
            P.op("pool", lambda h, i=i: h.memset(G.XC[i][:, 0:3], 0.0), writes=[G.XCb[i]])
        P.op("pool", lambda h: h.memset(G.S32[:, :], 0.0), writes=[G.S32b])
        P.op("pool", lambda h: h.memset(G.Sbf[:, :], 0.0), writes=[G.Sbfb])
        xb = Rot(P, 2, [128, 8, 512], BF16, "gxb_")
        XG = None
        nb = SEQ // 512
        for tb in range(nb):
            rk, cb = tb // (TPC // 512), (tb % (TPC // 512)) * 512
            x, xbuf = xb.next()
            P.dma("sp", lambda h, x=x, rk=rk, cb=cb: h.dma_start(
                out=x[:, :, :], in_=io["XGp"][cb // 512][rk * 1024:(rk + 1) * 1024, :].rearrange("(k p) n -> p k n", p=128)),
                reads=[io["XGpb"][cb // 512]], writes=[xbuf])

            def store(om, omb, tb=tb):
                P.dma("sp", lambda h: h.dma_start(
                    out=io["MXinp"][tb // 4][128:256, (tb % 4) * 512:(tb % 4 + 1) * 512], in_=om[:, :]),
                      reads=[omb], writes=[io["MXinb"][tb]])
            G.block(x, xbuf, tb == 0, store)
            if tb % (PIECE // 512) == PIECE // 512 - 1:
                gather_one(P, io, "MX", tb // (PIECE // 512))
        P.dma("sp", lambda h: h.dma_start(out=io["o_gstate"][:, :], in_=G.S32[:, :]), reads=[G.S32b])
        for i in range(3):
            P.dma("sp", lambda h, i=i: h.dma_start(out=io["o_gconv"][i], in_=G.XC[i][:, 0:3]), reads=[G.XCb[i]])
        P.barrier()
        P.emit()


def phase_sample_even(P, C, io):
    from contextlib import ExitStack
    n = NS
    with ExitStack() as ph:
        P.stack = ph
        banks = [(P.ps([128, 512], F32, f"sbank{i}"), Buf(f"sbank{i}", excl=True)) for i in range(8)]
        x32 = P.sb([128, 8, n], F32, "sx32")
        xs = P.sb([128, 8, n], BF16, "sxb")
        xsb = Buf("sxb")
        P.dma("sp", lambda h: h.dma_start(out=x32[:, :, :], in_=io["X1v"][:, :, TPC:TPC + n]), reads=[io["X1b"][-1]],
              writes=[xsb])
        P.op("act", lambda h: h.activation(out=xs[:, :, :], in_=x32[:, :, :], func=AF.Copy), reads=[xsb], writes=[xsb])
        wfs = P.sb([128, 8, 1544], BF16, "wfs")
        wfsb = Buf("wfs")
        wst = Rot(P, 2, [128, 1544], F32, "wfst_")
        for kt in range(8):
            st, stb = wst.next()
            P.dma("act", lambda h, st=st, kt=kt: h.dma_start(out=st[:, :], in_=io["wfox_s"][:, kt * 1544:(kt + 1) * 1544]),
                  writes=[stb])
            P.op("pool", lambda h, st=st, kt=kt: h.tensor_copy(out=wfs[:, kt, :], in_=st[:, :]), reads=[stb], writes=[wfsb])
        bf = P.sb([8, 1], F32, "sbf")
        bfb = Buf("sbf")
        P.dma("sp", lambda h: h.dma_start(out=bf[:, :], in_=io["bf_s"][:, :]), writes=[bfb])
        P.op("dve", lambda h: h.tensor_scalar(out=bf[:, :], in0=bf[:, :], scalar1=-1.0, scalar2=None, op0=ALU.mult),
             reads=[bfb], writes=[bfb])
        F = Fox(P, C, 8, 1024 + n, n, wfs, bf)
        prot = [banks[0], banks[1]]
        pi = [0]

        def nb():
            it = prot[pi[0] % 2]
            pi[0] += 1
            return it
        kst = Rot(P, 2, [64, 1024], F32, "skst_")
        for hh in range(8):
            st, stb = kst.next()
            P.dma("sp", lambda h, st=st, hh=hh: h.dma_start(out=st[:, :], in_=io["pastk"][hh]), writes=[stb])
            P.op("act", lambda h, st=st, hh=hh: h.activation(out=F.KT[hh][0:64, 0:1024], in_=st[:, :], func=AF.Copy),
                 reads=[stb], writes=[F.KTb[hh]])
        vst = P.sb([128, 8, 512], F32, "svst")
        vstb = Buf("svst")
        P.dma("sp", lambda h: h.dma_start(out=vst[:, :, :], in_=io["pastv"].rearrange("(kb p) f -> p kb f", p=128)),
              writes=[vstb])
        for kb in range(8):
            P.op("pool", lambda h, kb=kb: h.tensor_copy(out=F.VA[:, kb, :, 0:64],
                                                        in_=vst[:, kb, :].rearrange("p (h d) -> p h d", d=64)),
                 reads=[vstb], writes=[F.VAb])
        lp = P.sb([8, 1024], F32, "slp")
        lpb = Buf("slp")
        P.dma("sp", lambda h: h.dma_start(out=lp[:, :], in_=io["pastlf"][:, :]), writes=[lpb])
        for half in range(2):
            P.op("dve", lambda h, half=half: h.tensor_scalar(out=F.sp_[:, :], in0=lp[:, half * 512:(half + 1) * 512],
                                                            scalar1=-1.0, scalar2=None, op0=ALU.mult),
                 reads=[lpb], writes=[F.spb])
            F.scan(512)
            for sub in range(4):
                pst, pstb = nb()
                F.ck_block(pst, pstb, half * 4 + sub, sub * 128, 128)
        pf, pfb = nb()
        for kt in range(8):
            P.op("pe", lambda h, kt=kt: h.matmul(pf[0:8, :n], lhsT=wfs[:, kt, 1536:1544], rhs=xs[:, kt, :],
                                                 start=(kt == 0), stop=(kt == 7)), reads=[wfsb, xsb], writes=[pfb])
        P.op("act", lambda h: h.activation(out=F.sp_[:, :n], in_=pf[0:8, :n], func=AF.Exp, scale=-1.0, bias=bf[:, 0:1]),
             reads=[pfb, bfb], writes=[F.spb])
        P.op("act", lambda h: h.activation(out=F.sp_[:, :n], in_=F.sp_[:, :n], func=AF.Ln, bias=1.0), reads=[F.spb],
             writes=[F.spb])
        F.scan(n)
        F.split_q(n)
        ls = P.sb([8, n], F32, "sls")
        lsb = Buf("sls")
        P.op("pool", lambda h: h.tensor_scalar(out=ls[:, :], in0=F.sp_[:, :n], scalar1=-1.0, scalar2=None, op0=ALU.mult),
             reads=[F.spb], writes=[lsb])
        P.dma("sp", lambda h: h.dma_start(out=io["o_slogf"][:, :], in_=ls[:, :]), reads=[lsb])
        pst, pstb = nb()
        F.ck_block(pst, pstb, 8, 0, n)
        kso = P.sb([64, 8, n], F32, "skso")
        ksob = Buf("skso")
        for hh in range(8):
            pk, pkb = nb()
            for kt in range(8):
                P.op("pe", lambda h, pk=pk, kt=kt, hh=hh: h.matmul(
                    pk[0:64, :n], lhsT=wfs[:, kt, 512 + hh * 64:512 + (hh + 1) * 64], rhs=xs[:, kt, :],
                    start=(kt == 0), stop=(kt == 7)), reads=[wfsb, xsb], writes=[pkb])
            P.op("act", lambda h, pk=pk, hh=hh: h.activation(out=F.KT[hh][0:64, 1024:1024 + n], in_=pk[0:64, :n],
                                                             func=AF.Copy), reads=[pkb], writes=[F.KTb[hh]])
            P.op("dve", lambda h, pk=pk, hh=hh: h.tensor_copy(out=kso[:, hh, :], in_=pk[0:64, :n]), reads=[pkb],
                 writes=[ksob])
            pq, pqb = nb()
            for kt in range(8):
                P.op("pe", lambda h, pq=pq, kt=kt, hh=hh: h.matmul(
                    pq[0:64, :n], lhsT=wfs[:, kt, hh * 64:(hh + 1) * 64], rhs=xs[:, kt, :],
                    start=(kt == 0), stop=(kt == 7)), reads=[wfsb, xsb], writes=[pqb])
            F.q_aug(pq, pqb, hh, n)
        P.dma("sp", lambda h: h.dma_start(out=io["o_sfoxk"].rearrange("(h d) n -> d h n", d=64), in_=kso[:, :, :]),
              reads=[ksob])
        pv, pvb = nb()
        for kt in range(8):
            P.op("pe", lambda h, kt=kt: h.matmul(pv[0:n, 0:512], lhsT=xs[:, kt, :], rhs=wfs[:, kt, 1024:1536],
                                                 start=(kt == 0), stop=(kt == 7)), reads=[wfsb, xsb], writes=[pvb])
        P.op("act", lambda h: h.activation(out=F.VA[0:n, 8, :, 0:64],
                                           in_=pv[0:n, 0:512].rearrange("p (h d) -> p h d", d=64), func=AF.Copy),
             reads=[pvb], writes=[F.VAb])
        vso = P.sb([n, 512], F32, "svso")
        vsob = Buf("svso")
        P.op("dve", lambda h: h.tensor_copy(out=vso[:, :], in_=pv[0:n, 0:512]), reads=[pvb], writes=[vsob])
        P.dma("sp", lambda h: h.dma_start(out=io["o_sfoxv"][:, :], in_=vso[:, :]), reads=[vsob])
        ps_rot = Rot(P, 1, [128, 512], F32, "unused", psum=False)
        ps_rot.items = [banks[2], banks[3], banks[4]]
        kbl = [(kb, 128, 0, False) for kb in range(8)] + [(8, n, 0, True)]
        for h0_ in range(0, 8, 2):
            stores = {}
            for hh in (h0_, h0_ + 1):
                def store(om, omb, hh=hh):
                    P.dma("sp", lambda h: h.dma_start(out=io["SMX"][hh * 64:(hh + 1) * 64, :], in_=om[:, :n]),
                          reads=[omb], writes=[io["SMXb"]])
                stores[hh] = store
            F.attend_multi([h0_, h0_ + 1], n, kbl, ps_rot, {h0_: banks[5], h0_ + 1: banks[6]}, banks[7][0], banks[7][1],
                           stores)
        G = Gdn(P, C, n, n, banks)
        stage = P.sb([128, 8, 514], F32, "swgst")
        for hd in range(4):
            G.load_params(io["wgdn_s"][hd], io["gcw_s"][hd], io["gsc_s"][hd], io["gng"], stage)
            for i in range(3):
                P.dma("sp", lambda h, i=i, hd=hd: h.dma_start(out=G.XC[i][:, 0:3], in_=io["sconv"][hd, i]),
                      writes=[G.XCb[i]])
            P.dma("sp", lambda h, hd=hd: h.dma_start(out=G.S32[:, :], in_=io["sstate"][hd]), writes=[G.S32b])
            P.op("act", lambda h: h.activation(out=G.Sbf[:, :], in_=G.S32[:, :], func=AF.Copy), reads=[G.S32b],
                 writes=[G.Sbfb])

            def store(om, omb, hd=hd):
                P.dma("sp", lambda h: h.dma_start(out=io["SMX"][512 + hd * 128:512 + (hd + 1) * 128, :], in_=om[:, :]),
                      reads=[omb], writes=[io["SMXb"]])
            G.block(xs, xsb, True, store)
            P.dma("sp", lambda h, hd=hd: h.dma_start(out=io["o_sgstate"][hd], in_=G.S32[:, :]), reads=[G.S32b])
            for i in range(3):
                P.dma("sp", lambda h, i=i, hd=hd: h.dma_start(out=io["o_sgconv"][hd, i], in_=G.XC[i][:, 0:3]),
                      reads=[G.XCb[i]])
        P.barrier()
        P.emit()


def groups_():
    return [(i * 512, 512) for i in range(TPC // 512)] + [(TPC, NS)]


def ffn_pass(P, C, io, consts, lnp_sb, fidx, lnidx, src_v, src_b, dst_v, dst_b, xg=None, out_final=None):
    from contextlib import ExitStack
    with ExitStack() as ph:
        P.stack = ph
        T = TPhase(P, consts)
        T.load_wout(io["w_out"][fidx])
        xbufs = {}
        pending = []
        eps = LN_EPS / (ALPHA * ALPHA)
        grp = groups_()
        WS = io["WBF"]
        WSb = io["WBFb"]
        sched = [(gi, jp) for gi in range(len(grp)) for jp in range(11)]
        issued = [0]
        fifo = []
        wbq = []
        LA = 3

        def issue_one():
            gi, jp = sched[issued[0]]
            issued[0] += 1
            w, wb = T.win.next()
            if gi == 0:
                ws, wsb = T.wstage.next()
                P.dma("sp", lambda h: h.dma_start(out=ws[:, :, :], in_=io["w_in"][fidx][jp].rearrange("p (k c) -> p k c", c=512)),
                      writes=[wsb])
                P.op("pool", lambda h: h.tensor_copy(out=w[:, :, :], in_=ws[:, :, :]), reads=[wsb], writes=[wb])
                while wbq:
                    wbq.pop(0)()
                wbq.append(lambda: P.dma("sp", lambda h: h.dma_start(
                    out=WS[jp].rearrange("p (k c) -> p k c", c=512), in_=w[:, :, :]), reads=[wb], writes=[WSb[jp]]))
            else:
                while wbq:
                    wbq.pop(0)()
                P.dma("sp", lambda h: h.dma_start(out=w[:, :, :], in_=WS[jp].rearrange("p (k c) -> p k c", c=512)),
                      reads=[WSb[jp]], writes=[wb])
            fifo.append((w, wb))

        def provider():
            while issued[0] < len(sched) and len(fifo) < LA:
                issue_one()
            return fifo.pop(0)

        def load_x(gi):
            t0, nt = grp[gi]
            x32, x32b0 = T.x32.next()
            xb, xbb0 = T.xb.next()
            if id(x32b0) not in xbufs:
                xbufs[id(x32b0)] = ([Buf(f"x32m{m}") for m in range(8)], [Buf(f"xbm{m}") for m in range(8)])
            x32bs, xbbs = xbufs[id(x32b0)]
            P.dma("sp", lambda h: h.dma_start(out=x32[:, :, :nt], in_=src_v[:, :, t0:t0 + nt]),
                  reads=[src_b[gi]] if src_b else [], writes=x32bs)
            P.op("act", lambda h: h.activation(out=xb[:, :, :nt], in_=x32[:, :, :nt], func=AF.Copy),
                 reads=x32bs, writes=xbbs)
            return x32, x32bs, xb, xbbs

        nxt = load_x(0)
        for gi, (t0, nt) in enumerate(grp):
            x32, x32bs, xb, xbbs = nxt
            T.ffn_in(xb, xbbs, nt, provider, pending)
            if gi + 1 < len(grp):
                nxt = load_x(gi + 1)
            T.ffn_out(x32, x32bs, nt, eps)
            pending = T.norm_items(x32, x32bs, xb, xbbs, nt, lnp_sb[:, 0, lnidx, :], lnp_sb[:, 1, lnidx, :])

            def stores(gi=gi, t0=t0, nt=nt, x32=x32, xb=xb, x32bs=x32bs, xbbs=xbbs):
                if out_final is not None:
                    if gi < TPC // 512:
                        P.dma("sp", lambda h: h.dma_start(
                            out=out_final[0].rearrange("(k p) n -> p k n", p=128)[:, :, t0:t0 + nt], in_=x32[:, :, :nt]),
                            reads=x32bs)
                    else:
                        P.dma("sp", lambda h: h.dma_start(
                            out=out_final[1].rearrange("(k p) n -> p k n", p=128), in_=x32[:, :, :nt]), reads=x32bs)
                else:
                    P.dma("sp", lambda h: h.dma_start(out=dst_v[:, :, t0:t0 + nt], in_=x32[:, :, :nt]),
                          reads=x32bs, writes=[dst_b[gi]])
                if xg and gi < TPC // 512:
                    P.dma("sp", lambda h: h.dma_start(
                        out=io["XGinp"][gi].rearrange("(k p) n -> p k n", p=128), in_=xb[:, :, :nt]),
                        reads=xbbs, writes=io["XGinpb"][gi])
                    gather_one(P, io, "XG", gi)
            pending.append(stores)
        while pending:
            pending.pop(0)()
        P.barrier()
        P.emit()


RG_ = [[0, 1, 2, 3], [4, 5, 6, 7]]
PIECE = 2048


def gather_one(P, io, name, j):
    inp, outp = io[name + "inp"][j], io[name + "p"][j]
    P.coll(lambda h: h.collective_compute("AllGather", ALU.bypass, replica_groups=RG_, ins=[inp.opt()],
                                          outs=[outp.opt()]),
           reads=io[name + "inpb"][j], writes=[io[name + "pb"][j]])


def gather_pieces(P, io, name):
    for j in range(len(io[name + "p"])):
        gather_one(P, io, name, j)


class ProjPass(LNBase):
    def __init__(self, P, consts, nw):
        self._ln_alloc(P, consts)
        self.x32 = Rot(P, 2, [128, 8, 512], F32, "px32_")
        self.xb = Rot(P, 2, [128, 8, 512], BF16, "pxb_")
        self.A = Rot(P, 2, [128, 8, 512], BF16, "pA_")
        self.Bq = Rot(P, 4, [128, 8, 512], BF16, "pB_")
        self.w = [P.sb([128, 8, 1024], BF16, f"pw{i}") for i in range(nw)]
        self.wb = [Buf(f"pw{i}") for i in range(nw)]
        self.wst = Rot(P, 2, [128, 2, 1024], F32, "pwst_")
        self.po = Rot(P, 2, [128, 512], F32, "ppo_", psum=True)
        self.pg = Rot(P, 2, [128, 512], F32, "ppg_", psum=True)
        self.rm = P.sb([128, 4], F32, "rmask")
        self.rmb = Buf("rmask")
        self.tmp = Rot(P, 4, [128, 512], F32, "ptmp_")
        self.xbufs = {}

    def bufs(self, x32b0):
        if id(x32b0) not in self.xbufs:
            self.xbufs[id(x32b0)] = ([Buf(f"px32m{m}") for m in range(8)], [Buf(f"pxbm{m}") for m in range(8)])
        return self.xbufs[id(x32b0)]

    def ln_now(self, x32, x32bs, xb, xbbs, nt, g_ap, b_ap):
        self.ln_stats(x32, x32bs, nt, LN_EPS / (ALPHA * ALPHA))
        for it in self.norm_items(x32, x32bs, xb, xbbs, nt, g_ap, b_ap):
            it()

    def load_w(self, i, ap):
        P = self.P
        for q in range(4):
            st, stb = self.wst.next()
            P.dma("act", lambda h, st=st, q=q: h.dma_start(
                out=st[:, :, :], in_=ap.rearrange("(k p) c -> p k c", p=128)[:, 2 * q:2 * q + 2, :]), writes=[stb])
            P.op("pool", lambda h, st=st, q=q: h.tensor_copy(out=self.w[i][:, 2 * q:2 * q + 2, :], in_=st[:, :, :]),
                 reads=[stb], writes=[self.wb[i]])

    def combine(self, gathered, gb, t0, nt, A, Ab):
        P = self.P
        for q in range(4):
            Bt, Bb = self.Bq.next()
            tok = q * TPC + t0
            pj, pc = tok // PIECE, tok % PIECE
            P.dma("sp", lambda h, Bt=Bt, pj=pj, pc=pc: h.dma_start(
                out=Bt[:, :, :nt], in_=gathered[pj].rearrange("(k p) n -> p k n", p=128)[:, :, pc:pc + nt]),
                reads=[gb[pj]], writes=[Bb])
            if q == 0:
                P.op("pool", lambda h, Bt=Bt: h.tensor_scalar(out=A[:, :, :nt], in0=Bt[:, :, :nt],
                                                              scalar1=self.rm[:, 0:1], scalar2=None, op0=ALU.mult),
                     reads=[Bb, self.rmb], writes=[Ab])
            else:
                P.op("dve", lambda h, Bt=Bt, q=q: h.scalar_tensor_tensor(out=A[:, :, :nt], in0=Bt[:, :, :nt],
                                                                        scalar=self.rm[:, q:q + 1], in1=A[:, :, :nt],
                                                                        op0=ALU.mult, op1=ALU.add),
                     reads=[Bb, self.rmb, Ab], writes=[Ab])


def proj_pass_even(P, C, io, consts, lnp_sb, src_v, src_b, dst_v, dst_b):
    from contextlib import ExitStack
    with ExitStack() as ph:
        P.stack = ph
        T = ProjPass(P, consts, 1)
        P.dma("sp", lambda h: h.dma_start(out=T.rm[:, :], in_=io["rmask"][:, :]), writes=[T.rmb])
        T.load_w(0, io["even_w_out"])
        perm = [0, 4, 1, 5, 2, 6, 3, 7]
        for gi, (t0, nt) in enumerate(groups_()):
            x32, x32b0 = T.x32.next()
            xb, xbb = T.xb.next()
            x32bs, xbbs = T.bufs(x32b0)
            A, Ab = T.A.next()
            P.dma("sp", lambda h, x32=x32, t0=t0, nt=nt: h.dma_start(out=x32[:, :, :nt], in_=src_v[:, :, t0:t0 + nt]),
                  reads=[src_b[gi]], writes=x32bs)
            if gi < TPC // 512:
                T.combine(io["MXp"], io["MXpb"], t0, nt, A, Ab)
                kmap = perm
            else:
                P.dma("sp", lambda h, A=A, nt=nt: h.dma_start(out=A[:, :, :nt],
                                                              in_=io["SMX"].rearrange("(k p) n -> p k n", p=128)),
                      reads=[io["SMXb"]], writes=[Ab])
                kmap = list(range(8))
            for m in range(8):
                po, pob = T.po.next()
                for kt in range(8):
                    P.op("pe", lambda h, po=po, kt=kt, m=m, A=A, kmap=kmap, nt=nt: h.matmul(
                        po[:, :nt], lhsT=T.w[0][:, kmap[kt], m * 128:(m + 1) * 128], rhs=A[:, kt, :nt],
                        start=(kt == 0), stop=(kt == 7)), reads=[T.wb[0], Ab], writes=[pob])
                P.op("dve", lambda h, po=po, m=m, x32=x32, nt=nt: h.scalar_tensor_tensor(
                    out=x32[:, m, :nt], in0=po[:, :nt], scalar=1.0 / ALPHA, in1=x32[:, m, :nt], op0=ALU.mult,
                    op1=ALU.add), reads=[pob, x32bs[m]], writes=[x32bs[m]])
            T.ln_now(x32, x32bs, xb, xbbs, nt, lnp_sb[:, 0, 1, :], lnp_sb[:, 1, 1, :])
            P.dma("sp", lambda h, x32=x32, t0=t0, nt=nt: h.dma_start(out=dst_v[:, :, t0:t0 + nt], in_=x32[:, :, :nt]),
                  reads=x32bs, writes=[dst_b[gi]])
        P.barrier()
        P.emit()


def proj_pass_odd(P, C, io, consts, lnp_sb, src_v, src_b, dst_v, dst_b):
    from contextlib import ExitStack
    with ExitStack() as ph:
        P.stack = ph
        T = ProjPass(P, consts, 2)
        P.dma("sp", lambda h: h.dma_start(out=T.rm[:, :], in_=io["rmask"][:, :]), writes=[T.rmb])
        T.load_w(0, io["glu_w"])
        T.load_w(1, io["odd_w_out"])
        gbias = P.sb([128, 8], F32, "glub")
        gbb = Buf("glub")
        P.dma("sp", lambda h: h.dma_start(out=gbias[:, :], in_=io["glu_b"][:, :]), writes=[gbb])
        zz32 = Rot(P, 1, [128, 8, 512], F32, "zz32_")
        zzb = Rot(P, 2, [128, 8, 512], BF16, "zzb_")
        sg = Rot(P, 2, [128, 512], F32, "sg_")
        for gi, (t0, nt) in enumerate(groups_()):
            x32, x32b0 = T.x32.next()
            xb, xbb = T.xb.next()
            x32bs, xbbs = T.bufs(x32b0)
            A, Ab = T.A.next()
            P.dma("sp", lambda h, x32=x32, t0=t0, nt=nt: h.dma_start(out=x32[:, :, :nt], in_=src_v[:, :, t0:t0 + nt]),
                  reads=[src_b[gi]], writes=x32bs)
            if gi < TPC // 512:
                T.combine(io["YSp"], io["YSpb"], t0, nt, A, Ab)
            else:
                P.dma("sp", lambda h, A=A, nt=nt: h.dma_start(out=A[:, :, :nt],
                                                              in_=io["SYS"].rearrange("(k p) n -> p k n", p=128)),
                      reads=[io["SYSb"]], writes=[Ab])
            z32, z32b = zz32.next()
            zb, zbb = zzb.next()
            P.op("act", lambda h, A=A, z32=z32, nt=nt: h.activation(out=z32[:, :, :nt], in_=A[:, :, :nt], func=AF.Square),
                 reads=[Ab], writes=[z32b])
            P.op("dve", lambda h, z32=z32, nt=nt: h.tensor_scalar(out=z32[:, :, :nt], in0=z32[:, :, :nt], scalar1=0.044715,
                                                                scalar2=1.0, op0=ALU.mult, op1=ALU.add),
                 reads=[z32b], writes=[z32b])
            P.op("dve", lambda h, A=A, z32=z32, nt=nt: h.tensor_tensor(out=z32[:, :, :nt], in0=z32[:, :, :nt],
                                                                      in1=A[:, :, :nt], op=ALU.mult),
                 reads=[z32b, Ab], writes=[z32b])
            P.op("act", lambda h, z32=z32, nt=nt: h.activation(out=z32[:, :, :nt], in_=z32[:, :, :nt], func=AF.Sigmoid,
                                                              scale=1.5957691216057308), reads=[z32b], writes=[z32b])
            P.op("dve", lambda h, A=A, z32=z32, nt=nt: h.tensor_tensor(out=z32[:, :, :nt], in0=z32[:, :, :nt],
                                                                      in1=A[:, :, :nt], op=ALU.mult),
                 reads=[z32b, Ab], writes=[z32b])
            P.op("pool", lambda h, zb=zb, z32=z32, nt=nt: h.tensor_copy(out=zb[:, :, :nt], in_=z32[:, :, :nt]),
                 reads=[z32b], writes=[zbb])
            for m in range(8):
                pg, pgb = T.pg.next()
                for kt in range(8):
                    P.op("pe", lambda h, pg=pg, kt=kt, m=m, zb=zb, nt=nt: h.matmul(
                        pg[:, :nt], lhsT=T.w[0][:, kt, m * 128:(m + 1) * 128], rhs=zb[:, kt, :nt],
                        start=(kt == 0), stop=(kt == 7)), reads=[T.wb[0], zbb], writes=[pgb])
                s_, sb_ = sg.next()
                P.op("act", lambda h, pg=pg, s_=s_, m=m, nt=nt: h.activation(out=s_[:, :nt], in_=pg[:, :nt],
                                                                            func=AF.Sigmoid, bias=gbias[:, m:m + 1]),
                     reads=[pgb, gbb], writes=[sb_])
                P.op("dve", lambda h, s_=s_, m=m, A=A, z32=z32, nt=nt: h.tensor_tensor(
                    out=A[:, m, :nt], in0=z32[:, m, :nt], in1=s_[:, :nt], op=ALU.mult), reads=[z32b, sb_, Ab],
                    writes=[Ab])
            for m in range(8):
                po, pob = T.po.next()
                for kt in range(8):
                    P.op("pe", lambda h, po=po, kt=kt, m=m, A=A, nt=nt: h.matmul(
                        po[:, :nt], lhsT=T.w[1][:, kt, m * 128:(m + 1) * 128], rhs=A[:, kt, :nt],
                        start=(kt == 0), stop=(kt == 7)), reads=[T.wb[1], Ab], writes=[pob])
                P.op("dve", lambda h, po=po, m=m, x32=x32, nt=nt: h.scalar_tensor_tensor(
                    out=x32[:, m, :nt], in0=po[:, :nt], scalar=1.0 / ALPHA, in1=x32[:, m, :nt], op0=ALU.mult,
                    op1=ALU.add), reads=[pob, x32bs[m]], writes=[x32bs[m]])
            T.ln_now(x32, x32bs, xb, xbbs, nt, lnp_sb[:, 0, 4, :], lnp_sb[:, 1, 4, :])
            P.dma("sp", lambda h, x32=x32, t0=t0, nt=nt: h.dma_start(out=dst_v[:, :, t0:t0 + nt], in_=x32[:, :, :nt]),
                  reads=x32bs, writes=[dst_b[gi]])
        P.barrier()
        P.emit()


class S5:
    def __init__(self, P, C, NSt, nct, NB, banks):
        self.P, self.C, self.NSt, self.nct, self.NB, self.B = P, C, NSt, nct, NB, banks
        sb = P.sb
        self.wu = sb([128, 8, nct * 128], BF16, "s5wu")
        self.wub = Buf("s5wu")
        self.par = sb([128, 3, NSt], F32, "s5par")
        self.BB = [sb([128, nct, 128], BF16, f"s5BB{i}") for i in range(2)]
        self.CC = [sb([128, NSt, 32], BF16, f"s5CC{i}") for i in range(2)]
        self.dv = sb([128, nct], F32, "s5d")
        self.pb = Buf("s5params")
        self.tab = {k: sb([128, NSt, NB], F32, "s5t_" + k) for k in ("Er", "Ei", "PRr", "PRi")}
        self.tabb = Buf("s5tab")
        self.sm = {k: sb([128, NSt], F32, "s5s_" + k) for k in
                   ("dl", "rho", "th", "t1", "t2", "sn", "cs", "ckr", "cki", "kr", "ki", "nr", "ni", "den",
                    "inr", "ini", "wlr", "wli", "hr", "hi")}
        self.smb = Buf("s5small")
        self.initb = Buf("s5init")
        self.wlb = Buf("s5wl")
        self.tmp = [sb([128, NSt, max(NB // 2, 1)], F32, f"s5tmp{i}") for i in range(2)]
        self.tmpb = Buf("s5tmp")
        self.u32 = sb([128, nct, NB], F32, "s5u32")
        self.ub = sb([128, nct, NB], BF16, "s5ub")
        self.ubuf = Buf("s5u")
        self.t = Rot(P, 16, [128, NB], F32, "s5w_")
        self.bp = Rot(P, 8, [128, NB], F32, "s5bp_")
        self.wv = Rot(P, 8, [128, NB], F32, "s5wv_")
        self.xv = Rot(P, 8, [128, NB], BF16, "s5xv_")
        self.yo = Rot(P, 2, [128, NB], BF16, "s5yo_")
        self.rot = 0

    def load(self, wu_ap, par_ap, bbr_ap, bbi_ap, ccr_ap, cci_ap, d_ap, h0=None):
        P, NSt, nct, NB = self.P, self.NSt, self.nct, self.NB
        wst = Rot(P, 2, [128, nct * 128], F32, "s5wst_")
        for kt in range(8):
            st, stb = wst.next()
            P.dma("act", lambda h, st=st, kt=kt: h.dma_start(out=st[:, :], in_=wu_ap[:, kt * nct * 128:(kt + 1) * nct * 128]),
                  writes=[stb])
            P.op("pool", lambda h, st=st, kt=kt: h.tensor_copy(out=self.wu[:, kt, :], in_=st[:, :]), reads=[stb],
                 writes=[self.wub])
        P.dma("sp", lambda h: h.dma_start(out=self.par[:, :, :], in_=par_ap), writes=[self.pb])
        P.dma("sp", lambda h: h.dma_start(out=self.dv[:, :], in_=d_ap), writes=[self.pb])
        bst = P.sb([128, nct, 128], F32, "s5bst")
        cst_ = P.sb([128, NSt, 32], F32, "s5cst")
        for i, ap in enumerate((bbr_ap, bbi_ap)):
            P.dma("sp", lambda h, ap=ap: h.dma_start(out=bst[:, :, :], in_=ap), writes=[self.pb])
            P.op("pool", lambda h, i=i: h.tensor_copy(out=self.BB[i][:, :, :], in_=bst[:, :, :]), reads=[self.pb],
                 writes=[self.pb])
        for i, ap in enumerate((ccr_ap, cci_ap)):
            P.dma("sp", lambda h, ap=ap: h.dma_start(out=cst_[:, :, :], in_=ap), writes=[self.pb])
            if i == 0:
                P.op("pool", lambda h: h.tensor_copy(out=self.CC[0][:, :, :], in_=cst_[:, :, :]), reads=[self.pb],
                     writes=[self.pb])
            else:
                P.op("pool", lambda h: h.tensor_scalar(out=self.CC[1][:, :, :], in0=cst_[:, :, :], scalar1=-1.0,
                                                       scalar2=None, op0=ALU.mult), reads=[self.pb], writes=[self.pb])
        sm, smb = self.sm, self.smb
        lre, lim, lst = self.par[:, 0, :], self.par[:, 1, :], self.par[:, 2, :]
        PI = float(np.pi)

        def v(fn, reads=(), writes=()):
            P.op("dve", fn, reads=[self.pb, smb] + list(reads), writes=[smb] + list(writes))

        P.op("act", lambda h: h.activation(out=sm["dl"][:, :], in_=lst, func=AF.Exp), reads=[self.pb], writes=[smb])
        v(lambda h: h.tensor_tensor(out=sm["t1"][:, :], in0=sm["dl"][:, :], in1=lre, op=ALU.mult))
        P.op("act", lambda h: h.activation(out=sm["rho"][:, :], in_=sm["t1"][:, :], func=AF.Exp), reads=[smb], writes=[smb])
        v(lambda h: h.tensor_tensor(out=sm["th"][:, :], in0=sm["dl"][:, :], in1=lim, op=ALU.mult))
        for (dst, shift) in (("sn", 0.0), ("cs", PI / 2)):
            v(lambda h, shift=shift: h.tensor_scalar(out=sm["t1"][:, :], in0=sm["th"][:, :], scalar1=shift, scalar2=None,
                                                     op0=ALU.add))
            v(lambda h: h.tensor_copy(out=sm["t2"][:, :], in_=sm["t1"][:, :]))
            for thr in (PI, 3 * PI, 5 * PI, 7 * PI):
                v(lambda h, thr=thr: h.tensor_scalar(out=sm["nr"][:, :], in0=sm["t1"][:, :], scalar1=thr,
                                                     scalar2=-2.0 * PI, op0=ALU.is_gt, op1=ALU.mult))
                v(lambda h: h.tensor_tensor(out=sm["t2"][:, :], in0=sm["t2"][:, :], in1=sm["nr"][:, :], op=ALU.add))
            P.op("act", lambda h, dst=dst: h.activation(out=sm[dst][:, :], in_=sm["t2"][:, :], func=AF.Sin), reads=[smb],
                 writes=[smb])
        v(lambda h: h.tensor_tensor(out=sm["nr"][:, :], in0=sm["rho"][:, :], in1=sm["cs"][:, :], op=ALU.mult))
        v(lambda h: h.tensor_scalar(out=sm["nr"][:, :], in0=sm["nr"][:, :], scalar1=-1.0, scalar2=None, op0=ALU.add))
        v(lambda h: h.tensor_tensor(out=sm["ni"][:, :], in0=sm["rho"][:, :], in1=sm["sn"][:, :], op=ALU.mult))
        v(lambda h: h.tensor_tensor(out=sm["den"][:, :], in0=lre, in1=lre, op=ALU.mult))
        v(lambda h: h.tensor_tensor(out=sm["t1"][:, :], in0=lim, in1=lim, op=ALU.mult))
        v(lambda h: h.tensor_tensor(out=sm["den"][:, :], in0=sm["den"][:, :], in1=sm["t1"][:, :], op=ALU.add))
        v(lambda h: h.reciprocal(out=sm["den"][:, :], in_=sm["den"][:, :]))
        v(lambda h: h.tensor_tensor(out=sm["t1"][:, :], in0=sm["nr"][:, :], in1=lre, op=ALU.mult))
        v(lambda h: h.tensor_tensor(out=sm["t2"][:, :], in0=sm["ni"][:, :], in1=lim, op=ALU.mult))
        v(lambda h: h.tensor_tensor(out=sm["kr"][:, :], in0=sm["t1"][:, :], in1=sm["t2"][:, :], op=ALU.add))
        v(lambda h: h.tensor_tensor(out=sm["kr"][:, :], in0=sm["kr"][:, :], in1=sm["den"][:, :], op=ALU.mult))
        v(lambda h: h.tensor_tensor(out=sm["t1"][:, :], in0=sm["ni"][:, :], in1=lre, op=ALU.mult))
        v(lambda h: h.tensor_tensor(out=sm["t2"][:, :], in0=sm["nr"][:, :], in1=lim, op=ALU.mult))
        v(lambda h: h.tensor_tensor(out=sm["ki"][:, :], in0=sm["t1"][:, :], in1=sm["t2"][:, :], op=ALU.subtract))
        v(lambda h: h.tensor_tensor(out=sm["ki"][:, :], in0=sm["ki"][:, :], in1=sm["den"][:, :], op=ALU.mult))
        v(lambda h: h.tensor_copy(out=sm["ckr"][:, :], in_=sm["cs"][:, :]))
        v(lambda h: h.tensor_copy(out=sm["cki"][:, :], in_=sm["sn"][:, :]))
        Er, Ei, PRr, PRi = (self.tab[k] for k in ("Er", "Ei", "PRr", "PRi"))
        tb = self.tabb
        t0_, t1_ = self.tmp

        def bc(name, w):
            return sm[name][:, :].unsqueeze(2).to_broadcast([128, NSt, w])

        def tv(fn):
            P.op("dve", fn, reads=[smb, tb, self.tmpb], writes=[tb, self.tmpb])

        tv(lambda h: h.memset(Er[:, :, 0:1], 1.0))
        tv(lambda h: h.memset(Ei[:, :, 0:1], 0.0))
        w = 1
        while w < NB:
            tv(lambda h, w=w: h.tensor_tensor(out=t0_[:, :, 0:w], in0=Er[:, :, 0:w], in1=bc("ckr", w), op=ALU.mult))
            tv(lambda h, w=w: h.tensor_tensor(out=t1_[:, :, 0:w], in0=Ei[:, :, 0:w], in1=bc("cki", w), op=ALU.mult))
            tv(lambda h, w=w: h.tensor_tensor(out=Er[:, :, w:2 * w], in0=t0_[:, :, 0:w], in1=t1_[:, :, 0:w],
                                              op=ALU.subtract))
            tv(lambda h, w=w: h.tensor_tensor(out=t0_[:, :, 0:w], in0=Er[:, :, 0:w], in1=bc("cki", w), op=ALU.mult))
            tv(lambda h, w=w: h.tensor_tensor(out=t1_[:, :, 0:w], in0=Ei[:, :, 0:w], in1=bc("ckr", w), op=ALU.mult))
            tv(lambda h, w=w: h.tensor_tensor(out=Ei[:, :, w:2 * w], in0=t0_[:, :, 0:w], in1=t1_[:, :, 0:w], op=ALU.add))
            v(lambda h: h.tensor_tensor(out=sm["t1"][:, :], in0=sm["ckr"][:, :], in1=sm["ckr"][:, :], op=ALU.mult))
            v(lambda h: h.tensor_tensor(out=sm["t2"][:, :], in0=sm["cki"][:, :], in1=sm["cki"][:, :], op=ALU.mult))
            v(lambda h: h.tensor_tensor(out=sm["cki"][:, :], in0=sm["ckr"][:, :], in1=sm["cki"][:, :], op=ALU.mult))
            v(lambda h: h.tensor_scalar(out=sm["cki"][:, :], in0=sm["cki"][:, :], scalar1=2.0, scalar2=None, op0=ALU.mult))
            v(lambda h: h.tensor_tensor(out=sm["ckr"][:, :], in0=sm["t1"][:, :], in1=sm["t2"][:, :], op=ALU.subtract))
            w *= 2
        hw_ = max(NB // 2, 1)
        for lo in range(0, NB, hw_):
            sl = slice(lo, lo + hw_)
            tv(lambda h, sl=sl: h.tensor_tensor(out=t0_[:, :, 0:hw_], in0=Er[:, :, sl], in1=bc("kr", hw_), op=ALU.mult))
            tv(lambda h, sl=sl: h.tensor_tensor(out=t1_[:, :, 0:hw_], in0=Ei[:, :, sl], in1=bc("ki", hw_), op=ALU.mult))
            tv(lambda h, sl=sl: h.tensor_tensor(out=PRr[:, :, sl], in0=t0_[:, :, 0:hw_], in1=t1_[:, :, 0:hw_], op=ALU.add))
            tv(lambda h, sl=sl: h.tensor_tensor(out=t0_[:, :, 0:hw_], in0=Er[:, :, sl], in1=bc("ki", hw_), op=ALU.mult))
            tv(lambda h, sl=sl: h.tensor_tensor(out=t1_[:, :, 0:hw_], in0=Ei[:, :, sl], in1=bc("kr", hw_), op=ALU.mult))
            tv(lambda h, sl=sl: h.tensor_tensor(out=PRi[:, :, sl], in0=t0_[:, :, 0:hw_], in1=t1_[:, :, 0:hw_],
                                                op=ALU.subtract))
        ib = self.initb
        if h0 is None:
            P.op("dve", lambda h: h.memset(sm["inr"][:, :], 0.0), writes=[ib])
            P.op("dve", lambda h: h.memset(sm["ini"][:, :], 0.0), writes=[ib])
        else:
            P.dma("sp", lambda h: h.dma_start(out=sm["hr"][:, :], in_=h0[0]), writes=[ib])
            P.dma("sp", lambda h: h.dma_start(out=sm["hi"][:, :], in_=h0[1]), writes=[ib])
            self.cmul(sm["inr"], sm["ini"], sm["hr"], sm["hi"], sm["cs"], sm["sn"], 1.0, [ib, smb], [ib])

    def cmul(self, outr, outi, ar, ai, br, bi, sgn, reads, writes):
        P, sm, smb = self.P, self.sm, self.smb

        def v(fn):
            P.op("dve", fn, reads=list(reads) + [smb], writes=list(writes) + [smb])
        v(lambda h: h.tensor_tensor(out=sm["t1"][:, :], in0=ar[:, :], in1=br[:, :], op=ALU.mult))
        v(lambda h: h.tensor_tensor(out=sm["t2"][:, :], in0=ai[:, :], in1=bi[:, :], op=ALU.mult))
        v(lambda h: h.tensor_tensor(out=sm["nr"][:, :], in0=ar[:, :], in1=bi[:, :], op=ALU.mult))
        v(lambda h: h.tensor_tensor(out=sm["ni"][:, :], in0=ai[:, :], in1=br[:, :], op=ALU.mult))
        if sgn > 0:
            v(lambda h: h.tensor_tensor(out=outr[:, :], in0=sm["t1"][:, :], in1=sm["t2"][:, :], op=ALU.subtract))
            v(lambda h: h.tensor_tensor(out=outi[:, :], in0=sm["nr"][:, :], in1=sm["ni"][:, :], op=ALU.add))
        else:
            v(lambda h: h.tensor_tensor(out=outr[:, :], in0=sm["t1"][:, :], in1=sm["t2"][:, :], op=ALU.add))
            v(lambda h: h.tensor_tensor(out=outi[:, :], in0=sm["ni"][:, :], in1=sm["nr"][:, :], op=ALU.subtract))

    def gb(self):
        it = self.B[self.rot % 4]
        self.rot += 1
        return it

    def block(self, x, xbuf, store_y):
        P, C, NSt, nct, n = self.P, self.C, self.NSt, self.nct, self.NB
        sm, smb = self.sm, self.smb
        Er, Ei, PRr, PRi = (self.tab[k] for k in ("Er", "Ei", "PRr", "PRi"))
        for ct in range(nct):
            pu, pub = self.gb()
            for kt in range(8):
                P.op("pe", lambda h, pu=pu, kt=kt, ct=ct: h.matmul(pu[:, :n], lhsT=self.wu[:, kt, ct * 128:(ct + 1) * 128],
                                                                 rhs=x[:, kt, :n], start=(kt == 0), stop=(kt == 7)),
                     reads=[self.wub, xbuf], writes=[pub])
            P.op("act", lambda h, pu=pu, ct=ct: h.activation(out=self.u32[:, ct, :], in_=pu[:, :n], func=AF.Copy),
                 reads=[pub], writes=[self.ubuf])
            P.op("pool", lambda h, ct=ct: h.tensor_copy(out=self.ub[:, ct, :], in_=self.u32[:, ct, :]), reads=[self.ubuf],
                 writes=[self.ubuf])
        for ct in range(nct):
            py, pyb = self.B[4 + ct % 2]
            for j in range(4):
                m = ct * 4 + j
                pr, prb = self.gb()
                pi_, pib = self.gb()
                P.op("pe", lambda h, pr=pr, j=j, ct=ct: h.matmul(pr[:, :n], lhsT=self.BB[0][j * 32:(j + 1) * 32, ct, :],
                                                                rhs=self.ub[j * 32:(j + 1) * 32, ct, :], start=True,
                                                                stop=True, tile_position=(j * 32, 0)),
                     reads=[self.pb, self.ubuf], writes=[prb])
                P.op("pe", lambda h, pi_=pi_, j=j, ct=ct: h.matmul(pi_[:, :n], lhsT=self.BB[1][j * 32:(j + 1) * 32, ct, :],
                                                                  rhs=self.ub[j * 32:(j + 1) * 32, ct, :], start=True,
                                                                  stop=True, tile_position=(j * 32, 0)),
                     reads=[self.pb, self.ubuf], writes=[pib])
                ts = [self.t.next() for _ in range(4)]
                for (tt, ttb), (src, srcb), tabn in zip(ts, ((pr, prb), (pi_, pib), (pr, prb), (pi_, pib)),
                                                       (PRr, PRi, PRi, PRr)):
                    P.op("dve", lambda h, tt=tt, src=src, tabn=tabn, m=m: h.tensor_tensor(
                        out=tt[:, :], in0=src[:, :n], in1=tabn[:, m, :], op=ALU.mult), reads=[srcb, self.tabb],
                        writes=[ttb])
                (bpr, bprb), (bpi, bpib) = self.bp.next(), self.bp.next()
                P.op("pool", lambda h, bpr=bpr, a=ts[0][0], b=ts[1][0]: h.tensor_tensor(out=bpr[:, :], in0=a[:, :], in1=b[:, :],
                                                                                       op=ALU.subtract),
                     reads=[ts[0][1], ts[1][1]], writes=[bprb])
                P.op("dve", lambda h, bpi=bpi, a=ts[2][0], b=ts[3][0]: h.tensor_tensor(out=bpi[:, :], in0=a[:, :], in1=b[:, :],
                                                                                      op=ALU.add),
                     reads=[ts[2][1], ts[3][1]], writes=[bpib])
                (wr, wrb), (wi, wib) = self.wv.next(), self.wv.next()
                rho_bc = sm["rho"][:, m:m + 1].to_broadcast([128, n])
                P.op("dve", lambda h, wr=wr, bpr=bpr, m=m, rho_bc=rho_bc: h.tensor_tensor_scan(
                    out=wr[:, :], data0=rho_bc, data1=bpr[:, :], initial=sm["inr"][:, m:m + 1], op0=ALU.mult,
                    op1=ALU.add), reads=[bprb, smb, self.initb], writes=[wrb])
                P.op("dve", lambda h, wi=wi, bpi=bpi, m=m, rho_bc=rho_bc: h.tensor_tensor_scan(
                    out=wi[:, :], data0=rho_bc, data1=bpi[:, :], initial=sm["ini"][:, m:m + 1], op0=ALU.mult,
                    op1=ALU.add), reads=[bpib, smb, self.initb], writes=[wib])
                P.op("act", lambda h, wr=wr, m=m: h.activation(out=sm["wlr"][:, m:m + 1], in_=wr[:, n - 1:n], func=AF.Copy),
                     reads=[wrb], writes=[self.wlb])
                P.op("act", lambda h, wi=wi, m=m: h.activation(out=sm["wli"][:, m:m + 1], in_=wi[:, n - 1:n], func=AF.Copy),
                     reads=[wib], writes=[self.wlb])
                ts = [self.t.next() for _ in range(4)]
                for ii, ((tt, ttb), (src, srcb), tabn) in enumerate(zip(ts, ((wr, wrb), (wi, wib), (wr, wrb), (wi, wib)),
                                                                     (Er, Ei, Ei, Er))):
                    P.op("dve" if ii < 2 else "pool", lambda h, tt=tt, src=src, tabn=tabn, m=m: h.tensor_tensor(
                        out=tt[:, :], in0=src[:, :], in1=tabn[:, m, :], op=ALU.mult), reads=[srcb, self.tabb],
                        writes=[ttb])
                (xr, xrb), (xi, xib) = self.xv.next(), self.xv.next()
                P.op("pool", lambda h, xr=xr, a=ts[0][0], b=ts[1][0]: h.tensor_tensor(out=xr[:, :], in0=a[:, :], in1=b[:, :],
                                                                                     op=ALU.subtract),
                     reads=[ts[0][1], ts[1][1]], writes=[xrb])
                P.op("pool", lambda h, xi=xi, a=ts[2][0], b=ts[3][0]: h.tensor_tensor(out=xi[:, :], in0=a[:, :], in1=b[:, :],
                                                                                     op=ALU.add),
                     reads=[ts[2][1], ts[3][1]], writes=[xib])
                P.op("pe", lambda h, xr=xr, m=m, j=j: h.matmul(py[j * 32:(j + 1) * 32, :n], lhsT=self.CC[0][:, m, :],
                                                              rhs=xr[:, :], start=True, stop=False,
                                                              tile_position=(0, j * 32)),
                     reads=[self.pb, xrb], writes=[pyb])
                P.op("pe", lambda h, xi=xi, m=m, j=j: h.matmul(py[j * 32:(j + 1) * 32, :n], lhsT=self.CC[1][:, m, :],
                                                              rhs=xi[:, :], start=False, stop=True,
                                                              tile_position=(0, j * 32)),
                     reads=[self.pb, xib], writes=[pyb])
            yo, yob = self.yo.next()
            P.op("dve", lambda h, yo=yo, ct=ct: h.scalar_tensor_tensor(out=yo[:, :], in0=self.u32[:, ct, :],
                                                                      scalar=self.dv[:, ct:ct + 1], in1=py[:, :n],
                                                                      op0=ALU.mult, op1=ALU.add),
                 reads=[self.ubuf, self.pb, pyb], writes=[yob])
            store_y(ct, yo, yob)
        self.cmul(sm["inr"], sm["ini"], sm["wlr"], sm["wli"], sm["ckr"], sm["cki"], 1.0, [self.wlb, self.initb],
                  [self.initb])

    def final_state(self, out_r, out_i):
        P, sm = self.P, self.sm
        self.cmul(sm["hr"], sm["hi"], sm["inr"], sm["ini"], sm["cs"], sm["sn"], -1.0, [self.initb], [self.initb])
        P.dma("sp", lambda h: h.dma_start(out=out_r, in_=sm["hr"][:, :]), reads=[self.initb, self.smb])
        P.dma("sp", lambda h: h.dma_start(out=out_i, in_=sm["hi"][:, :]), reads=[self.initb, self.smb])


def phase_s5_prompt(P, C, io):
    from contextlib import ExitStack
    with ExitStack() as ph:
        P.stack = ph
        banks = [(P.ps([128, 512], F32, f"s5bank{i}"), Buf(f"s5bank{i}", excl=True)) for i in range(6)]
        S = S5(P, C, 8, 2, 512, banks)
        S.load(io["s5wu"], io["s5par"], io["s5bbr"], io["s5bbi"], io["s5ccr"], io["s5cci"], io["s5d"])
        xb = Rot(P, 2, [128, 8, 512], BF16, "s5xb_")
        XG = None
        for tb in range(SEQ // 512):
            rk, cb = tb // (TPC // 512), (tb % (TPC // 512)) * 512
            x, xbuf = xb.next()
            P.dma("sp", lambda h, x=x, rk=rk, cb=cb: h.dma_start(
                out=x[:, :, :], in_=io["XGp"][cb // 512][rk * 1024:(rk + 1) * 1024, :].rearrange("(k p) n -> p k n", p=128)),
                reads=[io["XGpb"][cb // 512]], writes=[xbuf])

            def store(ct, yo, yob, tb=tb):
                P.dma("sp", lambda h: h.dma_start(
                    out=io["YSinp"][tb // 4][ct * 128:(ct + 1) * 128, (tb % 4) * 512:(tb % 4 + 1) * 512],
                    in_=yo[:, :]), reads=[yob], writes=[io["YSinb"][tb]])
            S.block(x, xbuf, store)
            if tb % (PIECE // 512) == PIECE // 512 - 1:
                gather_one(P, io, "YS", tb // (PIECE // 512))
        S.final_state(io["o_s5re"], io["o_s5im"])
        P.barrier()
        P.emit()


def phase_s5_sample(P, C, io):
    from contextlib import ExitStack
    n = NS
    with ExitStack() as ph:
        P.stack = ph
        banks = [(P.ps([128, 512], F32, f"s5sbank{i}"), Buf(f"s5sbank{i}", excl=True)) for i in range(6)]
        S = S5(P, C, 32, 8, n, banks)
        S.load(io["s5wu_s"], io["s5par_s"], io["s5bbr_s"], io["s5bbi_s"], io["s5ccr_s"], io["s5cci_s"], io["s5d_s"],
               h0=(io["s5h0r"], io["s5h0i"]))
        x32 = P.sb([128, 8, n], F32, "s5sx32")
        xs = P.sb([128, 8, n], BF16, "s5sxb")
        xsb = Buf("s5sxb")
        P.dma("sp", lambda h: h.dma_start(out=x32[:, :, :], in_=io["X4v"][:, :, TPC:TPC + n]), reads=[io["X4b"][-1]],
              writes=[xsb])
        P.op("act", lambda h: h.activation(out=xs[:, :, :], in_=x32[:, :, :], func=AF.Copy), reads=[xsb], writes=[xsb])

        def store(ct, yo, yob):
            P.dma("sp", lambda h: h.dma_start(out=io["SYS"][ct * 128:(ct + 1) * 128, :], in_=yo[:, :]), reads=[yob],
                  writes=[io["SYSb"]])
        S.block(xs, xsb, store)
        S.final_state(io["o_ss5re"], io["o_ss5im"])
        P.barrier()
        P.emit()


def build(stop="all"):
    from contextlib import ExitStack
    nc = bass.Bass("TRN2", target_bir_lowering=False)
    NTOK = TPC + NS
    io = {}

    def din(name, shape):
        io[name] = nc.dram_tensor(name, list(shape), F32, kind="ExternalInput").ap()
        return io[name]

    def dout(name, shape):
        io[name] = nc.dram_tensor(name, list(shape), F32, kind="ExternalOutput").ap()
        return io[name]

    xT = din("xT", [D, NTOK])
    din("w_in", [4, 11, 128, 4096])
    din("w_out", [4, 128, 22 * 1024])
    lnp = din("lnp", [128, 2, 6, 8])
    cst = din("cst", [128, NCST])
    din("rmask", [128, 4])
    din("wfox", [128, 8 * 386]); din("bfx", [2, 1])
    din("wgdn", [128, 8 * 514]); din("gcw", [128, 3, 4]); din("gsc", [1, 2]); din("gng", [128, 1])
    din("wfox_s", [128, 8 * 1544]); din("bf_s", [8, 1]); din("pastk", [8, 64, 1024]); din("pastv", [1024, 512])
    din("pastlf", [8, 1024]); din("wgdn_s", [4, 128, 8 * 514]); din("gcw_s", [4, 128, 3, 4]); din("gsc_s", [4, 1, 2])
    din("sconv", [4, 3, 128, 3]); din("sstate", [4, 128, 128])
    din("even_w_out", [D, D]); din("glu_w", [D, D]); din("odd_w_out", [D, D]); din("glu_b", [128, 8])
    din("s5wu", [128, 8 * 256]); din("s5par", [128, 3, 8]); din("s5bbr", [128, 2, 128]); din("s5bbi", [128, 2, 128])
    din("s5ccr", [128, 8, 32]); din("s5cci", [128, 8, 32]); din("s5d", [128, 2])
    din("s5wu_s", [128, 8 * 1024]); din("s5par_s", [128, 3, 32]); din("s5bbr_s", [128, 8, 128])
    din("s5bbi_s", [128, 8, 128]); din("s5ccr_s", [128, 32, 32]); din("s5cci_s", [128, 32, 32]); din("s5d_s", [128, 8])
    din("s5h0r", [128, 32]); din("s5h0i", [128, 32])
    dout("o_y", [D, TPC]); dout("o_ys", [D, NS])
    dout("o_logf", [2, SEQ]); dout("o_foxk", [128, SEQ]); dout("o_foxv", [SEQ, 128])
    dout("o_gstate", [128, 128]); dout("o_gconv", [3, 128, 3])
    dout("o_slogf", [8, NS]); dout("o_sfoxk", [512, NS]); dout("o_sfoxv", [NS, 512])
    dout("o_sgstate", [4, 128, 128]); dout("o_sgconv", [4, 3, 128, 3])
    dout("o_s5re", [128, 8]); dout("o_s5im", [128, 8]); dout("o_ss5re", [128, 32]); dout("o_ss5im", [128, 32])
    dbg = dout("dbg", [256, SEQ]) if (stop != "all" and not DEBUG.get("nodump")) else None
    XA = nc.dram_tensor("XA", [D, NTOK], F32).ap()
    XB = nc.dram_tensor("XB", [D, NTOK], F32).ap()
    io["WBF"] = [nc.dram_tensor(f"WBF{j}", [128, 4096], BF16).ap() for j in range(11)]
    io["WBFb"] = [Buf(f"WBF{j}") for j in range(11)]
    ngp = TPC // 512
    io["XGinp"] = [nc.dram_tensor(f"XGin{g}", [D, 512], BF16).ap() for g in range(ngp)]
    io["XGp"] = [nc.dram_tensor(f"XG{g}", [4 * D, 512], BF16).ap() for g in range(ngp)]
    io["XGinpb"] = [[Buf(f"XGin{g}")] for g in range(ngp)]
    io["XGpb"] = [Buf(f"XG{g}") for g in range(ngp)]
    npc = SEQ // PIECE
    for nm in ("MX", "YS"):
        io[nm + "inp"] = [nc.dram_tensor(f"{nm}in{j}", [256, PIECE], BF16).ap() for j in range(npc)]
        io[nm + "p"] = [nc.dram_tensor(f"{nm}{j}", [4 * 256, PIECE], BF16).ap() for j in range(npc)]
        io[nm + "inb"] = [Buf(f"{nm}in{i}") for i in range(SEQ // 512)]
        io[nm + "inpb"] = [io[nm + "inb"][j * (PIECE // 512):(j + 1) * (PIECE // 512)] for j in range(npc)]
        io[nm + "pb"] = [Buf(f"{nm}{j}") for j in range(npc)]
    io["SMX"] = nc.dram_tensor("SMX", [1024, NS], BF16).ap()
    io["SMXb"] = Buf("SMX")
    io["SYS"] = nc.dram_tensor("SYS", [1024, NS], BF16).ap()
    io["SYSb"] = Buf("SYS")
    ng = len(groups_())

    with ExitStack() as gstack:
        P = Prog(nc, gstack)
        C = Consts(P, nc, cst)
        lnp_sb = P.sb([128, 2, 6, 8], F32, "lnp_sb")
        P.dma("sp", lambda h: h.dma_start(out=lnp_sb[:, :, :, :], in_=lnp[:, :, :, :]), writes=[C.b])
        consts = {"constb": C.b, "ones_f32": C.f32[:, C_ONES:C_ONES + 128]}
        xTv = xT.rearrange("(k p) n -> p k n", p=128)
        XAv = XA.rearrange("(k p) n -> p k n", p=128)
        XBv = XB.rearrange("(k p) n -> p k n", p=128)
        XAb = [Buf(f"XA_{i}") for i in range(ng)]
        XBb = [Buf(f"XB_{i}") for i in range(ng)]
        io["X1v"], io["X1b"] = XAv, XAb
        io["X4v"], io["X4b"] = XBv, XBb
        xg = True

        def done():
            P.stack = gstack
            P.finish()
            P.emit()
            return nc

        def dump(src_ap, rows0, deps):
            if DEBUG.get("nodump"):
                return
            with ExitStack() as ph:
                P.stack = ph
                t16 = P.sb([128, 2048], BF16, "d16")
                t32 = P.sb([128, 2048], F32, "d32")
                tb_ = Buf("d")
                for i in range(SEQ // 2048):
                    P.dma("sp", lambda h, i=i: h.dma_start(out=t16[:, :], in_=src_ap[rows0:rows0 + 128, i * 2048:(i + 1) * 2048]),
                          reads=deps, writes=[tb_])
                    P.op("dve", lambda h: h.tensor_copy(out=t32[:, :], in_=t16[:, :]), reads=[tb_], writes=[tb_])
                    P.dma("sp", lambda h, i=i: h.dma_start(out=dbg[0:128, i * 2048:(i + 1) * 2048], in_=t32[:, :]),
                          reads=[tb_], writes=[tb_])
                P.barrier()
                P.emit()

        ffn_pass(P, C, io, consts, lnp_sb, 0, 0, xTv, None, XAv, XAb, xg=xg)
        if stop == "p0":
            return done()
        if not DEBUG.get("nofox"):
            phase_fox_prompt(P, C, io)
        if stop == "h0f":
            dump(io["MXinp"][0], 0, io["MXinb"])
            return done()
        if not DEBUG.get("nogdn"):
            phase_gdn_prompt(P, C, io)
        if stop == "h0g":
            dump(io["MXinp"][0], 128, io["MXinb"])
            return done()
        phase_sample_even(P, C, io)
        if stop == "h0s" and DEBUG.get("nodump"):
            return done()
        if stop == "h0s":
            with ExitStack() as ph:
                P.stack = ph
                t16 = P.sb([128, 8, NS], BF16, "d16")
                t32 = P.sb([128, 8, NS], F32, "d32")
                tb_ = Buf("d")
                P.dma("sp", lambda h: h.dma_start(out=t16[:, :, :], in_=io["SMX"].rearrange("(k p) n -> p k n", p=128)),
                      reads=[io["SMXb"]], writes=[tb_])
                P.op("dve", lambda h: h.tensor_copy(out=t32[:, :, :], in_=t16[:, :, :]), reads=[tb_], writes=[tb_])
                P.dma("sp", lambda h: h.dma_start(out=dbg[:, 0:8 * NS].rearrange("p (k n) -> p k n", n=NS)[0:128],
                                                  in_=t32[:, :, :]), reads=[tb_], writes=[tb_])
                P.barrier()
                P.emit()
            return done()
        proj_pass_even(P, C, io, consts, lnp_sb, XAv, XAb, XBv, XBb)
        if stop == "p1a":
            return done()
        ffn_pass(P, C, io, consts, lnp_sb, 1, 2, XBv, XBb, XAv, XAb)
        ffn_pass(P, C, io, consts, lnp_sb, 2, 3, XAv, XAb, XBv, XBb, xg=xg)
        phase_s5_prompt(P, C, io)
        if stop == "h1":
            return done()
        phase_s5_sample(P, C, io)
        if stop == "h1s":
            return done()
        proj_pass_odd(P, C, io, consts, lnp_sb, XBv, XBb, XAv, XAb)
        if stop == "p3a":
            return done()
        ffn_pass(P, C, io, consts, lnp_sb, 3, 5, XAv, XAb, None, None, out_final=(io["o_y"], io["o_ys"]))
        return done()


def _s5_layouts(inputs, groups):
    ng = len(groups)
    nst, nct = ng // 2, ng // 8
    lre, lim, lst = inputs["s5_lam_re"][0], inputs["s5_lam_im"][0], inputs["s5_log_step"][0]
    bre, bim = inputs["s5_b_re"][0], inputs["s5_b_im"][0]
    cre, cim = inputs["s5_c_re"][0], inputs["s5_c_im"][0]
    par = np.zeros((128, 3, nst), np.float32)
    bbr = np.zeros((128, nct, 128), np.float32); bbi = np.zeros((128, nct, 128), np.float32)
    ccr = np.zeros((128, nst, 32), np.float32); cci = np.zeros((128, nst, 32), np.float32)
    for m in range(nst):
        for hh in range(2):
            g = groups[2 * m + hh]
            sl = slice(hh * 64, (hh + 1) * 64)
            par[sl, 0, m] = lre[g]; par[sl, 1, m] = lim[g]; par[sl, 2, m] = lst[g]
            prow = (m % 4) * 32 + hh * 16
            bbr[prow:prow + 16, m // 4, sl] = bre[g].T
            bbi[prow:prow + 16, m // 4, sl] = bim[g].T
            ccr[sl, m, hh * 16:(hh + 1) * 16] = cre[g].T
            cci[sl, m, hh * 16:(hh + 1) * 16] = cim[g].T
    return par, bbr, bbi, ccr, cci


def _state_tiles(h):
    ng = h.shape[0]
    return np.ascontiguousarray(h.reshape(ng // 2, 128).T)


def host_prep(inputs, c):
    b, r = c // 4, c % 4
    xp = inputs["x_prompt"][b, r * TPC:(r + 1) * TPC, :]
    xs = inputs["x_sample"][c]
    xT = np.ascontiguousarray(np.concatenate([xp, xs], 0).T)
    ew = inputs["even_w_in"][0]
    cols = np.concatenate([np.arange(r * 128, (r + 1) * 128), 512 + np.arange(r * 128, (r + 1) * 128),
                           1024 + np.arange(r * 128, (r + 1) * 128), 1536 + np.arange(2 * r, 2 * r + 2)])
    wfox = np.ascontiguousarray(ew[:, cols].reshape(8, 128, 386).transpose(1, 0, 2)).reshape(128, 8 * 386)
    bfx = np.ascontiguousarray(inputs["fox_b_f"][0, 2 * r:2 * r + 2].reshape(2, 1))
    gc = np.concatenate([1544 + np.arange(r * 128, (r + 1) * 128), 1544 + 512 + np.arange(r * 128, (r + 1) * 128),
                         1544 + 1024 + np.arange(r * 128, (r + 1) * 128), 3088 + np.arange(r * 128, (r + 1) * 128),
                         [3080 + r], [3084 + r]])
    wgdn = np.ascontiguousarray(ew[:, gc].reshape(8, 128, 514).transpose(1, 0, 2)).reshape(128, 8 * 514)
    cwf = inputs["gdn_conv_w"][0]
    gcw = np.ascontiguousarray(np.stack([cwf[:, s_ * 512 + r * 128:s_ * 512 + (r + 1) * 128].T for s_ in range(3)], 1))
    gsc = np.array([[inputs["gdn_a_log"][0, r], inputs["gdn_dt_bias"][0, r]]], np.float32)
    pastk = np.ascontiguousarray(inputs["cache_fox_k"][0, c].transpose(1, 2, 0))
    pastv = np.ascontiguousarray(inputs["cache_fox_v"][0, c].reshape(1024, 512))
    pastlf = np.ascontiguousarray(inputs["cache_fox_logf"][0, c].T)
    sconv = np.ascontiguousarray(inputs["state_gdn_conv"][0, c].reshape(3, 3, 4, 128).transpose(2, 1, 3, 0))
    sstate = np.ascontiguousarray(inputs["state_gdn"][0, c])
    rmask = np.zeros((128, 4), np.float32)
    rmask[:, r] = 1.0
    ow = inputs["odd_w_in"][0]
    s5wu = np.ascontiguousarray(ow[:, r * 256:(r + 1) * 256].reshape(8, 128, 256).transpose(1, 0, 2)).reshape(128, 2048)
    par, bbr, bbi, ccr, cci = _s5_layouts(inputs, list(range(16 * r, 16 * r + 16)))
    s5d = np.ascontiguousarray(inputs["s5_d"][0, r * 256:(r + 1) * 256].reshape(2, 128).T)
    return {"xT": xT, "wfox": wfox, "bfx": bfx, "wgdn": wgdn, "gcw": gcw, "gsc": gsc,
            "pastk": pastk, "pastv": pastv, "pastlf": pastlf, "sconv": sconv, "sstate": sstate, "rmask": rmask,
            "s5wu": s5wu, "s5par": par, "s5bbr": bbr, "s5bbi": bbi, "s5ccr": ccr, "s5cci": cci, "s5d": s5d,
            "s5h0r": _state_tiles(inputs["state_s5_re"][0, c]), "s5h0i": _state_tiles(inputs["state_s5_im"][0, c])}


def shared_prep(inputs):
    w_in = inputs["ffn_w_in"].reshape(4, 8, 128, 2, 11, 256)
    w_in = np.ascontiguousarray(w_in.transpose(0, 4, 2, 1, 3, 5)).reshape(4, 11, 128, 4096)
    w_out = inputs["ffn_w_out"].reshape(4, 22, 128, 1024)
    w_out = np.ascontiguousarray(w_out.transpose(0, 2, 1, 3)).reshape(4, 128, 22 * 1024)
    g = inputs["ln_g"].reshape(6, 8, 128)
    bb = inputs["ln_b"].reshape(6, 8, 128)
    lnp = np.ascontiguousarray(np.stack([g, bb], 0).transpose(3, 0, 1, 2))
    ew = inputs["even_w_in"][0]
    wfox_s = np.ascontiguousarray(ew[:, :1544].reshape(8, 128, 1544).transpose(1, 0, 2)).reshape(128, 8 * 1544)
    bf_s = np.ascontiguousarray(inputs["fox_b_f"][0].reshape(8, 1))
    cwf = inputs["gdn_conv_w"][0]
    wg_l, cw_l, sc_l = [], [], []
    for r in range(4):
        gc = np.concatenate([1544 + np.arange(r * 128, (r + 1) * 128), 1544 + 512 + np.arange(r * 128, (r + 1) * 128),
                             1544 + 1024 + np.arange(r * 128, (r + 1) * 128), 3088 + np.arange(r * 128, (r + 1) * 128),
                             [3080 + r], [3084 + r]])
        wg_l.append(np.ascontiguousarray(ew[:, gc].reshape(8, 128, 514).transpose(1, 0, 2)).reshape(128, 8 * 514))
        cw_l.append(np.stack([cwf[:, s_ * 512 + r * 128:s_ * 512 + (r + 1) * 128].T for s_ in range(3)], 1))
        sc_l.append(np.array([[inputs["gdn_a_log"][0, r], inputs["gdn_dt_bias"][0, r]]], np.float32))
    ow = inputs["odd_w_in"][0]
    s5wu_s = np.ascontiguousarray(ow.reshape(8, 128, 1024).transpose(1, 0, 2)).reshape(128, 8192)
    par, bbr, bbi, ccr, cci = _s5_layouts(inputs, list(range(64)))
    return {"w_in": w_in, "w_out": w_out, "lnp": lnp, "cst": make_cst(), "wfox_s": wfox_s, "bf_s": bf_s,
            "wgdn_s": np.ascontiguousarray(np.stack(wg_l)), "gcw_s": np.ascontiguousarray(np.stack(cw_l)),
            "gsc_s": np.ascontiguousarray(np.stack(sc_l)),
            "gng": np.ascontiguousarray(inputs["gdn_norm_g"][0].reshape(128, 1)),
            "even_w_out": np.ascontiguousarray(inputs["even_w_out"][0]), "glu_w": np.ascontiguousarray(inputs["s5_glu_w"][0]),
            "odd_w_out": np.ascontiguousarray(inputs["odd_w_out"][0]),
            "glu_b": np.ascontiguousarray(inputs["s5_glu_b"][0].reshape(8, 128).T),
            "s5wu_s": s5wu_s, "s5par_s": par, "s5bbr_s": bbr, "s5bbi_s": bbi, "s5ccr_s": ccr, "s5cci_s": cci,
            "s5d_s": np.ascontiguousarray(inputs["s5_d"][0].reshape(8, 128).T)}


def assemble(results):
    B, T = 2, SEQ
    f = np.float32
    y_p = np.zeros((B, T, D), f); y_s = np.zeros((8, NS, D), f)
    fk = np.zeros((1, B, T, 8, 64), f); fv = np.zeros((1, B, T, 8, 64), f); fl = np.zeros((1, B, T, 8), f)
    gs = np.zeros((1, B, 4, 128, 128), f); gcv = np.zeros((1, B, 3, 1536), f)
    sre = np.zeros((1, B, 64, 64), f); sim_ = np.zeros((1, B, 64, 64), f)
    sk = np.zeros((1, 8, NS, 8, 64), f); sv = np.zeros((1, 8, NS, 8, 64), f); sl = np.zeros((1, 8, NS, 8), f)
    sgs = np.zeros((1, 8, 4, 128, 128), f); sgc = np.zeros((1, 8, 3, 1536), f)
    ssre = np.zeros((1, 8, 64, 64), f); ssim = np.zeros((1, 8, 64, 64), f)
    for c in range(NCORE):
        d = results[c]
        b, r = c // 4, c % 4
        y_p[b, r * TPC:(r + 1) * TPC] = d["o_y"].T
        y_s[c] = d["o_ys"].T
        for hl in range(2):
            fk[0, b, :, 2 * r + hl, :] = d["o_foxk"][hl * 64:(hl + 1) * 64].T
            fv[0, b, :, 2 * r + hl, :] = d["o_foxv"][:, hl * 64:(hl + 1) * 64]
            fl[0, b, :, 2 * r + hl] = d["o_logf"][hl]
        gs[0, b, r] = d["o_gstate"]
        for s_ in range(3):
            gcv[0, b, :, s_ * 512 + r * 128:s_ * 512 + (r + 1) * 128] = d["o_gconv"][s_].T
        sre[0, b, 16 * r:16 * r + 16] = d["o_s5re"].T.reshape(16, 64)
        sim_[0, b, 16 * r:16 * r + 16] = d["o_s5im"].T.reshape(16, 64)
        sk[0, c] = d["o_sfoxk"].T.reshape(NS, 8, 64)
        sv[0, c] = d["o_sfoxv"].reshape(NS, 8, 64)
        sl[0, c] = d["o_slogf"].T
        sgs[0, c] = d["o_sgstate"]
        sgc[0, c] = d["o_sgconv"].transpose(3, 1, 0, 2).reshape(3, 1536)
        ssre[0, c] = d["o_ss5re"].T.reshape(64, 64)
        ssim[0, c] = d["o_ss5im"].T.reshape(64, 64)
    return (y_p, y_s, fk, fv, fl, gs, gcv, sre, sim_, sk, sv, sl, sgs, sgc, ssre, ssim)


def kernel(**inputs):
    inputs = {k: np.asarray(v) for k, v in inputs.items()}
    nc = build()
    sh = shared_prep(inputs)
    in_maps = []
    for c in range(NCORE):
        m = dict(sh)
        m.update(host_prep(inputs, c))
        in_maps.append(m)
    res = run_bass_kernel_spmd(nc, in_maps, core_ids=list(range(NCORE)))
    return assemble(res.results)
```
